# Optimizing a Trainium2 kernel written in Bass

```python
import jax, jax.numpy as jnp
from jax import lax
import numpy as np

D_MODEL = 1024
BATCH = 8
SEQ = 4096
DEPTH = 1

HEAD_DIM = 64
RWKV_HEADS = 8
FOX_HEADS = 8
RWKV_WIDTH = RWKV_HEADS * HEAD_DIM
FOX_WIDTH = FOX_HEADS * HEAD_DIM
DECAY_RANK = 64
ICLR_RANK = 64
GATE_RANK = 128
D_FF = 4 * D_MODEL
Q_BLOCK = 128
NORM_EPS = 1e-6
GN_EPS = 64e-5
N_MOD = 6

RWKV_COLS = (RWKV_WIDTH, DECAY_RANK, RWKV_WIDTH, RWKV_WIDTH, ICLR_RANK, GATE_RANK)
FOX_COLS = (FOX_WIDTH, FOX_WIDTH, FOX_WIDTH, FOX_HEADS)
GATE_COLS = (D_MODEL, D_MODEL)
N_RWKV = 3 * RWKV_WIDTH + DECAY_RANK + ICLR_RANK + GATE_RANK
N_FOX = 3 * FOX_WIDTH + FOX_HEADS
N_GATE = 2 * D_MODEL
N_IN = N_RWKV + N_FOX + N_GATE

kernel_name = "rwkv7_fox_gated_hybrid_block"


def _split(t, sizes):
    idx = np.cumsum(np.array(sizes))[:-1].tolist()
    return jnp.split(t, idx, axis=-1)


def _rmsnorm(x, g):
    xf = x.astype(jnp.float32)
    y = xf * lax.rsqrt(jnp.mean(xf * xf, axis=-1, keepdims=True) + NORM_EPS)
    return (y * g.astype(jnp.float32)).astype(x.dtype)


def _modulate(x, g, shift, scale):
    return _rmsnorm(x, g) * (1.0 + scale[:, None, :]) + shift[:, None, :]


def _rwkv7_mix(p, w_decay_up, decay_base, w_iclr_up, iclr_base, w_gate_up,
               kk_scale, k_iclr_mix, r_bonus, lnx_w, lnx_b):
    B, S, _ = p.shape
    H, N = RWKV_HEADS, HEAD_DIM
    f32 = jnp.float32
    r, wd, k, v, ad, gd = _split(p, RWKV_COLS)
    w = -jax.nn.softplus(-(decay_base + jnp.tanh(wd) @ w_decay_up)) - 0.5
    decay = jnp.exp(-jnp.exp(w.astype(f32)))
    a = jax.nn.sigmoid(iclr_base + ad @ w_iclr_up)
    g = jax.nn.sigmoid(gd) @ w_gate_up
    heads = lambda t: t.reshape(B, S, H, N).astype(f32)
    kk = heads(k * kk_scale)
    kk = kk / jnp.maximum(jnp.sqrt(jnp.sum(kk * kk, axis=-1, keepdims=True)), 1e-12)
    k = heads(k * (1.0 + (a - 1.0) * k_iclr_mix))
    r_h, v_h, a_h, w_h = heads(r), heads(v), heads(a), heads(decay)
    a_vec = -kk
    b_vec = kk * a_h
    tm = lambda t: jnp.transpose(t, (1, 0, 2, 3))

    def step(state, inp):
        r_t, w_t, k_t, v_t, a_t, b_t = inp
        sa = jnp.einsum('bhij,bhj->bhi', state, a_t)
        state = (state * w_t[:, :, None, :] + sa[..., None] * b_t[:, :, None, :]
                 + v_t[..., None] * k_t[:, :, None, :])
        y_t = jnp.einsum('bhij,bhj->bhi', state, r_t)
        return state, y_t

    state0 = jnp.zeros((B, H, N, N), f32)
    _, y = lax.scan(step, state0, (tm(r_h), tm(w_h), tm(k), tm(v_h), tm(a_vec), tm(b_vec)))
    y = jnp.transpose(y, (1, 0, 2, 3))
    mu = jnp.mean(y, axis=-1, keepdims=True)
    var = jnp.mean(jnp.square(y - mu), axis=-1, keepdims=True)
    y = ((y - mu) * lax.rsqrt(var + GN_EPS)).reshape(B, S, H * N)
    y = y * lnx_w.astype(f32) + lnx_b.astype(f32)
    bonus = jnp.sum(r_h * k * r_bonus.astype(f32), axis=-1, keepdims=True) * v_h
    y = y + bonus.reshape(B, S, H * N)
    return (y.astype(p.dtype) * g)


def _forgetting_attention(q, k, v, f_logit, f_bias):
    B, S, _ = q.shape
    H, Dh = FOX_HEADS, HEAD_DIM
    nb = S // Q_BLOCK
    scale = 1.0 / np.sqrt(Dh).astype(np.float32)
    to_h = lambda t: t.reshape(B, S, H, Dh).transpose(0, 2, 1, 3)
    q, k, v = to_h(q), to_h(k), to_h(v)
    logf = jax.nn.log_sigmoid((f_logit + f_bias).astype(jnp.float32))
    cum = jnp.cumsum(logf, axis=1).transpose(0, 2, 1)
    qb = q.reshape(B, H, nb, Q_BLOCK, Dh).transpose(2, 0, 1, 3, 4)
    cb = cum.reshape(B, H, nb, Q_BLOCK).transpose(2, 0, 1, 3)
    kpos = jnp.arange(S)

    def block(args):
        qi, ci, i = args
        qpos = i * Q_BLOCK + jnp.arange(Q_BLOCK)
        s = (jnp.einsum('bhqd,bhkd->bhqk', qi, k).astype(jnp.float32) * scale
             + ci[..., :, None] - cum[..., None, :])
        s = jnp.where(kpos[None, :] <= qpos[:, None], s, -jnp.inf)
        pr = jax.nn.softmax(s, axis=-1)
        return jnp.einsum('bhqk,bhkd->bhqd', pr.astype(v.dtype), v)

    o = lax.map(block, (qb, cb, jnp.arange(nb)))
    return o.transpose(1, 0, 3, 2, 4).reshape(B, S, H * Dh)


def setup_inputs(seed: int = 0) -> dict:
    key = jax.random.key(seed)
    ks = jax.random.split(key, 32)
    L, D = DEPTH, D_MODEL
    nrm = lambda k, shape, s: jax.random.normal(k, shape, jnp.float32) * s
    uni = lambda k, shape, lo, hi: jax.random.uniform(k, shape, jnp.float32, lo, hi)
    return {
        "x": nrm(ks[0], (BATCH, SEQ, D), 1.0),
        "c": nrm(ks[1], (BATCH, D), 1.0),
        "w_ada": nrm(ks[2], (L, D, N_MOD * D), 0.5 * D ** -0.5),
        "b_ada": nrm(ks[3], (L, N_MOD * D), 0.02),
        "norm1_g": 1.0 + nrm(ks[4], (L, D), 0.02),
        "w_in": nrm(ks[5], (L, D, N_IN), D ** -0.5),
        "mu_shift": uni(ks[6], (L, N_RWKV), 0.0, 1.0),
        "w_decay_up": nrm(ks[7], (L, DECAY_RANK, RWKV_WIDTH), DECAY_RANK ** -0.5),
        "decay_base": uni(ks[8], (L, RWKV_WIDTH), -5.0, 1.0),
        "w_iclr_up": nrm(ks[9], (L, ICLR_RANK, RWKV_WIDTH), ICLR_RANK ** -0.5),
        "iclr_base": nrm(ks[10], (L, RWKV_WIDTH), 0.1),
        "w_gate_up": nrm(ks[11], (L, GATE_RANK, RWKV_WIDTH), GATE_RANK ** -0.5),
        "kk_scale": 0.85 + nrm(ks[12], (L, RWKV_WIDTH), 0.05),
        "k_iclr_mix": 1.0 + nrm(ks[13], (L, RWKV_WIDTH), 0.05),
        "r_bonus": nrm(ks[14], (L, RWKV_HEADS, HEAD_DIM), 0.1),
        "lnx_w": 1.0 + nrm(ks[15], (L, RWKV_WIDTH), 0.02),
        "lnx_b": nrm(ks[16], (L, RWKV_WIDTH), 0.02),
        "fox_f_bias": uni(ks[17], (L, FOX_HEADS), 1.0, 4.0),
        "w_o_rwkv": nrm(ks[18], (L, RWKV_WIDTH, D), RWKV_WIDTH ** -0.5),
        "w_o_fox": nrm(ks[19], (L, FOX_WIDTH, D), FOX_WIDTH ** -0.5),
        "w_out": nrm(ks[20], (L, D, D), D ** -0.5),
        "norm2_g": 1.0 + nrm(ks[21], (L, D), 0.02),
        "w_ff1": nrm(ks[22], (L, D, D_FF), D ** -0.5),
        "w_ff2": nrm(ks[23], (L, D_FF, D), D_FF ** -0.5),
        "final_g": 1.0 + nrm(ks[24], (D,), 0.02),
    }


def reference(x, c, w_ada, b_ada, norm1_g, w_in, mu_shift, w_decay_up, decay_base, w_iclr_up,
              iclr_base, w_gate_up, kk_scale, k_iclr_mix, r_bonus, lnx_w, lnx_b, fox_f_bias,
              w_o_rwkv, w_o_fox, w_out, norm2_g, w_ff1, w_ff2, final_g):
    c_act = jax.nn.silu(c)
    for l in range(DEPTH):
        mod = c_act @ w_ada[l] + b_ada[l]
        sh1, sc1, gt1, sh2, sc2, gt2 = jnp.split(mod, N_MOD, axis=-1)

        h = _modulate(x, norm1_g[l], sh1, sc1)
        proj = h @ w_in[l]
        p_rwkv, p_fox, p_gate = _split(proj, (N_RWKV, N_FOX, N_GATE))
        prev = jnp.pad(p_rwkv[:, :-1], ((0, 0), (1, 0), (0, 0)))
        p_rwkv = p_rwkv + mu_shift[l] * (prev - p_rwkv)
        y_a = _rwkv7_mix(p_rwkv, w_decay_up[l], decay_base[l], w_iclr_up[l], iclr_base[l],
                         w_gate_up[l], kk_scale[l], k_iclr_mix[l], r_bonus[l], lnx_w[l], lnx_b[l])
        fq, fk, fv, ff = _split(p_fox, FOX_COLS)
        y_b = _forgetting_attention(fq, fk, fv, ff, fox_f_bias[l])
        g_a, g_b = _split(p_gate, GATE_COLS)
        merged = (jax.nn.sigmoid(g_a) * (y_a @ w_o_rwkv[l])
                  + jax.nn.sigmoid(g_b) * (y_b @ w_o_fox[l]))
        x = x + gt1[:, None, :] * (merged @ w_out[l])

        h2 = _modulate(x, norm2_g[l], sh2, sc2)
        ff_out = jnp.square(jax.nn.relu(h2 @ w_ff1[l])) @ w_ff2[l]
        x = x + gt2[:, None, :] * ff_out
    return _rmsnorm(x, final_g)
```

```python
import numpy as np
from contextlib import ExitStack
import concourse.bass as bass
import concourse.mybir as mybir
from concourse.alu_op_type import AluOpType as ALU
from concourse.bass_utils import run_bass_kernel_spmd

F32 = mybir.dt.float32
BF16 = mybir.dt.bfloat16
AF = mybir.ActivationFunctionType
AX = mybir.AxisListType

D = 1024
NRW = 1792
NFX = 1544
NIN = 5384
EXPM05 = 0.6065306597126334
NORM_EPS = 1e-6
GN_EPS = 64e-5

PP_BADA, PP_G1, PP_G2, PP_C, PP_DB, PP_IB, PP_KS, PP_MIX, PP_RB, NPP = 0, 48, 56, 64, 72, 76, 80, 84, 88, 92
BC_MU, BC_LW, BC_LB, BC_FB, BC_FG, NBC = 0, 1792, 2304, 2816, 2824, 3848


class Buf:
    __slots__ = ("name", "lw", "rd")

    def __init__(self, name=""):
        self.name = name
        self.lw = None
        self.rd = {}


class Sync:
    def __init__(self, nc, es, n_dma_sems=32):
        self.nc = nc
        self.eng = {"pe": nc.tensor, "act": nc.scalar, "dve": nc.vector, "pool": nc.gpsimd, "sp": nc.sync}
        self.sem = {k: es.enter_context(nc.semaphore("sem_" + k)) for k in self.eng}
        self.cnt = {k: 0 for k in self.eng}
        self.dsem = [es.enter_context(nc.semaphore("dsem%d" % i)) for i in range(n_dma_sems)]
        self.dcnt = [0] * n_dma_sems
        self.dnext = 0
        self.dnext_sw = 0
        self.seen = {k: {} for k in self.eng}
        self.dead = False
        self.lazy = {"pe"}
        self.unflushed = {}
        self.last_inst = {}

    def _flush(self, key):
        if self.unflushed.get(key):
            self.last_inst[key].then_inc(self.sem[key], 1)
            self.cnt[key] += 1
            self.unflushed[key] = False

    def _wait(self, e, key, val):
        if self.seen[e].get(key, 0) >= val:
            return
        if isinstance(key, str) and val > self.cnt[key]:
            assert key in self.lazy and val == self.cnt[key] + 1, (key, val, self.cnt[key])
            self._flush(key)
        sem = self.sem[key] if isinstance(key, str) else self.dsem[key]
        self.eng[e].wait_ge(sem, val)
        self.seen[e][key] = val

    def _deps(self, e, reads, writes):
        deps = {}

        def add(k, v):
            if deps.get(k, 0) < v:
                deps[k] = v
        for b in reads:
            if b.lw is not None:
                add(*b.lw)
        for b in writes:
            if b.lw is not None:
                add(*b.lw)
            for k, v in b.rd.items():
                add(k, v)
        for k, v in deps.items():
            if k == e and e == "pe":
                continue
            self._wait(e, k, v)

    def _post(self, ev, reads, writes):
        for b in reads:
            if b.rd.get(ev[0], 0) < ev[1]:
                b.rd[ev[0]] = ev[1]
        for b in writes:
            b.lw = ev
            b.rd = {}

    def op(self, e, fn, reads=(), writes=()):
        if self.dead:
            return None
        if e != "pe":
            pr_ = [b for b in reads if b.name.startswith("bank")]
            if pr_:
                reads = [b for b in reads if not b.name.startswith("bank")]
                writes = list(writes) + pr_
        self._deps(e, reads, writes)
        inst = fn(self.eng[e])
        if e in self.lazy:
            self.last_inst[e] = inst
            self.unflushed[e] = True
            self._post((e, self.cnt[e] + 1), reads, writes)
            return inst
        self.cnt[e] += 1
        inst.then_inc(self.sem[e], 1)
        self._post((e, self.cnt[e]), reads, writes)
        return inst

    def dma(self, e, out, in_, reads=(), writes=()):
        if self.dead:
            return None
        nsw = 8
        if e == "pool":
            k = self.dnext_sw
            self.dnext_sw = (self.dnext_sw + 1) % nsw
        else:
            k = nsw + self.dnext
            self.dnext = (self.dnext + 1) % (len(self.dsem) - nsw)
        if self.dcnt[k] > 0:
            self._wait(e, k, self.dcnt[k])
        self._deps(e, reads, writes)
        inst = self.eng[e].dma_start(out=out, in_=in_)
        self.dcnt[k] += 16
        inst.then_inc(self.dsem[k], 16)
        self._post((k, self.dcnt[k]), reads, writes)
        return inst

    def barrier(self):
        for k in list(self.lazy):
            self._flush(k)
        for e in self.eng:
            for k in self.eng:
                if self.cnt[k]:
                    self._wait(e, k, self.cnt[k])
            for k in range(len(self.dsem)):
                if self.dcnt[k]:
                    self._wait(e, k, self.dcnt[k])

    def drain(self, e="sp"):
        for k in list(self.lazy):
            self._flush(k)
        for k in range(len(self.dsem)):
            if self.dcnt[k]:
                self._wait(e, k, self.dcnt[k])
        for k in self.eng:
            if k != e and self.cnt[k]:
                self._wait(e, k, self.cnt[k])


class T:
    def __init__(self, es, nc, name, shape, dtype):
        self.t = es.enter_context(nc.sbuf_tensor("sb_" + name, shape, dtype))
        self.b = Buf(name)

    def __getitem__(self, idx):
        return self.t[idx]


class _Stop(Exception):
    pass


RR = [1, 1, 1, 1]
ORDER = [0, 1, 2, 3]
NDUMMY = 0


def build(S, dbg=False, stop=None):
    assert S % 512 == 0

    def CP(name):
        if stop == name:
            sy_box[0].dead = True

    sy_box = [None]
    NT = S // 512
    NS = S // 128
    nc = bass.Bass("TRN2", target_bir_lowering=False)

    def din(n, shp, dt=F32):
        return nc.dram_tensor(n, shp, dt, kind="ExternalInput").ap()

    def dscr(n, shp, dt):
        return nc.dram_tensor(n, shp, dt, kind="ExternalOutput" if dbg else "Internal").ap()

    x_d = din("x", [S, D])
    pp_d = din("pp", [128, NPP])
    bc_d = din("bc", [128, NBC])
    brow_d = din("brow", [1, 2048])
    wada_d = din("w_ada", [D, 6 * D])
    win_d = din("w_in", [D, NIN])
    wdu_d = din("w_decay_up", [64, 512])
    wiu_d = din("w_iclr_up", [64, 512])
    wgu_d = din("w_gate_up", [128, 512])
    wor_d = din("w_o_rwkv", [512, D])
    wof_d = din("w_o_fox", [512, D])
    wout_d = din("w_out", [D, D])
    wff1_d = din("w_ff1", [D, 4 * D])
    wff2_d = din("w_ff2", [4 * D, D])
    out_d = nc.dram_tensor("out", [S, D], F32, kind="ExternalOutput").ap()

    hT_d = dscr("hT_s", [128, 8, S], BF16)
    yaT_d = dscr("yaT_s", [128, 4, S], BF16)
    ybT_d = dscr("ybT_s", [512, S], BF16)
    x1_d = dscr("x1_s", [S, D], F32)
    h2T_d = dscr("h2T_s", [128, 8, S], BF16)
    gt_d = dscr("gt_s", [128, 2048], F32)
    B_hT, B_yaT, B_ybT, B_x1, B_h2T, B_gt, B_out = (Buf(n) for n in "hT yaT ybT x1 h2T gt out".split())

    with ExitStack() as es:
        sy = Sync(nc, es)
        sy_box[0] = sy
        OP = sy.op
        DMA = sy.dma

        def SB(scope, name, shape, dt):
            return T(scope, nc, name, shape, dt)

        banks = [es.enter_context(nc.psum_tensor("bank%d" % i, [128, 512], F32)) for i in range(8)]
        bbufs = [Buf("bank%d" % i) for i in range(8)]
        pstate = {"i": 0}

        held = set()

        def PS(hold=False):
            assert len(held) < 8, "all PSUM banks held"
            while True:
                i = pstate["i"]
                pstate["i"] = (i + 1) % 8
                if i not in held:
                    break
            if hold:
                held.add(i)
            return banks[i], bbufs[i]

        def PSREL(bank):
            held.discard(banks.index(bank))

        def MM(out, lhsT, rhs, start, stop, reads, pb):
            OP("pe", lambda e: e.matmul(out, lhsT, rhs, start=start, stop=stop, skip_group_check=True),
               reads=reads, writes=[pb])

        def TR(out, in_, ident, reads, pb):
            OP("pe", lambda e: e.transpose(out, in_, ident), reads=reads, writes=[pb])

        ppt = SB(es, "ppt", [128, NPP], F32)
        ident = SB(es, "ident", [128, 128], BF16)
        modT = SB(es, "modT", [128, 48], F32)
        GS = SB(es, "GS", [128, 32], F32)
        DMA("sp", ppt[:], pp_d, writes=[ppt.b])
        OP("pool", lambda e: e.memset(ident[:], 1.0), writes=[ident.b])
        OP("pool", lambda e: e.affine_select(out=ident[:], in_=ident[:], pattern=[[-1, 128]],
                                             compare_op=ALU.is_equal, fill=0.0, base=0, channel_multiplier=1),
           reads=[ident.b], writes=[ident.b])

        with ExitStack() as s0:
            wada = SB(s0, "wada", [128, 8, 6 * D], BF16)
            wada_b = [Buf("wada%d" % k) for k in range(8)]
            cact = SB(s0, "cact", [128, 8], BF16)
            crep = SB(s0, "crep", [128, 8, 128], BF16)
            onesf = SB(s0, "onesf", [1, 128], F32)
            brow = SB(s0, "brow", [1, 2048], F32)
            gtbc = SB(s0, "gtbc", [128, 2048], F32)
            wadaB_b = [Buf("wadaB%d" % k) for k in range(8)]
            for kc in range(8):
                DMA("pool", wada[:, kc, 0:2 * D], wada_d[kc * 128:(kc + 1) * 128, 0:2 * D], writes=[wada_b[kc]])
            for kc in range(8):
                DMA("pool", wada[:, kc, 2 * D:6 * D], wada_d[kc * 128:(kc + 1) * 128, 2 * D:6 * D], writes=[wadaB_b[kc]])
            DMA("sp", brow[:], brow_d, writes=[brow.b])
            OP("dve", lambda e: e.memset(onesf[:], 1.0), writes=[onesf.b])
            OP("act", lambda e: e.activation(out=cact[:], in_=ppt[:, PP_C:PP_C + 8], func=AF.Silu),
               reads=[ppt.b], writes=[cact.b])
            for kc in range(8):
                OP("dve", lambda e, kc=kc: e.tensor_copy(out=crep[:, kc, :],
                                                         in_=cact[:, kc:kc + 1].broadcast_to([128, 128])),
                   reads=[cact.b], writes=[crep.b])
            pa, pab = PS(hold=True)
            for j in range(16):
                for kc in range(8):
                    MM(pa[:, j:j + 1], wada[:, kc, j * 128:(j + 1) * 128], cact[:, kc:kc + 1],
                       kc == 0, kc == 7, [wada_b[kc], cact.b], pab)
            OP("dve", lambda e: e.tensor_tensor(out=modT[:, 0:16], in0=pa[:, 0:16], in1=ppt[:, PP_BADA:PP_BADA + 16],
                                                op=ALU.add), reads=[pab, ppt.b], writes=[modT.b])
            PSREL(pa)
            OP("dve", lambda e: e.scalar_tensor_tensor(out=GS[:, 0:8], in0=modT[:, 8:16], scalar=1.0,
                                                       in1=ppt[:, PP_G1:PP_G1 + 8], op0=ALU.add, op1=ALU.mult),
               reads=[modT.b, ppt.b], writes=[GS.b])
            OP("dve", lambda e: e.tensor_copy(out=GS[:, 8:16], in_=modT[:, 0:8]), reads=[modT.b], writes=[GS.b])

            def mod_part2():
                GS2 = Buf("GS2")
                pa2_, pa2b_ = PS(hold=True)
                for j in range(16, 48):
                    for kc in range(8):
                        MM(pa2_[:, j:j + 1], wada[:, kc, j * 128:(j + 1) * 128], cact[:, kc:kc + 1],
                           kc == 0, kc == 7, [wadaB_b[kc], cact.b], pa2b_)
                OP("dve", lambda e: e.tensor_tensor(out=modT[:, 16:48], in0=pa2_[:, 16:48], in1=ppt[:, PP_BADA + 16:PP_BADA + 48],
                                                    op=ALU.add), reads=[pa2b_, ppt.b], writes=[modT.b])
                PSREL(pa2_)
                OP("dve", lambda e: e.scalar_tensor_tensor(out=GS[:, 16:24], in0=modT[:, 32:40], scalar=1.0,
                                                           in1=ppt[:, PP_G2:PP_G2 + 8], op0=ALU.add, op1=ALU.mult),
                   reads=[modT.b, ppt.b], writes=[GS.b])
                OP("dve", lambda e: e.tensor_copy(out=GS[:, 24:32], in_=modT[:, 24:32]), reads=[modT.b], writes=[GS.b])
                for part, col0 in enumerate((2 * D, 5 * D)):
                    for half in range(2):
                        pg, pgb = PS(hold=True)
                        for kc in range(8):
                            MM(pg[:, :], crep[:, kc, :], wada[:, kc, col0 + half * 512: col0 + (half + 1) * 512],
                               kc == 0, False, [crep.b, wadaB_b[kc]], pgb)
                        o = part * 1024 + half * 512
                        MM(pg[:, :], onesf[0:1, :], brow[0:1, o:o + 512], False, True, [onesf.b, brow.b], pgb)
                        OP("act", lambda e, pg=pg, o=o: e.copy(out=gtbc[:, o:o + 512], in_=pg[:, :]),
                           reads=[pgb], writes=[gtbc.b])
                        PSREL(pg)
                DMA("sp", gt_d, gtbc[:], reads=[gtbc.b], writes=[B_gt])

            xb = [SB(s0, "xb%d" % i, [128, 4, D], F32) for i in range(2)]
            xn = [SB(s0, "xn%d" % i, [128, 4, D], BF16) for i in range(2)]
            hTs = [SB(s0, "hTs%d" % i, [128, 8, 512], BF16) for i in range(2)]
            junk = SB(s0, "junk0", [128, D], BF16)
            ssq = [SB(s0, "ssq%d" % i, [128, 4], F32) for i in range(2)]
            rstd = [SB(s0, "rstd%d" % i, [128, 4], F32) for i in range(2)]
            def load_xt(i):
                if i < NT:
                    DMA("sp", xb[i % 2][:], x_d[i * 512:(i + 1) * 512, :].rearrange("(s p) d -> p s d", p=128), writes=[xb[i % 2].b])

            def norm_a(i):
                xt, xnt, sq, rs = xb[i % 2], xn[i % 2], ssq[i % 2], rstd[i % 2]
                for s in range(4):
                    OP("act", lambda e, s=s: e.activation(out=junk[:], in_=xt[:, s, :], func=AF.Square, accum_out=sq[:, s:s + 1]),
                       reads=[xt.b], writes=[junk.b, sq.b])
                OP("dve", lambda e: e.tensor_scalar(out=rs[:], in0=sq[:], scalar1=1.0 / D, scalar2=NORM_EPS,
                                                    op0=ALU.mult, op1=ALU.add), reads=[sq.b], writes=[rs.b])
                OP("act", lambda e: e.activation(out=rs[:], in_=rs[:], func=AF.Sqrt), reads=[rs.b], writes=[rs.b])
                OP("dve", lambda e: e.reciprocal(out=rs[:], in_=rs[:]), reads=[rs.b], writes=[rs.b])
                for s in range(4):
                    if s % 2 == 0:
                        OP("dve", lambda e, s=s: e.tensor_scalar(out=xnt[:, s, :], in0=xt[:, s, :],
                                                                 scalar1=rs[:, s:s + 1], scalar2=None, op0=ALU.mult),
                           reads=[xt.b, rs.b], writes=[xnt.b])
                    else:
                        OP("act", lambda e, s=s: e.activation(out=xnt[:, s, :], in_=xt[:, s, :], func=AF.Copy, scale=rs[:, s:s + 1]),
                           reads=[xt.b, rs.b], writes=[xnt.b])

            def norm_b(i):
                xnt, ht = xn[i % 2], hTs[i % 2]
                for kc in range(8):
                    p, pb = PS()
                    pv = p.bitcast(BF16)
                    for s in range(4):
                        TR(pv[:, s * 128:(s + 1) * 128], xnt[:, s, kc * 128:(kc + 1) * 128], ident[:], [xnt.b, ident.b], pb)
                    if kc % 2 == 0:
                        OP("dve", lambda e, kc=kc, pv=pv: e.tensor_scalar(
                            out=ht[:, kc, :], in0=pv[:, 0:512], scalar1=GS[:, kc:kc + 1], scalar2=GS[:, 8 + kc:9 + kc],
                            op0=ALU.mult, op1=ALU.add), reads=[pb, GS.b], writes=[ht.b])
                    else:
                        OP("act", lambda e, kc=kc, pv=pv: e.activation(
                            out=ht[:, kc, :], in_=pv[:, 0:512], func=AF.Identity, scale=GS[:, kc:kc + 1],
                            bias=GS[:, 8 + kc:9 + kc]), reads=[pb, GS.b], writes=[ht.b])
                DMA("sp", hT_d[:, :, i * 512:(i + 1) * 512], ht[:], reads=[ht.b], writes=[B_hT])

            load_xt(0)
            load_xt(1)
            norm_a(0)
            for i in range(NT):
                if i + 1 < NT:
                    norm_a(i + 1)
                load_xt(i + 2)
                norm_b(i)
            mod_part2()


        sy.barrier()
        try:
          with ExitStack() as sr:
              WA = SB(sr, "WA", [128, 8, NRW], BF16)
              WB = SB(sr, "WB", [128, 8, NRW], BF16)
              wdu = SB(sr, "wdu", [64, 512], BF16)
              wiu = SB(sr, "wiu", [128, 512], BF16)
              wgu = SB(sr, "wgu", [128, 512], BF16)
              lnw = SB(sr, "lnw", [128, 1024], F32)
              DMA("pool", wdu[:], wdu_d, writes=[wdu.b])
              DMA("pool", wiu[64:128, :], wiu_d, writes=[wiu.b])
              DMA("pool", wgu[:], wgu_d, writes=[wgu.b])
              DMA("sp", lnw[:], bc_d[:, BC_LW:BC_LW + 1024], writes=[lnw.b])
              with ExitStack() as sw:
                  mub = SB(sw, "mub", [128, NRW], F32)
                  omub = SB(sw, "omub", [128, NRW], F32)
                  wst = [SB(sw, "wst%d" % i, [128, NRW], F32) for i in range(2)]
                  DMA("sp", mub[:], bc_d[:, BC_MU:BC_MU + NRW], writes=[mub.b])
                  OP("dve", lambda e: e.tensor_scalar(out=omub[:], in0=mub[:], scalar1=-1.0, scalar2=1.0,
                                                      op0=ALU.mult, op1=ALU.add), reads=[mub.b], writes=[omub.b])
                  segs = [(0, 512, 0), (576, 1088, 512), (1088, 1600, 1024), (512, 576, 1536), (1600, 1664, 1600),
                          (1664, 1792, 1664)]
                  for kc in range(8):
                      w = wst[kc % 2]
                      for (a, b_, o) in segs:
                          DMA("sp", w[:, o:o + (b_ - a)], win_d[kc * 128:(kc + 1) * 128, a:b_], writes=[w.b])
                      OP("dve", lambda e, kc=kc, w=w: e.tensor_tensor(out=WB[:, kc, :], in0=w[:], in1=mub[:], op=ALU.mult),
                         reads=[w.b, mub.b], writes=[WB.b])
                      OP("dve", lambda e, kc=kc, w=w: e.tensor_tensor(out=WA[:, kc, :], in0=w[:], in1=omub[:], op=ALU.mult),
                         reads=[w.b, omub.b], writes=[WA.b])
              sy.barrier()
              CP('w')

              blk1 = SB(sr, "blk1", [128, 128], F32)
              hsel = SB(sr, "hsel", [128, 2], BF16)
              mskX = SB(sr, "mskX", [128, 4, 2, 64], F32)
              mskL = SB(sr, "mskL", [128, 8, 64], F32)
              i64 = SB(sr, "i64", [128, 64], F32)
              scm = SB(sr, "scm", [128, 512], F32)
              omix = SB(sr, "omix", [128, 4], F32)
              OP("pool", lambda e: e.memset(blk1[:], 0.0), writes=[blk1.b])
              OP("pool", lambda e: e.memset(blk1[0:64, 0:64], 1.0), writes=[blk1.b])
              OP("pool", lambda e: e.memset(blk1[64:128, 64:128], 1.0), writes=[blk1.b])
              OP("pool", lambda e: e.memset(hsel[:], 0.0), writes=[hsel.b])
              OP("pool", lambda e: e.memset(hsel[0:64, 0:1], 1.0), writes=[hsel.b])
              OP("pool", lambda e: e.memset(hsel[64:128, 1:2], 1.0), writes=[hsel.b])
              OP("pool", lambda e: e.memset(mskX[:], 1.0), writes=[mskX.b])
              OP("pool", lambda e: e.memset(mskL[:], 1.0), writes=[mskL.b])
              OP("pool", lambda e: e.memset(i64[:], 1.0), writes=[i64.b])
              for hf in range(2):
                  ps_ = slice(hf * 64, (hf + 1) * 64)
                  OP("pool", lambda e, ps_=ps_: e.affine_select(out=mskX[ps_], in_=mskX[ps_], pattern=[[0, 4], [1, 2], [1, 64]],
                                                                compare_op=ALU.is_gt, fill=0.0, base=0, channel_multiplier=-1),
                     reads=[mskX.b], writes=[mskX.b])
                  OP("pool", lambda e, ps_=ps_: e.affine_select(out=mskL[ps_], in_=mskL[ps_], pattern=[[0, 8], [-1, 64]],
                                                                compare_op=ALU.is_gt, fill=0.0, base=0, channel_multiplier=1),
                     reads=[mskL.b], writes=[mskL.b])
                  OP("pool", lambda e, ps_=ps_: e.affine_select(out=i64[ps_], in_=i64[ps_], pattern=[[-1, 64]],
                                                                compare_op=ALU.is_equal, fill=0.0, base=0, channel_multiplier=1),
                     reads=[i64.b], writes=[i64.b])
              OP("pool", lambda e: e.memset(scm[:], 1.0), writes=[scm.b])
              OP("pool", lambda e: e.memset(scm[:].rearrange("p (a b) -> p a b", b=64)[:, :, 0:1], 0.0), writes=[scm.b])
              OP("dve", lambda e: e.tensor_scalar(out=omix[:], in0=ppt[:, PP_MIX:PP_MIX + 4], scalar1=-1.0, scalar2=1.0,
                                                  op0=ALU.mult, op1=ALU.add), reads=[ppt.b], writes=[omix.b])

              CP('c')

              def bc4(col0):
                  return ppt[:, col0:col0 + 4].unsqueeze(2).broadcast_to([128, 4, 128])

              Hs = SB(sr, "Hs", [128, 4, 64], F32)
              Hb2 = [SB(sr, "Hb%d" % i_, [128, 4, 64], BF16) for i_ in range(2)]
              OP("dve", lambda e: e.memset(Hs[:], 0.0), writes=[Hs.b])
              OP("dve", lambda e: e.memset(Hb2[0][:], 0.0), writes=[Hb2[0].b])

              def DB(name, shape, dt):
                  return [SB(sr, "%s_%d" % (name, i_), shape, dt) for i_ in range(2)]
              hcs = DB("hc", [128, 8, 128], BF16)
              hps = DB("hp", [128, 8, 128], BF16)
              rk2 = DB("rk", [128, 8, 128], F32)
              tw2 = DB("tw", [64, 128], BF16)
              adb2 = DB("adb", [128, 128], BF16)
              sgd2 = DB("sgd", [128, 128], BF16)
              Vf2 = [SB(sr, "Vf_%d" % i_, [128, 512], F32) for i_ in range(4)]
              Vb2 = [SB(sr, "Vb_%d" % i_, [128, 512], BF16) for i_ in range(4)]

              def TB(name, shape, dt):
                  return [SB(sr, "%s_%d" % (name, i_), shape, dt) for i_ in range(3)]
              sw_ = SB(sr, "sw_", [128, 512], F32)
              aic = SB(sr, "aic", [128, 4, 128], F32)
              gsb2 = TB("gsb", [128, 512], F32)
              ld = SB(sr, "ld", [128, 512], F32)
              lP = SB(sr, "lP", [128, 512], F32)
              lPx = SB(sr, "lPx", [128, 512], F32)
              Pc2 = TB("Pc", [128, 4, 128], F32)
              Pinv = SB(sr, "Pinv", [128, 4, 128], F32)
              Pex = SB(sr, "Pex", [128, 4, 128], F32)
              ksc = SB(sr, "ksc", [128, 4, 128], F32)
              ksq = SB(sr, "ksq", [128, 4, 128], F32)
              rn = SB(sr, "rn", [128, 4, 128], F32)
              kk = SB(sr, "kk", [128, 4, 128], F32)
              t1 = SB(sr, "t1", [128, 4, 128], F32)
              kmod = SB(sr, "kmod", [128, 4, 128], F32)
              t2 = SB(sr, "t2", [128, 4, 128], F32)
              AR2 = TB("AR", [128, 4, 2, 2, 64], BF16)
              bT = SB(sr, "bT", [128, 4, 128], BF16)
              kT = SB(sr, "kT", [128, 4, 128], BF16)
              rkb2 = TB("rkb", [128, 4, 128], BF16)
              bt2 = TB("bt", [128, 512], BF16)
              kt2 = TB("kt", [128, 512], BF16)
              MB2 = TB("MB", [128, 8, 2, 64], BF16)
              MK2 = TB("MK", [128, 8, 2, 64], BF16)
              Acur2 = [[SB(sr, "Acur%d_%d" % (j_, i), [128, 8, 64], BF16) for i in range(2)] for j_ in range(2)]
              Mcur2 = [[SB(sr, "Mcur%d_%d" % (j_, i), [128, 8, 64], BF16) for i in range(2)] for j_ in range(2)]
              Xc2 = [[SB(sr, "Xc%d_%d" % (j_, i), [128, 8, 64], BF16) for i in range(2)] for j_ in range(2)]
              W0sb2 = TB("W0sb", [128, 512], F32)
              Wsb = SB(sr, "Wsb", [128, 512], BF16)
              Usb = SB(sr, "Usb", [128, 512], BF16)
              Htmp = SB(sr, "Htmp", [128, 4, 64], F32)
              yf = SB(sr, "yf", [128, 8, 64], F32)
              ysq = SB(sr, "ysq", [128, 8, 64], F32)
              st = SB(sr, "st", [128, 32], F32)
              bs = SB(sr, "bs", [128, 8], F32)
              yg = SB(sr, "yg", [128, 512], BF16)
              yaTs = [SB(sr, "yaTs%d" % i, [128, 4, 128], BF16) for i in range(2)]

              def hsl(h):
                  return h // 2, slice((h % 2) * 64, (h % 2) * 64 + 64)
              fl = lambda t_: t_[:].rearrange("p a t -> p (a t)")
              v4 = lambda t_: t_[:].rearrange("p a (c t) -> p a c t", c=2)
              par = lambda ap_, e_: ap_.rearrange("p (a e v) -> p e a v", e=2, v=64)[:, e_]
              b8 = lambda a_: a_.unsqueeze(2).broadcast_to([128, 8, 64])
              Xfin = {}

              def phaseP(n):
                  d = n % 2
                  hh, hold, hp = hcs[d], hcs[1 - d], hps[d]
                  rk, tw, adb, sgd = rk2[d], tw2[d], adb2[d], sgd2[d]
                  Vf, Vb = Vf2[n % 4], Vb2[n % 4]
                  DMA("sp", hh[:], hT_d[:, :, n * 128:(n + 1) * 128], writes=[hh.b])
                  if n == 0:
                      OP("pool", lambda e: e.memset(hp[:, :, 0:1], 0.0), writes=[hp.b])
                      DMA("sp", hp[:, :, 1:128], hT_d[:, :, 0:127], writes=[hp.b])
                  else:
                      DMA("sp", hp[:], hT_d[:, :, n * 128 - 1:(n + 1) * 128 - 1], writes=[hp.b])
                  yield

                  def fproj(pt, ptb, slot, col0):
                      for i_ in range(16):
                          kc, sh = i_ % 8, i_ // 8
                          W = WA if sh == 0 else WB
                          hx = hh if sh == 0 else hp
                          MM(pt[:, slot * 128:(slot + 1) * 128], W[:, kc, col0:col0 + 128], hx[:, kc, :], i_ == 0, i_ == 15,
                             [W.b, hx.b], ptb)
                  pr, prb = PS(hold=True)
                  for c4 in range(4):
                      fproj(pr, prb, c4, c4 * 128)
                      if c4 % 2 == 1:
                          yield
                  OP("act", lambda e: e.copy(out=rk[:, 0:4, :], in_=pr[:, :].rearrange("p (a t) -> p a t", a=4)), reads=[prb], writes=[rk.b])
                  PSREL(pr)
                  pk, pkb = PS(hold=True)
                  for c4 in range(4):
                      fproj(pk, pkb, c4, 512 + c4 * 128)
                      if c4 % 2 == 1:
                          yield
                  OP("dve", lambda e: e.tensor_scalar(out=rk[:, 4:8, :], in0=pk[:, :].rearrange("p (a t) -> p a t", a=4), scalar1=1.0,
                                                      scalar2=None, op0=ALU.mult), reads=[pkb], writes=[rk.b])
                  PSREL(pk)
                  pw, pwb = PS(hold=True)
                  fproj(pw, pwb, 0, 1536)
                  yield
                  fproj(pw, pwb, 1, 1664)
                  OP("act", lambda e: e.activation(out=tw[:], in_=pw[0:64, 0:128], func=AF.Tanh), reads=[pwb], writes=[tw.b])
                  OP("dve", lambda e: e.tensor_scalar(out=adb[64:128, :], in0=pw[64:128, 0:128], scalar1=1.0, scalar2=None, op0=ALU.mult),
                     reads=[pwb], writes=[adb.b])
                  OP("act", lambda e: e.activation(out=sgd[:], in_=pw[:, 128:256], func=AF.Sigmoid), reads=[pwb], writes=[sgd.b])
                  PSREL(pw)
                  yield
                  pv_, pvb = PS(hold=True)
                  for i_ in range(16):
                      kc, sh = i_ % 8, i_ // 8
                      W = WA if sh == 0 else WB
                      hx = hh if sh == 0 else hp
                      MM(pv_[:, :], hx[:, kc, :], W[:, kc, 1024:1536], i_ == 0, i_ == 15, [W.b, hx.b], pvb)
                      if i_ == 7:
                          yield
                  OP("act", lambda e: e.copy(out=Vf[:], in_=pv_[:, :]), reads=[pvb], writes=[Vf.b])
                  OP("dve", lambda e: e.tensor_scalar(out=Vb[:], in0=pv_[:, :], scalar1=1.0, scalar2=None, op0=ALU.mult), reads=[pvb], writes=[Vb.b])
                  PSREL(pv_)

              def phaseA(n):
                  d = n % 2
                  t3 = n % 3
                  rk, tw, adb, sgd = rk2[d], tw2[d], adb2[d], sgd2[d]
                  Vf, Vb = Vf2[n % 4], Vb2[n % 4]
                  gsb, Pc, AR, rkb, bt, kt, MB, MK, W0sb = (gsb2[t3], Pc2[t3], AR2[t3], rkb2[t3], bt2[t3],
                                                           kt2[t3], MB2[t3], MK2[t3], W0sb2[t3])
                  Xc = Xc2[d]
                  Acur, Mcur = Acur2[d], Mcur2[d]
                  r4 = rk[:, 0:4, :]
                  k4 = rk[:, 4:8, :]
                  pz, pzb = PS(hold=True)
                  pa_, pab_ = PS(hold=True)
                  pg_, pgb_ = PS(hold=True)
                  for c4 in range(4):
                      MM(pz[:, c4 * 128:(c4 + 1) * 128], wdu[0:64, c4 * 128:(c4 + 1) * 128], tw[0:64, :], True, True, [wdu.b, tw.b], pzb)
                  for c4 in range(4):
                      MM(pa_[:, c4 * 128:(c4 + 1) * 128], wiu[64:128, c4 * 128:(c4 + 1) * 128], adb[64:128, :], True, True, [wiu.b, adb.b], pab_)
                  MM(pg_[:, :], sgd[:, :], wgu[:, :], True, True, [sgd.b, wgu.b], pgb_)
                  for c4 in range(4):
                      OP("act", lambda e, c4=c4: e.activation(out=sw_[:, c4 * 128:(c4 + 1) * 128], in_=pz[:, c4 * 128:(c4 + 1) * 128],
                                                              func=AF.Sigmoid, bias=ppt[:, PP_DB + c4:PP_DB + c4 + 1]),
                         reads=[pzb, ppt.b], writes=[sw_.b])
                  PSREL(pz)
                  for c4 in range(4):
                      OP("act", lambda e, c4=c4: e.activation(out=aic[:, c4, :], in_=pa_[:, c4 * 128:(c4 + 1) * 128],
                                                              func=AF.Sigmoid, bias=ppt[:, PP_IB + c4:PP_IB + c4 + 1]),
                         reads=[pab_, ppt.b], writes=[aic.b])
                  PSREL(pa_)
                  OP("act", lambda e: e.copy(out=gsb[:], in_=pg_[:, :]), reads=[pgb_], writes=[gsb.b])
                  PSREL(pg_)
                  yield
                  OP("dve", lambda e: e.tensor_tensor(out=ksc[:], in0=k4, in1=bc4(PP_KS), op=ALU.mult), reads=[rk.b, ppt.b], writes=[ksc.b])
                  OP("act", lambda e: e.activation(out=fl(ksq), in_=fl(ksc), func=AF.Square), reads=[ksc.b], writes=[ksq.b])
                  for c4 in range(4):
                      OP("act", lambda e, c4=c4: e.activation(out=t1[:, c4, :], in_=aic[:, c4, :], func=AF.Identity,
                                                              scale=ppt[:, PP_MIX + c4:PP_MIX + c4 + 1], bias=omix[:, c4:c4 + 1]),
                         reads=[aic.b, ppt.b, omix.b], writes=[t1.b])
                  OP("dve", lambda e: e.tensor_tensor(out=kmod[:], in0=k4, in1=t1[:], op=ALU.mult), reads=[rk.b, t1.b], writes=[kmod.b])
                  yield
                  OP("dve", lambda e: e.tensor_scalar(out=ld[:], in0=sw_[:], scalar1=-EXPM05, scalar2=None, op0=ALU.mult), reads=[sw_.b], writes=[ld.b])
                  OP("dve", lambda e: e.tensor_tensor_scan(out=lP[:], data0=scm[:], data1=ld[:], initial=0.0, op0=ALU.mult, op1=ALU.add),
                     reads=[scm.b, ld.b], writes=[lP.b])
                  OP("dve", lambda e: e.tensor_tensor(out=lPx[:], in0=lP[:], in1=ld[:], op=ALU.subtract), reads=[lP.b, ld.b], writes=[lPx.b])
                  pq, pqb = PS(hold=True)
                  MM(pq[:, :], blk1[:, :], fl(ksq), True, True, [blk1.b, ksq.b], pqb)
                  OP("act", lambda e: e.activation(out=fl(Pc), in_=lP[:], func=AF.Exp), reads=[lP.b], writes=[Pc.b])
                  OP("act", lambda e: e.activation(out=fl(Pinv), in_=lP[:], func=AF.Exp, scale=-1.0), reads=[lP.b], writes=[Pinv.b])
                  OP("act", lambda e: e.activation(out=fl(Pex), in_=lPx[:], func=AF.Exp), reads=[lPx.b], writes=[Pex.b])
                  OP("act", lambda e: e.activation(out=fl(rn), in_=pq[:, :], func=AF.Ln), reads=[pqb], writes=[rn.b])
                  PSREL(pq)
                  OP("act", lambda e: e.activation(out=fl(rn), in_=fl(rn), func=AF.Exp, scale=-0.5), reads=[rn.b], writes=[rn.b])
                  OP("dve", lambda e: e.tensor_tensor(out=kk[:], in0=ksc[:], in1=rn[:], op=ALU.mult), reads=[ksc.b, rn.b], writes=[kk.b])
                  yield
                  OP("dve", lambda e: e.scalar_tensor_tensor(out=AR[:, :, :, 0, :], in0=v4(kk), scalar=-1.0, in1=v4(Pex), op0=ALU.mult, op1=ALU.mult),
                     reads=[kk.b, Pex.b], writes=[AR.b])
                  OP("pool", lambda e: e.tensor_tensor(out=AR[:, :, :, 1, :], in0=r4.rearrange("p a (c t) -> p a c t", c=2), in1=v4(Pc), op=ALU.mult),
                     reads=[rk.b, Pc.b, AR.b], writes=[AR.b])
                  OP("dve", lambda e: e.tensor_tensor(out=t2[:], in0=kk[:], in1=aic[:], op=ALU.mult), reads=[kk.b, aic.b], writes=[t2.b])
                  OP("dve", lambda e: e.tensor_tensor(out=bT[:], in0=t2[:], in1=Pinv[:], op=ALU.mult), reads=[t2.b, Pinv.b], writes=[bT.b])
                  OP("pool", lambda e: e.tensor_tensor(out=kT[:], in0=kmod[:], in1=Pinv[:], op=ALU.mult), reads=[kmod.b, Pinv.b], writes=[kT.b])
                  OP("pool", lambda e: e.tensor_tensor(out=t1[:], in0=r4, in1=kmod[:], op=ALU.mult), reads=[rk.b, kmod.b, t1.b], writes=[t1.b])
                  OP("pool", lambda e: e.tensor_tensor(out=rkb[:], in0=t1[:], in1=bc4(PP_RB), op=ALU.mult), reads=[t1.b, ppt.b], writes=[rkb.b])
                  yield
                  yield
                  yield
                  ptb_, ptbb = PS(hold=True)
                  ptk_, ptkb = PS(hold=True)
                  ptbv = ptb_.bitcast(BF16)
                  ptkv = ptk_.bitcast(BF16)
                  for c4 in range(4):
                      TR(ptbv[:, c4 * 128:(c4 + 1) * 128], bT[:, c4, :], ident[:], [bT.b, ident.b], ptbb)
                  for c4 in range(4):
                      TR(ptkv[:, c4 * 128:(c4 + 1) * 128], kT[:, c4, :], ident[:], [kT.b, ident.b], ptkb)
                  OP("dve", lambda e: e.tensor_scalar(out=bt[:], in0=ptbv[:, 0:512], scalar1=1.0, scalar2=None, op0=ALU.mult), reads=[ptbb], writes=[bt.b])
                  OP("act", lambda e: e.copy(out=kt[:], in_=ptkv[:, 0:512]), reads=[ptkb], writes=[kt.b])
                  PSREL(ptb_)
                  PSREL(ptk_)
                  yield
                  mX3 = mskX[:].rearrange("p a k t -> p a (k t)")
                  A0, M0, X0 = Acur[0], Mcur[0], Xc[0]
                  for e_ in range(2):
                      hs = slice(e_ * 64, e_ * 64 + 64)
                      px, pxb = PS(hold=True)
                      py, pyb = PS(hold=True)
                      pA, pAb = PS(hold=True)
                      for pair in range(4):
                          for c in range(2):
                              cs = slice(c * 64, (c + 1) * 64)
                              mov = AR[hs, pair, c, :, :].rearrange("p k t -> p (k t)")
                              MM(px[cs, pair * 128:(pair + 1) * 128], bT[hs, pair, cs], mov, True, True, [bT.b, AR.b], pxb)
                              MM(py[cs, pair * 128:(pair + 1) * 128], kT[hs, pair, cs], mov, True, True, [kT.b, AR.b], pyb)
                              MM(pA[cs, pair * 64:(pair + 1) * 64], AR[hs, pair, c, 0, :], bT[hs, pair, cs], True, True, [bT.b, AR.b], pAb)
                      OP("dve", lambda e: e.tensor_tensor(out=MB[:].rearrange("p (a e) k t -> p e a (k t)", e=2)[:, e_],
                                                          in0=px[:, :].rearrange("p (a x) -> p a x", a=4), in1=mX3, op=ALU.mult),
                         reads=[pxb, mskX.b], writes=[MB.b])
                      OP("dve", lambda e: e.tensor_tensor(out=MK[:].rearrange("p (a e) k t -> p e a (k t)", e=2)[:, e_],
                                                          in0=py[:, :].rearrange("p (a x) -> p a x", a=4), in1=mX3, op=ALU.mult),
                         reads=[pyb, mskX.b], writes=[MK.b])
                      OP("dve", lambda e: e.tensor_tensor(out=A0[:].rearrange("p (a e) t -> p e a t", e=2)[:, e_],
                                                          in0=pA[:, 0:256].rearrange("p (a t) -> p a t", a=4), in1=mskL[:, 0:4, :], op=ALU.mult),
                         reads=[pAb, mskL.b], writes=[A0.b])
                      PSREL(px)
                      PSREL(py)
                      PSREL(pA)
                      yield
                  OP("dve", lambda e: e.tensor_tensor(out=X0[:], in0=MB[:, :, 0, :], in1=i64[:].unsqueeze(1).broadcast_to([128, 8, 64]), op=ALU.add),
                     reads=[MB.b, i64.b], writes=[X0.b])
                  pW0, pW0b = PS(hold=True)
                  for c in range(2):
                      cs = slice(c * 64, (c + 1) * 64)
                      for h in range(8):
                          MM(pW0[cs, h * 64:(h + 1) * 64], MK[cs, h, 0, :], Vb[cs, h * 64:(h + 1) * 64], True, True, [MK.b, Vb.b], pW0b)
                  OP("act", lambda e: e.copy(out=W0sb[:], in_=pW0[:, :]), reads=[pW0b], writes=[W0sb.b])
                  PSREL(pW0)


              def phaseA2(n):
                  d = n % 2
                  MB = MB2[n % 3]
                  Xc = Xc2[d]
                  Acur, Mcur = Acur2[d], Mcur2[d]
                  ci = 0
                  xi = 0
                  for lvl in range(5):
                      Ac, Mc = Acur[ci], Mcur[ci]
                      An, Mn = Acur[1 - ci], Mcur[1 - ci]
                      if lvl == 0:
                          class _MV:
                              b = MB.b

                              def __getitem__(self, idx):
                                  return MB[idx[0], idx[1], 0, :]
                          Mc = _MV()
                      p2, p2b = PS(hold=True)
                      for h in range(8):
                          for c in range(2):
                              cs = slice(c * 64, (c + 1) * 64)
                              MM(p2[cs, h * 64:(h + 1) * 64], Mc[cs, h, :], Ac[cs, h, :], True, True, [Mc.b, Ac.b], p2b)
                      OP("act", lambda e: e.copy(out=An[:].rearrange("p a t -> p (a t)"), in_=p2[:, :]), reads=[p2b], writes=[An.b])
                      PSREL(p2)
                      if lvl < 4:
                          p3, p3b = PS(hold=True)
                          for h in range(8):
                              for c in range(2):
                                  cs = slice(c * 64, (c + 1) * 64)
                                  MM(p3[cs, h * 64:(h + 1) * 64], Ac[cs, h, :], Mc[cs, h, :], True, True, [Mc.b, Ac.b], p3b)
                          OP("dve", lambda e: e.tensor_scalar(out=Mn[:].rearrange("p a t -> p (a t)"), in0=p3[:, :], scalar1=1.0, scalar2=None,
                                                              op0=ALU.mult), reads=[p3b], writes=[Mn.b])
                          PSREL(p3)
                      yield
                      Xo, Xn = Xc[xi], Xc[1 - xi]
                      p4, p4b = PS(hold=True)
                      for h in range(8):
                          for c in range(2):
                              cs = slice(c * 64, (c + 1) * 64)
                              MM(p4[cs, h * 64:(h + 1) * 64], An[cs, h, :], Xo[cs, h, :], True, True, [An.b, Xo.b], p4b)
                      OP("dve", lambda e: e.tensor_tensor(out=Xn[:].rearrange("p a t -> p (a t)"), in0=p4[:, :],
                                                          in1=Xo[:].rearrange("p a t -> p (a t)"), op=ALU.add), reads=[p4b, Xo.b], writes=[Xn.b])
                      PSREL(p4)
                      ci = 1 - ci
                      xi = 1 - xi
                      yield
                  Xfin[n] = Xc[xi]

              def phaseB(n):
                  d = n % 2
                  t3 = n % 3
                  Vf, Vb, gsb, Pc, AR, rkb, bt, kt, MB, MK, W0sb = (Vf2[n % 4], Vb2[n % 4], gsb2[t3], Pc2[t3], AR2[t3], rkb2[t3], bt2[t3],
                                                                   kt2[t3], MB2[t3], MK2[t3], W0sb2[t3])
                  X = Xfin.pop(n)
                  yf2 = yf[:].rearrange("p a v -> p (a v)")
                  for c in range(2):
                      cs = slice(c * 64, (c + 1) * 64)
                      Hb = Hb2[c]
                      Hbn = Hb2[1 - c]
                      pWe = [PS(hold=True), PS(hold=True)]
                      for e_ in range(2):
                          hs = slice(e_ * 64, e_ * 64 + 64)
                          for pair in range(4):
                              MM(pWe[e_][0][cs, pair * 64:(pair + 1) * 64], AR[hs, pair, c, 0, :], Hb[hs, pair, :], True, True,
                                 [AR.b, Hb.b], pWe[e_][1])
                      for e_ in range(2):
                          OP("dve", lambda e, e_=e_: e.tensor_tensor(
                              out=par(Wsb[cs, :], e_), in0=pWe[e_][0][cs, 0:256].rearrange("p (a v) -> p a v", a=4),
                              in1=par(W0sb[cs, :], e_), op=ALU.add), reads=[pWe[e_][1], W0sb.b], writes=[Wsb.b])
                          PSREL(pWe[e_][0])
                      yield
                      pU, pUb = PS(hold=True)
                      for h in range(8):
                          MM(pU[cs, h * 64:(h + 1) * 64], X[cs, h, :], Wsb[cs, h * 64:(h + 1) * 64], True, True, [X.b, Wsb.b], pUb)
                      OP("act", lambda e: e.copy(out=Usb[cs, :], in_=pU[cs, :]), reads=[pUb], writes=[Usb.b])
                      PSREL(pU)
                      yield
                      pH, pHb = PS(hold=True)
                      for h in range(8):
                          pair, hs = hsl(h)
                          o_ = pH[hs, pair * 64:(pair + 1) * 64]
                          MM(o_, kt[cs, h * 64:(h + 1) * 64], Vb[cs, h * 64:(h + 1) * 64], True, False, [kt.b, Vb.b], pHb)
                          MM(o_, bt[cs, h * 64:(h + 1) * 64], Usb[cs, h * 64:(h + 1) * 64], False, True, [bt.b, Usb.b], pHb)
                      OP("dve", lambda e: e.tensor_tensor(out=Htmp[:].rearrange("p a v -> p (a v)"), in0=pH[:, 0:256],
                                                          in1=Hs[:].rearrange("p a v -> p (a v)"), op=ALU.add), reads=[pHb, Hs.b], writes=[Htmp.b])
                      PSREL(pH)
                      pcb = Pc[:].rearrange("p a (c t) -> p a c t", c=2)[:, :, c, 63:64].broadcast_to([128, 4, 64])
                      OP("dve", lambda e: e.tensor_tensor(out=Hs[:], in0=Htmp[:], in1=pcb, op=ALU.mult), reads=[Htmp.b, Pc.b], writes=[Hs.b])
                      OP("act", lambda e: e.copy(out=Hbn[:], in_=Hs[:]), reads=[Hs.b], writes=[Hbn.b])
                      pYe = [PS(hold=True), PS(hold=True)]
                      for e_ in range(2):
                          hs = slice(e_ * 64, e_ * 64 + 64)
                          for pair in range(4):
                              MM(pYe[e_][0][cs, pair * 64:(pair + 1) * 64], AR[hs, pair, c, 1, :], Hb[hs, pair, :], True, True,
                                 [AR.b, Hb.b], pYe[e_][1])
                      pYc, pYcb = PS(hold=True)
                      for h in range(8):
                          o_ = pYc[cs, h * 64:(h + 1) * 64]
                          MM(o_, MK[cs, h, 1, :], Vb[cs, h * 64:(h + 1) * 64], True, False, [MK.b, Vb.b], pYcb)
                          MM(o_, MB[cs, h, 1, :], Usb[cs, h * 64:(h + 1) * 64], False, True, [MB.b, Usb.b], pYcb)
                      OP("act", lambda e: e.copy(out=yf2[cs, :], in_=pYc[cs, :]), reads=[pYcb], writes=[yf.b])
                      PSREL(pYc)
                      for e_ in range(2):
                          OP("dve", lambda e, e_=e_: e.tensor_tensor(
                              out=par(yf2[cs, :], e_), in0=pYe[e_][0][cs, 0:256].rearrange("p (a v) -> p a v", a=4),
                              in1=par(yf2[cs, :], e_), op=ALU.add), reads=[pYe[e_][1], yf.b], writes=[yf.b])
                          PSREL(pYe[e_][0])
                      yield
                  OP("dve", lambda e: e.tensor_reduce(out=st[:, 0:8], in_=yf[:], axis=AX.X, op=ALU.add), reads=[yf.b], writes=[st.b])
                  OP("act", lambda e: e.activation(out=ysq[:].rearrange("p a v -> p (a v)"), in_=yf2, func=AF.Square), reads=[yf.b], writes=[ysq.b])
                  OP("dve", lambda e: e.tensor_reduce(out=st[:, 8:16], in_=ysq[:], axis=AX.X, op=ALU.add), reads=[ysq.b], writes=[st.b])
                  OP("dve", lambda e: e.tensor_scalar(out=st[:, 16:24], in0=st[:, 0:8], scalar1=1.0 / 64, scalar2=None, op0=ALU.mult),
                     reads=[st.b], writes=[st.b])
                  OP("dve", lambda e: e.tensor_tensor(out=st[:, 24:32], in0=st[:, 16:24], in1=st[:, 16:24], op=ALU.mult), reads=[st.b], writes=[st.b])
                  OP("dve", lambda e: e.scalar_tensor_tensor(out=st[:, 24:32], in0=st[:, 8:16], scalar=1.0 / 64, in1=st[:, 24:32],
                                                             op0=ALU.mult, op1=ALU.subtract), reads=[st.b], writes=[st.b])
                  OP("dve", lambda e: e.tensor_scalar(out=st[:, 24:32], in0=st[:, 24:32], scalar1=GN_EPS, scalar2=None, op0=ALU.add),
                     reads=[st.b], writes=[st.b])
                  OP("act", lambda e: e.activation(out=st[:, 24:32], in_=st[:, 24:32], func=AF.Ln), reads=[st.b], writes=[st.b])
                  OP("act", lambda e: e.activation(out=st[:, 24:32], in_=st[:, 24:32], func=AF.Exp, scale=-0.5), reads=[st.b], writes=[st.b])
                  yield
                  OP("dve", lambda e: e.tensor_tensor(out=yf[:], in0=yf[:], in1=b8(st[:, 16:24]), op=ALU.subtract), reads=[yf.b, st.b], writes=[yf.b])
                  OP("dve", lambda e: e.tensor_tensor(out=yf[:], in0=yf[:], in1=b8(st[:, 24:32]), op=ALU.mult), reads=[yf.b, st.b], writes=[yf.b])
                  OP("pool", lambda e: e.tensor_tensor(out=yf2, in0=yf2, in1=lnw[:, 0:512], op=ALU.mult), reads=[yf.b, lnw.b], writes=[yf.b])
                  OP("pool", lambda e: e.tensor_tensor(out=yf2, in0=yf2, in1=lnw[:, 512:1024], op=ALU.add), reads=[yf.b, lnw.b], writes=[yf.b])
                  pb_, pbb = PS(hold=True)
                  for c4 in range(4):
                      MM(pb_[:, c4 * 2:c4 * 2 + 2], rkb[:, c4, :], hsel[:, :], True, True, [rkb.b, hsel.b], pbb)
                  OP("act", lambda e: e.copy(out=bs[:], in_=pb_[:, 0:8]), reads=[pbb], writes=[bs.b])
                  PSREL(pb_)
                  yield
                  OP("dve", lambda e: e.tensor_tensor(out=ysq[:], in0=Vf[:].rearrange("p (a v) -> p a v", a=8), in1=b8(bs[:, :]), op=ALU.mult),
                     reads=[Vf.b, bs.b, ysq.b], writes=[ysq.b])
                  OP("pool", lambda e: e.tensor_tensor(out=yf[:], in0=yf[:], in1=ysq[:], op=ALU.add), reads=[yf.b, ysq.b], writes=[yf.b])
                  OP("dve", lambda e: e.tensor_tensor(out=yg[:], in0=yf2, in1=gsb[:], op=ALU.mult), reads=[yf.b, gsb.b], writes=[yg.b])
                  yield
                  pt_, ptb2 = PS(hold=True)
                  ptv = pt_.bitcast(BF16)
                  for c4 in range(4):
                      TR(ptv[:, c4 * 128:(c4 + 1) * 128], yg[:, c4 * 128:(c4 + 1) * 128], ident[:], [yg.b, ident.b], ptb2)
                  ya_ = yaTs[n % 2]
                  OP("act", lambda e: e.copy(out=ya_[:].rearrange("p a t -> p (a t)"), in_=ptv[:, 0:512]), reads=[ptb2], writes=[ya_.b])
                  PSREL(pt_)
                  DMA("sp", yaT_d[:, :, n * 128:(n + 1) * 128], ya_[:], reads=[ya_.b], writes=[B_yaT])

              dummy = PS(hold=True) if NDUMMY else None

              def run_rr(gens):
                  alive = [g is not None for g, _ in gens]
                  while any(alive):
                      for i_, (g, k_) in enumerate(gens):
                          for _ in range(k_):
                              if not alive[i_]:
                                  break
                              try:
                                  next(g)
                              except StopIteration:
                                  alive[i_] = False
                          for _ in range(NDUMMY):
                              MM(dummy[0][:, 0:128], ident[:, :], ident[:, :], True, True, [ident.b], dummy[1])

              gP = lambda n: phaseP(n) if n < NS else None
              gA1 = lambda n: phaseA(n) if n < NS else None
              gA2 = lambda n: phaseA2(n) if n < NS else None
              run_rr([(gP(0), 1)])
              run_rr([(gA1(0), RR[2]), (gP(1), RR[3])])
              run_rr([(gA2(0), RR[1]), (gA1(1), RR[2]), (gP(2), RR[3])])
              for n in range(NS):
                  streams = [(phaseB(n), RR[0]), (gA2(n + 1), RR[1]), (gA1(n + 2), RR[2]), (gP(n + 3), RR[3])]
                  run_rr([streams[k_] for k_ in ORDER])
              if dummy is not None:
                  PSREL(dummy[0])

        except _Stop:
            pass
        sy.barrier()

        CP('F')
        with ExitStack() as sf:
            Wf = SB(sf, "Wf", [128, 8, NFX], BF16)
            Wf_b = [Buf("Wf%d" % k) for k in range(8)]
            KT = SB(sf, "KT", [65, 8, S], BF16)
            Va = SB(sf, "Va", [128, NS, 8, 65], BF16)
            QT = [SB(sf, "QT%d" % i, [65, 8, 512], BF16) for i in range(2)]
            ncum = SB(sf, "ncum", [128, NS, 8], F32)
            PTs = [SB(sf, "PT%d" % i, [128, 512], BF16) for i in range(4)]
            hTf = [SB(sf, "hTf%d" % i, [128, 8, 512], BF16) for i in range(2)]
            tri = SB(sf, "tri", [128, 128], F32)
            onesq = SB(sf, "onesq", [128, 128], F32)
            trib = SB(sf, "trib", [128, 128], BF16)
            acc = SB(sf, "acc", [128, 8], F32)
            fbb = SB(sf, "fbb", [128, 8], F32)
            lf = SB(sf, "lf", [128, 8], F32)
            cq8 = SB(sf, "cq8", [8, 512], BF16)
            oT = [SB(sf, "oT%d" % i, [65, 512], F32) for i in range(4)]
            ybh = [SB(sf, "ybh%d" % i, [64, 512], BF16) for i in range(4)]
            for kc in range(8):
                DMA("pool", Wf[:, kc, :], win_d[kc * 128:(kc + 1) * 128, NRW:NRW + NFX], writes=[Wf_b[kc]])
            DMA("sp", fbb[:], bc_d[:, BC_FB:BC_FB + 8], writes=[fbb.b])
            OP("pool", lambda e: e.memset(tri[:], 1.0), writes=[tri.b])
            OP("pool", lambda e: e.affine_select(out=tri[:], in_=tri[:], pattern=[[1, 128]], compare_op=ALU.is_ge, fill=0.0,
                                                 base=0, channel_multiplier=-1), reads=[tri.b], writes=[tri.b])
            negm = SB(sf, "negm", [128, 128], F32)
            OP("pool", lambda e: e.memset(negm[:], 0.0), writes=[negm.b])
            OP("pool", lambda e: e.affine_select(out=negm[:], in_=negm[:], pattern=[[1, 128]], compare_op=ALU.is_ge, fill=-1.0e4,
                                                 base=0, channel_multiplier=-1), reads=[negm.b], writes=[negm.b])
            OP("pool", lambda e: e.memset(trib[:], 1.0), writes=[trib.b])
            OP("pool", lambda e: e.affine_select(out=trib[:], in_=trib[:], pattern=[[1, 128]], compare_op=ALU.is_ge, fill=0.0,
                                                 base=0, channel_multiplier=-1), reads=[trib.b], writes=[trib.b])
            OP("pool", lambda e: e.memset(onesq[:], 1.0), writes=[onesq.b])
            OP("dve", lambda e: e.memset(acc[:], 0.0), writes=[acc.b])
            lf4 = SB(sf, "lf4", [128, 4, 8], F32)
            acc5 = SB(sf, "acc5", [128, 5, 8], F32)
            cq8s = [SB(sf, "cq8_%d" % i_, [8, 512], BF16) for i_ in range(2)]
            qkst = [SB(sf, "qkst%d" % i_, [128, 512], BF16) for i_ in range(4)]
            PT6 = [SB(sf, "PTx%d" % i, [128, 512], BF16) for i in range(8)]
            OP("dve", lambda e: e.memset(acc5[:], 0.0), writes=[acc5.b])
            KT_b = [Buf("KTt%d" % i_) for i_ in range(NT)]
            Va_b = [Buf("Vat%d" % i_) for i_ in range(NT)]
            nc_b = [Buf("nct%d" % i_) for i_ in range(NT)]
            OP("dve", lambda e: e.memset(KT[64:65, :, :], 1.0), writes=KT_b)
            OP("dve", lambda e: e.memset(Va[:, :, :, 64:65], 1.0), writes=Va_b)
            LOOK = 6
            unit_ctr = [0]

            def front(i):
                ht = hTf[i % 2]
                qt = QT[i % 2]
                cq8_ = cq8s[i % 2]
                DMA("sp", ht[:], hT_d[:, :, i * 512:(i + 1) * 512], writes=[ht.b])
                yield
                pf, pfb = PS(hold=True)
                for sub in range(4):
                    ts_ = slice(sub * 128, (sub + 1) * 128)
                    for kc in range(8):
                        MM(pf[:, sub * 8:(sub + 1) * 8], ht[:, kc, ts_], Wf[:, kc, 1536:1544], kc == 0, kc == 7, [Wf_b[kc], ht.b], pfb)
                OP("dve", lambda e: e.tensor_tensor(out=lf4[:], in0=pf[:, 0:32].rearrange("p (a h) -> p a h", a=4),
                                                    in1=fbb[:].unsqueeze(1).broadcast_to([128, 4, 8]), op=ALU.add),
                   reads=[pfb, fbb.b], writes=[lf4.b])
                PSREL(pf)
                OP("act", lambda e: e.activation(out=lf4[:], in_=lf4[:], func=AF.Sigmoid), reads=[lf4.b], writes=[lf4.b])
                OP("act", lambda e: e.activation(out=lf4[:], in_=lf4[:], func=AF.Ln), reads=[lf4.b], writes=[lf4.b])
                if i > 0:
                    OP("dve", lambda e: e.tensor_scalar(out=acc5[:, 0, :], in0=acc5[:, 4, :], scalar1=1.0, scalar2=None, op0=ALU.mult),
                       reads=[acc5.b], writes=[acc5.b])
                for sub in range(4):
                    OP("dve", lambda e, sub=sub: e.tensor_tensor(out=acc5[:, sub + 1, :], in0=acc5[:, sub, :], in1=lf4[:, sub, :], op=ALU.add),
                       reads=[acc5.b, lf4.b], writes=[acc5.b])
                yield
                for pair in range(4):
                    for which in range(2):
                        col0 = which * 512 + pair * 128
                        pq_, pqb_ = PS(hold=True)
                        for kc in range(8):
                            MM(pq_[:, :], Wf[:, kc, col0:col0 + 128], ht[:, kc, :], kc == 0, kc == 7, [Wf_b[kc], ht.b], pqb_)
                        stg = qkst[(pair * 2 + which) % 4]
                        if which == 0:
                            dst_even = qt[0:64, 2 * pair, :]
                            dst_odd = qt[0:64, 2 * pair + 1, :]
                            dbuf = qt.b
                        else:
                            dst_even = KT[0:64, 2 * pair, i * 512:(i + 1) * 512]
                            dst_odd = KT[0:64, 2 * pair + 1, i * 512:(i + 1) * 512]
                            dbuf = KT_b[i]
                        OP("dve", lambda e: e.tensor_scalar(out=dst_even, in0=pq_[0:64, :], scalar1=1.0, scalar2=None, op0=ALU.mult),
                           reads=[pqb_], writes=[dbuf])
                        OP("act", lambda e: e.copy(out=stg[64:128, :], in_=pq_[64:128, :]), reads=[pqb_], writes=[stg.b])
                        PSREL(pq_)
                        DMA("sp", dst_odd, stg[64:128, :], reads=[stg.b], writes=[dbuf])
                        yield
                for sub in range(4):
                    g = i * 4 + sub
                    ts_ = slice(sub * 128, (sub + 1) * 128)
                    pv2, pv2b = PS(hold=True)
                    for kc in range(8):
                        MM(pv2[:, :], ht[:, kc, ts_], Wf[:, kc, 1024:1536], kc == 0, kc == 7, [Wf_b[kc], ht.b], pv2b)
                    OP("dve", lambda e: e.tensor_scalar(out=Va[:, g, :, 0:64], in0=pv2[:, :].rearrange("p (a v) -> p a v", a=8),
                                                        scalar1=1.0, scalar2=None, op0=ALU.mult), reads=[pv2b], writes=[Va_b[i]])
                    PSREL(pv2)
                    yield
                pc, pcb_ = PS(hold=True)
                pcT, pcTb = PS(hold=True)
                for sub in range(4):
                    ts_ = slice(sub * 128, (sub + 1) * 128)
                    MM(pc[:, sub * 8:(sub + 1) * 8], tri[:, :], lf4[:, sub, :], True, False, [tri.b, lf4.b], pcb_)
                    MM(pc[:, sub * 8:(sub + 1) * 8], onesq[:, :], acc5[:, sub, :], False, True, [onesq.b, acc5.b], pcb_)
                    MM(pcT[0:8, ts_], lf4[:, sub, :], tri[:, :], sub == 0, False, [tri.b, lf4.b], pcTb)
                    MM(pcT[0:8, ts_], acc5[:, sub, :], onesq[:, :], False, sub == 3, [onesq.b, acc5.b], pcTb)
                OP("dve", lambda e: e.tensor_scalar(out=ncum[:, i * 4:(i + 1) * 4, :], in0=pc[:, 0:32].rearrange("p (a h) -> p a h", a=4),
                                                    scalar1=-1.0, scalar2=None, op0=ALU.mult), reads=[pcb_], writes=[nc_b[i]])
                PSREL(pc)
                OP("dve", lambda e: e.tensor_scalar(out=cq8_[:], in0=pcT[0:8, :], scalar1=8.0, scalar2=None, op0=ALU.mult),
                   reads=[pcTb], writes=[cq8_.b])
                PSREL(pcT)
                for h in range(8):
                    DMA("sp", qt[64:65, h, :], cq8_[h:h + 1, :], reads=[cq8_.b], writes=[qt.b])

            def attention(i):
                qt = QT[i % 2]
                nkb = 4 * i + 4
                tasks = [(h, j) for h in range(8) for j in range(nkb)]
                NTK = len(tasks)
                pts = {}
                pos = {}
                tails = []

                def emit_S(t):
                    h, j = tasks[t]
                    q0 = max(0, j - 4 * i) * 128
                    ps_, psb_ = PS()
                    MM(ps_[:, q0:512], KT[0:65, h, j * 128:(j + 1) * 128], qt[0:65, h, q0:512], True, True, [KT_b[j // 4], qt.b], psb_)
                    PT = PT6[t % 8]
                    if j >= 4 * i:
                        OP("dve", lambda e: e.tensor_tensor(out=ps_[:, q0:q0 + 128], in0=ps_[:, q0:q0 + 128], in1=negm[:], op=ALU.add),
                           reads=[psb_, negm.b], writes=[psb_])
                    OP("act", lambda e: e.activation(out=PT[:, q0:512], in_=ps_[:, q0:512], func=AF.Exp, scale=0.125,
                                                     bias=ncum[:, j, h:h + 1]), reads=[psb_, nc_b[j // 4]], writes=[PT.b])
                    pts[t] = (PT, q0)

                def emit_PV(t):
                    h, j = tasks[t]
                    if j == 0:
                        pos[h] = PS(hold=True)
                    po, pob = pos[h]
                    PT, q0 = pts.pop(t)
                    MM(po[0:65, q0:512], Va[:, j, h, :], PT[:, q0:512], j == 0, j == nkb - 1, [Va_b[j // 4], PT.b], pob)
                    if j == nkb - 1:
                        u = unit_ctr[0]
                        unit_ctr[0] += 1
                        o_ = oT[u % 4]
                        yb_ = ybh[u % 4]
                        OP("dve", lambda e: e.tensor_scalar(out=o_[:], in0=po[0:65, :], scalar1=1.0, scalar2=None, op0=ALU.mult),
                           reads=[pob], writes=[o_.b])
                        PSREL(po)
                        OP("dve", lambda e: e.reciprocal(out=o_[64:65, :], in_=o_[64:65, :]), reads=[o_.b], writes=[o_.b])

                        def tail(h=h, o_=o_, yb_=yb_):
                            pr_, prb_ = PS()
                            MM(pr_[0:64, :], onesq[64:65, 0:64], o_[64:65, :], True, True, [onesq.b, o_.b], prb_)
                            OP("dve", lambda e: e.tensor_tensor(out=yb_[:], in0=pr_[0:64, :], in1=o_[0:64, :], op=ALU.mult),
                               reads=[prb_, o_.b], writes=[yb_.b])
                            DMA("sp", ybT_d[h * 64:(h + 1) * 64, i * 512:(i + 1) * 512], yb_[:], reads=[yb_.b], writes=[B_ybT])
                        tails.append((t + 11, tail))

                for t in range(min(LOOK, NTK)):
                    emit_S(t)
                for t in range(NTK):
                    if t + LOOK < NTK:
                        emit_S(t + LOOK)
                    emit_PV(t)
                    while tails and tails[0][0] <= t:
                        tails.pop(0)[1]()
                    yield
                while tails:
                    tails.pop(0)[1]()

            for _ in front(0):
                pass
            for i in range(NT):
                gA = attention(i)
                gF = front(i + 1) if i + 1 < NT else None
                ntk = 8 * (4 * i + 4)
                every = max(1, ntk // 28)
                cnt_ = 0
                for _ in gA:
                    cnt_ += 1
                    if gF is not None and cnt_ % every == 0:
                        try:
                            next(gF)
                        except StopIteration:
                            gF = None
                if gF is not None:
                    for _ in gF:
                        pass
        sy.barrier()

        CP('C1')
        with ExitStack() as sc:
            Wg = SB(sc, "Wg", [128, 8, 2048], BF16)
            Wg_b = [Buf("Wg%d" % k) for k in range(8)]
            Wor = SB(sc, "Wor", [128, 4, D], BF16)
            Wof = SB(sc, "Wof", [128, 4, D], BF16)
            Wout = SB(sc, "Wout", [128, 8, D], BF16)
            gtb = SB(sc, "gtb", [128, D], F32)
            hTc = [SB(sc, "hTc%d" % i, [128, 8, 512], BF16) for i in range(2)]
            yaTt = [SB(sc, "yaTt%d" % i, [128, 4, 512], BF16) for i in range(2)]
            ybTt = [SB(sc, "ybTt%d" % i, [128, 4, 512], BF16) for i in range(2)]
            Gsig2 = [SB(sc, "Gsig%d" % i_, [128, 16, 512], BF16) for i_ in range(2)]
            mrg = SB(sc, "mrg", [128, 8, 512], BF16)
            tm1 = [SB(sc, "tm1_%d" % i, [128, 512], F32) for i in range(2)]
            tm2 = [SB(sc, "tm2_%d" % i, [128, 512], F32) for i in range(2)]
            tm3c = [SB(sc, "tm3c_%d" % i, [128, 512], F32) for i in range(2)]
            xs = [SB(sc, "xs%d" % i, [128, D], F32) for i in range(2)]
            x1s = [SB(sc, "x1s%d" % i, [128, D], F32) for i in range(2)]
            xn2 = [SB(sc, "xn2_%d" % i, [128, D], BF16) for i in range(2)]
            h2t = [SB(sc, "h2t%d" % i, [128, 8, 512], BF16) for i in range(2)]
            junk1 = SB(sc, "junk1", [128, D], BF16)
            st2 = [SB(sc, "st2_%d" % i, [128, 2], F32) for i in range(2)]
            for kc in range(8):
                DMA("pool", Wg[:, kc, :], win_d[kc * 128:(kc + 1) * 128, NRW + NFX:NIN], writes=[Wg_b[kc]])
            DMA("pool", Wor[:], wor_d.rearrange("(c p) n -> p c n", p=128), writes=[Wor.b])
            DMA("pool", Wof[:], wof_d.rearrange("(c p) n -> p c n", p=128), writes=[Wof.b])
            DMA("pool", Wout[:], wout_d.rearrange("(c p) n -> p c n", p=128), writes=[Wout.b])
            DMA("sp", gtb[:], gt_d[:, 0:D], reads=[B_gt], writes=[gtb.b])
            ybT_v = ybT_d.rearrange("(c p) s -> p c s", p=128)

            def load_tile(i):
                tsl_ = slice(i * 512, (i + 1) * 512)
                DMA("sp", hTc[i % 2][:], hT_d[:, :, tsl_], reads=[B_hT], writes=[hTc[i % 2].b])
                DMA("sp", yaTt[i % 2][:], yaT_d[:, :, tsl_], reads=[B_yaT], writes=[yaTt[i % 2].b])
                DMA("sp", ybTt[i % 2][:], ybT_v[:, :, tsl_], reads=[B_ybT], writes=[ybTt[i % 2].b])

            def load_x(k_):
                if k_ < NS:
                    DMA("sp", xs[k_ % 2][:], x_d[k_ * 128:(k_ + 1) * 128, :], writes=[xs[k_ % 2].b])

            def gates(i, part):
                if i >= NT:
                    return
                ht_, G_ = hTc[i % 2], Gsig2[i % 2]
                for g in range(part * 8, part * 8 + 8):
                    pg2, pg2b = PS()
                    for kc in range(8):
                        MM(pg2[:, :], Wg[:, kc, g * 128:(g + 1) * 128], ht_[:, kc, :], kc == 0, kc == 7, [Wg_b[kc], ht_.b], pg2b)
                    OP("act", lambda e, g=g, pg2=pg2: e.activation(out=G_[:, g, :], in_=pg2[:, :], func=AF.Sigmoid),
                       reads=[pg2b], writes=[G_.b])

            load_tile(0)
            load_x(0)
            gates(0, 0)
            gates(0, 1)
            for i in range(NT):
                ht, ya_t, yb_t, h2 = hTc[i % 2], yaTt[i % 2], ybTt[i % 2], h2t[i % 2]
                Gsig = Gsig2[i % 2]
                tsl = slice(i * 512, (i + 1) * 512)
                if i + 1 < NT:
                    load_tile(i + 1)
                for m in range(8):
                    pa2, pa2b = PS()
                    for c in range(4):
                        MM(pa2[:, :], Wor[:, c, m * 128:(m + 1) * 128], ya_t[:, c, :], c == 0, c == 3, [Wor.b, ya_t.b], pa2b)
                    pb2, pb2b = PS()
                    for c in range(4):
                        MM(pb2[:, :], Wof[:, c, m * 128:(m + 1) * 128], yb_t[:, c, :], c == 0, c == 3, [Wof.b, yb_t.b], pb2b)
                    t1_, t2_ = tm1[m % 2], tm2[m % 2]
                    OP("dve", lambda e, m=m, pa2=pa2, t1_=t1_: e.tensor_tensor(out=t1_[:], in0=pa2[:, :], in1=Gsig[:, m, :], op=ALU.mult),
                       reads=[pa2b, Gsig.b], writes=[t1_.b])
                    OP("dve", lambda e, m=m, pb2=pb2, t2_=t2_: e.tensor_tensor(out=t2_[:], in0=pb2[:, :], in1=Gsig[:, 8 + m, :], op=ALU.mult),
                       reads=[pb2b, Gsig.b], writes=[t2_.b])
                    OP("pool", lambda e, m=m, t1_=t1_, t2_=t2_: e.tensor_tensor(out=mrg[:, m, :], in0=t1_[:], in1=t2_[:], op=ALU.add),
                       reads=[t1_.b, t2_.b], writes=[mrg.b])
                gates(i + 1, 0)
                pT2 = [PS(hold=True) for _ in range(4)]

                def z_part(sub):
                    k_ = i * 4 + sub
                    ts_ = slice(sub * 128, (sub + 1) * 128)
                    xt, x1t, xnt, s2 = xs[k_ % 2], x1s[k_ % 2], xn2[k_ % 2], st2[k_ % 2]
                    load_x(k_ + 1)
                    for half in range(2):
                        hsl_ = slice(half * 512, (half + 1) * 512)
                        pz2, pz2b = PS()
                        for c in range(8):
                            MM(pz2[:, :], mrg[:, c, ts_], Wout[:, c, hsl_], c == 0, c == 7, [mrg.b, Wout.b], pz2b)
                        t1_ = tm3c[half]
                        OP("dve", lambda e, pz2=pz2, t1_=t1_, hsl_=hsl_: e.tensor_tensor(out=t1_[:], in0=pz2[:, :], in1=gtb[:, hsl_], op=ALU.mult),
                           reads=[pz2b, gtb.b], writes=[t1_.b])
                        OP("dve", lambda e, t1_=t1_, hsl_=hsl_, xt=xt, x1t=x1t: e.tensor_tensor(out=x1t[:, hsl_], in0=t1_[:], in1=xt[:, hsl_], op=ALU.add),
                           reads=[t1_.b, xt.b], writes=[x1t.b])
                    DMA("sp", x1_d[i * 512 + sub * 128: i * 512 + (sub + 1) * 128, :], x1t[:], reads=[x1t.b], writes=[B_x1])
                    OP("act", lambda e, x1t=x1t, s2=s2: e.activation(out=junk1[:], in_=x1t[:], func=AF.Square, accum_out=s2[:, 0:1]),
                       reads=[x1t.b], writes=[junk1.b, s2.b])
                    OP("dve", lambda e, s2=s2: e.tensor_scalar(out=s2[:, 1:2], in0=s2[:, 0:1], scalar1=1.0 / D, scalar2=NORM_EPS,
                                                               op0=ALU.mult, op1=ALU.add), reads=[s2.b], writes=[s2.b])
                    OP("act", lambda e, s2=s2: e.activation(out=s2[:, 1:2], in_=s2[:, 1:2], func=AF.Sqrt), reads=[s2.b], writes=[s2.b])
                    OP("dve", lambda e, s2=s2: e.reciprocal(out=s2[:, 1:2], in_=s2[:, 1:2]), reads=[s2.b], writes=[s2.b])
                    OP("act", lambda e, x1t=x1t, xnt=xnt, s2=s2: e.activation(out=xnt[:], in_=x1t[:], func=AF.Copy, scale=s2[:, 1:2]),
                       reads=[x1t.b, s2.b], writes=[xnt.b])

                def t_part(sub):
                    k_ = i * 4 + sub
                    xnt = xn2[k_ % 2]
                    for kc in range(8):
                        pv3 = pT2[kc // 2][0].bitcast(BF16)
                        TR(pv3[:, (kc % 2) * 512 + sub * 128:(kc % 2) * 512 + (sub + 1) * 128], xnt[:, kc * 128:(kc + 1) * 128], ident[:],
                           [xnt.b, ident.b], pT2[kc // 2][1])

                z_part(0)
                for sub in range(4):
                    if sub + 1 < 4:
                        z_part(sub + 1)
                    if sub == 1:
                        gates(i + 1, 1)
                    t_part(sub)
                for hb in range(4):
                    pv3 = pT2[hb][0].bitcast(BF16)
                    for kq in range(2):
                        kc = hb * 2 + kq
                        if kc % 2 == 0:
                            OP("dve", lambda e, kc=kc, kq=kq, pv3=pv3: e.tensor_scalar(
                                out=h2[:, kc, :], in0=pv3[:, kq * 512:(kq + 1) * 512], scalar1=GS[:, 16 + kc:17 + kc],
                                scalar2=GS[:, 24 + kc:25 + kc], op0=ALU.mult, op1=ALU.add), reads=[pT2[hb][1], GS.b], writes=[h2.b])
                        else:
                            OP("act", lambda e, kc=kc, kq=kq, pv3=pv3: e.activation(
                                out=h2[:, kc, :], in_=pv3[:, kq * 512:(kq + 1) * 512], func=AF.Identity, scale=GS[:, 16 + kc:17 + kc],
                                bias=GS[:, 24 + kc:25 + kc]), reads=[pT2[hb][1], GS.b], writes=[h2.b])
                for hb in range(4):
                    PSREL(pT2[hb][0])
                DMA("sp", h2T_d[:, :, tsl], h2[:], reads=[h2.b], writes=[B_h2T])
        sy.barrier()

        CP('C2')
        with ExitStack() as s2c:
            W1 = SB(s2c, "W1", [128, 8, 4 * D], BF16)
            W1_b = [Buf("W1_%d" % k) for k in range(8)]
            W2 = SB(s2c, "W2", [128, 32, D], BF16)
            W2_b = [Buf("W2_%d" % k) for k in range(4)]
            gt2 = SB(s2c, "gt2", [128, D], F32)
            fgb = SB(s2c, "fgb", [128, D], F32)
            h2c = [SB(s2c, "h2c%d" % i, [128, 8, 256], BF16) for i in range(2)]
            hid = SB(s2c, "hid", [128, 32, 256], BF16)
            rl = [SB(s2c, "rl%d" % i, [128, 512], F32) for i in range(2)]
            x1c = [SB(s2c, "x1c%d" % i, [128, D], F32) for i in range(2)]
            x2c = [SB(s2c, "x2c%d" % i, [128, D], F32) for i in range(2)]
            oc = [SB(s2c, "oc%d" % i, [128, D], F32) for i in range(2)]
            tm3 = [SB(s2c, "tm3_%d" % i, [128, 512], F32) for i in range(2)]
            junk2 = SB(s2c, "junk2", [128, D], BF16)
            st3 = [SB(s2c, "st3_%d" % i, [128, 2], F32) for i in range(2)]
            for kc in range(8):
                DMA("pool", W1[:, kc, :], wff1_d[kc * 128:(kc + 1) * 128, :], writes=[W1_b[kc]])
            for q4 in range(4):
                DMA("pool", W2[:, q4 * 8:(q4 + 1) * 8, :],
                    wff2_d[q4 * 1024:(q4 + 1) * 1024, :].rearrange("(k p) n -> p k n", p=128), writes=[W2_b[q4]])
            DMA("sp", gt2[:], gt_d[:, D:2 * D], reads=[B_gt], writes=[gt2.b])
            DMA("sp", fgb[:], bc_d[:, BC_FG:BC_FG + D], writes=[fgb.b])
            def load_h2(i2):
                if i2 < S // 256:
                    DMA("sp", h2c[i2 % 2][:], h2T_d[:, :, i2 * 256:(i2 + 1) * 256], reads=[B_h2T], writes=[h2c[i2 % 2].b])

            def load_x1(k_):
                if k_ < NS:
                    DMA("sp", x1c[k_ % 2][:], x1_d[k_ * 128:(k_ + 1) * 128, :], reads=[B_x1], writes=[x1c[k_ % 2].b])

            load_h2(0)
            load_x1(0)
            for i2 in range(S // 256):
                hc = h2c[i2 % 2]
                load_h2(i2 + 1)
                for f2 in range(16):
                    pf2, pf2b = PS()
                    for fh in range(2):
                        f = f2 * 2 + fh
                        for kc in range(8):
                            MM(pf2[:, fh * 256:(fh + 1) * 256], W1[:, kc, f * 128:(f + 1) * 128], hc[:, kc, :], kc == 0, kc == 7,
                               [W1_b[kc], hc.b], pf2b)
                    r_ = rl[f2 % 2]
                    OP("act", lambda e, pf2=pf2, r_=r_: e.activation(out=r_[:], in_=pf2[:, :], func=AF.Relu), reads=[pf2b], writes=[r_.b])
                    OP("pool", lambda e, f2=f2, r_=r_: e.tensor_tensor(out=hid[:, f2 * 2:f2 * 2 + 2, :].rearrange("p a t -> p (a t)"),
                                                                       in0=r_[:], in1=r_[:], op=ALU.mult), reads=[r_.b], writes=[hid.b])
                for sub in range(2):
                    k_ = i2 * 2 + sub
                    r0 = i2 * 256 + sub * 128
                    ts_ = slice(sub * 128, (sub + 1) * 128)
                    x1t, x2t, ot, s3 = x1c[k_ % 2], x2c[k_ % 2], oc[k_ % 2], st3[k_ % 2]
                    load_x1(k_ + 1)
                    for half in range(2):
                        hsl_ = slice(half * 512, (half + 1) * 512)
                        po2, po2b = PS()
                        for kk_ in range(32):
                            MM(po2[:, :], hid[:, kk_, ts_], W2[:, kk_, hsl_], kk_ == 0, kk_ == 31, [hid.b, W2_b[kk_ // 8]], po2b)
                        t3 = tm3[half]
                        OP("dve", lambda e, po2=po2, t3=t3, hsl_=hsl_: e.tensor_tensor(out=t3[:], in0=po2[:, :], in1=gt2[:, hsl_], op=ALU.mult),
                           reads=[po2b, gt2.b], writes=[t3.b])
                        OP("dve", lambda e, t3=t3, hsl_=hsl_, x1t=x1t, x2t=x2t: e.tensor_tensor(out=x2t[:, hsl_], in0=t3[:], in1=x1t[:, hsl_], op=ALU.add),
                           reads=[t3.b, x1t.b], writes=[x2t.b])
                    OP("act", lambda e, x2t=x2t, s3=s3: e.activation(out=junk2[:], in_=x2t[:], func=AF.Square, accum_out=s3[:, 0:1]),
                       reads=[x2t.b], writes=[junk2.b, s3.b])
                    OP("dve", lambda e, s3=s3: e.tensor_scalar(out=s3[:, 1:2], in0=s3[:, 0:1], scalar1=1.0 / D, scalar2=NORM_EPS,
                                                               op0=ALU.mult, op1=ALU.add), reads=[s3.b], writes=[s3.b])
                    OP("act", lambda e, s3=s3: e.activation(out=s3[:, 1:2], in_=s3[:, 1:2], func=AF.Sqrt), reads=[s3.b], writes=[s3.b])
                    OP("dve", lambda e, s3=s3: e.reciprocal(out=s3[:, 1:2], in_=s3[:, 1:2]), reads=[s3.b], writes=[s3.b])
                    OP("dve", lambda e, x2t=x2t, ot=ot, s3=s3: e.scalar_tensor_tensor(out=ot[:], in0=x2t[:], scalar=s3[:, 1:2], in1=fgb[:],
                                                                                     op0=ALU.mult, op1=ALU.mult),
                       reads=[x2t.b, s3.b, fgb.b], writes=[ot.b])
                    DMA("sp", out_d[r0:r0 + 128, :], ot[:], reads=[ot.b], writes=[B_out])
        sy.barrier()
        sy.drain("sp")
    return nc


def _pack_inputs(inp, b):
    f = lambda a: np.ascontiguousarray(np.asarray(a, dtype=np.float32))
    pp = np.zeros((128, NPP), np.float32)
    pp[:, PP_BADA:PP_BADA + 48] = f(inp["b_ada"])[0].reshape(48, 128).T
    pp[:, PP_G1:PP_G1 + 8] = f(inp["norm1_g"])[0].reshape(8, 128).T
    pp[:, PP_G2:PP_G2 + 8] = f(inp["norm2_g"])[0].reshape(8, 128).T
    pp[:, PP_C:PP_C + 8] = f(inp["c"])[b].reshape(8, 128).T
    pp[:, PP_DB:PP_DB + 4] = f(inp["decay_base"])[0].reshape(4, 128).T
    pp[:, PP_IB:PP_IB + 4] = f(inp["iclr_base"])[0].reshape(4, 128).T
    pp[:, PP_KS:PP_KS + 4] = f(inp["kk_scale"])[0].reshape(4, 128).T
    pp[:, PP_MIX:PP_MIX + 4] = f(inp["k_iclr_mix"])[0].reshape(4, 128).T
    pp[:, PP_RB:PP_RB + 4] = f(inp["r_bonus"])[0].reshape(512).reshape(4, 128).T
    mu = f(inp["mu_shift"])[0]
    mu_r = np.concatenate([mu[0:512], mu[576:1088], mu[1088:1600], mu[512:576], mu[1600:1664], mu[1664:1792]])
    row = np.concatenate([mu_r, f(inp["lnx_w"])[0], f(inp["lnx_b"])[0], f(inp["fox_f_bias"])[0], f(inp["final_g"])])
    bc = np.ascontiguousarray(np.broadcast_to(row[None, :], (128, NBC)))
    ba = f(inp["b_ada"])[0]
    brow = np.concatenate([ba[2 * D:3 * D], ba[5 * D:6 * D]])[None, :]
    return pp, bc, np.ascontiguousarray(brow)


_NC_CACHE = {}


def make_in_maps(inp, S, nb):
    shared = {
        "w_ada": np.ascontiguousarray(np.asarray(inp["w_ada"], np.float32)[0]),
        "w_in": np.ascontiguousarray(np.asarray(inp["w_in"], np.float32)[0]),
        "w_decay_up": np.ascontiguousarray(np.asarray(inp["w_decay_up"], np.float32)[0]),
        "w_iclr_up": np.ascontiguousarray(np.asarray(inp["w_iclr_up"], np.float32)[0]),
        "w_gate_up": np.ascontiguousarray(np.asarray(inp["w_gate_up"], np.float32)[0]),
        "w_o_rwkv": np.ascontiguousarray(np.asarray(inp["w_o_rwkv"], np.float32)[0]),
        "w_o_fox": np.ascontiguousarray(np.asarray(inp["w_o_fox"], np.float32)[0]),
        "w_out": np.ascontiguousarray(np.asarray(inp["w_out"], np.float32)[0]),
        "w_ff1": np.ascontiguousarray(np.asarray(inp["w_ff1"], np.float32)[0]),
        "w_ff2": np.ascontiguousarray(np.asarray(inp["w_ff2"], np.float32)[0]),
    }
    maps = []
    x = np.asarray(inp["x"], np.float32)
    for b in range(nb):
        pp, bc, brow = _pack_inputs(inp, b)
        m = dict(shared)
        m.update({"x": np.ascontiguousarray(x[b]), "pp": pp, "bc": bc, "brow": brow})
        maps.append(m)
    return maps


def kernel(**inputs):
    x = np.asarray(inputs["x"])
    nb, S = x.shape[0], x.shape[1]
    if S not in _NC_CACHE:
        _NC_CACHE[S] = build(S)
    nc = _NC_CACHE[S]
    maps = make_in_maps(inputs, S, nb)
    res = run_bass_kernel_spmd(nc, maps, core_ids=list(range(nb)))
    return np.stack([np.asarray(r["out"], np.float32) for r in res.results], axis=0)
```

```python
import numpy as np
from contextlib import ExitStack
import concourse.bass as bass
import concourse.mybir as mybir
from concourse.alu_op_type import AluOpType as ALU
from concourse.bass_utils import run_bass_kernel_spmd

F32 = mybir.dt.float32
BF16 = mybir.dt.bfloat16
AF = mybir.ActivationFunctionType
AX = mybir.AxisListType

D = 1024
NRW = 1792
NFX = 1544
NIN = 5384
EXPM05 = 0.6065306597126334
NORM_EPS = 1e-6
GN_EPS = 64e-5

PP_BADA, PP_G1, PP_G2, PP_C, PP_DB, PP_IB, PP_KS, PP_MIX, PP_RB, NPP = 0, 48, 56, 64, 72, 76, 80, 84, 88, 92
BC_MU, BC_LW, BC_LB, BC_FB, BC_FG, NBC = 0, 1792, 2304, 2816, 2824, 3848


class Buf:
    __slots__ = ("name", "lw", "rd")

    def __init__(self, name=""):
        self.name = name
        self.lw = None
        self.rd = {}


class Sync:
    def __init__(self, nc, es, n_dma_sems=32):
        self.nc = nc
        self.eng = {"pe": nc.tensor, "act": nc.scalar, "dve": nc.vector, "pool": nc.gpsimd, "sp": nc.sync}
        self.sem = {k: es.enter_context(nc.semaphore("sem_" + k)) for k in self.eng}
        self.cnt = {k: 0 for k in self.eng}
        self.dsem = [es.enter_context(nc.semaphore("dsem%d" % i)) for i in range(n_dma_sems)]
        self.dcnt = [0] * n_dma_sems
        self.dnext = 0
        self.dnext_sw = 0
        self.seen = {k: {} for k in self.eng}
        self.dead = False
        self.lazy = {"pe"}
        self.unflushed = {}
        self.last_inst = {}

    def _flush(self, key):
        if self.unflushed.get(key):
            self.last_inst[key].then_inc(self.sem[key], 1)
            self.cnt[key] += 1
            self.unflushed[key] = False

    def _wait(self, e, key, val):
        if self.seen[e].get(key, 0) >= val:
            return
        if isinstance(key, str) and val > self.cnt[key]:
            assert key in self.lazy and val == self.cnt[key] + 1, (key, val, self.cnt[key])
            self._flush(key)
        sem = self.sem[key] if isinstance(key, str) else self.dsem[key]
        self.eng[e].wait_ge(sem, val)
        self.seen[e][key] = val

    def _deps(self, e, reads, writes):
        deps = {}

        def add(k, v):
            if deps.get(k, 0) < v:
                deps[k] = v
        for b in reads:
            if b.lw is not None:
                add(*b.lw)
        for b in writes:
            if b.lw is not None:
                add(*b.lw)
            for k, v in b.rd.items():
                add(k, v)
        for k, v in deps.items():
            if k == e and e == "pe":
                continue
            self._wait(e, k, v)

    def _post(self, ev, reads, writes):
        for b in reads:
            if b.rd.get(ev[0], 0) < ev[1]:
                b.rd[ev[0]] = ev[1]
        for b in writes:
            b.lw = ev
            b.rd = {}

    def op(self, e, fn, reads=(), writes=()):
        if self.dead:
            return None
        if e != "pe":
            pr_ = [b for b in reads if b.name.startswith("bank")]
            if pr_:
                reads = [b for b in reads if not b.name.startswith("bank")]
                writes = list(writes) + pr_
        self._deps(e, reads, writes)
        inst = fn(self.eng[e])
        if e in self.lazy:
            self.last_inst[e] = inst
            self.unflushed[e] = True
            self._post((e, self.cnt[e] + 1), reads, writes)
            return inst
        self.cnt[e] += 1
        inst.then_inc(self.sem[e], 1)
        self._post((e, self.cnt[e]), reads, writes)
        return inst

    def dma(self, e, out, in_, reads=(), writes=()):
        if self.dead:
            return None
        nsw = 8
        if e == "pool":
            k = self.dnext_sw
            self.dnext_sw = (self.dnext_sw + 1) % nsw
        else:
            k = nsw + self.dnext
            self.dnext = (self.dnext + 1) % (len(self.dsem) - nsw)
        if self.dcnt[k] > 0:
            self._wait(e, k, self.dcnt[k])
        self._deps(e, reads, writes)
        inst = self.eng[e].dma_start(out=out, in_=in_)
        self.dcnt[k] += 16
        inst.then_inc(self.dsem[k], 16)
        self._post((k, self.dcnt[k]), reads, writes)
        return inst

    def barrier(self):
        for k in list(self.lazy):
            self._flush(k)
        for e in self.eng:
            for k in self.eng:
                if self.cnt[k]:
                    self._wait(e, k, self.cnt[k])
            for k in range(len(self.dsem)):
                if self.dcnt[k]:
                    self._wait(e, k, self.dcnt[k])

    def drain(self, e="sp"):
        for k in list(self.lazy):
            self._flush(k)
        for k in range(len(self.dsem)):
            if self.dcnt[k]:
                self._wait(e, k, self.dcnt[k])
        for k in self.eng:
            if k != e and self.cnt[k]:
                self._wait(e, k, self.cnt[k])


class T:
    def __init__(self, es, nc, name, shape, dtype):
        self.t = es.enter_context(nc.sbuf_tensor("sb_" + name, shape, dtype))
        self.b = Buf(name)

    def __getitem__(self, idx):
        return self.t[idx]


class _Stop(Exception):
    pass


RR = [1, 1, 1, 1]
ORDER = [0, 1, 2, 3]
NDUMMY = 0


def build(S, dbg=False, stop=None):
    assert S % 512 == 0

    def CP(name):
        if stop == name:
            sy_box[0].dead = True

    sy_box = [None]
    NT = S // 512
    NS = S // 128
    nc = bass.Bass("TRN2", target_bir_lowering=False)

    def din(n, shp, dt=F32):
        return nc.dram_tensor(n, shp, dt, kind="ExternalInput").ap()

    def dscr(n, shp, dt):
        return nc.dram_tensor(n, shp, dt, kind="ExternalOutput" if dbg else "Internal").ap()

    x_d = din("x", [S, D])
    pp_d = din("pp", [128, NPP])
    bc_d = din("bc", [128, NBC])
    brow_d = din("brow", [1, 2048])
    wada_d = din("w_ada", [D, 6 * D])
    win_d = din("w_in", [D, NIN])
    wdu_d = din("w_decay_up", [64, 512])
    wiu_d = din("w_iclr_up", [64, 512])
    wgu_d = din("w_gate_up", [128, 512])
    wor_d = din("w_o_rwkv", [512, D])
    wof_d = din("w_o_fox", [512, D])
    wout_d = din("w_out", [D, D])
    wff1_d = din("w_ff1", [D, 4 * D])
    wff2_d = din("w_ff2", [4 * D, D])
    out_d = nc.dram_tensor("out", [S, D], F32, kind="ExternalOutput").ap()

    hT_d = dscr("hT_s", [128, 8, S], BF16)
    yaT_d = dscr("yaT_s", [128, 4, S], BF16)
    ybT_d = dscr("ybT_s", [512, S], BF16)
    x1_d = dscr("x1_s", [S, D], F32)
    h2T_d = dscr("h2T_s", [128, 8, S], BF16)
    gt_d = dscr("gt_s", [128, 2048], F32)
    B_hT, B_yaT, B_ybT, B_x1, B_h2T, B_gt, B_out = (Buf(n) for n in "hT yaT ybT x1 h2T gt out".split())

    with ExitStack() as es:
        sy = Sync(nc, es)
        sy_box[0] = sy
        OP = sy.op
        DMA = sy.dma

        def SB(scope, name, shape, dt):
            return T(scope, nc, name, shape, dt)

        banks = [es.enter_context(nc.psum_tensor("bank%d" % i, [128, 512], F32)) for i in range(8)]
        bbufs = [Buf("bank%d" % i) for i in range(8)]
        pstate = {"i": 0}

        held = set()

        def PS(hold=False):
            assert len(held) < 8, "all PSUM banks held"
            while True:
                i = pstate["i"]
                pstate["i"] = (i + 1) % 8
                if i not in held:
                    break
            if hold:
                held.add(i)
            return banks[i], bbufs[i]

        def PSREL(bank):
            held.discard(banks.index(bank))

        def MM(out, lhsT, rhs, start, stop, reads, pb):
            OP("pe", lambda e: e.matmul(out, lhsT, rhs, start=start, stop=stop, skip_group_check=True),
               reads=reads, writes=[pb])

        def TR(out, in_, ident, reads, pb):
            OP("pe", lambda e: e.transpose(out, in_, ident), reads=reads, writes=[pb])

        ppt = SB(es, "ppt", [128, NPP], F32)
        ident = SB(es, "ident", [128, 128], BF16)
        modT = SB(es, "modT", [128, 48], F32)
        GS = SB(es, "GS", [128, 32], F32)
        DMA("sp", ppt[:], pp_d, writes=[ppt.b])
        OP("pool", lambda e: e.memset(ident[:], 1.0), writes=[ident.b])
        OP("pool", lambda e: e.affine_select(out=ident[:], in_=ident[:], pattern=[[-1, 128]],
                                             compare_op=ALU.is_equal, fill=0.0, base=0, channel_multiplier=1),
           reads=[ident.b], writes=[ident.b])

        with ExitStack() as s0:
            wada = SB(s0, "wada", [128, 8, 6 * D], BF16)
            wada_b = [Buf("wada%d" % k) for k in range(8)]
            cact = SB(s0, "cact", [128, 8], BF16)
            crep = SB(s0, "crep", [128, 8, 128], BF16)
            onesf = SB(s0, "onesf", [1, 128], F32)
            brow = SB(s0, "brow", [1, 2048], F32)
            gtbc = SB(s0, "gtbc", [128, 2048], F32)
            wadaB_b = [Buf("wadaB%d" % k) for k in range(8)]
            for kc in range(8):
                DMA("pool", wada[:, kc, 0:2 * D], wada_d[kc * 128:(kc + 1) * 128, 0:2 * D], writes=[wada_b[kc]])
            for kc in range(8):
                DMA("pool", wada[:, kc, 2 * D:6 * D], wada_d[kc * 128:(kc + 1) * 128, 2 * D:6 * D], writes=[wadaB_b[kc]])
            DMA("sp", brow[:], brow_d, writes=[brow.b])
            OP("dve", lambda e: e.memset(onesf[:], 1.0), writes=[onesf.b])
            OP("act", lambda e: e.activation(out=cact[:], in_=ppt[:, PP_C:PP_C + 8], func=AF.Silu),
               reads=[ppt.b], writes=[cact.b])
            for kc in range(8):
                OP("dve", lambda e, kc=kc: e.tensor_copy(out=crep[:, kc, :],
                                                         in_=cact[:, kc:kc + 1].broadcast_to([128, 128])),
                   reads=[cact.b], writes=[crep.b])
            pa, pab = PS(hold=True)
            for j in range(16):
                for kc in range(8):
                    MM(pa[:, j:j + 1], wada[:, kc, j * 128:(j + 1) * 128], cact[:, kc:kc + 1],
                       kc == 0, kc == 7, [wada_b[kc], cact.b], pab)
            OP("dve", lambda e: e.tensor_tensor(out=modT[:, 0:16], in0=pa[:, 0:16], in1=ppt[:, PP_BADA:PP_BADA + 16],
                                                op=ALU.add), reads=[pab, ppt.b], writes=[modT.b])
            PSREL(pa)
            OP("dve", lambda e: e.scalar_tensor_tensor(out=GS[:, 0:8], in0=modT[:, 8:16], scalar=1.0,
                                                       in1=ppt[:, PP_G1:PP_G1 + 8], op0=ALU.add, op1=ALU.mult),
               reads=[modT.b, ppt.b], writes=[GS.b])
            OP("dve", lambda e: e.tensor_copy(out=GS[:, 8:16], in_=modT[:, 0:8]), reads=[modT.b], writes=[GS.b])

            def mod_part2():
                GS2 = Buf("GS2")
                pa2_, pa2b_ = PS(hold=True)
                for j in range(16, 48):
                    for kc in range(8):
                        MM(pa2_[:, j:j + 1], wada[:, kc, j * 128:(j + 1) * 128], cact[:, kc:kc + 1],
                           kc == 0, kc == 7, [wadaB_b[kc], cact.b], pa2b_)
                OP("dve", lambda e: e.tensor_tensor(out=modT[:, 16:48], in0=pa2_[:, 16:48], in1=ppt[:, PP_BADA + 16:PP_BADA + 48],
                                                    op=ALU.add), reads=[pa2b_, ppt.b], writes=[modT.b])
                PSREL(pa2_)
                OP("dve", lambda e: e.scalar_tensor_tensor(out=GS[:, 16:24], in0=modT[:, 32:40], scalar=1.0,
                                                           in1=ppt[:, PP_G2:PP_G2 + 8], op0=ALU.add, op1=ALU.mult),
                   reads=[modT.b, ppt.b], writes=[GS.b])
                OP("dve", lambda e: e.tensor_copy(out=GS[:, 24:32], in_=modT[:, 24:32]), reads=[modT.b], writes=[GS.b])
                for part, col0 in enumerate((2 * D, 5 * D)):
                    for half in range(2):
                        pg, pgb = PS(hold=True)
                        for kc in range(8):
                            MM(pg[:, :], crep[:, kc, :], wada[:, kc, col0 + half * 512: col0 + (half + 1) * 512],
                               kc == 0, False, [crep.b, wadaB_b[kc]], pgb)
                        o = part * 1024 + half * 512
                        MM(pg[:, :], onesf[0:1, :], brow[0:1, o:o + 512], False, True, [onesf.b, brow.b], pgb)
                        OP("act", lambda e, pg=pg, o=o: e.copy(out=gtbc[:, o:o + 512], in_=pg[:, :]),
                           reads=[pgb], writes=[gtbc.b])
                        PSREL(pg)
                DMA("sp", gt_d, gtbc[:], reads=[gtbc.b], writes=[B_gt])

            xb = [SB(s0, "xb%d" % i, [128, 4, D], F32) for i in range(2)]
            xn = [SB(s0, "xn%d" % i, [128, 4, D], BF16) for i in range(2)]
            hTs = [SB(s0, "hTs%d" % i, [128, 8, 512], BF16) for i in range(2)]
            junk = SB(s0, "junk0", [128, D], BF16)
            ssq = [SB(s0, "ssq%d" % i, [128, 4], F32) for i in range(2)]
            rstd = [SB(s0, "rstd%d" % i, [128, 4], F32) for i in range(2)]
            def load_xt(i):
                if i < NT:
                    DMA("sp", xb[i % 2][:], x_d[i * 512:(i + 1) * 512, :].rearrange("(s p) d -> p s d", p=128), writes=[xb[i % 2].b])

            def norm_a(i):
                xt, xnt, sq, rs = xb[i % 2], xn[i % 2], ssq[i % 2], rstd[i % 2]
                for s in range(4):
                    OP("act", lambda e, s=s: e.activation(out=junk[:], in_=xt[:, s, :], func=AF.Square, accum_out=sq[:, s:s + 1]),
                       reads=[xt.b], writes=[junk.b, sq.b])
                OP("dve", lambda e: e.tensor_scalar(out=rs[:], in0=sq[:], scalar1=1.0 / D, scalar2=NORM_EPS,
                                                    op0=ALU.mult, op1=ALU.add), reads=[sq.b], writes=[rs.b])
                OP("act", lambda e: e.activation(out=rs[:], in_=rs[:], func=AF.Sqrt), reads=[rs.b], writes=[rs.b])
                OP("dve", lambda e: e.reciprocal(out=rs[:], in_=rs[:]), reads=[rs.b], writes=[rs.b])
                for s in range(4):
                    if s % 2 == 0:
                        OP("dve", lambda e, s=s: e.tensor_scalar(out=xnt[:, s, :], in0=xt[:, s, :],
                                                                 scalar1=rs[:, s:s + 1], scalar2=None, op0=ALU.mult),
                           reads=[xt.b, rs.b], writes=[xnt.b])
                    else:
                        OP("act", lambda e, s=s: e.activation(out=xnt[:, s, :], in_=xt[:, s, :], func=AF.Copy, scale=rs[:, s:s + 1]),
                           reads=[xt.b, rs.b], writes=[xnt.b])

            def norm_b(i):
                xnt, ht = xn[i % 2], hTs[i % 2]
                for kc in range(8):
                    p, pb = PS()
                    pv = p.bitcast(BF16)
                    for s in range(4):
                        TR(pv[:, s * 128:(s + 1) * 128], xnt[:, s, kc * 128:(kc + 1) * 128], ident[:], [xnt.b, ident.b], pb)
                    if kc % 2 == 0:
                        OP("dve", lambda e, kc=kc, pv=pv: e.tensor_scalar(
                            out=ht[:, kc, :], in0=pv[:, 0:512], scalar1=GS[:, kc:kc + 1], scalar2=GS[:, 8 + kc:9 + kc],
                            op0=ALU.mult, op1=ALU.add), reads=[pb, GS.b], writes=[ht.b])
                    else:
                        OP("act", lambda e, kc=kc, pv=pv: e.activation(
                            out=ht[:, kc, :], in_=pv[:, 0:512], func=AF.Identity, scale=GS[:, kc:kc + 1],
                            bias=GS[:, 8 + kc:9 + kc]), reads=[pb, GS.b], writes=[ht.b])
                DMA("sp", hT_d[:, :, i * 512:(i + 1) * 512], ht[:], reads=[ht.b], writes=[B_hT])

            load_xt(0)
            load_xt(1)
            norm_a(0)
            for i in range(NT):
                if i + 1 < NT:
                    norm_a(i + 1)
                load_xt(i + 2)
                norm_b(i)
            mod_part2()


        sy.barrier()
        try:
          with ExitStack() as sr:
              WA = SB(sr, "WA", [128, 8, NRW], BF16)
              WB = SB(sr, "WB", [128, 8, NRW], BF16)
              wdu = SB(sr, "wdu", [64, 512], BF16)
              wiu = SB(sr, "wiu", [128, 512], BF16)
              wgu = SB(sr, "wgu", [128, 512], BF16)
              lnw = SB(sr, "lnw", [128, 1024], F32)
              DMA("pool", wdu[:], wdu_d, writes=[wdu.b])
              DMA("pool", wiu[64:128, :], wiu_d, writes=[wiu.b])
              DMA("pool", wgu[:], wgu_d, writes=[wgu.b])
              DMA("sp", lnw[:], bc_d[:, BC_LW:BC_LW + 1024], writes=[lnw.b])
              with ExitStack() as sw:
                  mub = SB(sw, "mub", [128, NRW], F32)
                  omub = SB(sw, "omub", [128, NRW], F32)
                  wst = [SB(sw, "wst%d" % i, [128, NRW], F32) for i in range(2)]
                  DMA("sp", mub[:], bc_d[:, BC_MU:BC_MU + NRW], writes=[mub.b])
                  OP("dve", lambda e: e.tensor_scalar(out=omub[:], in0=mub[:], scalar1=-1.0, scalar2=1.0,
                                                      op0=ALU.mult, op1=ALU.add), reads=[mub.b], writes=[omub.b])
                  segs = [(0, 512, 0), (576, 1088, 512), (1088, 1600, 1024), (512, 576, 1536), (1600, 1664, 1600),
                          (1664, 1792, 1664)]
                  seg_b = [[Buf("wseg%d_%d" % (i_, j_)) for j_ in range(len(segs))] for i_ in range(2)]
                  for kc in range(8):
                      w = wst[kc % 2]
                      for j_, (a, b_, o) in enumerate(segs):
                          DMA("sp", w[:, o:o + (b_ - a)], win_d[kc * 128:(kc + 1) * 128, a:b_], reads=[], writes=[seg_b[kc % 2][j_]])
                      OP("dve", lambda e, kc=kc, w=w: e.tensor_tensor(out=WB[:, kc, :], in0=w[:], in1=mub[:], op=ALU.mult),
                         reads=seg_b[kc % 2] + [mub.b], writes=[WB.b])
                      OP("dve", lambda e, kc=kc, w=w: e.tensor_tensor(out=WA[:, kc, :], in0=w[:], in1=omub[:], op=ALU.mult),
                         reads=seg_b[kc % 2] + [omub.b], writes=[WA.b])
              sy.barrier()
              CP('w')

              blk1 = SB(sr, "blk1", [128, 128], F32)
              hsel = SB(sr, "hsel", [128, 2], BF16)
              mskX = SB(sr, "mskX", [128, 4, 2, 64], F32)
              mskL = SB(sr, "mskL", [128, 8, 64], F32)
              i64 = SB(sr, "i64", [128, 64], F32)
              scm = SB(sr, "scm", [128, 512], F32)
              omix = SB(sr, "omix", [128, 4], F32)
              OP("pool", lambda e: e.memset(blk1[:], 0.0), writes=[blk1.b])
              OP("pool", lambda e: e.memset(blk1[0:64, 0:64], 1.0), writes=[blk1.b])
              OP("pool", lambda e: e.memset(blk1[64:128, 64:128], 1.0), writes=[blk1.b])
              OP("pool", lambda e: e.memset(hsel[:], 0.0), writes=[hsel.b])
              OP("pool", lambda e: e.memset(hsel[0:64, 0:1], 1.0), writes=[hsel.b])
              OP("pool", lambda e: e.memset(hsel[64:128, 1:2], 1.0), writes=[hsel.b])
              OP("pool", lambda e: e.memset(mskX[:], 1.0), writes=[mskX.b])
              OP("pool", lambda e: e.memset(mskL[:], 1.0), writes=[mskL.b])
              OP("pool", lambda e: e.memset(i64[:], 1.0), writes=[i64.b])
              for hf in range(2):
                  ps_ = slice(hf * 64, (hf + 1) * 64)
                  OP("pool", lambda e, ps_=ps_: e.affine_select(out=mskX[ps_], in_=mskX[ps_], pattern=[[0, 4], [1, 2], [1, 64]],
                                                                compare_op=ALU.is_gt, fill=0.0, base=0, channel_multiplier=-1),
                     reads=[mskX.b], writes=[mskX.b])
                  OP("pool", lambda e, ps_=ps_: e.affine_select(out=mskL[ps_], in_=mskL[ps_], pattern=[[0, 8], [-1, 64]],
                                                                compare_op=ALU.is_gt, fill=0.0, base=0, channel_multiplier=1),
                     reads=[mskL.b], writes=[mskL.b])
                  OP("pool", lambda e, ps_=ps_: e.affine_select(out=i64[ps_], in_=i64[ps_], pattern=[[-1, 64]],
                                                                compare_op=ALU.is_equal, fill=0.0, base=0, channel_multiplier=1),
                     reads=[i64.b], writes=[i64.b])
              OP("pool", lambda e: e.memset(scm[:], 1.0), writes=[scm.b])
              OP("pool", lambda e: e.memset(scm[:].rearrange("p (a b) -> p a b", b=64)[:, :, 0:1], 0.0), writes=[scm.b])
              OP("dve", lambda e: e.tensor_scalar(out=omix[:], in0=ppt[:, PP_MIX:PP_MIX + 4], scalar1=-1.0, scalar2=1.0,
                                                  op0=ALU.mult, op1=ALU.add), reads=[ppt.b], writes=[omix.b])

              CP('c')

              def bc4(col0):
                  return ppt[:, col0:col0 + 4].unsqueeze(2).broadcast_to([128, 4, 128])

              Hs = SB(sr, "Hs", [128, 4, 64], F32)
              Hb2 = [SB(sr, "Hb%d" % i_, [128, 4, 64], BF16) for i_ in range(2)]
              OP("dve", lambda e: e.memset(Hs[:], 0.0), writes=[Hs.b])
              OP("dve", lambda e: e.memset(Hb2[0][:], 0.0), writes=[Hb2[0].b])

              def DB(name, shape, dt):
                  return [SB(sr, "%s_%d" % (name, i_), shape, dt) for i_ in range(2)]
              hcs = DB("hc", [128, 8, 128], BF16)
              hps = DB("hp", [128, 8, 128], BF16)
              rk2 = DB("rk", [128, 8, 128], F32)
              tw2 = DB("tw", [64, 128], BF16)
              adb2 = DB("adb", [128, 128], BF16)
              sgd2 = DB("sgd", [128, 128], BF16)
              Vf2 = [SB(sr, "Vf_%d" % i_, [128, 512], F32) for i_ in range(4)]
              Vb2 = [SB(sr, "Vb_%d" % i_, [128, 512], BF16) for i_ in range(4)]

              def TB(name, shape, dt):
                  return [SB(sr, "%s_%d" % (name, i_), shape, dt) for i_ in range(3)]
              sw_ = SB(sr, "sw_", [128, 512], F32)
              aic = SB(sr, "aic", [128, 4, 128], F32)
              gsb2 = TB("gsb", [128, 512], F32)
              ld = SB(sr, "ld", [128, 512], F32)
              lP = SB(sr, "lP", [128, 512], F32)
              lPx = SB(sr, "lPx", [128, 512], F32)
              Pc2 = TB("Pc", [128, 4, 128], F32)
              Pinv = SB(sr, "Pinv", [128, 4, 128], F32)
              Pex = SB(sr, "Pex", [128, 4, 128], F32)
              ksc = SB(sr, "ksc", [128, 4, 128], F32)
              ksq = SB(sr, "ksq", [128, 4, 128], F32)
              rn = SB(sr, "rn", [128, 4, 128], F32)
              kk = SB(sr, "kk", [128, 4, 128], F32)
              t1 = SB(sr, "t1", [128, 4, 128], F32)
              kmod = SB(sr, "kmod", [128, 4, 128], F32)
              t2 = SB(sr, "t2", [128, 4, 128], F32)
              AR2 = TB("AR", [128, 4, 2, 2, 64], BF16)
              bT = SB(sr, "bT", [128, 4, 128], BF16)
              kT = SB(sr, "kT", [128, 4, 128], BF16)
              rkb2 = TB("rkb", [128, 4, 128], BF16)
              bt2 = TB("bt", [128, 512], BF16)
              kt2 = TB("kt", [128, 512], BF16)
              MB2 = TB("MB", [128, 8, 2, 64], BF16)
              MK2 = TB("MK", [128, 8, 2, 64], BF16)
              Acur2 = [[SB(sr, "Acur%d_%d" % (j_, i), [128, 8, 64], BF16) for i in range(2)] for j_ in range(2)]
              Mcur2 = [[SB(sr, "Mcur%d_%d" % (j_, i), [128, 8, 64], BF16) for i in range(2)] for j_ in range(2)]
              Xc2 = [[SB(sr, "Xc%d_%d" % (j_, i), [128, 8, 64], BF16) for i in range(2)] for j_ in range(2)]
              W0sb2 = TB("W0sb", [128, 512], F32)
              Wsb = SB(sr, "Wsb", [128, 512], BF16)
              Usb = SB(sr, "Usb", [128, 512], BF16)
              Htmp = SB(sr, "Htmp", [128, 4, 64], F32)
              yf = SB(sr, "yf", [128, 8, 64], F32)
              ysq = SB(sr, "ysq", [128, 8, 64], F32)
              st = SB(sr, "st", [128, 32], F32)
              bs = SB(sr, "bs", [128, 8], F32)
              yg = SB(sr, "yg", [128, 512], BF16)
              yaTs = [SB(sr, "yaTs%d" % i, [128, 4, 128], BF16) for i in range(2)]

              def hsl(h):
                  return h // 2, slice((h % 2) * 64, (h % 2) * 64 + 64)
              fl = lambda t_: t_[:].rearrange("p a t -> p (a t)")
              v4 = lambda t_: t_[:].rearrange("p a (c t) -> p a c t", c=2)
              par = lambda ap_, e_: ap_.rearrange("p (a e v) -> p e a v", e=2, v=64)[:, e_]
              b8 = lambda a_: a_.unsqueeze(2).broadcast_to([128, 8, 64])
              Xfin = {}

              def phaseP(n):
                  d = n % 2
                  hh, hold, hp = hcs[d], hcs[1 - d], hps[d]
                  rk, tw, adb, sgd = rk2[d], tw2[d], adb2[d], sgd2[d]
                  Vf, Vb = Vf2[n % 4], Vb2[n % 4]
                  DMA("sp", hh[:], hT_d[:, :, n * 128:(n + 1) * 128], writes=[hh.b])
                  if n == 0:
                      OP("pool", lambda e: e.memset(hp[:, :, 0:1], 0.0), writes=[hp.b])
                      DMA("sp", hp[:, :, 1:128], hT_d[:, :, 0:127], writes=[hp.b])
                  else:
                      DMA("sp", hp[:], hT_d[:, :, n * 128 - 1:(n + 1) * 128 - 1], writes=[hp.b])
                  yield

                  def fproj(pt, ptb, slot, col0):
                      for i_ in range(16):
                          kc, sh = i_ % 8, i_ // 8
                          W = WA if sh == 0 else WB
                          hx = hh if sh == 0 else hp
                          MM(pt[:, slot * 128:(slot + 1) * 128], W[:, kc, col0:col0 + 128], hx[:, kc, :], i_ == 0, i_ == 15,
                             [W.b, hx.b], ptb)
                  pr, prb = PS(hold=True)
                  for c4 in range(4):
                      fproj(pr, prb, c4, c4 * 128)
                      if c4 % 2 == 1:
                          yield
                  OP("act", lambda e: e.copy(out=rk[:, 0:4, :], in_=pr[:, :].rearrange("p (a t) -> p a t", a=4)), reads=[prb], writes=[rk.b])
                  PSREL(pr)
                  pk, pkb = PS(hold=True)
                  for c4 in range(4):
                      fproj(pk, pkb, c4, 512 + c4 * 128)
                      if c4 % 2 == 1:
                          yield
                  OP("dve", lambda e: e.tensor_scalar(out=rk[:, 4:8, :], in0=pk[:, :].rearrange("p (a t) -> p a t", a=4), scalar1=1.0,
                                                      scalar2=None, op0=ALU.mult), reads=[pkb], writes=[rk.b])
                  PSREL(pk)
                  pw, pwb = PS(hold=True)
                  fproj(pw, pwb, 0, 1536)
                  yield
                  fproj(pw, pwb, 1, 1664)
                  OP("act", lambda e: e.activation(out=tw[:], in_=pw[0:64, 0:128], func=AF.Tanh), reads=[pwb], writes=[tw.b])
                  OP("dve", lambda e: e.tensor_scalar(out=adb[64:128, :], in0=pw[64:128, 0:128], scalar1=1.0, scalar2=None, op0=ALU.mult),
                     reads=[pwb], writes=[adb.b])
                  OP("act", lambda e: e.activation(out=sgd[:], in_=pw[:, 128:256], func=AF.Sigmoid), reads=[pwb], writes=[sgd.b])
                  PSREL(pw)
                  yield
                  pv_, pvb = PS(hold=True)
                  for i_ in range(16):
                      kc, sh = i_ % 8, i_ // 8
                      W = WA if sh == 0 else WB
                      hx = hh if sh == 0 else hp
                      MM(pv_[:, :], hx[:, kc, :], W[:, kc, 1024:1536], i_ == 0, i_ == 15, [W.b, hx.b], pvb)
                      if i_ == 7:
                          yield
                  OP("act", lambda e: e.copy(out=Vf[:], in_=pv_[:, :]), reads=[pvb], writes=[Vf.b])
                  OP("dve", lambda e: e.tensor_scalar(out=Vb[:], in0=pv_[:, :], scalar1=1.0, scalar2=None, op0=ALU.mult), reads=[pvb], writes=[Vb.b])
                  PSREL(pv_)

              def phaseA(n):
                  d = n % 2
                  t3 = n % 3
                  rk, tw, adb, sgd = rk2[d], tw2[d], adb2[d], sgd2[d]
                  Vf, Vb = Vf2[n % 4], Vb2[n % 4]
                  gsb, Pc, AR, rkb, bt, kt, MB, MK, W0sb = (gsb2[t3], Pc2[t3], AR2[t3], rkb2[t3], bt2[t3],
                                                           kt2[t3], MB2[t3], MK2[t3], W0sb2[t3])
                  Xc = Xc2[d]
                  Acur, Mcur = Acur2[d], Mcur2[d]
                  r4 = rk[:, 0:4, :]
                  k4 = rk[:, 4:8, :]
                  pz, pzb = PS(hold=True)
                  pa_, pab_ = PS(hold=True)
                  pg_, pgb_ = PS(hold=True)
                  for c4 in range(4):
                      MM(pz[:, c4 * 128:(c4 + 1) * 128], wdu[0:64, c4 * 128:(c4 + 1) * 128], tw[0:64, :], True, True, [wdu.b, tw.b], pzb)
                  for c4 in range(4):
                      MM(pa_[:, c4 * 128:(c4 + 1) * 128], wiu[64:128, c4 * 128:(c4 + 1) * 128], adb[64:128, :], True, True, [wiu.b, adb.b], pab_)
                  MM(pg_[:, :], sgd[:, :], wgu[:, :], True, True, [sgd.b, wgu.b], pgb_)
                  for c4 in range(4):
                      OP("act", lambda e, c4=c4: e.activation(out=sw_[:, c4 * 128:(c4 + 1) * 128], in_=pz[:, c4 * 128:(c4 + 1) * 128],
                                                              func=AF.Sigmoid, bias=ppt[:, PP_DB + c4:PP_DB + c4 + 1]),
                         reads=[pzb, ppt.b], writes=[sw_.b])
                  PSREL(pz)
                  for c4 in range(4):
                      OP("act", lambda e, c4=c4: e.activation(out=aic[:, c4, :], in_=pa_[:, c4 * 128:(c4 + 1) * 128],
                                                              func=AF.Sigmoid, bias=ppt[:, PP_IB + c4:PP_IB + c4 + 1]),
                         reads=[pab_, ppt.b], writes=[aic.b])
                  PSREL(pa_)
                  OP("act", lambda e: e.copy(out=gsb[:], in_=pg_[:, :]), reads=[pgb_], writes=[gsb.b])
                  PSREL(pg_)
                  yield
                  OP("dve", lambda e: e.tensor_tensor(out=ksc[:], in0=k4, in1=bc4(PP_KS), op=ALU.mult), reads=[rk.b, ppt.b], writes=[ksc.b])
                  OP("act", lambda e: e.activation(out=fl(ksq), in_=fl(ksc), func=AF.Square), reads=[ksc.b], writes=[ksq.b])
                  for c4 in range(4):
                      OP("act", lambda e, c4=c4: e.activation(out=t1[:, c4, :], in_=aic[:, c4, :], func=AF.Identity,
                                                              scale=ppt[:, PP_MIX + c4:PP_MIX + c4 + 1], bias=omix[:, c4:c4 + 1]),
                         reads=[aic.b, ppt.b, omix.b], writes=[t1.b])
                  OP("dve", lambda e: e.tensor_tensor(out=kmod[:], in0=k4, in1=t1[:], op=ALU.mult), reads=[rk.b, t1.b], writes=[kmod.b])
                  yield
                  OP("dve", lambda e: e.tensor_scalar(out=ld[:], in0=sw_[:], scalar1=-EXPM05, scalar2=None, op0=ALU.mult), reads=[sw_.b], writes=[ld.b])
                  OP("dve", lambda e: e.tensor_tensor_scan(out=lP[:], data0=scm[:], data1=ld[:], initial=0.0, op0=ALU.mult, op1=ALU.add),
                     reads=[scm.b, ld.b], writes=[lP.b])
                  OP("dve", lambda e: e.tensor_tensor(out=lPx[:], in0=lP[:], in1=ld[:], op=ALU.subtract), reads=[lP.b, ld.b], writes=[lPx.b])
                  pq, pqb = PS(hold=True)
                  MM(pq[:, :], blk1[:, :], fl(ksq), True, True, [blk1.b, ksq.b], pqb)
                  OP("act", lambda e: e.activation(out=fl(Pc), in_=lP[:], func=AF.Exp), reads=[lP.b], writes=[Pc.b])
                  OP("act", lambda e: e.activation(out=fl(Pinv), in_=lP[:], func=AF.Exp, scale=-1.0), reads=[lP.b], writes=[Pinv.b])
                  OP("act", lambda e: e.activation(out=fl(Pex), in_=lPx[:], func=AF.Exp), reads=[lPx.b], writes=[Pex.b])
                  OP("act", lambda e: e.activation(out=fl(rn), in_=pq[:, :], func=AF.Ln), reads=[pqb], writes=[rn.b])
                  PSREL(pq)
                  OP("act", lambda e: e.activation(out=fl(rn), in_=fl(rn), func=AF.Exp, scale=-0.5), reads=[rn.b], writes=[rn.b])
                  OP("dve", lambda e: e.tensor_tensor(out=kk[:], in0=ksc[:], in1=rn[:], op=ALU.mult), reads=[ksc.b, rn.b], writes=[kk.b])
                  yield
                  OP("dve", lambda e: e.scalar_tensor_tensor(out=AR[:, :, :, 0, :], in0=v4(kk), scalar=-1.0, in1=v4(Pex), op0=ALU.mult, op1=ALU.mult),
                     reads=[kk.b, Pex.b], writes=[AR.b])
                  OP("pool", lambda e: e.tensor_tensor(out=AR[:, :, :, 1, :], in0=r4.rearrange("p a (c t) -> p a c t", c=2), in1=v4(Pc), op=ALU.mult),
                     reads=[rk.b, Pc.b, AR.b], writes=[AR.b])
                  OP("dve", lambda e: e.tensor_tensor(out=t2[:], in0=kk[:], in1=aic[:], op=ALU.mult), reads=[kk.b, aic.b], writes=[t2.b])
                  OP("dve", lambda e: e.tensor_tensor(out=bT[:], in0=t2[:], in1=Pinv[:], op=ALU.mult), reads=[t2.b, Pinv.b], writes=[bT.b])
                  OP("pool", lambda e: e.tensor_tensor(out=kT[:], in0=kmod[:], in1=Pinv[:], op=ALU.mult), reads=[kmod.b, Pinv.b], writes=[kT.b])
                  OP("pool", lambda e: e.tensor_tensor(out=t1[:], in0=r4, in1=kmod[:], op=ALU.mult), reads=[rk.b, kmod.b, t1.b], writes=[t1.b])
                  OP("pool", lambda e: e.tensor_tensor(out=rkb[:], in0=t1[:], in1=bc4(PP_RB), op=ALU.mult), reads=[t1.b, ppt.b], writes=[rkb.b])
                  yield
                  yield
                  yield
                  ptb_, ptbb = PS(hold=True)
                  ptk_, ptkb = PS(hold=True)
                  ptbv = ptb_.bitcast(BF16)
                  ptkv = ptk_.bitcast(BF16)
                  for c4 in range(4):
                      TR(ptbv[:, c4 * 128:(c4 + 1) * 128], bT[:, c4, :], ident[:], [bT.b, ident.b], ptbb)
                  for c4 in range(4):
                      TR(ptkv[:, c4 * 128:(c4 + 1) * 128], kT[:, c4, :], ident[:], [kT.b, ident.b], ptkb)
                  OP("dve", lambda e: e.tensor_scalar(out=bt[:], in0=ptbv[:, 0:512], scalar1=1.0, scalar2=None, op0=ALU.mult), reads=[ptbb], writes=[bt.b])
                  OP("act", lambda e: e.copy(out=kt[:], in_=ptkv[:, 0:512]), reads=[ptkb], writes=[kt.b])
                  PSREL(ptb_)
                  PSREL(ptk_)
                  yield
                  mX3 = mskX[:].rearrange("p a k t -> p a (k t)")
                  A0, M0, X0 = Acur[0], Mcur[0], Xc[0]
                  for e_ in range(2):
                      hs = slice(e_ * 64, e_ * 64 + 64)
                      px, pxb = PS(hold=True)
                      py, pyb = PS(hold=True)
                      pA, pAb = PS(hold=True)
                      for pair in range(4):
                          for c in range(2):
                              cs = slice(c * 64, (c + 1) * 64)
                              mov = AR[hs, pair, c, :, :].rearrange("p k t -> p (k t)")
                              MM(px[cs, pair * 128:(pair + 1) * 128], bT[hs, pair, cs], mov, True, True, [bT.b, AR.b], pxb)
                              MM(py[cs, pair * 128:(pair + 1) * 128], kT[hs, pair, cs], mov, True, True, [kT.b, AR.b], pyb)
                              MM(pA[cs, pair * 64:(pair + 1) * 64], AR[hs, pair, c, 0, :], bT[hs, pair, cs], True, True, [bT.b, AR.b], pAb)
                      OP("dve", lambda e: e.tensor_tensor(out=MB[:].rearrange("p (a e) k t -> p e a (k t)", e=2)[:, e_],
                                                          in0=px[:, :].rearrange("p (a x) -> p a x", a=4), in1=mX3, op=ALU.mult),
                         reads=[pxb, mskX.b], writes=[MB.b])
                      OP("dve", lambda e: e.tensor_tensor(out=MK[:].rearrange("p (a e) k t -> p e a (k t)", e=2)[:, e_],
                                                          in0=py[:, :].rearrange("p (a x) -> p a x", a=4), in1=mX3, op=ALU.mult),
                         reads=[pyb, mskX.b], writes=[MK.b])
                      OP("dve", lambda e: e.tensor_tensor(out=A0[:].rearrange("p (a e) t -> p e a t", e=2)[:, e_],
                                                          in0=pA[:, 0:256].rearrange("p (a t) -> p a t", a=4), in1=mskL[:, 0:4, :], op=ALU.mult),
                         reads=[pAb, mskL.b], writes=[A0.b])
                      PSREL(px)
                      PSREL(py)
                      PSREL(pA)
                      yield
                  OP("dve", lambda e: e.tensor_tensor(out=X0[:], in0=MB[:, :, 0, :], in1=i64[:].unsqueeze(1).broadcast_to([128, 8, 64]), op=ALU.add),
                     reads=[MB.b, i64.b], writes=[X0.b])
                  pW0, pW0b = PS(hold=True)
                  for c in range(2):
                      cs = slice(c * 64, (c + 1) * 64)
                      for h in range(8):
                          MM(pW0[cs, h * 64:(h + 1) * 64], MK[cs, h, 0, :], Vb[cs, h * 64:(h + 1) * 64], True, True, [MK.b, Vb.b], pW0b)
                  OP("act", lambda e: e.copy(out=W0sb[:], in_=pW0[:, :]), reads=[pW0b], writes=[W0sb.b])
                  PSREL(pW0)


              def phaseA2(n):
                  d = n % 2
                  MB = MB2[n % 3]
                  Xc = Xc2[d]
                  Acur, Mcur = Acur2[d], Mcur2[d]
                  ci = 0
                  xi = 0
                  for lvl in range(5):
                      Ac, Mc = Acur[ci], Mcur[ci]
                      An, Mn = Acur[1 - ci], Mcur[1 - ci]
                      if lvl == 0:
                          class _MV:
                              b = MB.b

                              def __getitem__(self, idx):
                                  return MB[idx[0], idx[1], 0, :]
                          Mc = _MV()
                      p2, p2b = PS(hold=True)
                      for h in range(8):
                          for c in range(2):
                              cs = slice(c * 64, (c + 1) * 64)
                              MM(p2[cs, h * 64:(h + 1) * 64], Mc[cs, h, :], Ac[cs, h, :], True, True, [Mc.b, Ac.b], p2b)
                      OP("act", lambda e: e.copy(out=An[:].rearrange("p a t -> p (a t)"), in_=p2[:, :]), reads=[p2b], writes=[An.b])
                      PSREL(p2)
                      if lvl < 4:
                          p3, p3b = PS(hold=True)
                          for h in range(8):
                              for c in range(2):
                                  cs = slice(c * 64, (c + 1) * 64)
                                  MM(p3[cs, h * 64:(h + 1) * 64], Ac[cs, h, :], Mc[cs, h, :], True, True, [Mc.b, Ac.b], p3b)
                          OP("dve", lambda e: e.tensor_scalar(out=Mn[:].rearrange("p a t -> p (a t)"), in0=p3[:, :], scalar1=1.0, scalar2=None,
                                                              op0=ALU.mult), reads=[p3b], writes=[Mn.b])
                          PSREL(p3)
                      yield
                      Xo, Xn = Xc[xi], Xc[1 - xi]
                      p4, p4b = PS(hold=True)
                      for h in range(8):
                          for c in range(2):
                              cs = slice(c * 64, (c + 1) * 64)
                              MM(p4[cs, h * 64:(h + 1) * 64], An[cs, h, :], Xo[cs, h, :], True, True, [An.b, Xo.b], p4b)
                      OP("dve", lambda e: e.tensor_tensor(out=Xn[:].rearrange("p a t -> p (a t)"), in0=p4[:, :],
                                                          in1=Xo[:].rearrange("p a t -> p (a t)"), op=ALU.add), reads=[p4b, Xo.b], writes=[Xn.b])
                      PSREL(p4)
                      ci = 1 - ci
                      xi = 1 - xi
                      yield
                  Xfin[n] = Xc[xi]

              def phaseB(n):
                  d = n % 2
                  t3 = n % 3
                  Vf, Vb, gsb, Pc, AR, rkb, bt, kt, MB, MK, W0sb = (Vf2[n % 4], Vb2[n % 4], gsb2[t3], Pc2[t3], AR2[t3], rkb2[t3], bt2[t3],
                                                                   kt2[t3], MB2[t3], MK2[t3], W0sb2[t3])
                  X = Xfin.pop(n)
                  yf2 = yf[:].rearrange("p a v -> p (a v)")
                  for c in range(2):
                      cs = slice(c * 64, (c + 1) * 64)
                      Hb = Hb2[c]
                      Hbn = Hb2[1 - c]
                      pWe = [PS(hold=True), PS(hold=True)]
                      for e_ in range(2):
                          hs = slice(e_ * 64, e_ * 64 + 64)
                          for pair in range(4):
                              MM(pWe[e_][0][cs, pair * 64:(pair + 1) * 64], AR[hs, pair, c, 0, :], Hb[hs, pair, :], True, True,
                                 [AR.b, Hb.b], pWe[e_][1])
                      for e_ in range(2):
                          OP("dve", lambda e, e_=e_: e.tensor_tensor(
                              out=par(Wsb[cs, :], e_), in0=pWe[e_][0][cs, 0:256].rearrange("p (a v) -> p a v", a=4),
                              in1=par(W0sb[cs, :], e_), op=ALU.add), reads=[pWe[e_][1], W0sb.b], writes=[Wsb.b])
                          PSREL(pWe[e_][0])
                      yield
                      pU, pUb = PS(hold=True)
                      for h in range(8):
                          MM(pU[cs, h * 64:(h + 1) * 64], X[cs, h, :], Wsb[cs, h * 64:(h + 1) * 64], True, True, [X.b, Wsb.b], pUb)
                      OP("act", lambda e: e.copy(out=Usb[cs, :], in_=pU[cs, :]), reads=[pUb], writes=[Usb.b])
                      PSREL(pU)
                      yield
                      pH, pHb = PS(hold=True)
                      for h in range(8):
                          pair, hs = hsl(h)
                          o_ = pH[hs, pair * 64:(pair + 1) * 64]
                          MM(o_, kt[cs, h * 64:(h + 1) * 64], Vb[cs, h * 64:(h + 1) * 64], True, False, [kt.b, Vb.b], pHb)
                          MM(o_, bt[cs, h * 64:(h + 1) * 64], Usb[cs, h * 64:(h + 1) * 64], False, True, [bt.b, Usb.b], pHb)
                      OP("dve", lambda e: e.tensor_tensor(out=Htmp[:].rearrange("p a v -> p (a v)"), in0=pH[:, 0:256],
                                                          in1=Hs[:].rearrange("p a v -> p (a v)"), op=ALU.add), reads=[pHb, Hs.b], writes=[Htmp.b])
                      PSREL(pH)
                      pcb = Pc[:].rearrange("p a (c t) -> p a c t", c=2)[:, :, c, 63:64].broadcast_to([128, 4, 64])
                      OP("dve", lambda e: e.tensor_tensor(out=Hs[:], in0=Htmp[:], in1=pcb, op=ALU.mult), reads=[Htmp.b, Pc.b], writes=[Hs.b])
                      OP("act", lambda e: e.copy(out=Hbn[:], in_=Hs[:]), reads=[Hs.b], writes=[Hbn.b])
                      pYe = [PS(hold=True), PS(hold=True)]
                      for e_ in range(2):
                          hs = slice(e_ * 64, e_ * 64 + 64)
                          for pair in range(4):
                              MM(pYe[e_][0][cs, pair * 64:(pair + 1) * 64], AR[hs, pair, c, 1, :], Hb[hs, pair, :], True, True,
                                 [AR.b, Hb.b], pYe[e_][1])
                      pYc, pYcb = PS(hold=True)
                      for h in range(8):
                          o_ = pYc[cs, h * 64:(h + 1) * 64]
                          MM(o_, MK[cs, h, 1, :], Vb[cs, h * 64:(h + 1) * 64], True, False, [MK.b, Vb.b], pYcb)
                          MM(o_, MB[cs, h, 1, :], Usb[cs, h * 64:(h + 1) * 64], False, True, [MB.b, Usb.b], pYcb)
                      OP("act", lambda e: e.copy(out=yf2[cs, :], in_=pYc[cs, :]), reads=[pYcb], writes=[yf.b])
                      PSREL(pYc)
                      for e_ in range(2):
                          OP("dve", lambda e, e_=e_: e.tensor_tensor(
                              out=par(yf2[cs, :], e_), in0=pYe[e_][0][cs, 0:256].rearrange("p (a v) -> p a v", a=4),
                              in1=par(yf2[cs, :], e_), op=ALU.add), reads=[pYe[e_][1], yf.b], writes=[yf.b])
                          PSREL(pYe[e_][0])
                      yield
                  OP("dve", lambda e: e.tensor_reduce(out=st[:, 0:8], in_=yf[:], axis=AX.X, op=ALU.add), reads=[yf.b], writes=[st.b])
                  OP("act", lambda e: e.activation(out=ysq[:].rearrange("p a v -> p (a v)"), in_=yf2, func=AF.Square), reads=[yf.b], writes=[ysq.b])
                  OP("dve", lambda e: e.tensor_reduce(out=st[:, 8:16], in_=ysq[:], axis=AX.X, op=ALU.add), reads=[ysq.b], writes=[st.b])
                  OP("dve", lambda e: e.tensor_scalar(out=st[:, 16:24], in0=st[:, 0:8], scalar1=1.0 / 64, scalar2=None, op0=ALU.mult),
                     reads=[st.b], writes=[st.b])
                  OP("dve", lambda e: e.tensor_tensor(out=st[:, 24:32], in0=st[:, 16:24], in1=st[:, 16:24], op=ALU.mult), reads=[st.b], writes=[st.b])
                  OP("dve", lambda e: e.scalar_tensor_tensor(out=st[:, 24:32], in0=st[:, 8:16], scalar=1.0 / 64, in1=st[:, 24:32],
                                                             op0=ALU.mult, op1=ALU.subtract), reads=[st.b], writes=[st.b])
                  OP("dve", lambda e: e.tensor_scalar(out=st[:, 24:32], in0=st[:, 24:32], scalar1=GN_EPS, scalar2=None, op0=ALU.add),
                     reads=[st.b], writes=[st.b])
                  OP("act", lambda e: e.activation(out=st[:, 24:32], in_=st[:, 24:32], func=AF.Ln), reads=[st.b], writes=[st.b])
                  OP("act", lambda e: e.activation(out=st[:, 24:32], in_=st[:, 24:32], func=AF.Exp, scale=-0.5), reads=[st.b], writes=[st.b])
                  yield
                  OP("dve", lambda e: e.tensor_tensor(out=yf[:], in0=yf[:], in1=b8(st[:, 16:24]), op=ALU.subtract), reads=[yf.b, st.b], writes=[yf.b])
                  OP("dve", lambda e: e.tensor_tensor(out=yf[:], in0=yf[:], in1=b8(st[:, 24:32]), op=ALU.mult), reads=[yf.b, st.b], writes=[yf.b])
                  OP("pool", lambda e: e.tensor_tensor(out=yf2, in0=yf2, in1=lnw[:, 0:512], op=ALU.mult), reads=[yf.b, lnw.b], writes=[yf.b])
                  OP("pool", lambda e: e.tensor_tensor(out=yf2, in0=yf2, in1=lnw[:, 512:1024], op=ALU.add), reads=[yf.b, lnw.b], writes=[yf.b])
                  pb_, pbb = PS(hold=True)
                  for c4 in range(4):
                      MM(pb_[:, c4 * 2:c4 * 2 + 2], rkb[:, c4, :], hsel[:, :], True, True, [rkb.b, hsel.b], pbb)
                  OP("act", lambda e: e.copy(out=bs[:], in_=pb_[:, 0:8]), reads=[pbb], writes=[bs.b])
                  PSREL(pb_)
                  yield
                  OP("dve", lambda e: e.tensor_tensor(out=ysq[:], in0=Vf[:].rearrange("p (a v) -> p a v", a=8), in1=b8(bs[:, :]), op=ALU.mult),
                     reads=[Vf.b, bs.b, ysq.b], writes=[ysq.b])
                  OP("pool", lambda e: e.tensor_tensor(out=yf[:], in0=yf[:], in1=ysq[:], op=ALU.add), reads=[yf.b, ysq.b], writes=[yf.b])
                  OP("dve", lambda e: e.tensor_tensor(out=yg[:], in0=yf2, in1=gsb[:], op=ALU.mult), reads=[yf.b, gsb.b], writes=[yg.b])
                  yield
                  pt_, ptb2 = PS(hold=True)
                  ptv = pt_.bitcast(BF16)
                  for c4 in range(4):
                      TR(ptv[:, c4 * 128:(c4 + 1) * 128], yg[:, c4 * 128:(c4 + 1) * 128], ident[:], [yg.b, ident.b], ptb2)
                  ya_ = yaTs[n % 2]
                  OP("act", lambda e: e.copy(out=ya_[:].rearrange("p a t -> p (a t)"), in_=ptv[:, 0:512]), reads=[ptb2], writes=[ya_.b])
                  PSREL(pt_)
                  DMA("sp", yaT_d[:, :, n * 128:(n + 1) * 128], ya_[:], reads=[ya_.b], writes=[B_yaT])

              dummy = PS(hold=True) if NDUMMY else None

              def run_rr(gens):
                  alive = [g is not None for g, _ in gens]
                  while any(alive):
                      for i_, (g, k_) in enumerate(gens):
                          for _ in range(k_):
                              if not alive[i_]:
                                  break
                              try:
                                  next(g)
                              except StopIteration:
                                  alive[i_] = False
                          for _ in range(NDUMMY):
                              MM(dummy[0][:, 0:128], ident[:, :], ident[:, :], True, True, [ident.b], dummy[1])

              gP = lambda n: phaseP(n) if n < NS else None
              gA1 = lambda n: phaseA(n) if n < NS else None
              gA2 = lambda n: phaseA2(n) if n < NS else None
              run_rr([(gP(0), 1)])
              run_rr([(gA1(0), RR[2]), (gP(1), RR[3])])
              run_rr([(gA2(0), RR[1]), (gA1(1), RR[2]), (gP(2), RR[3])])
              for n in range(NS):
                  streams = [(phaseB(n), RR[0]), (gA2(n + 1), RR[1]), (gA1(n + 2), RR[2]), (gP(n + 3), RR[3])]
                  run_rr([streams[k_] for k_ in ORDER])
              if dummy is not None:
                  PSREL(dummy[0])

        except _Stop:
            pass
        sy.barrier()

        CP('F')
        with ExitStack() as sf:
            Wf = SB(sf, "Wf", [128, 8, NFX], BF16)
            Wf_b = [Buf("Wf%d" % k) for k in range(8)]
            KT = SB(sf, "KT", [65, 8, S], BF16)
            Va = SB(sf, "Va", [128, NS, 8, 65], BF16)
            QT = [SB(sf, "QT%d" % i, [65, 8, 512], BF16) for i in range(2)]
            ncum = SB(sf, "ncum", [128, NS, 8], F32)
            PTs = [SB(sf, "PT%d" % i, [128, 512], BF16) for i in range(4)]
            hTf = [SB(sf, "hTf%d" % i, [128, 8, 512], BF16) for i in range(2)]
            tri = SB(sf, "tri", [128, 128], F32)
            onesq = SB(sf, "onesq", [128, 128], F32)
            trib = SB(sf, "trib", [128, 128], BF16)
            acc = SB(sf, "acc", [128, 8], F32)
            fbb = SB(sf, "fbb", [128, 8], F32)
            lf = SB(sf, "lf", [128, 8], F32)
            cq8 = SB(sf, "cq8", [8, 512], BF16)
            oT = [SB(sf, "oT%d" % i, [65, 512], F32) for i in range(4)]
            ybh = [SB(sf, "ybh%d" % i, [64, 512], BF16) for i in range(4)]
            for kc in range(8):
                DMA("pool", Wf[:, kc, :], win_d[kc * 128:(kc + 1) * 128, NRW:NRW + NFX], writes=[Wf_b[kc]])
            DMA("sp", fbb[:], bc_d[:, BC_FB:BC_FB + 8], writes=[fbb.b])
            OP("pool", lambda e: e.memset(tri[:], 1.0), writes=[tri.b])
            OP("pool", lambda e: e.affine_select(out=tri[:], in_=tri[:], pattern=[[1, 128]], compare_op=ALU.is_ge, fill=0.0,
                                                 base=0, channel_multiplier=-1), reads=[tri.b], writes=[tri.b])
            negm = SB(sf, "negm", [128, 128], F32)
            OP("pool", lambda e: e.memset(negm[:], 0.0), writes=[negm.b])
            OP("pool", lambda e: e.affine_select(out=negm[:], in_=negm[:], pattern=[[1, 128]], compare_op=ALU.is_ge, fill=-1.0e4,
                                                 base=0, channel_multiplier=-1), reads=[negm.b], writes=[negm.b])
            OP("pool", lambda e: e.memset(trib[:], 1.0), writes=[trib.b])
            OP("pool", lambda e: e.affine_select(out=trib[:], in_=trib[:], pattern=[[1, 128]], compare_op=ALU.is_ge, fill=0.0,
                                                 base=0, channel_multiplier=-1), reads=[trib.b], writes=[trib.b])
            OP("pool", lambda e: e.memset(onesq[:], 1.0), writes=[onesq.b])
            OP("dve", lambda e: e.memset(acc[:], 0.0), writes=[acc.b])
            lf4 = SB(sf, "lf4", [128, 4, 8], F32)
            acc5 = SB(sf, "acc5", [128, 5, 8], F32)
            cq8s = [SB(sf, "cq8_%d" % i_, [8, 512], BF16) for i_ in range(2)]
            qkst = [SB(sf, "qkst%d" % i_, [128, 512], BF16) for i_ in range(4)]
            PT6 = [SB(sf, "PTx%d" % i, [128, 512], BF16) for i in range(8)]
            OP("dve", lambda e: e.memset(acc5[:], 0.0), writes=[acc5.b])
            KT_b = [Buf("KTt%d" % i_) for i_ in range(NT)]
            Va_b = [Buf("Vat%d" % i_) for i_ in range(NT)]
            nc_b = [Buf("nct%d" % i_) for i_ in range(NT)]
            OP("dve", lambda e: e.memset(KT[64:65, :, :], 1.0), writes=KT_b)
            OP("dve", lambda e: e.memset(Va[:, :, :, 64:65], 1.0), writes=Va_b)
            LOOK = 6
            qrow_b = [[Buf("qrow%d_%d" % (i_, h_)) for h_ in range(8)] for i_ in range(2)]
            unit_ctr = [0]

            def front(i):
                ht = hTf[i % 2]
                qt = QT[i % 2]
                cq8_ = cq8s[i % 2]
                DMA("sp", ht[:], hT_d[:, :, i * 512:(i + 1) * 512], writes=[ht.b])
                yield
                pf, pfb = PS(hold=True)
                for sub in range(4):
                    ts_ = slice(sub * 128, (sub + 1) * 128)
                    for kc in range(8):
                        MM(pf[:, sub * 8:(sub + 1) * 8], ht[:, kc, ts_], Wf[:, kc, 1536:1544], kc == 0, kc == 7, [Wf_b[kc], ht.b], pfb)
                OP("dve", lambda e: e.tensor_tensor(out=lf4[:], in0=pf[:, 0:32].rearrange("p (a h) -> p a h", a=4),
                                                    in1=fbb[:].unsqueeze(1).broadcast_to([128, 4, 8]), op=ALU.add),
                   reads=[pfb, fbb.b], writes=[lf4.b])
                PSREL(pf)
                OP("act", lambda e: e.activation(out=lf4[:], in_=lf4[:], func=AF.Sigmoid), reads=[lf4.b], writes=[lf4.b])
                OP("act", lambda e: e.activation(out=lf4[:], in_=lf4[:], func=AF.Ln), reads=[lf4.b], writes=[lf4.b])
                if i > 0:
                    OP("dve", lambda e: e.tensor_scalar(out=acc5[:, 0, :], in0=acc5[:, 4, :], scalar1=1.0, scalar2=None, op0=ALU.mult),
                       reads=[acc5.b], writes=[acc5.b])
                for sub in range(4):
                    OP("dve", lambda e, sub=sub: e.tensor_tensor(out=acc5[:, sub + 1, :], in0=acc5[:, sub, :], in1=lf4[:, sub, :], op=ALU.add),
                       reads=[acc5.b, lf4.b], writes=[acc5.b])
                yield
                for pair in range(4):
                    for which in range(2):
                        col0 = which * 512 + pair * 128
                        pq_, pqb_ = PS(hold=True)
                        for kc in range(8):
                            MM(pq_[:, :], Wf[:, kc, col0:col0 + 128], ht[:, kc, :], kc == 0, kc == 7, [Wf_b[kc], ht.b], pqb_)
                        stg = qkst[(pair * 2 + which) % 4]
                        if which == 0:
                            dst_even = qt[0:64, 2 * pair, :]
                            dst_odd = qt[0:64, 2 * pair + 1, :]
                            dbuf = qt.b
                        else:
                            dst_even = KT[0:64, 2 * pair, i * 512:(i + 1) * 512]
                            dst_odd = KT[0:64, 2 * pair + 1, i * 512:(i + 1) * 512]
                            dbuf = KT_b[i]
                        OP("dve", lambda e: e.tensor_scalar(out=dst_even, in0=pq_[0:64, :], scalar1=1.0, scalar2=None, op0=ALU.mult),
                           reads=[pqb_], writes=[dbuf])
                        OP("act", lambda e: e.copy(out=stg[64:128, :], in_=pq_[64:128, :]), reads=[pqb_], writes=[stg.b])
                        PSREL(pq_)
                        DMA("sp", dst_odd, stg[64:128, :], reads=[stg.b], writes=[dbuf])
                        yield
                for sub in range(4):
                    g = i * 4 + sub
                    ts_ = slice(sub * 128, (sub + 1) * 128)
                    pv2, pv2b = PS(hold=True)
                    for kc in range(8):
                        MM(pv2[:, :], ht[:, kc, ts_], Wf[:, kc, 1024:1536], kc == 0, kc == 7, [Wf_b[kc], ht.b], pv2b)
                    OP("dve", lambda e: e.tensor_scalar(out=Va[:, g, :, 0:64], in0=pv2[:, :].rearrange("p (a v) -> p a v", a=8),
                                                        scalar1=1.0, scalar2=None, op0=ALU.mult), reads=[pv2b], writes=[Va_b[i]])
                    PSREL(pv2)
                    yield
                pc, pcb_ = PS(hold=True)
                pcT, pcTb = PS(hold=True)
                for sub in range(4):
                    ts_ = slice(sub * 128, (sub + 1) * 128)
                    MM(pc[:, sub * 8:(sub + 1) * 8], tri[:, :], lf4[:, sub, :], True, False, [tri.b, lf4.b], pcb_)
                    MM(pc[:, sub * 8:(sub + 1) * 8], onesq[:, :], acc5[:, sub, :], False, True, [onesq.b, acc5.b], pcb_)
                    MM(pcT[0:8, ts_], lf4[:, sub, :], tri[:, :], sub == 0, False, [tri.b, lf4.b], pcTb)
                    MM(pcT[0:8, ts_], acc5[:, sub, :], onesq[:, :], False, sub == 3, [onesq.b, acc5.b], pcTb)
                OP("dve", lambda e: e.tensor_scalar(out=ncum[:, i * 4:(i + 1) * 4, :], in0=pc[:, 0:32].rearrange("p (a h) -> p a h", a=4),
                                                    scalar1=-1.0, scalar2=None, op0=ALU.mult), reads=[pcb_], writes=[nc_b[i]])
                PSREL(pc)
                OP("dve", lambda e: e.tensor_scalar(out=cq8_[:], in0=pcT[0:8, :], scalar1=8.0, scalar2=None, op0=ALU.mult),
                   reads=[pcTb], writes=[cq8_.b])
                PSREL(pcT)
                for h in range(8):
                    DMA("sp", qt[64:65, h, :], cq8_[h:h + 1, :], reads=[cq8_.b], writes=[qrow_b[i % 2][h]])

            def attention(i):
                qt = QT[i % 2]
                nkb = 4 * i + 4
                tasks = [(h, j) for h in range(8) for j in range(nkb)]
                NTK = len(tasks)
                pts = {}
                pos = {}
                tails = []

                def emit_S(t):
                    h, j = tasks[t]
                    q0 = max(0, j - 4 * i) * 128
                    ps_, psb_ = PS()
                    MM(ps_[:, q0:512], KT[0:65, h, j * 128:(j + 1) * 128], qt[0:65, h, q0:512], True, True,
                       [KT_b[j // 4], qt.b, qrow_b[i % 2][h]], psb_)
                    PT = PT6[t % 8]
                    if j >= 4 * i:
                        OP("dve", lambda e: e.tensor_tensor(out=ps_[:, q0:q0 + 128], in0=ps_[:, q0:q0 + 128], in1=negm[:], op=ALU.add),
                           reads=[psb_, negm.b], writes=[psb_])
                    OP("act", lambda e: e.activation(out=PT[:, q0:512], in_=ps_[:, q0:512], func=AF.Exp, scale=0.125,
                                                     bias=ncum[:, j, h:h + 1]), reads=[psb_, nc_b[j // 4]], writes=[PT.b])
                    pts[t] = (PT, q0)

                def emit_PV(t):
                    h, j = tasks[t]
                    if j == 0:
                        pos[h] = PS(hold=True)
                    po, pob = pos[h]
                    PT, q0 = pts.pop(t)
                    MM(po[0:65, q0:512], Va[:, j, h, :], PT[:, q0:512], j == 0, j == nkb - 1, [Va_b[j // 4], PT.b], pob)
                    if j == nkb - 1:
                        u = unit_ctr[0]
                        unit_ctr[0] += 1
                        o_ = oT[u % 4]
                        yb_ = ybh[u % 4]
                        OP("dve", lambda e: e.tensor_scalar(out=o_[:], in0=po[0:65, :], scalar1=1.0, scalar2=None, op0=ALU.mult),
                           reads=[pob], writes=[o_.b])
                        PSREL(po)
                        OP("dve", lambda e: e.reciprocal(out=o_[64:65, :], in_=o_[64:65, :]), reads=[o_.b], writes=[o_.b])

                        def tail(h=h, o_=o_, yb_=yb_):
                            pr_, prb_ = PS()
                            MM(pr_[0:64, :], onesq[64:65, 0:64], o_[64:65, :], True, True, [onesq.b, o_.b], prb_)
                            OP("dve", lambda e: e.tensor_tensor(out=yb_[:], in0=pr_[0:64, :], in1=o_[0:64, :], op=ALU.mult),
                               reads=[prb_, o_.b], writes=[yb_.b])
                            DMA("sp", ybT_d[h * 64:(h + 1) * 64, i * 512:(i + 1) * 512], yb_[:], reads=[yb_.b], writes=[B_ybT])
                        tails.append((t + 11, tail))

                for t in range(min(LOOK, NTK)):
                    emit_S(t)
                for t in range(NTK):
                    if t + LOOK < NTK:
                        emit_S(t + LOOK)
                    emit_PV(t)
                    while tails and tails[0][0] <= t:
                        tails.pop(0)[1]()
                    yield
                while tails:
                    tails.pop(0)[1]()

            for _ in front(0):
                pass
            for i in range(NT):
                gA = attention(i)
                gF = front(i + 1) if i + 1 < NT else None
                ntk = 8 * (4 * i + 4)
                every = max(1, ntk // 28)
                cnt_ = 0
                for _ in gA:
                    cnt_ += 1
                    if gF is not None and cnt_ % every == 0:
                        try:
                            next(gF)
                        except StopIteration:
                            gF = None
                if gF is not None:
                    for _ in gF:
                        pass
        sy.barrier()

        CP('C1')
        with ExitStack() as sc:
            Wg = SB(sc, "Wg", [128, 8, 2048], BF16)
            Wg_b = [Buf("Wg%d" % k) for k in range(8)]
            Wor = SB(sc, "Wor", [128, 4, D], BF16)
            Wof = SB(sc, "Wof", [128, 4, D], BF16)
            Wout = SB(sc, "Wout", [128, 8, D], BF16)
            gtb = SB(sc, "gtb", [128, D], F32)
            hTc = [SB(sc, "hTc%d" % i, [128, 8, 512], BF16) for i in range(2)]
            yaTt = [SB(sc, "yaTt%d" % i, [128, 4, 512], BF16) for i in range(2)]
            ybTt = [SB(sc, "ybTt%d" % i, [128, 4, 512], BF16) for i in range(2)]
            Gsig2 = [SB(sc, "Gsig%d" % i_, [128, 16, 512], BF16) for i_ in range(2)]
            mrg = SB(sc, "mrg", [128, 8, 512], BF16)
            tm1 = [SB(sc, "tm1_%d" % i, [128, 512], F32) for i in range(2)]
            tm2 = [SB(sc, "tm2_%d" % i, [128, 512], F32) for i in range(2)]
            tm3c = [SB(sc, "tm3c_%d" % i, [128, 512], F32) for i in range(2)]
            xs = [SB(sc, "xs%d" % i, [128, D], F32) for i in range(2)]
            x1s = [SB(sc, "x1s%d" % i, [128, D], F32) for i in range(2)]
            xn2 = [SB(sc, "xn2_%d" % i, [128, D], BF16) for i in range(2)]
            h2t = [SB(sc, "h2t%d" % i, [128, 8, 512], BF16) for i in range(2)]
            junk1 = SB(sc, "junk1", [128, D], BF16)
            st2 = [SB(sc, "st2_%d" % i, [128, 2], F32) for i in range(2)]
            for kc in range(8):
                DMA("pool", Wg[:, kc, :], win_d[kc * 128:(kc + 1) * 128, NRW + NFX:NIN], writes=[Wg_b[kc]])
            DMA("pool", Wor[:], wor_d.rearrange("(c p) n -> p c n", p=128), writes=[Wor.b])
            DMA("pool", Wof[:], wof_d.rearrange("(c p) n -> p c n", p=128), writes=[Wof.b])
            DMA("pool", Wout[:], wout_d.rearrange("(c p) n -> p c n", p=128), writes=[Wout.b])
            DMA("sp", gtb[:], gt_d[:, 0:D], reads=[B_gt], writes=[gtb.b])
            ybT_v = ybT_d.rearrange("(c p) s -> p c s", p=128)

            def load_tile(i):
                tsl_ = slice(i * 512, (i + 1) * 512)
                DMA("sp", hTc[i % 2][:], hT_d[:, :, tsl_], reads=[B_hT], writes=[hTc[i % 2].b])
                DMA("sp", yaTt[i % 2][:], yaT_d[:, :, tsl_], reads=[B_yaT], writes=[yaTt[i % 2].b])
                DMA("sp", ybTt[i % 2][:], ybT_v[:, :, tsl_], reads=[B_ybT], writes=[ybTt[i % 2].b])

            def load_x(k_):
                if k_ < NS:
                    DMA("sp", xs[k_ % 2][:], x_d[k_ * 128:(k_ + 1) * 128, :], writes=[xs[k_ % 2].b])

            def gates(i, part):
                if i >= NT:
                    return
                ht_, G_ = hTc[i % 2], Gsig2[i % 2]
                for g in range(part * 8, part * 8 + 8):
                    pg2, pg2b = PS()
                    for kc in range(8):
                        MM(pg2[:, :], Wg[:, kc, g * 128:(g + 1) * 128], ht_[:, kc, :], kc == 0, kc == 7, [Wg_b[kc], ht_.b], pg2b)
                    OP("act", lambda e, g=g, pg2=pg2: e.activation(out=G_[:, g, :], in_=pg2[:, :], func=AF.Sigmoid),
                       reads=[pg2b], writes=[G_.b])

            load_tile(0)
            load_x(0)
            gates(0, 0)
            gates(0, 1)
            for i in range(NT):
                ht, ya_t, yb_t, h2 = hTc[i % 2], yaTt[i % 2], ybTt[i % 2], h2t[i % 2]
                Gsig = Gsig2[i % 2]
                tsl = slice(i * 512, (i + 1) * 512)
                if i + 1 < NT:
                    load_tile(i + 1)
                for m in range(8):
                    pa2, pa2b = PS()
                    for c in range(4):
                        MM(pa2[:, :], Wor[:, c, m * 128:(m + 1) * 128], ya_t[:, c, :], c == 0, c == 3, [Wor.b, ya_t.b], pa2b)
                    pb2, pb2b = PS()
                    for c in range(4):
                        MM(pb2[:, :], Wof[:, c, m * 128:(m + 1) * 128], yb_t[:, c, :], c == 0, c == 3, [Wof.b, yb_t.b], pb2b)
                    t1_, t2_ = tm1[m % 2], tm2[m % 2]
                    OP("dve", lambda e, m=m, pa2=pa2, t1_=t1_: e.tensor_tensor(out=t1_[:], in0=pa2[:, :], in1=Gsig[:, m, :], op=ALU.mult),
                       reads=[pa2b, Gsig.b], writes=[t1_.b])
                    OP("dve", lambda e, m=m, pb2=pb2, t2_=t2_: e.tensor_tensor(out=t2_[:], in0=pb2[:, :], in1=Gsig[:, 8 + m, :], op=ALU.mult),
                       reads=[pb2b, Gsig.b], writes=[t2_.b])
                    OP("pool", lambda e, m=m, t1_=t1_, t2_=t2_: e.tensor_tensor(out=mrg[:, m, :], in0=t1_[:], in1=t2_[:], op=ALU.add),
                       reads=[t1_.b, t2_.b], writes=[mrg.b])
                gates(i + 1, 0)
                pT2 = [PS(hold=True) for _ in range(4)]

                def z_part(sub):
                    k_ = i * 4 + sub
                    ts_ = slice(sub * 128, (sub + 1) * 128)
                    xt, x1t, xnt, s2 = xs[k_ % 2], x1s[k_ % 2], xn2[k_ % 2], st2[k_ % 2]
                    load_x(k_ + 1)
                    for half in range(2):
                        hsl_ = slice(half * 512, (half + 1) * 512)
                        pz2, pz2b = PS()
                        for c in range(8):
                            MM(pz2[:, :], mrg[:, c, ts_], Wout[:, c, hsl_], c == 0, c == 7, [mrg.b, Wout.b], pz2b)
                        t1_ = tm3c[half]
                        OP("dve", lambda e, pz2=pz2, t1_=t1_, hsl_=hsl_: e.tensor_tensor(out=t1_[:], in0=pz2[:, :], in1=gtb[:, hsl_], op=ALU.mult),
                           reads=[pz2b, gtb.b], writes=[t1_.b])
                        OP("dve", lambda e, t1_=t1_, hsl_=hsl_, xt=xt, x1t=x1t: e.tensor_tensor(out=x1t[:, hsl_], in0=t1_[:], in1=xt[:, hsl_], op=ALU.add),
                           reads=[t1_.b, xt.b], writes=[x1t.b])
                    DMA("sp", x1_d[i * 512 + sub * 128: i * 512 + (sub + 1) * 128, :], x1t[:], reads=[x1t.b], writes=[B_x1])
                    OP("act", lambda e, x1t=x1t, s2=s2: e.activation(out=junk1[:], in_=x1t[:], func=AF.Square, accum_out=s2[:, 0:1]),
                       reads=[x1t.b], writes=[junk1.b, s2.b])
                    OP("dve", lambda e, s2=s2: e.tensor_scalar(out=s2[:, 1:2], in0=s2[:, 0:1], scalar1=1.0 / D, scalar2=NORM_EPS,
                                                               op0=ALU.mult, op1=ALU.add), reads=[s2.b], writes=[s2.b])
                    OP("act", lambda e, s2=s2: e.activation(out=s2[:, 1:2], in_=s2[:, 1:2], func=AF.Sqrt), reads=[s2.b], writes=[s2.b])
                    OP("dve", lambda e, s2=s2: e.reciprocal(out=s2[:, 1:2], in_=s2[:, 1:2]), reads=[s2.b], writes=[s2.b])
                    OP("act", lambda e, x1t=x1t, xnt=xnt, s2=s2: e.activation(out=xnt[:], in_=x1t[:], func=AF.Copy, scale=s2[:, 1:2]),
                       reads=[x1t.b, s2.b], writes=[xnt.b])

                def t_part(sub):
                    k_ = i * 4 + sub
                    xnt = xn2[k_ % 2]
                    for kc in range(8):
                        pv3 = pT2[kc // 2][0].bitcast(BF16)
                        TR(pv3[:, (kc % 2) * 512 + sub * 128:(kc % 2) * 512 + (sub + 1) * 128], xnt[:, kc * 128:(kc + 1) * 128], ident[:],
                           [xnt.b, ident.b], pT2[kc // 2][1])

                z_part(0)
                for sub in range(4):
                    if sub + 1 < 4:
                        z_part(sub + 1)
                    if sub == 1:
                        gates(i + 1, 1)
                    t_part(sub)
                for hb in range(4):
                    pv3 = pT2[hb][0].bitcast(BF16)
                    for kq in range(2):
                        kc = hb * 2 + kq
                        if kc % 2 == 0:
                            OP("dve", lambda e, kc=kc, kq=kq, pv3=pv3: e.tensor_scalar(
                                out=h2[:, kc, :], in0=pv3[:, kq * 512:(kq + 1) * 512], scalar1=GS[:, 16 + kc:17 + kc],
                                scalar2=GS[:, 24 + kc:25 + kc], op0=ALU.mult, op1=ALU.add), reads=[pT2[hb][1], GS.b], writes=[h2.b])
                        else:
                            OP("act", lambda e, kc=kc, kq=kq, pv3=pv3: e.activation(
                                out=h2[:, kc, :], in_=pv3[:, kq * 512:(kq + 1) * 512], func=AF.Identity, scale=GS[:, 16 + kc:17 + kc],
                                bias=GS[:, 24 + kc:25 + kc]), reads=[pT2[hb][1], GS.b], writes=[h2.b])
                for hb in range(4):
                    PSREL(pT2[hb][0])
                DMA("sp", h2T_d[:, :, tsl], h2[:], reads=[h2.b], writes=[B_h2T])
        sy.barrier()

        CP('C2')
        with ExitStack() as s2c:
            W1 = SB(s2c, "W1", [128, 8, 4 * D], BF16)
            W1_b = [Buf("W1_%d" % k) for k in range(8)]
            W2 = SB(s2c, "W2", [128, 32, D], BF16)
            W2_b = [Buf("W2_%d" % k) for k in range(4)]
            gt2 = SB(s2c, "gt2", [128, D], F32)
            fgb = SB(s2c, "fgb", [128, D], F32)
            h2c = [SB(s2c, "h2c%d" % i, [128, 8, 256], BF16) for i in range(2)]
            hid = SB(s2c, "hid", [128, 32, 256], BF16)
            rl = [SB(s2c, "rl%d" % i, [128, 512], F32) for i in range(2)]
            x1c = [SB(s2c, "x1c%d" % i, [128, D], F32) for i in range(2)]
            x2c = [SB(s2c, "x2c%d" % i, [128, D], F32) for i in range(2)]
            oc = [SB(s2c, "oc%d" % i, [128, D], F32) for i in range(2)]
            tm3 = [SB(s2c, "tm3_%d" % i, [128, 512], F32) for i in range(2)]
            junk2 = SB(s2c, "junk2", [128, D], BF16)
            st3 = [SB(s2c, "st3_%d" % i, [128, 2], F32) for i in range(2)]
            for kc in range(8):
                DMA("pool", W1[:, kc, :], wff1_d[kc * 128:(kc + 1) * 128, :], writes=[W1_b[kc]])
            for q4 in range(4):
                DMA("pool", W2[:, q4 * 8:(q4 + 1) * 8, :],
                    wff2_d[q4 * 1024:(q4 + 1) * 1024, :].rearrange("(k p) n -> p k n", p=128), writes=[W2_b[q4]])
            DMA("sp", gt2[:], gt_d[:, D:2 * D], reads=[B_gt], writes=[gt2.b])
            DMA("sp", fgb[:], bc_d[:, BC_FG:BC_FG + D], writes=[fgb.b])
            def load_h2(i2):
                if i2 < S // 256:
                    DMA("sp", h2c[i2 % 2][:], h2T_d[:, :, i2 * 256:(i2 + 1) * 256], reads=[B_h2T], writes=[h2c[i2 % 2].b])

            def load_x1(k_):
                if k_ < NS:
                    DMA("sp", x1c[k_ % 2][:], x1_d[k_ * 128:(k_ + 1) * 128, :], reads=[B_x1], writes=[x1c[k_ % 2].b])

            load_h2(0)
            load_x1(0)
            for i2 in range(S // 256):
                hc = h2c[i2 % 2]
                load_h2(i2 + 1)
                for f2 in range(16):
                    pf2, pf2b = PS()
                    for fh in range(2):
                        f = f2 * 2 + fh
                        for kc in range(8):
                            MM(pf2[:, fh * 256:(fh + 1) * 256], W1[:, kc, f * 128:(f + 1) * 128], hc[:, kc, :], kc == 0, kc == 7,
                               [W1_b[kc], hc.b], pf2b)
                    r_ = rl[f2 % 2]
                    OP("act", lambda e, pf2=pf2, r_=r_: e.activation(out=r_[:], in_=pf2[:, :], func=AF.Relu), reads=[pf2b], writes=[r_.b])
                    OP("pool", lambda e, f2=f2, r_=r_: e.tensor_tensor(out=hid[:, f2 * 2:f2 * 2 + 2, :].rearrange("p a t -> p (a t)"),
                                                                       in0=r_[:], in1=r_[:], op=ALU.mult), reads=[r_.b], writes=[hid.b])
                for sub in range(2):
                    k_ = i2 * 2 + sub
                    r0 = i2 * 256 + sub * 128
                    ts_ = slice(sub * 128, (sub + 1) * 128)
                    x1t, x2t, ot, s3 = x1c[k_ % 2], x2c[k_ % 2], oc[k_ % 2], st3[k_ % 2]
                    load_x1(k_ + 1)
                    for half in range(2):
                        hsl_ = slice(half * 512, (half + 1) * 512)
                        po2, po2b = PS()
                        for kk_ in range(32):
                            MM(po2[:, :], hid[:, kk_, ts_], W2[:, kk_, hsl_], kk_ == 0, kk_ == 31, [hid.b, W2_b[kk_ // 8]], po2b)
                        t3 = tm3[half]
                        OP("dve", lambda e, po2=po2, t3=t3, hsl_=hsl_: e.tensor_tensor(out=t3[:], in0=po2[:, :], in1=gt2[:, hsl_], op=ALU.mult),
                           reads=[po2b, gt2.b], writes=[t3.b])
                        OP("dve", lambda e, t3=t3, hsl_=hsl_, x1t=x1t, x2t=x2t: e.tensor_tensor(out=x2t[:, hsl_], in0=t3[:], in1=x1t[:, hsl_], op=ALU.add),
                           reads=[t3.b, x1t.b], writes=[x2t.b])
                    OP("act", lambda e, x2t=x2t, s3=s3: e.activation(out=junk2[:], in_=x2t[:], func=AF.Square, accum_out=s3[:, 0:1]),
                       reads=[x2t.b], writes=[junk2.b, s3.b])
                    OP("dve", lambda e, s3=s3: e.tensor_scalar(out=s3[:, 1:2], in0=s3[:, 0:1], scalar1=1.0 / D, scalar2=NORM_EPS,
                                                               op0=ALU.mult, op1=ALU.add), reads=[s3.b], writes=[s3.b])
                    OP("act", lambda e, s3=s3: e.activation(out=s3[:, 1:2], in_=s3[:, 1:2], func=AF.Sqrt), reads=[s3.b], writes=[s3.b])
                    OP("dve", lambda e, s3=s3: e.reciprocal(out=s3[:, 1:2], in_=s3[:, 1:2]), reads=[s3.b], writes=[s3.b])
                    OP("dve", lambda e, x2t=x2t, ot=ot, s3=s3: e.scalar_tensor_tensor(out=ot[:], in0=x2t[:], scalar=s3[:, 1:2], in1=fgb[:],
                                                                                     op0=ALU.mult, op1=ALU.mult),
                       reads=[x2t.b, s3.b, fgb.b], writes=[ot.b])
                    DMA("sp", out_d[r0:r0 + 128, :], ot[:], reads=[ot.b], writes=[B_out])
        sy.barrier()
        sy.drain("sp")
    return nc


def _pack_inputs(inp, b):
    f = lambda a: np.ascontiguousarray(np.asarray(a, dtype=np.float32))
    pp = np.zeros((128, NPP), np.float32)
    pp[:, PP_BADA:PP_BADA + 48] = f(inp["b_ada"])[0].reshape(48, 128).T
    pp[:, PP_G1:PP_G1 + 8] = f(inp["norm1_g"])[0].reshape(8, 128).T
    pp[:, PP_G2:PP_G2 + 8] = f(inp["norm2_g"])[0].reshape(8, 128).T
    pp[:, PP_C:PP_C + 8] = f(inp["c"])[b].reshape(8, 128).T
    pp[:, PP_DB:PP_DB + 4] = f(inp["decay_base"])[0].reshape(4, 128).T
    pp[:, PP_IB:PP_IB + 4] = f(inp["iclr_base"])[0].reshape(4, 128).T
    pp[:, PP_KS:PP_KS + 4] = f(inp["kk_scale"])[0].reshape(4, 128).T
    pp[:, PP_MIX:PP_MIX + 4] = f(inp["k_iclr_mix"])[0].reshape(4, 128).T
    pp[:, PP_RB:PP_RB + 4] = f(inp["r_bonus"])[0].reshape(512).reshape(4, 128).T
    mu = f(inp["mu_shift"])[0]
    mu_r = np.concatenate([mu[0:512], mu[576:1088], mu[1088:1600], mu[512:576], mu[1600:1664], mu[1664:1792]])
    row = np.concatenate([mu_r, f(inp["lnx_w"])[0], f(inp["lnx_b"])[0], f(inp["fox_f_bias"])[0], f(inp["final_g"])])
    bc = np.ascontiguousarray(np.broadcast_to(row[None, :], (128, NBC)))
    ba = f(inp["b_ada"])[0]
    brow = np.concatenate([ba[2 * D:3 * D], ba[5 * D:6 * D]])[None, :]
    return pp, bc, np.ascontiguousarray(brow)


_NC_CACHE = {}


def make_in_maps(inp, S, nb):
    shared = {
        "w_ada": np.ascontiguousarray(np.asarray(inp["w_ada"], np.float32)[0]),
        "w_in": np.ascontiguousarray(np.asarray(inp["w_in"], np.float32)[0]),
        "w_decay_up": np.ascontiguousarray(np.asarray(inp["w_decay_up"], np.float32)[0]),
        "w_iclr_up": np.ascontiguousarray(np.asarray(inp["w_iclr_up"], np.float32)[0]),
        "w_gate_up": np.ascontiguousarray(np.asarray(inp["w_gate_up"], np.float32)[0]),
        "w_o_rwkv": np.ascontiguousarray(np.asarray(inp["w_o_rwkv"], np.float32)[0]),
        "w_o_fox": np.ascontiguousarray(np.asarray(inp["w_o_fox"], np.float32)[0]),
        "w_out": np.ascontiguousarray(np.asarray(inp["w_out"], np.float32)[0]),
        "w_ff1": np.ascontiguousarray(np.asarray(inp["w_ff1"], np.float32)[0]),
        "w_ff2": np.ascontiguousarray(np.asarray(inp["w_ff2"], np.float32)[0]),
    }
    maps = []
    x = np.asarray(inp["x"], np.float32)
    for b in range(nb):
        pp, bc, brow = _pack_inputs(inp, b)
        m = dict(shared)
        m.update({"x": np.ascontiguousarray(x[b]), "pp": pp, "bc": bc, "brow": brow})
        maps.append(m)
    return maps


def kernel(**inputs):
    x = np.asarray(inputs["x"])
    nb, S = x.shape[0], x.shape[1]
    if S not in _NC_CACHE:
        _NC_CACHE[S] = build(S)
    nc = _NC_CACHE[S]
    maps = make_in_maps(inputs, S, nb)
    res = run_bass_kernel_spmd(nc, maps, core_ids=list(range(nb)))
    return np.stack([np.asarray(r["out"], np.float32) for r in res.results], axis=0)
```

```python
import numpy as np
from contextlib import ExitStack
import concourse.bass as bass
import concourse.mybir as mybir
from concourse.alu_op_type import AluOpType as ALU
from concourse.bass_utils import run_bass_kernel_spmd

F32 = mybir.dt.float32
BF16 = mybir.dt.bfloat16
AF = mybir.ActivationFunctionType
AX = mybir.AxisListType

D = 1024
NRW = 1792
NFX = 1544
NIN = 5384
EXPM05 = 0.6065306597126334
NORM_EPS = 1e-6
GN_EPS = 64e-5

PP_BADA, PP_G1, PP_G2, PP_C, PP_DB, PP_IB, PP_KS, PP_MIX, PP_RB, NPP = 0, 48, 56, 64, 72, 76, 80, 84, 88, 92
BC_MU, BC_LW, BC_LB, BC_FB, BC_FG, NBC = 0, 1792, 2304, 2816, 2824, 3848


class Buf:
    __slots__ = ("name", "lw", "rd")

    def __init__(self, name=""):
        self.name = name
        self.lw = None
        self.rd = {}


class Sync:
    def __init__(self, nc, es, n_dma_sems=32):
        self.nc = nc
        self.eng = {"pe": nc.tensor, "act": nc.scalar, "dve": nc.vector, "pool": nc.gpsimd, "sp": nc.sync}
        self.sem = {k: es.enter_context(nc.semaphore("sem_" + k)) for k in self.eng}
        self.cnt = {k: 0 for k in self.eng}
        self.dsem = [es.enter_context(nc.semaphore("dsem%d" % i)) for i in range(n_dma_sems)]
        self.dcnt = [0] * n_dma_sems
        self.dnext = 0
        self.dnext_sw = 0
        self.seen = {k: {} for k in self.eng}
        self.dead = False
        self.lazy = {"pe"}
        self.unflushed = {}
        self.last_inst = {}

    def _flush(self, key):
        if self.unflushed.get(key):
            self.last_inst[key].then_inc(self.sem[key], 1)
            self.cnt[key] += 1
            self.unflushed[key] = False

    def _wait(self, e, key, val):
        if self.seen[e].get(key, 0) >= val:
            return
        if isinstance(key, str) and val > self.cnt[key]:
            assert key in self.lazy and val == self.cnt[key] + 1, (key, val, self.cnt[key])
            self._flush(key)
        sem = self.sem[key] if isinstance(key, str) else self.dsem[key]
        self.eng[e].wait_ge(sem, val)
        self.seen[e][key] = val

    def _deps(self, e, reads, writes):
        deps = {}

        def add(k, v):
            if deps.get(k, 0) < v:
                deps[k] = v
        for b in reads:
            if b.lw is not None:
                add(*b.lw)
        for b in writes:
            if b.lw is not None:
                add(*b.lw)
            for k, v in b.rd.items():
                add(k, v)
        for k, v in deps.items():
            if k == e and e == "pe":
                continue
            self._wait(e, k, v)

    def _post(self, ev, reads, writes):
        for b in reads:
            if b.rd.get(ev[0], 0) < ev[1]:
                b.rd[ev[0]] = ev[1]
        for b in writes:
            b.lw = ev
            b.rd = {}

    def op(self, e, fn, reads=(), writes=()):
        if self.dead:
            return None
        if e != "pe":
            pr_ = [b for b in reads if b.name.startswith("bank")]
            if pr_:
                reads = [b for b in reads if not b.name.startswith("bank")]
                writes = list(writes) + pr_
        self._deps(e, reads, writes)
        inst = fn(self.eng[e])
        if e in self.lazy:
            self.last_inst[e] = inst
            self.unflushed[e] = True
            self._post((e, self.cnt[e] + 1), reads, writes)
            return inst
        self.cnt[e] += 1
        inst.then_inc(self.sem[e], 1)
        self._post((e, self.cnt[e]), reads, writes)
        return inst

    def dma(self, e, out, in_, reads=(), writes=()):
        if self.dead:
            return None
        nsw = 8
        if e == "pool":
            k = self.dnext_sw
            self.dnext_sw = (self.dnext_sw + 1) % nsw
        else:
            k = nsw + self.dnext
            self.dnext = (self.dnext + 1) % (len(self.dsem) - nsw)
        if self.dcnt[k] > 0:
            self._wait(e, k, self.dcnt[k])
        self._deps(e, reads, writes)
        inst = self.eng[e].dma_start(out=out, in_=in_)
        self.dcnt[k] += 16
        inst.then_inc(self.dsem[k], 16)
        self._post((k, self.dcnt[k]), reads, writes)
        return inst

    def barrier(self):
        for k in list(self.lazy):
            self._flush(k)
        for e in self.eng:
            for k in self.eng:
                if self.cnt[k]:
                    self._wait(e, k, self.cnt[k])
            for k in range(len(self.dsem)):
                if self.dcnt[k]:
                    self._wait(e, k, self.dcnt[k])

    def drain(self, e="sp"):
        for k in list(self.lazy):
            self._flush(k)
        for k in range(len(self.dsem)):
            if self.dcnt[k]:
                self._wait(e, k, self.dcnt[k])
        for k in self.eng:
            if k != e and self.cnt[k]:
                self._wait(e, k, self.cnt[k])


class T:
    def __init__(self, es, nc, name, shape, dtype):
        self.t = es.enter_context(nc.sbuf_tensor("sb_" + name, shape, dtype))
        self.b = Buf(name)

    def __getitem__(self, idx):
        return self.t[idx]


class _Stop(Exception):
    pass


RR = [1, 1, 1, 1]
ORDER = [0, 1, 2, 3]
NDUMMY = 0


def build(S, dbg=False, stop=None):
    assert S % 512 == 0

    def CP(name):
        if stop == name:
            sy_box[0].dead = True

    sy_box = [None]
    NT = S // 512
    NS = S // 128
    nc = bass.Bass("TRN2", target_bir_lowering=False)

    def din(n, shp, dt=F32):
        return nc.dram_tensor(n, shp, dt, kind="ExternalInput").ap()

    def dscr(n, shp, dt):
        return nc.dram_tensor(n, shp, dt, kind="ExternalOutput" if dbg else "Internal").ap()

    x_d = din("x", [S, D])
    pp_d = din("pp", [128, NPP])
    bc_d = din("bc", [128, NBC])
    brow_d = din("brow", [1, 2048])
    wada_d = din("w_ada", [D, 6 * D])
    win_d = din("w_in", [D, NIN])
    wdu_d = din("w_decay_up", [64, 512])
    wiu_d = din("w_iclr_up", [64, 512])
    wgu_d = din("w_gate_up", [128, 512])
    wor_d = din("w_o_rwkv", [512, D])
    wof_d = din("w_o_fox", [512, D])
    wout_d = din("w_out", [D, D])
    wff1_d = din("w_ff1", [D, 4 * D])
    wff2_d = din("w_ff2", [4 * D, D])
    out_d = nc.dram_tensor("out", [S, D], F32, kind="ExternalOutput").ap()

    hT_d = dscr("hT_s", [128, 8, S], BF16)
    yaT_d = dscr("yaT_s", [128, 4, S], BF16)
    ybT_d = dscr("ybT_s", [512, S], BF16)
    x1_d = dscr("x1_s", [S, D], F32)
    h2T_d = dscr("h2T_s", [128, 8, S], BF16)
    gt_d = dscr("gt_s", [128, 2048], F32)
    B_hT, B_yaT, B_ybT, B_x1, B_h2T, B_gt, B_out = (Buf(n) for n in "hT yaT ybT x1 h2T gt out".split())

    with ExitStack() as es:
        sy = Sync(nc, es)
        sy_box[0] = sy
        OP = sy.op
        DMA = sy.dma

        def SB(scope, name, shape, dt):
            return T(scope, nc, name, shape, dt)

        banks = [es.enter_context(nc.psum_tensor("bank%d" % i, [128, 512], F32)) for i in range(8)]
        bbufs = [Buf("bank%d" % i) for i in range(8)]
        pstate = {"i": 0}

        held = set()

        def PS(hold=False):
            assert len(held) < 8, "all PSUM banks held"
            while True:
                i = pstate["i"]
                pstate["i"] = (i + 1) % 8
                if i not in held:
                    break
            if hold:
                held.add(i)
            return banks[i], bbufs[i]

        def PSREL(bank):
            held.discard(banks.index(bank))

        def MM(out, lhsT, rhs, start, stop, reads, pb):
            OP("pe", lambda e: e.matmul(out, lhsT, rhs, start=start, stop=stop, skip_group_check=True),
               reads=reads, writes=[pb])

        def TR(out, in_, ident, reads, pb):
            OP("pe", lambda e: e.transpose(out, in_, ident), reads=reads, writes=[pb])

        ppt = SB(es, "ppt", [128, NPP], F32)
        ident = SB(es, "ident", [128, 128], BF16)
        modT = SB(es, "modT", [128, 48], F32)
        GS = SB(es, "GS", [128, 32], F32)
        DMA("sp", ppt[:], pp_d, writes=[ppt.b])
        OP("pool", lambda e: e.memset(ident[:], 1.0), writes=[ident.b])
        OP("pool", lambda e: e.affine_select(out=ident[:], in_=ident[:], pattern=[[-1, 128]],
                                             compare_op=ALU.is_equal, fill=0.0, base=0, channel_multiplier=1),
           reads=[ident.b], writes=[ident.b])

        with ExitStack() as s0:
            wada = SB(s0, "wada", [128, 8, 6 * D], BF16)
            wada_b = [Buf("wada%d" % k) for k in range(8)]
            cact = SB(s0, "cact", [128, 8], BF16)
            crep = SB(s0, "crep", [128, 8, 128], BF16)
            onesf = SB(s0, "onesf", [1, 128], F32)
            brow = SB(s0, "brow", [1, 2048], F32)
            gtbc = SB(s0, "gtbc", [128, 2048], F32)
            wadaB_b = [Buf("wadaB%d" % k) for k in range(8)]
            for kc in range(8):
                DMA("pool", wada[:, kc, 0:2 * D], wada_d[kc * 128:(kc + 1) * 128, 0:2 * D], writes=[wada_b[kc]])
            for kc in range(8):
                DMA("pool", wada[:, kc, 2 * D:6 * D], wada_d[kc * 128:(kc + 1) * 128, 2 * D:6 * D], writes=[wadaB_b[kc]])
            DMA("sp", brow[:], brow_d, writes=[brow.b])
            OP("dve", lambda e: e.memset(onesf[:], 1.0), writes=[onesf.b])
            OP("act", lambda e: e.activation(out=cact[:], in_=ppt[:, PP_C:PP_C + 8], func=AF.Silu),
               reads=[ppt.b], writes=[cact.b])
            for kc in range(8):
                OP("dve", lambda e, kc=kc: e.tensor_copy(out=crep[:, kc, :],
                                                         in_=cact[:, kc:kc + 1].broadcast_to([128, 128])),
                   reads=[cact.b], writes=[crep.b])
            pa, pab = PS(hold=True)
            for j in range(16):
                for kc in range(8):
                    MM(pa[:, j:j + 1], wada[:, kc, j * 128:(j + 1) * 128], cact[:, kc:kc + 1],
                       kc == 0, kc == 7, [wada_b[kc], cact.b], pab)
            OP("dve", lambda e: e.tensor_tensor(out=modT[:, 0:16], in0=pa[:, 0:16], in1=ppt[:, PP_BADA:PP_BADA + 16],
                                                op=ALU.add), reads=[pab, ppt.b], writes=[modT.b])
            PSREL(pa)
            OP("dve", lambda e: e.scalar_tensor_tensor(out=GS[:, 0:8], in0=modT[:, 8:16], scalar=1.0,
                                                       in1=ppt[:, PP_G1:PP_G1 + 8], op0=ALU.add, op1=ALU.mult),
               reads=[modT.b, ppt.b], writes=[GS.b])
            OP("dve", lambda e: e.tensor_copy(out=GS[:, 8:16], in_=modT[:, 0:8]), reads=[modT.b], writes=[GS.b])

            def mod_part2():
                GS2 = Buf("GS2")
                pa2_, pa2b_ = PS(hold=True)
                for j in range(16, 48):
                    for kc in range(8):
                        MM(pa2_[:, j:j + 1], wada[:, kc, j * 128:(j + 1) * 128], cact[:, kc:kc + 1],
                           kc == 0, kc == 7, [wadaB_b[kc], cact.b], pa2b_)
                OP("dve", lambda e: e.tensor_tensor(out=modT[:, 16:48], in0=pa2_[:, 16:48], in1=ppt[:, PP_BADA + 16:PP_BADA + 48],
                                                    op=ALU.add), reads=[pa2b_, ppt.b], writes=[modT.b])
                PSREL(pa2_)
                OP("dve", lambda e: e.scalar_tensor_tensor(out=GS[:, 16:24], in0=modT[:, 32:40], scalar=1.0,
                                                           in1=ppt[:, PP_G2:PP_G2 + 8], op0=ALU.add, op1=ALU.mult),
                   reads=[modT.b, ppt.b], writes=[GS.b])
                OP("dve", lambda e: e.tensor_copy(out=GS[:, 24:32], in_=modT[:, 24:32]), reads=[modT.b], writes=[GS.b])
                for part, col0 in enumerate((2 * D, 5 * D)):
                    for half in range(2):
                        pg, pgb = PS(hold=True)
                        for kc in range(8):
                            MM(pg[:, :], crep[:, kc, :], wada[:, kc, col0 + half * 512: col0 + (half + 1) * 512],
                               kc == 0, False, [crep.b, wadaB_b[kc]], pgb)
                        o = part * 1024 + half * 512
                        MM(pg[:, :], onesf[0:1, :], brow[0:1, o:o + 512], False, True, [onesf.b, brow.b], pgb)
                        OP("act", lambda e, pg=pg, o=o: e.copy(out=gtbc[:, o:o + 512], in_=pg[:, :]),
                           reads=[pgb], writes=[gtbc.b])
                        PSREL(pg)
                DMA("sp", gt_d, gtbc[:], reads=[gtbc.b], writes=[B_gt])

            xb = [SB(s0, "xb%d" % i, [128, 4, D], F32) for i in range(2)]
            xn = [SB(s0, "xn%d" % i, [128, 4, D], BF16) for i in range(2)]
            hTs = [SB(s0, "hTs%d" % i, [128, 8, 512], BF16) for i in range(2)]
            junk = SB(s0, "junk0", [128, D], BF16)
            ssq = [SB(s0, "ssq%d" % i, [128, 4], F32) for i in range(2)]
            rstd = [SB(s0, "rstd%d" % i, [128, 4], F32) for i in range(2)]
            def load_xt(i):
                if i < NT:
                    DMA("sp", xb[i % 2][:], x_d[i * 512:(i + 1) * 512, :].rearrange("(s p) d -> p s d", p=128), writes=[xb[i % 2].b])

            xn_b = [[Buf("xn%d_%d" % (i_, s_)) for s_ in range(4)] for i_ in range(2)]
            ht_b = [[Buf("ht%d_%d" % (i_, k_)) for k_ in range(8)] for i_ in range(2)]

            def norm_a(i):
                xt, xnt, sq, rs = xb[i % 2], xn[i % 2], ssq[i % 2], rstd[i % 2]
                for s in range(4):
                    OP("act", lambda e, s=s: e.activation(out=junk[:], in_=xt[:, s, :], func=AF.Square, accum_out=sq[:, s:s + 1]),
                       reads=[xt.b], writes=[junk.b, sq.b])
                OP("dve", lambda e: e.tensor_scalar(out=rs[:], in0=sq[:], scalar1=1.0 / D, scalar2=NORM_EPS,
                                                    op0=ALU.mult, op1=ALU.add), reads=[sq.b], writes=[rs.b])
                OP("act", lambda e: e.activation(out=rs[:], in_=rs[:], func=AF.Sqrt), reads=[rs.b], writes=[rs.b])
                OP("dve", lambda e: e.reciprocal(out=rs[:], in_=rs[:]), reads=[rs.b], writes=[rs.b])
                for s in range(4):
                    if s % 2 == 0:
                        OP("dve", lambda e, s=s: e.tensor_scalar(out=xnt[:, s, :], in0=xt[:, s, :],
                                                                 scalar1=rs[:, s:s + 1], scalar2=None, op0=ALU.mult),
                           reads=[xt.b, rs.b], writes=[xn_b[i % 2][s]])
                    else:
                        OP("act", lambda e, s=s: e.activation(out=xnt[:, s, :], in_=xt[:, s, :], func=AF.Copy, scale=rs[:, s:s + 1]),
                           reads=[xt.b, rs.b], writes=[xn_b[i % 2][s]])

            def norm_b(i):
                xnt, ht = xn[i % 2], hTs[i % 2]
                for kc in range(8):
                    p, pb = PS()
                    pv = p.bitcast(BF16)
                    for s in range(4):
                        TR(pv[:, s * 128:(s + 1) * 128], xnt[:, s, kc * 128:(kc + 1) * 128], ident[:], [xn_b[i % 2][s], ident.b], pb)
                    if kc % 2 == 0:
                        OP("dve", lambda e, kc=kc, pv=pv: e.tensor_scalar(
                            out=ht[:, kc, :], in0=pv[:, 0:512], scalar1=GS[:, kc:kc + 1], scalar2=GS[:, 8 + kc:9 + kc],
                            op0=ALU.mult, op1=ALU.add), reads=[pb, GS.b], writes=[ht_b[i % 2][kc]])
                    else:
                        OP("act", lambda e, kc=kc, pv=pv: e.activation(
                            out=ht[:, kc, :], in_=pv[:, 0:512], func=AF.Identity, scale=GS[:, kc:kc + 1],
                            bias=GS[:, 8 + kc:9 + kc]), reads=[pb, GS.b], writes=[ht_b[i % 2][kc]])
                DMA("sp", hT_d[:, :, i * 512:(i + 1) * 512], ht[:], reads=ht_b[i % 2], writes=[B_hT])

            load_xt(0)
            load_xt(1)
            norm_a(0)
            for i in range(NT):
                if i + 1 < NT:
                    norm_a(i + 1)
                load_xt(i + 2)
                norm_b(i)
            mod_part2()


        sy.barrier()
        try:
          with ExitStack() as sr:
              WA = SB(sr, "WA", [128, 8, NRW], BF16)
              WB = SB(sr, "WB", [128, 8, NRW], BF16)
              wdu = SB(sr, "wdu", [64, 512], BF16)
              wiu = SB(sr, "wiu", [128, 512], BF16)
              wgu = SB(sr, "wgu", [128, 512], BF16)
              lnw = SB(sr, "lnw", [128, 1024], F32)
              DMA("pool", wdu[:], wdu_d, writes=[wdu.b])
              DMA("pool", wiu[64:128, :], wiu_d, writes=[wiu.b])
              DMA("pool", wgu[:], wgu_d, writes=[wgu.b])
              DMA("sp", lnw[:], bc_d[:, BC_LW:BC_LW + 1024], writes=[lnw.b])
              with ExitStack() as sw:
                  mub = SB(sw, "mub", [128, NRW], F32)
                  omub = SB(sw, "omub", [128, NRW], F32)
                  wst = [SB(sw, "wst%d" % i, [128, NRW], F32) for i in range(2)]
                  DMA("sp", mub[:], bc_d[:, BC_MU:BC_MU + NRW], writes=[mub.b])
                  OP("dve", lambda e: e.tensor_scalar(out=omub[:], in0=mub[:], scalar1=-1.0, scalar2=1.0,
                                                      op0=ALU.mult, op1=ALU.add), reads=[mub.b], writes=[omub.b])
                  segs = [(0, 512, 0), (576, 1088, 512), (1088, 1600, 1024), (512, 576, 1536), (1600, 1664, 1600),
                          (1664, 1792, 1664)]
                  seg_b = [[Buf("wseg%d_%d" % (i_, j_)) for j_ in range(len(segs))] for i_ in range(2)]
                  for kc in range(8):
                      w = wst[kc % 2]
                      for j_, (a, b_, o) in enumerate(segs):
                          DMA("sp", w[:, o:o + (b_ - a)], win_d[kc * 128:(kc + 1) * 128, a:b_], reads=[], writes=[seg_b[kc % 2][j_]])
                      OP("dve", lambda e, kc=kc, w=w: e.tensor_tensor(out=WB[:, kc, :], in0=w[:], in1=mub[:], op=ALU.mult),
                         reads=seg_b[kc % 2] + [mub.b], writes=[WB.b])
                      OP("dve", lambda e, kc=kc, w=w: e.tensor_tensor(out=WA[:, kc, :], in0=w[:], in1=omub[:], op=ALU.mult),
                         reads=seg_b[kc % 2] + [omub.b], writes=[WA.b])
              sy.barrier()
              CP('w')

              blk1 = SB(sr, "blk1", [128, 128], F32)
              hsel = SB(sr, "hsel", [128, 2], BF16)
              mskX = SB(sr, "mskX", [128, 4, 2, 64], F32)
              mskL = SB(sr, "mskL", [128, 8, 64], F32)
              i64 = SB(sr, "i64", [128, 64], F32)
              scm = SB(sr, "scm", [128, 512], F32)
              omix = SB(sr, "omix", [128, 4], F32)
              OP("pool", lambda e: e.memset(blk1[:], 0.0), writes=[blk1.b])
              OP("pool", lambda e: e.memset(blk1[0:64, 0:64], 1.0), writes=[blk1.b])
              OP("pool", lambda e: e.memset(blk1[64:128, 64:128], 1.0), writes=[blk1.b])
              OP("pool", lambda e: e.memset(hsel[:], 0.0), writes=[hsel.b])
              OP("pool", lambda e: e.memset(hsel[0:64, 0:1], 1.0), writes=[hsel.b])
              OP("pool", lambda e: e.memset(hsel[64:128, 1:2], 1.0), writes=[hsel.b])
              OP("pool", lambda e: e.memset(mskX[:], 1.0), writes=[mskX.b])
              OP("pool", lambda e: e.memset(mskL[:], 1.0), writes=[mskL.b])
              OP("pool", lambda e: e.memset(i64[:], 1.0), writes=[i64.b])
              for hf in range(2):
                  ps_ = slice(hf * 64, (hf + 1) * 64)
                  OP("pool", lambda e, ps_=ps_: e.affine_select(out=mskX[ps_], in_=mskX[ps_], pattern=[[0, 4], [1, 2], [1, 64]],
                                                                compare_op=ALU.is_gt, fill=0.0, base=0, channel_multiplier=-1),
                     reads=[mskX.b], writes=[mskX.b])
                  OP("pool", lambda e, ps_=ps_: e.affine_select(out=mskL[ps_], in_=mskL[ps_], pattern=[[0, 8], [-1, 64]],
                                                                compare_op=ALU.is_gt, fill=0.0, base=0, channel_multiplier=1),
                     reads=[mskL.b], writes=[mskL.b])
                  OP("pool", lambda e, ps_=ps_: e.affine_select(out=i64[ps_], in_=i64[ps_], pattern=[[-1, 64]],
                                                                compare_op=ALU.is_equal, fill=0.0, base=0, channel_multiplier=1),
                     reads=[i64.b], writes=[i64.b])
              OP("pool", lambda e: e.memset(scm[:], 1.0), writes=[scm.b])
              OP("pool", lambda e: e.memset(scm[:].rearrange("p (a b) -> p a b", b=64)[:, :, 0:1], 0.0), writes=[scm.b])
              OP("dve", lambda e: e.tensor_scalar(out=omix[:], in0=ppt[:, PP_MIX:PP_MIX + 4], scalar1=-1.0, scalar2=1.0,
                                                  op0=ALU.mult, op1=ALU.add), reads=[ppt.b], writes=[omix.b])

              CP('c')

              def bc4(col0):
                  return ppt[:, col0:col0 + 4].unsqueeze(2).broadcast_to([128, 4, 128])

              Hs = SB(sr, "Hs", [128, 4, 64], F32)
              Hb2 = [SB(sr, "Hb%d" % i_, [128, 4, 64], BF16) for i_ in range(2)]
              OP("dve", lambda e: e.memset(Hs[:], 0.0), writes=[Hs.b])
              OP("dve", lambda e: e.memset(Hb2[0][:], 0.0), writes=[Hb2[0].b])

              def DB(name, shape, dt):
                  return [SB(sr, "%s_%d" % (name, i_), shape, dt) for i_ in range(2)]
              hcs = DB("hc", [128, 8, 128], BF16)
              hps = DB("hp", [128, 8, 128], BF16)
              rk2 = DB("rk", [128, 8, 128], F32)
              tw2 = DB("tw", [64, 128], BF16)
              adb2 = DB("adb", [128, 128], BF16)
              sgd2 = DB("sgd", [128, 128], BF16)
              Vf2 = [SB(sr, "Vf_%d" % i_, [128, 512], F32) for i_ in range(4)]
              Vb2 = [SB(sr, "Vb_%d" % i_, [128, 512], BF16) for i_ in range(4)]

              def TB(name, shape, dt):
                  return [SB(sr, "%s_%d" % (name, i_), shape, dt) for i_ in range(3)]
              sw_ = SB(sr, "sw_", [128, 512], F32)
              aic = SB(sr, "aic", [128, 4, 128], F32)
              gsb2 = TB("gsb", [128, 512], F32)
              ld = SB(sr, "ld", [128, 512], F32)
              lP = SB(sr, "lP", [128, 512], F32)
              lPx = SB(sr, "lPx", [128, 512], F32)
              Pc2 = TB("Pc", [128, 4, 128], F32)
              Pinv = SB(sr, "Pinv", [128, 4, 128], F32)
              Pex = SB(sr, "Pex", [128, 4, 128], F32)
              ksc = SB(sr, "ksc", [128, 4, 128], F32)
              ksq = SB(sr, "ksq", [128, 4, 128], F32)
              rn = SB(sr, "rn", [128, 4, 128], F32)
              kk = SB(sr, "kk", [128, 4, 128], F32)
              t1 = SB(sr, "t1", [128, 4, 128], F32)
              kmod = SB(sr, "kmod", [128, 4, 128], F32)
              t2 = SB(sr, "t2", [128, 4, 128], F32)
              AR2 = TB("AR", [128, 4, 2, 2, 64], BF16)
              bT = SB(sr, "bT", [128, 4, 128], BF16)
              kT = SB(sr, "kT", [128, 4, 128], BF16)
              rkb2 = TB("rkb", [128, 4, 128], BF16)
              bt2 = TB("bt", [128, 512], BF16)
              kt2 = TB("kt", [128, 512], BF16)
              MB2 = TB("MB", [128, 8, 2, 64], BF16)
              MK2 = TB("MK", [128, 8, 2, 64], BF16)
              Acur2 = [[SB(sr, "Acur%d_%d" % (j_, i), [128, 8, 64], BF16) for i in range(2)] for j_ in range(2)]
              Mcur2 = [[SB(sr, "Mcur%d_%d" % (j_, i), [128, 8, 64], BF16) for i in range(2)] for j_ in range(2)]
              Xc2 = [[SB(sr, "Xc%d_%d" % (j_, i), [128, 8, 64], BF16) for i in range(2)] for j_ in range(2)]
              W0sb2 = TB("W0sb", [128, 512], F32)
              Wsb = SB(sr, "Wsb", [128, 512], BF16)
              Usb = SB(sr, "Usb", [128, 512], BF16)
              Htmp = SB(sr, "Htmp", [128, 4, 64], F32)
              yf = SB(sr, "yf", [128, 8, 64], F32)
              ysq = SB(sr, "ysq", [128, 8, 64], F32)
              st = SB(sr, "st", [128, 32], F32)
              bs = SB(sr, "bs", [128, 8], F32)
              yg = SB(sr, "yg", [128, 512], BF16)
              yaTs = [SB(sr, "yaTs%d" % i, [128, 4, 128], BF16) for i in range(2)]

              def hsl(h):
                  return h // 2, slice((h % 2) * 64, (h % 2) * 64 + 64)
              fl = lambda t_: t_[:].rearrange("p a t -> p (a t)")
              v4 = lambda t_: t_[:].rearrange("p a (c t) -> p a c t", c=2)
              par = lambda ap_, e_: ap_.rearrange("p (a e v) -> p e a v", e=2, v=64)[:, e_]
              b8 = lambda a_: a_.unsqueeze(2).broadcast_to([128, 8, 64])
              Xfin = {}

              def phaseP(n):
                  d = n % 2
                  hh, hold, hp = hcs[d], hcs[1 - d], hps[d]
                  rk, tw, adb, sgd = rk2[d], tw2[d], adb2[d], sgd2[d]
                  Vf, Vb = Vf2[n % 4], Vb2[n % 4]
                  DMA("sp", hh[:], hT_d[:, :, n * 128:(n + 1) * 128], writes=[hh.b])
                  if n == 0:
                      OP("pool", lambda e: e.memset(hp[:, :, 0:1], 0.0), writes=[hp.b])
                      DMA("sp", hp[:, :, 1:128], hT_d[:, :, 0:127], writes=[hp.b])
                  else:
                      DMA("sp", hp[:], hT_d[:, :, n * 128 - 1:(n + 1) * 128 - 1], writes=[hp.b])
                  yield

                  def fproj(pt, ptb, slot, col0):
                      for i_ in range(16):
                          kc, sh = i_ % 8, i_ // 8
                          W = WA if sh == 0 else WB
                          hx = hh if sh == 0 else hp
                          MM(pt[:, slot * 128:(slot + 1) * 128], W[:, kc, col0:col0 + 128], hx[:, kc, :], i_ == 0, i_ == 15,
                             [W.b, hx.b], ptb)
                  pr, prb = PS(hold=True)
                  for c4 in range(4):
                      fproj(pr, prb, c4, c4 * 128)
                      if c4 % 2 == 1:
                          yield
                  OP("act", lambda e: e.copy(out=rk[:, 0:4, :], in_=pr[:, :].rearrange("p (a t) -> p a t", a=4)), reads=[prb], writes=[rk.b])
                  PSREL(pr)
                  pk, pkb = PS(hold=True)
                  for c4 in range(4):
                      fproj(pk, pkb, c4, 512 + c4 * 128)
                      if c4 % 2 == 1:
                          yield
                  OP("dve", lambda e: e.tensor_scalar(out=rk[:, 4:8, :], in0=pk[:, :].rearrange("p (a t) -> p a t", a=4), scalar1=1.0,
                                                      scalar2=None, op0=ALU.mult), reads=[pkb], writes=[rk.b])
                  PSREL(pk)
                  pw, pwb = PS(hold=True)
                  fproj(pw, pwb, 0, 1536)
                  yield
                  fproj(pw, pwb, 1, 1664)
                  OP("act", lambda e: e.activation(out=tw[:], in_=pw[0:64, 0:128], func=AF.Tanh), reads=[pwb], writes=[tw.b])
                  OP("dve", lambda e: e.tensor_scalar(out=adb[64:128, :], in0=pw[64:128, 0:128], scalar1=1.0, scalar2=None, op0=ALU.mult),
                     reads=[pwb], writes=[adb.b])
                  OP("act", lambda e: e.activation(out=sgd[:], in_=pw[:, 128:256], func=AF.Sigmoid), reads=[pwb], writes=[sgd.b])
                  PSREL(pw)
                  yield
                  pv_, pvb = PS(hold=True)
                  for i_ in range(16):
                      kc, sh = i_ % 8, i_ // 8
                      W = WA if sh == 0 else WB
                      hx = hh if sh == 0 else hp
                      MM(pv_[:, :], hx[:, kc, :], W[:, kc, 1024:1536], i_ == 0, i_ == 15, [W.b, hx.b], pvb)
                      if i_ == 7:
                          yield
                  OP("act", lambda e: e.copy(out=Vf[:], in_=pv_[:, :]), reads=[pvb], writes=[Vf.b])
                  OP("dve", lambda e: e.tensor_scalar(out=Vb[:], in0=pv_[:, :], scalar1=1.0, scalar2=None, op0=ALU.mult), reads=[pvb], writes=[Vb.b])
                  PSREL(pv_)

              def phaseA(n):
                  d = n % 2
                  t3 = n % 3
                  rk, tw, adb, sgd = rk2[d], tw2[d], adb2[d], sgd2[d]
                  Vf, Vb = Vf2[n % 4], Vb2[n % 4]
                  gsb, Pc, AR, rkb, bt, kt, MB, MK, W0sb = (gsb2[t3], Pc2[t3], AR2[t3], rkb2[t3], bt2[t3],
                                                           kt2[t3], MB2[t3], MK2[t3], W0sb2[t3])
                  Xc = Xc2[d]
                  Acur, Mcur = Acur2[d], Mcur2[d]
                  r4 = rk[:, 0:4, :]
                  k4 = rk[:, 4:8, :]
                  pz, pzb = PS(hold=True)
                  pa_, pab_ = PS(hold=True)
                  pg_, pgb_ = PS(hold=True)
                  for c4 in range(4):
                      MM(pz[:, c4 * 128:(c4 + 1) * 128], wdu[0:64, c4 * 128:(c4 + 1) * 128], tw[0:64, :], True, True, [wdu.b, tw.b], pzb)
                  for c4 in range(4):
                      MM(pa_[:, c4 * 128:(c4 + 1) * 128], wiu[64:128, c4 * 128:(c4 + 1) * 128], adb[64:128, :], True, True, [wiu.b, adb.b], pab_)
                  MM(pg_[:, :], sgd[:, :], wgu[:, :], True, True, [sgd.b, wgu.b], pgb_)
                  for c4 in range(4):
                      OP("act", lambda e, c4=c4: e.activation(out=sw_[:, c4 * 128:(c4 + 1) * 128], in_=pz[:, c4 * 128:(c4 + 1) * 128],
                                                              func=AF.Sigmoid, bias=ppt[:, PP_DB + c4:PP_DB + c4 + 1]),
                         reads=[pzb, ppt.b], writes=[sw_.b])
                  PSREL(pz)
                  for c4 in range(4):
                      OP("act", lambda e, c4=c4: e.activation(out=aic[:, c4, :], in_=pa_[:, c4 * 128:(c4 + 1) * 128],
                                                              func=AF.Sigmoid, bias=ppt[:, PP_IB + c4:PP_IB + c4 + 1]),
                         reads=[pab_, ppt.b], writes=[aic.b])
                  PSREL(pa_)
                  OP("act", lambda e: e.copy(out=gsb[:], in_=pg_[:, :]), reads=[pgb_], writes=[gsb.b])
                  PSREL(pg_)
                  yield
                  OP("dve", lambda e: e.tensor_tensor(out=ksc[:], in0=k4, in1=bc4(PP_KS), op=ALU.mult), reads=[rk.b, ppt.b], writes=[ksc.b])
                  OP("act", lambda e: e.activation(out=fl(ksq), in_=fl(ksc), func=AF.Square), reads=[ksc.b], writes=[ksq.b])
                  for c4 in range(4):
                      OP("act", lambda e, c4=c4: e.activation(out=t1[:, c4, :], in_=aic[:, c4, :], func=AF.Identity,
                                                              scale=ppt[:, PP_MIX + c4:PP_MIX + c4 + 1], bias=omix[:, c4:c4 + 1]),
                         reads=[aic.b, ppt.b, omix.b], writes=[t1.b])
                  OP("dve", lambda e: e.tensor_tensor(out=kmod[:], in0=k4, in1=t1[:], op=ALU.mult), reads=[rk.b, t1.b], writes=[kmod.b])
                  yield
                  OP("dve", lambda e: e.tensor_scalar(out=ld[:], in0=sw_[:], scalar1=-EXPM05, scalar2=None, op0=ALU.mult), reads=[sw_.b], writes=[ld.b])
                  OP("dve", lambda e: e.tensor_tensor_scan(out=lP[:], data0=scm[:], data1=ld[:], initial=0.0, op0=ALU.mult, op1=ALU.add),
                     reads=[scm.b, ld.b], writes=[lP.b])
                  OP("dve", lambda e: e.tensor_tensor(out=lPx[:], in0=lP[:], in1=ld[:], op=ALU.subtract), reads=[lP.b, ld.b], writes=[lPx.b])
                  pq, pqb = PS(hold=True)
                  MM(pq[:, :], blk1[:, :], fl(ksq), True, True, [blk1.b, ksq.b], pqb)
                  OP("act", lambda e: e.activation(out=fl(Pc), in_=lP[:], func=AF.Exp), reads=[lP.b], writes=[Pc.b])
                  OP("act", lambda e: e.activation(out=fl(Pinv), in_=lP[:], func=AF.Exp, scale=-1.0), reads=[lP.b], writes=[Pinv.b])
                  OP("act", lambda e: e.activation(out=fl(Pex), in_=lPx[:], func=AF.Exp), reads=[lPx.b], writes=[Pex.b])
                  OP("act", lambda e: e.activation(out=fl(rn), in_=pq[:, :], func=AF.Ln), reads=[pqb], writes=[rn.b])
                  PSREL(pq)
                  OP("act", lambda e: e.activation(out=fl(rn), in_=fl(rn), func=AF.Exp, scale=-0.5), reads=[rn.b], writes=[rn.b])
                  OP("dve", lambda e: e.tensor_tensor(out=kk[:], in0=ksc[:], in1=rn[:], op=ALU.mult), reads=[ksc.b, rn.b], writes=[kk.b])
                  yield
                  OP("dve", lambda e: e.scalar_tensor_tensor(out=AR[:, :, :, 0, :], in0=v4(kk), scalar=-1.0, in1=v4(Pex), op0=ALU.mult, op1=ALU.mult),
                     reads=[kk.b, Pex.b], writes=[AR.b])
                  OP("pool", lambda e: e.tensor_tensor(out=AR[:, :, :, 1, :], in0=r4.rearrange("p a (c t) -> p a c t", c=2), in1=v4(Pc), op=ALU.mult),
                     reads=[rk.b, Pc.b, AR.b], writes=[AR.b])
                  OP("dve", lambda e: e.tensor_tensor(out=t2[:], in0=kk[:], in1=aic[:], op=ALU.mult), reads=[kk.b, aic.b], writes=[t2.b])
                  OP("dve", lambda e: e.tensor_tensor(out=bT[:], in0=t2[:], in1=Pinv[:], op=ALU.mult), reads=[t2.b, Pinv.b], writes=[bT.b])
                  OP("pool", lambda e: e.tensor_tensor(out=kT[:], in0=kmod[:], in1=Pinv[:], op=ALU.mult), reads=[kmod.b, Pinv.b], writes=[kT.b])
                  OP("pool", lambda e: e.tensor_tensor(out=t1[:], in0=r4, in1=kmod[:], op=ALU.mult), reads=[rk.b, kmod.b, t1.b], writes=[t1.b])
                  OP("pool", lambda e: e.tensor_tensor(out=rkb[:], in0=t1[:], in1=bc4(PP_RB), op=ALU.mult), reads=[t1.b, ppt.b], writes=[rkb.b])
                  yield
                  yield
                  yield
                  ptb_, ptbb = PS(hold=True)
                  ptk_, ptkb = PS(hold=True)
                  ptbv = ptb_.bitcast(BF16)
                  ptkv = ptk_.bitcast(BF16)
                  for c4 in range(4):
                      TR(ptbv[:, c4 * 128:(c4 + 1) * 128], bT[:, c4, :], ident[:], [bT.b, ident.b], ptbb)
                  for c4 in range(4):
                      TR(ptkv[:, c4 * 128:(c4 + 1) * 128], kT[:, c4, :], ident[:], [kT.b, ident.b], ptkb)
                  OP("dve", lambda e: e.tensor_scalar(out=bt[:], in0=ptbv[:, 0:512], scalar1=1.0, scalar2=None, op0=ALU.mult), reads=[ptbb], writes=[bt.b])
                  OP("act", lambda e: e.copy(out=kt[:], in_=ptkv[:, 0:512]), reads=[ptkb], writes=[kt.b])
                  PSREL(ptb_)
                  PSREL(ptk_)
                  yield
                  mX3 = mskX[:].rearrange("p a k t -> p a (k t)")
                  A0, M0, X0 = Acur[0], Mcur[0], Xc[0]
                  for e_ in range(2):
                      hs = slice(e_ * 64, e_ * 64 + 64)
                      px, pxb = PS(hold=True)
                      py, pyb = PS(hold=True)
                      pA, pAb = PS(hold=True)
                      for pair in range(4):
                          for c in range(2):
                              cs = slice(c * 64, (c + 1) * 64)
                              mov = AR[hs, pair, c, :, :].rearrange("p k t -> p (k t)")
                              MM(px[cs, pair * 128:(pair + 1) * 128], bT[hs, pair, cs], mov, True, True, [bT.b, AR.b], pxb)
                              MM(py[cs, pair * 128:(pair + 1) * 128], kT[hs, pair, cs], mov, True, True, [kT.b, AR.b], pyb)
                              MM(pA[cs, pair * 64:(pair + 1) * 64], AR[hs, pair, c, 0, :], bT[hs, pair, cs], True, True, [bT.b, AR.b], pAb)
                      OP("dve", lambda e: e.tensor_tensor(out=MB[:].rearrange("p (a e) k t -> p e a (k t)", e=2)[:, e_],
                                                          in0=px[:, :].rearrange("p (a x) -> p a x", a=4), in1=mX3, op=ALU.mult),
                         reads=[pxb, mskX.b], writes=[MB.b])
                      OP("dve", lambda e: e.tensor_tensor(out=MK[:].rearrange("p (a e) k t -> p e a (k t)", e=2)[:, e_],
                                                          in0=py[:, :].rearrange("p (a x) -> p a x", a=4), in1=mX3, op=ALU.mult),
                         reads=[pyb, mskX.b], writes=[MK.b])
                      OP("dve", lambda e: e.tensor_tensor(out=A0[:].rearrange("p (a e) t -> p e a t", e=2)[:, e_],
                                                          in0=pA[:, 0:256].rearrange("p (a t) -> p a t", a=4), in1=mskL[:, 0:4, :], op=ALU.mult),
                         reads=[pAb, mskL.b], writes=[A0.b])
                      PSREL(px)
                      PSREL(py)
                      PSREL(pA)
                      yield
                  OP("dve", lambda e: e.tensor_tensor(out=X0[:], in0=MB[:, :, 0, :], in1=i64[:].unsqueeze(1).broadcast_to([128, 8, 64]), op=ALU.add),
                     reads=[MB.b, i64.b], writes=[X0.b])
                  pW0, pW0b = PS(hold=True)
                  for c in range(2):
                      cs = slice(c * 64, (c + 1) * 64)
                      for h in range(8):
                          MM(pW0[cs, h * 64:(h + 1) * 64], MK[cs, h, 0, :], Vb[cs, h * 64:(h + 1) * 64], True, True, [MK.b, Vb.b], pW0b)
                  OP("act", lambda e: e.copy(out=W0sb[:], in_=pW0[:, :]), reads=[pW0b], writes=[W0sb.b])
                  PSREL(pW0)


              def phaseA2(n):
                  d = n % 2
                  MB = MB2[n % 3]
                  Xc = Xc2[d]
                  Acur, Mcur = Acur2[d], Mcur2[d]
                  ci = 0
                  xi = 0
                  for lvl in range(5):
                      Ac, Mc = Acur[ci], Mcur[ci]
                      An, Mn = Acur[1 - ci], Mcur[1 - ci]
                      if lvl == 0:
                          class _MV:
                              b = MB.b

                              def __getitem__(self, idx):
                                  return MB[idx[0], idx[1], 0, :]
                          Mc = _MV()
                      p2, p2b = PS(hold=True)
                      for h in range(8):
                          for c in range(2):
                              cs = slice(c * 64, (c + 1) * 64)
                              MM(p2[cs, h * 64:(h + 1) * 64], Mc[cs, h, :], Ac[cs, h, :], True, True, [Mc.b, Ac.b], p2b)
                      OP("act", lambda e: e.copy(out=An[:].rearrange("p a t -> p (a t)"), in_=p2[:, :]), reads=[p2b], writes=[An.b])
                      PSREL(p2)
                      if lvl < 4:
                          p3, p3b = PS(hold=True)
                          for h in range(8):
                              for c in range(2):
                                  cs = slice(c * 64, (c + 1) * 64)
                                  MM(p3[cs, h * 64:(h + 1) * 64], Ac[cs, h, :], Mc[cs, h, :], True, True, [Mc.b, Ac.b], p3b)
                          OP("dve", lambda e: e.tensor_scalar(out=Mn[:].rearrange("p a t -> p (a t)"), in0=p3[:, :], scalar1=1.0, scalar2=None,
                                                              op0=ALU.mult), reads=[p3b], writes=[Mn.b])
                          PSREL(p3)
                      yield
                      Xo, Xn = Xc[xi], Xc[1 - xi]
                      p4, p4b = PS(hold=True)
                      for h in range(8):
                          for c in range(2):
                              cs = slice(c * 64, (c + 1) * 64)
                              MM(p4[cs, h * 64:(h + 1) * 64], An[cs, h, :], Xo[cs, h, :], True, True, [An.b, Xo.b], p4b)
                      OP("dve", lambda e: e.tensor_tensor(out=Xn[:].rearrange("p a t -> p (a t)"), in0=p4[:, :],
                                                          in1=Xo[:].rearrange("p a t -> p (a t)"), op=ALU.add), reads=[p4b, Xo.b], writes=[Xn.b])
                      PSREL(p4)
                      ci = 1 - ci
                      xi = 1 - xi
                      yield
                  Xfin[n] = Xc[xi]

              def phaseB(n):
                  d = n % 2
                  t3 = n % 3
                  Vf, Vb, gsb, Pc, AR, rkb, bt, kt, MB, MK, W0sb = (Vf2[n % 4], Vb2[n % 4], gsb2[t3], Pc2[t3], AR2[t3], rkb2[t3], bt2[t3],
                                                                   kt2[t3], MB2[t3], MK2[t3], W0sb2[t3])
                  X = Xfin.pop(n)
                  yf2 = yf[:].rearrange("p a v -> p (a v)")
                  for c in range(2):
                      cs = slice(c * 64, (c + 1) * 64)
                      Hb = Hb2[c]
                      Hbn = Hb2[1 - c]
                      pWe = [PS(hold=True), PS(hold=True)]
                      for e_ in range(2):
                          hs = slice(e_ * 64, e_ * 64 + 64)
                          for pair in range(4):
                              MM(pWe[e_][0][cs, pair * 64:(pair + 1) * 64], AR[hs, pair, c, 0, :], Hb[hs, pair, :], True, True,
                                 [AR.b, Hb.b], pWe[e_][1])
                      for e_ in range(2):
                          OP("dve", lambda e, e_=e_: e.tensor_tensor(
                              out=par(Wsb[cs, :], e_), in0=pWe[e_][0][cs, 0:256].rearrange("p (a v) -> p a v", a=4),
                              in1=par(W0sb[cs, :], e_), op=ALU.add), reads=[pWe[e_][1], W0sb.b], writes=[Wsb.b])
                          PSREL(pWe[e_][0])
                      yield
                      pU, pUb = PS(hold=True)
                      for h in range(8):
                          MM(pU[cs, h * 64:(h + 1) * 64], X[cs, h, :], Wsb[cs, h * 64:(h + 1) * 64], True, True, [X.b, Wsb.b], pUb)
                      OP("act", lambda e: e.copy(out=Usb[cs, :], in_=pU[cs, :]), reads=[pUb], writes=[Usb.b])
                      PSREL(pU)
                      yield
                      pH, pHb = PS(hold=True)
                      for h in range(8):
                          pair, hs = hsl(h)
                          o_ = pH[hs, pair * 64:(pair + 1) * 64]
                          MM(o_, kt[cs, h * 64:(h + 1) * 64], Vb[cs, h * 64:(h + 1) * 64], True, False, [kt.b, Vb.b], pHb)
                          MM(o_, bt[cs, h * 64:(h + 1) * 64], Usb[cs, h * 64:(h + 1) * 64], False, True, [bt.b, Usb.b], pHb)
                      OP("dve", lambda e: e.tensor_tensor(out=Htmp[:].rearrange("p a v -> p (a v)"), in0=pH[:, 0:256],
                                                          in1=Hs[:].rearrange("p a v -> p (a v)"), op=ALU.add), reads=[pHb, Hs.b], writes=[Htmp.b])
                      PSREL(pH)
                      pcb = Pc[:].rearrange("p a (c t) -> p a c t", c=2)[:, :, c, 63:64].broadcast_to([128, 4, 64])
                      OP("dve", lambda e: e.tensor_tensor(out=Hs[:], in0=Htmp[:], in1=pcb, op=ALU.mult), reads=[Htmp.b, Pc.b], writes=[Hs.b])
                      OP("act", lambda e: e.copy(out=Hbn[:], in_=Hs[:]), reads=[Hs.b], writes=[Hbn.b])
                      pYe = [PS(hold=True), PS(hold=True)]
                      for e_ in range(2):
                          hs = slice(e_ * 64, e_ * 64 + 64)
                          for pair in range(4):
                              MM(pYe[e_][0][cs, pair * 64:(pair + 1) * 64], AR[hs, pair, c, 1, :], Hb[hs, pair, :], True, True,
                                 [AR.b, Hb.b], pYe[e_][1])
                      pYc, pYcb = PS(hold=True)
                      for h in range(8):
                          o_ = pYc[cs, h * 64:(h + 1) * 64]
                          MM(o_, MK[cs, h, 1, :], Vb[cs, h * 64:(h + 1) * 64], True, False, [MK.b, Vb.b], pYcb)
                          MM(o_, MB[cs, h, 1, :], Usb[cs, h * 64:(h + 1) * 64], False, True, [MB.b, Usb.b], pYcb)
                      OP("act", lambda e: e.copy(out=yf2[cs, :], in_=pYc[cs, :]), reads=[pYcb], writes=[yf.b])
                      PSREL(pYc)
                      for e_ in range(2):
                          OP("dve", lambda e, e_=e_: e.tensor_tensor(
                              out=par(yf2[cs, :], e_), in0=pYe[e_][0][cs, 0:256].rearrange("p (a v) -> p a v", a=4),
                              in1=par(yf2[cs, :], e_), op=ALU.add), reads=[pYe[e_][1], yf.b], writes=[yf.b])
                          PSREL(pYe[e_][0])
                      yield
                  OP("dve", lambda e: e.tensor_reduce(out=st[:, 0:8], in_=yf[:], axis=AX.X, op=ALU.add), reads=[yf.b], writes=[st.b])
                  OP("act", lambda e: e.activation(out=ysq[:].rearrange("p a v -> p (a v)"), in_=yf2, func=AF.Square), reads=[yf.b], writes=[ysq.b])
                  OP("dve", lambda e: e.tensor_reduce(out=st[:, 8:16], in_=ysq[:], axis=AX.X, op=ALU.add), reads=[ysq.b], writes=[st.b])
                  OP("dve", lambda e: e.tensor_scalar(out=st[:, 16:24], in0=st[:, 0:8], scalar1=1.0 / 64, scalar2=None, op0=ALU.mult),
                     reads=[st.b], writes=[st.b])
                  OP("dve", lambda e: e.tensor_tensor(out=st[:, 24:32], in0=st[:, 16:24], in1=st[:, 16:24], op=ALU.mult), reads=[st.b], writes=[st.b])
                  OP("dve", lambda e: e.scalar_tensor_tensor(out=st[:, 24:32], in0=st[:, 8:16], scalar=1.0 / 64, in1=st[:, 24:32],
                                                             op0=ALU.mult, op1=ALU.subtract), reads=[st.b], writes=[st.b])
                  OP("dve", lambda e: e.tensor_scalar(out=st[:, 24:32], in0=st[:, 24:32], scalar1=GN_EPS, scalar2=None, op0=ALU.add),
                     reads=[st.b], writes=[st.b])
                  OP("act", lambda e: e.activation(out=st[:, 24:32], in_=st[:, 24:32], func=AF.Ln), reads=[st.b], writes=[st.b])
                  OP("act", lambda e: e.activation(out=st[:, 24:32], in_=st[:, 24:32], func=AF.Exp, scale=-0.5), reads=[st.b], writes=[st.b])
                  yield
                  OP("dve", lambda e: e.tensor_tensor(out=yf[:], in0=yf[:], in1=b8(st[:, 16:24]), op=ALU.subtract), reads=[yf.b, st.b], writes=[yf.b])
                  OP("dve", lambda e: e.tensor_tensor(out=yf[:], in0=yf[:], in1=b8(st[:, 24:32]), op=ALU.mult), reads=[yf.b, st.b], writes=[yf.b])
                  OP("pool", lambda e: e.tensor_tensor(out=yf2, in0=yf2, in1=lnw[:, 0:512], op=ALU.mult), reads=[yf.b, lnw.b], writes=[yf.b])
                  OP("pool", lambda e: e.tensor_tensor(out=yf2, in0=yf2, in1=lnw[:, 512:1024], op=ALU.add), reads=[yf.b, lnw.b], writes=[yf.b])
                  pb_, pbb = PS(hold=True)
                  for c4 in range(4):
                      MM(pb_[:, c4 * 2:c4 * 2 + 2], rkb[:, c4, :], hsel[:, :], True, True, [rkb.b, hsel.b], pbb)
                  OP("act", lambda e: e.copy(out=bs[:], in_=pb_[:, 0:8]), reads=[pbb], writes=[bs.b])
                  PSREL(pb_)
                  yield
                  OP("dve", lambda e: e.tensor_tensor(out=ysq[:], in0=Vf[:].rearrange("p (a v) -> p a v", a=8), in1=b8(bs[:, :]), op=ALU.mult),
                     reads=[Vf.b, bs.b, ysq.b], writes=[ysq.b])
                  OP("pool", lambda e: e.tensor_tensor(out=yf[:], in0=yf[:], in1=ysq[:], op=ALU.add), reads=[yf.b, ysq.b], writes=[yf.b])
                  OP("dve", lambda e: e.tensor_tensor(out=yg[:], in0=yf2, in1=gsb[:], op=ALU.mult), reads=[yf.b, gsb.b], writes=[yg.b])
                  yield
                  pt_, ptb2 = PS(hold=True)
                  ptv = pt_.bitcast(BF16)
                  for c4 in range(4):
                      TR(ptv[:, c4 * 128:(c4 + 1) * 128], yg[:, c4 * 128:(c4 + 1) * 128], ident[:], [yg.b, ident.b], ptb2)
                  ya_ = yaTs[n % 2]
                  OP("act", lambda e: e.copy(out=ya_[:].rearrange("p a t -> p (a t)"), in_=ptv[:, 0:512]), reads=[ptb2], writes=[ya_.b])
                  PSREL(pt_)
                  DMA("sp", yaT_d[:, :, n * 128:(n + 1) * 128], ya_[:], reads=[ya_.b], writes=[B_yaT])

              dummy = PS(hold=True) if NDUMMY else None

              def run_rr(gens):
                  alive = [g is not None for g, _ in gens]
                  while any(alive):
                      for i_, (g, k_) in enumerate(gens):
                          for _ in range(k_):
                              if not alive[i_]:
                                  break
                              try:
                                  next(g)
                              except StopIteration:
                                  alive[i_] = False
                          for _ in range(NDUMMY):
                              MM(dummy[0][:, 0:128], ident[:, :], ident[:, :], True, True, [ident.b], dummy[1])

              gP = lambda n: phaseP(n) if n < NS else None
              gA1 = lambda n: phaseA(n) if n < NS else None
              gA2 = lambda n: phaseA2(n) if n < NS else None
              run_rr([(gP(0), 1)])
              run_rr([(gA1(0), RR[2]), (gP(1), RR[3])])
              run_rr([(gA2(0), RR[1]), (gA1(1), RR[2]), (gP(2), RR[3])])
              for n in range(NS):
                  streams = [(phaseB(n), RR[0]), (gA2(n + 1), RR[1]), (gA1(n + 2), RR[2]), (gP(n + 3), RR[3])]
                  run_rr([streams[k_] for k_ in ORDER])
              if dummy is not None:
                  PSREL(dummy[0])

        except _Stop:
            pass
        sy.barrier()

        CP('F')
        with ExitStack() as sf:
            Wf = SB(sf, "Wf", [128, 8, NFX], BF16)
            Wf_b = [Buf("Wf%d" % k) for k in range(8)]
            KT = SB(sf, "KT", [65, 8, S], BF16)
            Va = SB(sf, "Va", [128, NS, 8, 65], BF16)
            QT = [SB(sf, "QT%d" % i, [65, 8, 512], BF16) for i in range(2)]
            ncum = SB(sf, "ncum", [128, NS, 8], F32)
            PTs = [SB(sf, "PT%d" % i, [128, 512], BF16) for i in range(4)]
            hTf = [SB(sf, "hTf%d" % i, [128, 8, 512], BF16) for i in range(2)]
            tri = SB(sf, "tri", [128, 128], F32)
            onesq = SB(sf, "onesq", [128, 128], F32)
            trib = SB(sf, "trib", [128, 128], BF16)
            acc = SB(sf, "acc", [128, 8], F32)
            fbb = SB(sf, "fbb", [128, 8], F32)
            lf = SB(sf, "lf", [128, 8], F32)
            cq8 = SB(sf, "cq8", [8, 512], BF16)
            oT = [SB(sf, "oT%d" % i, [65, 512], F32) for i in range(4)]
            ybh = [SB(sf, "ybh%d" % i, [64, 512], BF16) for i in range(4)]
            for kc in range(8):
                DMA("pool", Wf[:, kc, :], win_d[kc * 128:(kc + 1) * 128, NRW:NRW + NFX], writes=[Wf_b[kc]])
            DMA("sp", fbb[:], bc_d[:, BC_FB:BC_FB + 8], writes=[fbb.b])
            OP("pool", lambda e: e.memset(tri[:], 1.0), writes=[tri.b])
            OP("pool", lambda e: e.affine_select(out=tri[:], in_=tri[:], pattern=[[1, 128]], compare_op=ALU.is_ge, fill=0.0,
                                                 base=0, channel_multiplier=-1), reads=[tri.b], writes=[tri.b])
            negm = SB(sf, "negm", [128, 128], F32)
            OP("pool", lambda e: e.memset(negm[:], 0.0), writes=[negm.b])
            OP("pool", lambda e: e.affine_select(out=negm[:], in_=negm[:], pattern=[[1, 128]], compare_op=ALU.is_ge, fill=-1.0e4,
                                                 base=0, channel_multiplier=-1), reads=[negm.b], writes=[negm.b])
            OP("pool", lambda e: e.memset(trib[:], 1.0), writes=[trib.b])
            OP("pool", lambda e: e.affine_select(out=trib[:], in_=trib[:], pattern=[[1, 128]], compare_op=ALU.is_ge, fill=0.0,
                                                 base=0, channel_multiplier=-1), reads=[trib.b], writes=[trib.b])
            OP("pool", lambda e: e.memset(onesq[:], 1.0), writes=[onesq.b])
            OP("dve", lambda e: e.memset(acc[:], 0.0), writes=[acc.b])
            lf4 = SB(sf, "lf4", [128, 4, 8], F32)
            acc5 = SB(sf, "acc5", [128, 5, 8], F32)
            cq8s = [SB(sf, "cq8_%d" % i_, [8, 512], BF16) for i_ in range(2)]
            qkst = [SB(sf, "qkst%d" % i_, [128, 512], BF16) for i_ in range(4)]
            PT6 = [SB(sf, "PTx%d" % i, [128, 512], BF16) for i in range(8)]
            OP("dve", lambda e: e.memset(acc5[:], 0.0), writes=[acc5.b])
            KT_b = [Buf("KTt%d" % i_) for i_ in range(NT)]
            Va_b = [Buf("Vat%d" % i_) for i_ in range(NT)]
            nc_b = [Buf("nct%d" % i_) for i_ in range(NT)]
            OP("dve", lambda e: e.memset(KT[64:65, :, :], 1.0), writes=KT_b)
            OP("dve", lambda e: e.memset(Va[:, :, :, 64:65], 1.0), writes=Va_b)
            LOOK = 6
            qrow_b = [[Buf("qrow%d_%d" % (i_, h_)) for h_ in range(8)] for i_ in range(2)]
            unit_ctr = [0]

            def front(i):
                ht = hTf[i % 2]
                qt = QT[i % 2]
                cq8_ = cq8s[i % 2]
                DMA("sp", ht[:], hT_d[:, :, i * 512:(i + 1) * 512], writes=[ht.b])
                yield
                pf, pfb = PS(hold=True)
                for sub in range(4):
                    ts_ = slice(sub * 128, (sub + 1) * 128)
                    for kc in range(8):
                        MM(pf[:, sub * 8:(sub + 1) * 8], ht[:, kc, ts_], Wf[:, kc, 1536:1544], kc == 0, kc == 7, [Wf_b[kc], ht.b], pfb)
                OP("dve", lambda e: e.tensor_tensor(out=lf4[:], in0=pf[:, 0:32].rearrange("p (a h) -> p a h", a=4),
                                                    in1=fbb[:].unsqueeze(1).broadcast_to([128, 4, 8]), op=ALU.add),
                   reads=[pfb, fbb.b], writes=[lf4.b])
                PSREL(pf)
                OP("act", lambda e: e.activation(out=lf4[:], in_=lf4[:], func=AF.Sigmoid), reads=[lf4.b], writes=[lf4.b])
                OP("act", lambda e: e.activation(out=lf4[:], in_=lf4[:], func=AF.Ln), reads=[lf4.b], writes=[lf4.b])
                if i > 0:
                    OP("dve", lambda e: e.tensor_scalar(out=acc5[:, 0, :], in0=acc5[:, 4, :], scalar1=1.0, scalar2=None, op0=ALU.mult),
                       reads=[acc5.b], writes=[acc5.b])
                for sub in range(4):
                    OP("dve", lambda e, sub=sub: e.tensor_tensor(out=acc5[:, sub + 1, :], in0=acc5[:, sub, :], in1=lf4[:, sub, :], op=ALU.add),
                       reads=[acc5.b, lf4.b], writes=[acc5.b])
                yield
                for pair in range(4):
                    for which in range(2):
                        col0 = which * 512 + pair * 128
                        pq_, pqb_ = PS(hold=True)
                        for kc in range(8):
                            MM(pq_[:, :], Wf[:, kc, col0:col0 + 128], ht[:, kc, :], kc == 0, kc == 7, [Wf_b[kc], ht.b], pqb_)
                        stg = qkst[(pair * 2 + which) % 4]
                        if which == 0:
                            dst_even = qt[0:64, 2 * pair, :]
                            dst_odd = qt[0:64, 2 * pair + 1, :]
                            dbuf = qt.b
                        else:
                            dst_even = KT[0:64, 2 * pair, i * 512:(i + 1) * 512]
                            dst_odd = KT[0:64, 2 * pair + 1, i * 512:(i + 1) * 512]
                            dbuf = KT_b[i]
                        OP("dve", lambda e: e.tensor_scalar(out=dst_even, in0=pq_[0:64, :], scalar1=1.0, scalar2=None, op0=ALU.mult),
                           reads=[pqb_], writes=[dbuf])
                        OP("act", lambda e: e.copy(out=stg[64:128, :], in_=pq_[64:128, :]), reads=[pqb_], writes=[stg.b])
                        PSREL(pq_)
                        DMA("sp", dst_odd, stg[64:128, :], reads=[stg.b], writes=[dbuf])
                        yield
                for sub in range(4):
                    g = i * 4 + sub
                    ts_ = slice(sub * 128, (sub + 1) * 128)
                    pv2, pv2b = PS(hold=True)
                    for kc in range(8):
                        MM(pv2[:, :], ht[:, kc, ts_], Wf[:, kc, 1024:1536], kc == 0, kc == 7, [Wf_b[kc], ht.b], pv2b)
                    OP("dve", lambda e: e.tensor_scalar(out=Va[:, g, :, 0:64], in0=pv2[:, :].rearrange("p (a v) -> p a v", a=8),
                                                        scalar1=1.0, scalar2=None, op0=ALU.mult), reads=[pv2b], writes=[Va_b[i]])
                    PSREL(pv2)
                    yield
                pc, pcb_ = PS(hold=True)
                pcT, pcTb = PS(hold=True)
                for sub in range(4):
                    ts_ = slice(sub * 128, (sub + 1) * 128)
                    MM(pc[:, sub * 8:(sub + 1) * 8], tri[:, :], lf4[:, sub, :], True, False, [tri.b, lf4.b], pcb_)
                    MM(pc[:, sub * 8:(sub + 1) * 8], onesq[:, :], acc5[:, sub, :], False, True, [onesq.b, acc5.b], pcb_)
                    MM(pcT[0:8, ts_], lf4[:, sub, :], tri[:, :], sub == 0, False, [tri.b, lf4.b], pcTb)
                    MM(pcT[0:8, ts_], acc5[:, sub, :], onesq[:, :], False, sub == 3, [onesq.b, acc5.b], pcTb)
                OP("dve", lambda e: e.tensor_scalar(out=ncum[:, i * 4:(i + 1) * 4, :], in0=pc[:, 0:32].rearrange("p (a h) -> p a h", a=4),
                                                    scalar1=-1.0, scalar2=None, op0=ALU.mult), reads=[pcb_], writes=[nc_b[i]])
                PSREL(pc)
                OP("dve", lambda e: e.tensor_scalar(out=cq8_[:], in0=pcT[0:8, :], scalar1=8.0, scalar2=None, op0=ALU.mult),
                   reads=[pcTb], writes=[cq8_.b])
                PSREL(pcT)
                for h in range(8):
                    DMA("sp", qt[64:65, h, :], cq8_[h:h + 1, :], reads=[cq8_.b], writes=[qrow_b[i % 2][h]])

            def attention(i):
                qt = QT[i % 2]
                nkb = 4 * i + 4
                tasks = [(h, j) for h in range(8) for j in range(nkb)]
                NTK = len(tasks)
                pts = {}
                pos = {}
                tails = []

                def emit_S(t):
                    h, j = tasks[t]
                    q0 = max(0, j - 4 * i) * 128
                    ps_, psb_ = PS()
                    MM(ps_[:, q0:512], KT[0:65, h, j * 128:(j + 1) * 128], qt[0:65, h, q0:512], True, True,
                       [KT_b[j // 4], qt.b, qrow_b[i % 2][h]], psb_)
                    PT = PT6[t % 8]
                    if j >= 4 * i:
                        OP("dve", lambda e: e.tensor_tensor(out=ps_[:, q0:q0 + 128], in0=ps_[:, q0:q0 + 128], in1=negm[:], op=ALU.add),
                           reads=[psb_, negm.b], writes=[psb_])
                    OP("act", lambda e: e.activation(out=PT[:, q0:512], in_=ps_[:, q0:512], func=AF.Exp, scale=0.125,
                                                     bias=ncum[:, j, h:h + 1]), reads=[psb_, nc_b[j // 4]], writes=[PT.b])
                    pts[t] = (PT, q0)

                def emit_PV(t):
                    h, j = tasks[t]
                    if j == 0:
                        pos[h] = PS(hold=True)
                    po, pob = pos[h]
                    PT, q0 = pts.pop(t)
                    MM(po[0:65, q0:512], Va[:, j, h, :], PT[:, q0:512], j == 0, j == nkb - 1, [Va_b[j // 4], PT.b], pob)
                    if j == nkb - 1:
                        u = unit_ctr[0]
                        unit_ctr[0] += 1
                        o_ = oT[u % 4]
                        yb_ = ybh[u % 4]
                        OP("dve", lambda e: e.tensor_scalar(out=o_[:], in0=po[0:65, :], scalar1=1.0, scalar2=None, op0=ALU.mult),
                           reads=[pob], writes=[o_.b])
                        PSREL(po)
                        OP("dve", lambda e: e.reciprocal(out=o_[64:65, :], in_=o_[64:65, :]), reads=[o_.b], writes=[o_.b])

                        def tail(h=h, o_=o_, yb_=yb_):
                            pr_, prb_ = PS()
                            MM(pr_[0:64, :], onesq[64:65, 0:64], o_[64:65, :], True, True, [onesq.b, o_.b], prb_)
                            OP("dve", lambda e: e.tensor_tensor(out=yb_[:], in0=pr_[0:64, :], in1=o_[0:64, :], op=ALU.mult),
                               reads=[prb_, o_.b], writes=[yb_.b])
                            DMA("sp", ybT_d[h * 64:(h + 1) * 64, i * 512:(i + 1) * 512], yb_[:], reads=[yb_.b], writes=[B_ybT])
                        tails.append((t + 11, tail))

                for t in range(min(LOOK, NTK)):
                    emit_S(t)
                for t in range(NTK):
                    if t + LOOK < NTK:
                        emit_S(t + LOOK)
                    emit_PV(t)
                    while tails and tails[0][0] <= t:
                        tails.pop(0)[1]()
                    yield
                while tails:
                    tails.pop(0)[1]()

            for _ in front(0):
                pass
            for i in range(NT):
                gA = attention(i)
                gF = front(i + 1) if i + 1 < NT else None
                ntk = 8 * (4 * i + 4)
                every = max(1, ntk // 28)
                cnt_ = 0
                for _ in gA:
                    cnt_ += 1
                    if gF is not None and cnt_ % every == 0:
                        try:
                            next(gF)
                        except StopIteration:
                            gF = None
                if gF is not None:
                    for _ in gF:
                        pass
        sy.barrier()

        CP('C1')
        with ExitStack() as sc:
            Wg = SB(sc, "Wg", [128, 8, 2048], BF16)
            Wg_b = [Buf("Wg%d" % k) for k in range(8)]
            Wor = SB(sc, "Wor", [128, 4, D], BF16)
            Wof = SB(sc, "Wof", [128, 4, D], BF16)
            Wout = SB(sc, "Wout", [128, 8, D], BF16)
            gtb = SB(sc, "gtb", [128, D], F32)
            hTc = [SB(sc, "hTc%d" % i, [128, 8, 512], BF16) for i in range(2)]
            yaTt = [SB(sc, "yaTt%d" % i, [128, 4, 512], BF16) for i in range(2)]
            ybTt = [SB(sc, "ybTt%d" % i, [128, 4, 512], BF16) for i in range(2)]
            Gsig2 = [SB(sc, "Gsig%d" % i_, [128, 16, 512], BF16) for i_ in range(2)]
            mrg = SB(sc, "mrg", [128, 8, 512], BF16)
            tm1 = [SB(sc, "tm1_%d" % i, [128, 512], F32) for i in range(2)]
            tm2 = [SB(sc, "tm2_%d" % i, [128, 512], F32) for i in range(2)]
            tm3c = [SB(sc, "tm3c_%d" % i, [128, 512], F32) for i in range(2)]
            xs = [SB(sc, "xs%d" % i, [128, D], F32) for i in range(2)]
            x1s = [SB(sc, "x1s%d" % i, [128, D], F32) for i in range(2)]
            xn2 = [SB(sc, "xn2_%d" % i, [128, D], BF16) for i in range(2)]
            h2t = [SB(sc, "h2t%d" % i, [128, 8, 512], BF16) for i in range(2)]
            junk1 = SB(sc, "junk1", [128, D], BF16)
            st2 = [SB(sc, "st2_%d" % i, [128, 2], F32) for i in range(2)]
            for kc in range(8):
                DMA("pool", Wg[:, kc, :], win_d[kc * 128:(kc + 1) * 128, NRW + NFX:NIN], writes=[Wg_b[kc]])
            DMA("pool", Wor[:], wor_d.rearrange("(c p) n -> p c n", p=128), writes=[Wor.b])
            DMA("pool", Wof[:], wof_d.rearrange("(c p) n -> p c n", p=128), writes=[Wof.b])
            DMA("pool", Wout[:], wout_d.rearrange("(c p) n -> p c n", p=128), writes=[Wout.b])
            DMA("sp", gtb[:], gt_d[:, 0:D], reads=[B_gt], writes=[gtb.b])
            ybT_v = ybT_d.rearrange("(c p) s -> p c s", p=128)

            def load_tile(i):
                tsl_ = slice(i * 512, (i + 1) * 512)
                DMA("sp", hTc[i % 2][:], hT_d[:, :, tsl_], reads=[B_hT], writes=[hTc[i % 2].b])
                DMA("sp", yaTt[i % 2][:], yaT_d[:, :, tsl_], reads=[B_yaT], writes=[yaTt[i % 2].b])
                DMA("sp", ybTt[i % 2][:], ybT_v[:, :, tsl_], reads=[B_ybT], writes=[ybTt[i % 2].b])

            def load_x(k_):
                if k_ < NS:
                    DMA("sp", xs[k_ % 2][:], x_d[k_ * 128:(k_ + 1) * 128, :], writes=[xs[k_ % 2].b])

            def gates(i, part):
                if i >= NT:
                    return
                ht_, G_ = hTc[i % 2], Gsig2[i % 2]
                for g in range(part * 8, part * 8 + 8):
                    pg2, pg2b = PS()
                    for kc in range(8):
                        MM(pg2[:, :], Wg[:, kc, g * 128:(g + 1) * 128], ht_[:, kc, :], kc == 0, kc == 7, [Wg_b[kc], ht_.b], pg2b)
                    OP("act", lambda e, g=g, pg2=pg2: e.activation(out=G_[:, g, :], in_=pg2[:, :], func=AF.Sigmoid),
                       reads=[pg2b], writes=[G_.b])

            h2_b = [[Buf("h2_%d_%d" % (i_, k_)) for k_ in range(8)] for i_ in range(2)]
            load_tile(0)
            load_x(0)
            gates(0, 0)
            gates(0, 1)
            for i in range(NT):
                ht, ya_t, yb_t, h2 = hTc[i % 2], yaTt[i % 2], ybTt[i % 2], h2t[i % 2]
                Gsig = Gsig2[i % 2]
                tsl = slice(i * 512, (i + 1) * 512)
                if i + 1 < NT:
                    load_tile(i + 1)
                for m in range(8):
                    pa2, pa2b = PS()
                    for c in range(4):
                        MM(pa2[:, :], Wor[:, c, m * 128:(m + 1) * 128], ya_t[:, c, :], c == 0, c == 3, [Wor.b, ya_t.b], pa2b)
                    pb2, pb2b = PS()
                    for c in range(4):
                        MM(pb2[:, :], Wof[:, c, m * 128:(m + 1) * 128], yb_t[:, c, :], c == 0, c == 3, [Wof.b, yb_t.b], pb2b)
                    t1_, t2_ = tm1[m % 2], tm2[m % 2]
                    OP("dve", lambda e, m=m, pa2=pa2, t1_=t1_: e.tensor_tensor(out=t1_[:], in0=pa2[:, :], in1=Gsig[:, m, :], op=ALU.mult),
                       reads=[pa2b, Gsig.b], writes=[t1_.b])
                    OP("dve", lambda e, m=m, pb2=pb2, t2_=t2_: e.tensor_tensor(out=t2_[:], in0=pb2[:, :], in1=Gsig[:, 8 + m, :], op=ALU.mult),
                       reads=[pb2b, Gsig.b], writes=[t2_.b])
                    OP("pool", lambda e, m=m, t1_=t1_, t2_=t2_: e.tensor_tensor(out=mrg[:, m, :], in0=t1_[:], in1=t2_[:], op=ALU.add),
                       reads=[t1_.b, t2_.b], writes=[mrg.b])
                gates(i + 1, 0)
                pT2 = [PS(hold=True) for _ in range(4)]

                def z_part(sub):
                    k_ = i * 4 + sub
                    ts_ = slice(sub * 128, (sub + 1) * 128)
                    xt, x1t, xnt, s2 = xs[k_ % 2], x1s[k_ % 2], xn2[k_ % 2], st2[k_ % 2]
                    load_x(k_ + 1)
                    for half in range(2):
                        hsl_ = slice(half * 512, (half + 1) * 512)
                        pz2, pz2b = PS()
                        for c in range(8):
                            MM(pz2[:, :], mrg[:, c, ts_], Wout[:, c, hsl_], c == 0, c == 7, [mrg.b, Wout.b], pz2b)
                        t1_ = tm3c[half]
                        OP("dve", lambda e, pz2=pz2, t1_=t1_, hsl_=hsl_: e.tensor_tensor(out=t1_[:], in0=pz2[:, :], in1=gtb[:, hsl_], op=ALU.mult),
                           reads=[pz2b, gtb.b], writes=[t1_.b])
                        OP("dve", lambda e, t1_=t1_, hsl_=hsl_, xt=xt, x1t=x1t: e.tensor_tensor(out=x1t[:, hsl_], in0=t1_[:], in1=xt[:, hsl_], op=ALU.add),
                           reads=[t1_.b, xt.b], writes=[x1t.b])
                    DMA("sp", x1_d[i * 512 + sub * 128: i * 512 + (sub + 1) * 128, :], x1t[:], reads=[x1t.b], writes=[B_x1])
                    OP("act", lambda e, x1t=x1t, s2=s2: e.activation(out=junk1[:], in_=x1t[:], func=AF.Square, accum_out=s2[:, 0:1]),
                       reads=[x1t.b], writes=[junk1.b, s2.b])
                    OP("dve", lambda e, s2=s2: e.tensor_scalar(out=s2[:, 1:2], in0=s2[:, 0:1], scalar1=1.0 / D, scalar2=NORM_EPS,
                                                               op0=ALU.mult, op1=ALU.add), reads=[s2.b], writes=[s2.b])
                    OP("act", lambda e, s2=s2: e.activation(out=s2[:, 1:2], in_=s2[:, 1:2], func=AF.Sqrt), reads=[s2.b], writes=[s2.b])
                    OP("dve", lambda e, s2=s2: e.reciprocal(out=s2[:, 1:2], in_=s2[:, 1:2]), reads=[s2.b], writes=[s2.b])
                    OP("act", lambda e, x1t=x1t, xnt=xnt, s2=s2: e.activation(out=xnt[:], in_=x1t[:], func=AF.Copy, scale=s2[:, 1:2]),
                       reads=[x1t.b, s2.b], writes=[xnt.b])

                def t_part(sub):
                    k_ = i * 4 + sub
                    xnt = xn2[k_ % 2]
                    for kc in range(8):
                        pv3 = pT2[kc // 2][0].bitcast(BF16)
                        TR(pv3[:, (kc % 2) * 512 + sub * 128:(kc % 2) * 512 + (sub + 1) * 128], xnt[:, kc * 128:(kc + 1) * 128], ident[:],
                           [xnt.b, ident.b], pT2[kc // 2][1])

                z_part(0)
                for sub in range(4):
                    if sub + 1 < 4:
                        z_part(sub + 1)
                    if sub == 1:
                        gates(i + 1, 1)
                    t_part(sub)
                for hb in range(4):
                    pv3 = pT2[hb][0].bitcast(BF16)
                    for kq in range(2):
                        kc = hb * 2 + kq
                        if kc % 2 == 0:
                            OP("dve", lambda e, kc=kc, kq=kq, pv3=pv3: e.tensor_scalar(
                                out=h2[:, kc, :], in0=pv3[:, kq * 512:(kq + 1) * 512], scalar1=GS[:, 16 + kc:17 + kc],
                                scalar2=GS[:, 24 + kc:25 + kc], op0=ALU.mult, op1=ALU.add), reads=[pT2[hb][1], GS.b], writes=[h2_b[i % 2][kc]])
                        else:
                            OP("act", lambda e, kc=kc, kq=kq, pv3=pv3: e.activation(
                                out=h2[:, kc, :], in_=pv3[:, kq * 512:(kq + 1) * 512], func=AF.Identity, scale=GS[:, 16 + kc:17 + kc],
                                bias=GS[:, 24 + kc:25 + kc]), reads=[pT2[hb][1], GS.b], writes=[h2_b[i % 2][kc]])
                for hb in range(4):
                    PSREL(pT2[hb][0])
                DMA("sp", h2T_d[:, :, tsl], h2[:], reads=h2_b[i % 2], writes=[B_h2T])
        sy.barrier()

        CP('C2')
        with ExitStack() as s2c:
            W1 = SB(s2c, "W1", [128, 8, 4 * D], BF16)
            W1_b = [Buf("W1_%d" % k) for k in range(8)]
            W2 = SB(s2c, "W2", [128, 32, D], BF16)
            W2_b = [Buf("W2_%d" % k) for k in range(4)]
            gt2 = SB(s2c, "gt2", [128, D], F32)
            fgb = SB(s2c, "fgb", [128, D], F32)
            h2c = [SB(s2c, "h2c%d" % i, [128, 8, 256], BF16) for i in range(2)]
            hid = SB(s2c, "hid", [128, 32, 256], BF16)
            rl = [SB(s2c, "rl%d" % i, [128, 512], F32) for i in range(2)]
            x1c = [SB(s2c, "x1c%d" % i, [128, D], F32) for i in range(2)]
            x2c = [SB(s2c, "x2c%d" % i, [128, D], F32) for i in range(2)]
            oc = [SB(s2c, "oc%d" % i, [128, D], F32) for i in range(2)]
            tm3 = [SB(s2c, "tm3_%d" % i, [128, 512], F32) for i in range(2)]
            junk2 = SB(s2c, "junk2", [128, D], BF16)
            st3 = [SB(s2c, "st3_%d" % i, [128, 2], F32) for i in range(2)]
            for kc in range(8):
                DMA("pool", W1[:, kc, :], wff1_d[kc * 128:(kc + 1) * 128, :], writes=[W1_b[kc]])
            for q4 in range(4):
                DMA("pool", W2[:, q4 * 8:(q4 + 1) * 8, :],
                    wff2_d[q4 * 1024:(q4 + 1) * 1024, :].rearrange("(k p) n -> p k n", p=128), writes=[W2_b[q4]])
            DMA("sp", gt2[:], gt_d[:, D:2 * D], reads=[B_gt], writes=[gt2.b])
            DMA("sp", fgb[:], bc_d[:, BC_FG:BC_FG + D], writes=[fgb.b])
            def load_h2(i2):
                if i2 < S // 256:
                    DMA("sp", h2c[i2 % 2][:], h2T_d[:, :, i2 * 256:(i2 + 1) * 256], reads=[B_h2T], writes=[h2c[i2 % 2].b])

            def load_x1(k_):
                if k_ < NS:
                    DMA("sp", x1c[k_ % 2][:], x1_d[k_ * 128:(k_ + 1) * 128, :], reads=[B_x1], writes=[x1c[k_ % 2].b])

            load_h2(0)
            load_x1(0)
            for i2 in range(S // 256):
                hc = h2c[i2 % 2]
                load_h2(i2 + 1)
                for f2 in range(16):
                    pf2, pf2b = PS()
                    for fh in range(2):
                        f = f2 * 2 + fh
                        for kc in range(8):
                            MM(pf2[:, fh * 256:(fh + 1) * 256], W1[:, kc, f * 128:(f + 1) * 128], hc[:, kc, :], kc == 0, kc == 7,
                               [W1_b[kc], hc.b], pf2b)
                    r_ = rl[f2 % 2]
                    OP("act", lambda e, pf2=pf2, r_=r_: e.activation(out=r_[:], in_=pf2[:, :], func=AF.Relu), reads=[pf2b], writes=[r_.b])
                    OP("pool", lambda e, f2=f2, r_=r_: e.tensor_tensor(out=hid[:, f2 * 2:f2 * 2 + 2, :].rearrange("p a t -> p (a t)"),
                                                                       in0=r_[:], in1=r_[:], op=ALU.mult), reads=[r_.b], writes=[hid.b])
                for sub in range(2):
                    k_ = i2 * 2 + sub
                    r0 = i2 * 256 + sub * 128
                    ts_ = slice(sub * 128, (sub + 1) * 128)
                    x1t, x2t, ot, s3 = x1c[k_ % 2], x2c[k_ % 2], oc[k_ % 2], st3[k_ % 2]
                    load_x1(k_ + 1)
                    for half in range(2):
                        hsl_ = slice(half * 512, (half + 1) * 512)
                        po2, po2b = PS()
                        for kk_ in range(32):
                            MM(po2[:, :], hid[:, kk_, ts_], W2[:, kk_, hsl_], kk_ == 0, kk_ == 31, [hid.b, W2_b[kk_ // 8]], po2b)
                        t3 = tm3[half]
                        OP("dve", lambda e, po2=po2, t3=t3, hsl_=hsl_: e.tensor_tensor(out=t3[:], in0=po2[:, :], in1=gt2[:, hsl_], op=ALU.mult),
                           reads=[po2b, gt2.b], writes=[t3.b])
                        OP("dve", lambda e, t3=t3, hsl_=hsl_, x1t=x1t, x2t=x2t: e.tensor_tensor(out=x2t[:, hsl_], in0=t3[:], in1=x1t[:, hsl_], op=ALU.add),
                           reads=[t3.b, x1t.b], writes=[x2t.b])
                    OP("act", lambda e, x2t=x2t, s3=s3: e.activation(out=junk2[:], in_=x2t[:], func=AF.Square, accum_out=s3[:, 0:1]),
                       reads=[x2t.b], writes=[junk2.b, s3.b])
                    OP("dve", lambda e, s3=s3: e.tensor_scalar(out=s3[:, 1:2], in0=s3[:, 0:1], scalar1=1.0 / D, scalar2=NORM_EPS,
                                                               op0=ALU.mult, op1=ALU.add), reads=[s3.b], writes=[s3.b])
                    OP("act", lambda e, s3=s3: e.activation(out=s3[:, 1:2], in_=s3[:, 1:2], func=AF.Sqrt), reads=[s3.b], writes=[s3.b])
                    OP("dve", lambda e, s3=s3: e.reciprocal(out=s3[:, 1:2], in_=s3[:, 1:2]), reads=[s3.b], writes=[s3.b])
                    OP("dve", lambda e, x2t=x2t, ot=ot, s3=s3: e.scalar_tensor_tensor(out=ot[:], in0=x2t[:], scalar=s3[:, 1:2], in1=fgb[:],
                                                                                     op0=ALU.mult, op1=ALU.mult),
                       reads=[x2t.b, s3.b, fgb.b], writes=[ot.b])
                    DMA("sp", out_d[r0:r0 + 128, :], ot[:], reads=[ot.b], writes=[B_out])
        sy.barrier()
        sy.drain("sp")
    return nc


def _pack_inputs(inp, b):
    f = lambda a: np.ascontiguousarray(np.asarray(a, dtype=np.float32))
    pp = np.zeros((128, NPP), np.float32)
    pp[:, PP_BADA:PP_BADA + 48] = f(inp["b_ada"])[0].reshape(48, 128).T
    pp[:, PP_G1:PP_G1 + 8] = f(inp["norm1_g"])[0].reshape(8, 128).T
    pp[:, PP_G2:PP_G2 + 8] = f(inp["norm2_g"])[0].reshape(8, 128).T
    pp[:, PP_C:PP_C + 8] = f(inp["c"])[b].reshape(8, 128).T
    pp[:, PP_DB:PP_DB + 4] = f(inp["decay_base"])[0].reshape(4, 128).T
    pp[:, PP_IB:PP_IB + 4] = f(inp["iclr_base"])[0].reshape(4, 128).T
    pp[:, PP_KS:PP_KS + 4] = f(inp["kk_scale"])[0].reshape(4, 128).T
    pp[:, PP_MIX:PP_MIX + 4] = f(inp["k_iclr_mix"])[0].reshape(4, 128).T
    pp[:, PP_RB:PP_RB + 4] = f(inp["r_bonus"])[0].reshape(512).reshape(4, 128).T
    mu = f(inp["mu_shift"])[0]
    mu_r = np.concatenate([mu[0:512], mu[576:1088], mu[1088:1600], mu[512:576], mu[1600:1664], mu[1664:1792]])
    row = np.concatenate([mu_r, f(inp["lnx_w"])[0], f(inp["lnx_b"])[0], f(inp["fox_f_bias"])[0], f(inp["final_g"])])
    bc = np.ascontiguousarray(np.broadcast_to(row[None, :], (128, NBC)))
    ba = f(inp["b_ada"])[0]
    brow = np.concatenate([ba[2 * D:3 * D], ba[5 * D:6 * D]])[None, :]
    return pp, bc, np.ascontiguousarray(brow)


_NC_CACHE = {}


def make_in_maps(inp, S, nb):
    shared = {
        "w_ada": np.ascontiguousarray(np.asarray(inp["w_ada"], np.float32)[0]),
        "w_in": np.ascontiguousarray(np.asarray(inp["w_in"], np.float32)[0]),
        "w_decay_up": np.ascontiguousarray(np.asarray(inp["w_decay_up"], np.float32)[0]),
        "w_iclr_up": np.ascontiguousarray(np.asarray(inp["w_iclr_up"], np.float32)[0]),
        "w_gate_up": np.ascontiguousarray(np.asarray(inp["w_gate_up"], np.float32)[0]),
        "w_o_rwkv": np.ascontiguousarray(np.asarray(inp["w_o_rwkv"], np.float32)[0]),
        "w_o_fox": np.ascontiguousarray(np.asarray(inp["w_o_fox"], np.float32)[0]),
        "w_out": np.ascontiguousarray(np.asarray(inp["w_out"], np.float32)[0]),
        "w_ff1": np.ascontiguousarray(np.asarray(inp["w_ff1"], np.float32)[0]),
        "w_ff2": np.ascontiguousarray(np.asarray(inp["w_ff2"], np.float32)[0]),
    }
    maps = []
    x = np.asarray(inp["x"], np.float32)
    for b in range(nb):
        pp, bc, brow = _pack_inputs(inp, b)
        m = dict(shared)
        m.update({"x": np.ascontiguousarray(x[b]), "pp": pp, "bc": bc, "brow": brow})
        maps.append(m)
    return maps


def kernel(**inputs):
    x = np.asarray(inputs["x"])
    nb, S = x.shape[0], x.shape[1]
    if S not in _NC_CACHE:
        _NC_CACHE[S] = build(S)
    nc = _NC_CACHE[S]
    maps = make_in_maps(inputs, S, nb)
    res = run_bass_kernel_spmd(nc, maps, core_ids=list(range(nb)))
    return np.stack([np.asarray(r["out"], np.float32) for r in res.results], axis=0)
```

```python
import numpy as np
from contextlib import ExitStack
import concourse.bass as bass
import concourse.mybir as mybir
from concourse.alu_op_type import AluOpType as ALU
from concourse.bass_utils import run_bass_kernel_spmd

F32 = mybir.dt.float32
BF16 = mybir.dt.bfloat16
AF = mybir.ActivationFunctionType
AX = mybir.AxisListType

D = 1024
NRW = 1792
NFX = 1544
NIN = 5384
EXPM05 = 0.6065306597126334
NORM_EPS = 1e-6
GN_EPS = 64e-5

PP_BADA, PP_G1, PP_G2, PP_C, PP_DB, PP_IB, PP_KS, PP_MIX, PP_RB, NPP = 0, 48, 56, 64, 72, 76, 80, 84, 88, 92
BC_MU, BC_LW, BC_LB, BC_FB, BC_FG, NBC = 0, 1792, 2304, 2816, 2824, 3848


class Buf:
    __slots__ = ("name", "lw", "rd")

    def __init__(self, name=""):
        self.name = name
        self.lw = None
        self.rd = {}


class Sync:
    def __init__(self, nc, es, n_dma_sems=32):
        self.nc = nc
        self.eng = {"pe": nc.tensor, "act": nc.scalar, "dve": nc.vector, "pool": nc.gpsimd, "sp": nc.sync}
        self.sem = {k: es.enter_context(nc.semaphore("sem_" + k)) for k in self.eng}
        self.cnt = {k: 0 for k in self.eng}
        self.dsem = [es.enter_context(nc.semaphore("dsem%d" % i)) for i in range(n_dma_sems)]
        self.dcnt = [0] * n_dma_sems
        self.dnext = 0
        self.dnext_sw = 0
        self.seen = {k: {} for k in self.eng}
        self.dead = False
        self.lazy = {"pe"}
        self.unflushed = {}
        self.last_inst = {}

    def _flush(self, key):
        if self.unflushed.get(key):
            self.last_inst[key].then_inc(self.sem[key], 1)
            self.cnt[key] += 1
            self.unflushed[key] = False

    def _wait(self, e, key, val):
        if self.seen[e].get(key, 0) >= val:
            return
        if isinstance(key, str) and val > self.cnt[key]:
            assert key in self.lazy and val == self.cnt[key] + 1, (key, val, self.cnt[key])
            self._flush(key)
        sem = self.sem[key] if isinstance(key, str) else self.dsem[key]
        self.eng[e].wait_ge(sem, val)
        self.seen[e][key] = val

    def _deps(self, e, reads, writes):
        deps = {}

        def add(k, v):
            if deps.get(k, 0) < v:
                deps[k] = v
        for b in reads:
            if b.lw is not None:
                add(*b.lw)
        for b in writes:
            if b.lw is not None:
                add(*b.lw)
            for k, v in b.rd.items():
                add(k, v)
        for k, v in deps.items():
            if k == e and e == "pe":
                continue
            self._wait(e, k, v)

    def _post(self, ev, reads, writes):
        for b in reads:
            if b.rd.get(ev[0], 0) < ev[1]:
                b.rd[ev[0]] = ev[1]
        for b in writes:
            b.lw = ev
            b.rd = {}

    def op(self, e, fn, reads=(), writes=()):
        if self.dead:
            return None
        if e != "pe":
            pr_ = [b for b in reads if b.name.startswith("bank")]
            if pr_:
                reads = [b for b in reads if not b.name.startswith("bank")]
                writes = list(writes) + pr_
        self._deps(e, reads, writes)
        inst = fn(self.eng[e])
        if e in self.lazy:
            self.last_inst[e] = inst
            self.unflushed[e] = True
            self._post((e, self.cnt[e] + 1), reads, writes)
            return inst
        self.cnt[e] += 1
        inst.then_inc(self.sem[e], 1)
        self._post((e, self.cnt[e]), reads, writes)
        return inst

    def dma(self, e, out, in_, reads=(), writes=()):
        if self.dead:
            return None
        nsw = 8
        if e == "pool":
            k = self.dnext_sw
            self.dnext_sw = (self.dnext_sw + 1) % nsw
        else:
            k = nsw + self.dnext
            self.dnext = (self.dnext + 1) % (len(self.dsem) - nsw)
        if self.dcnt[k] > 0:
            self._wait(e, k, self.dcnt[k])
        self._deps(e, reads, writes)
        inst = self.eng[e].dma_start(out=out, in_=in_)
        self.dcnt[k] += 16
        inst.then_inc(self.dsem[k], 16)
        self._post((k, self.dcnt[k]), reads, writes)
        return inst

    def barrier(self):
        for k in list(self.lazy):
            self._flush(k)
        for e in self.eng:
            for k in self.eng:
                if self.cnt[k]:
                    self._wait(e, k, self.cnt[k])
            for k in range(len(self.dsem)):
                if self.dcnt[k]:
                    self._wait(e, k, self.dcnt[k])

    def drain(self, e="sp"):
        for k in list(self.lazy):
            self._flush(k)
        for k in range(len(self.dsem)):
            if self.dcnt[k]:
                self._wait(e, k, self.dcnt[k])
        for k in self.eng:
            if k != e and self.cnt[k]:
                self._wait(e, k, self.cnt[k])


class T:
    def __init__(self, es, nc, name, shape, dtype):
        self.t = es.enter_context(nc.sbuf_tensor("sb_" + name, shape, dtype))
        self.b = Buf(name)

    def __getitem__(self, idx):
        return self.t[idx]


class _Stop(Exception):
    pass


RR = [1, 1, 1, 1]
ORDER = [0, 1, 2, 3]
NDUMMY = 0


def build(S, dbg=False, stop=None):
    assert S % 512 == 0

    def CP(name):
        if stop == name:
            sy_box[0].dead = True

    sy_box = [None]
    NT = S // 512
    NS = S // 128
    nc = bass.Bass("TRN2", target_bir_lowering=False)

    def din(n, shp, dt=F32):
        return nc.dram_tensor(n, shp, dt, kind="ExternalInput").ap()

    def dscr(n, shp, dt):
        return nc.dram_tensor(n, shp, dt, kind="ExternalOutput" if dbg else "Internal").ap()

    x_d = din("x", [S, D])
    pp_d = din("pp", [128, NPP])
    bc_d = din("bc", [128, NBC])
    brow_d = din("brow", [1, 2048])
    wada_d = din("w_ada", [D, 6 * D])
    win_d = din("w_in", [D, NIN])
    wdu_d = din("w_decay_up", [64, 512])
    wiu_d = din("w_iclr_up", [64, 512])
    wgu_d = din("w_gate_up", [128, 512])
    wor_d = din("w_o_rwkv", [512, D])
    wof_d = din("w_o_fox", [512, D])
    wout_d = din("w_out", [D, D])
    wff1_d = din("w_ff1", [D, 4 * D])
    wff2_d = din("w_ff2", [4 * D, D])
    out_d = nc.dram_tensor("out", [S, D], F32, kind="ExternalOutput").ap()

    hT_d = dscr("hT_s", [128, 8, S], BF16)
    yaT_d = dscr("yaT_s", [128, 4, S], BF16)
    ybT_d = dscr("ybT_s", [512, S], BF16)
    x1_d = dscr("x1_s", [S, D], F32)
    h2T_d = dscr("h2T_s", [128, 8, S], BF16)
    gt_d = dscr("gt_s", [128, 2048], F32)
    B_hT, B_yaT, B_ybT, B_x1, B_h2T, B_gt, B_out = (Buf(n) for n in "hT yaT ybT x1 h2T gt out".split())

    with ExitStack() as es:
        sy = Sync(nc, es)
        sy_box[0] = sy
        OP = sy.op
        DMA = sy.dma

        def SB(scope, name, shape, dt):
            return T(scope, nc, name, shape, dt)

        banks = [es.enter_context(nc.psum_tensor("bank%d" % i, [128, 512], F32)) for i in range(8)]
        bbufs = [Buf("bank%d" % i) for i in range(8)]
        pstate = {"i": 0}

        held = set()

        def PS(hold=False):
            assert len(held) < 8, "all PSUM banks held"
            while True:
                i = pstate["i"]
                pstate["i"] = (i + 1) % 8
                if i not in held:
                    break
            if hold:
                held.add(i)
            return banks[i], bbufs[i]

        def PSREL(bank):
            held.discard(banks.index(bank))

        def MM(out, lhsT, rhs, start, stop, reads, pb):
            OP("pe", lambda e: e.matmul(out, lhsT, rhs, start=start, stop=stop, skip_group_check=True),
               reads=reads, writes=[pb])

        def TR(out, in_, ident, reads, pb):
            OP("pe", lambda e: e.transpose(out, in_, ident), reads=reads, writes=[pb])

        ppt = SB(es, "ppt", [128, NPP], F32)
        ident = SB(es, "ident", [128, 128], BF16)
        modT = SB(es, "modT", [128, 48], F32)
        GS = SB(es, "GS", [128, 32], F32)
        DMA("sp", ppt[:], pp_d, writes=[ppt.b])
        OP("pool", lambda e: e.memset(ident[:], 1.0), writes=[ident.b])
        OP("pool", lambda e: e.affine_select(out=ident[:], in_=ident[:], pattern=[[-1, 128]],
                                             compare_op=ALU.is_equal, fill=0.0, base=0, channel_multiplier=1),
           reads=[ident.b], writes=[ident.b])

        with ExitStack() as s0:
            wada = SB(s0, "wada", [128, 8, 6 * D], BF16)
            wada_b = [Buf("wada%d" % k) for k in range(8)]
            cact = SB(s0, "cact", [128, 8], BF16)
            crep = SB(s0, "crep", [128, 8, 128], BF16)
            onesf = SB(s0, "onesf", [1, 128], F32)
            brow = SB(s0, "brow", [1, 2048], F32)
            gtbc = SB(s0, "gtbc", [128, 2048], F32)
            wadaB_b = [Buf("wadaB%d" % k) for k in range(8)]
            for kc in range(8):
                DMA("pool", wada[:, kc, 0:2 * D], wada_d[kc * 128:(kc + 1) * 128, 0:2 * D], writes=[wada_b[kc]])
            for kc in range(8):
                DMA("pool", wada[:, kc, 2 * D:6 * D], wada_d[kc * 128:(kc + 1) * 128, 2 * D:6 * D], writes=[wadaB_b[kc]])
            DMA("sp", brow[:], brow_d, writes=[brow.b])
            OP("dve", lambda e: e.memset(onesf[:], 1.0), writes=[onesf.b])
            OP("act", lambda e: e.activation(out=cact[:], in_=ppt[:, PP_C:PP_C + 8], func=AF.Silu),
               reads=[ppt.b], writes=[cact.b])
            for kc in range(8):
                OP("dve", lambda e, kc=kc: e.tensor_copy(out=crep[:, kc, :],
                                                         in_=cact[:, kc:kc + 1].broadcast_to([128, 128])),
                   reads=[cact.b], writes=[crep.b])
            pa, pab = PS(hold=True)
            for j in range(16):
                for kc in range(8):
                    MM(pa[:, j:j + 1], wada[:, kc, j * 128:(j + 1) * 128], cact[:, kc:kc + 1],
                       kc == 0, kc == 7, [wada_b[kc], cact.b], pab)
            OP("dve", lambda e: e.tensor_tensor(out=modT[:, 0:16], in0=pa[:, 0:16], in1=ppt[:, PP_BADA:PP_BADA + 16],
                                                op=ALU.add), reads=[pab, ppt.b], writes=[modT.b])
            PSREL(pa)
            OP("dve", lambda e: e.scalar_tensor_tensor(out=GS[:, 0:8], in0=modT[:, 8:16], scalar=1.0,
                                                       in1=ppt[:, PP_G1:PP_G1 + 8], op0=ALU.add, op1=ALU.mult),
               reads=[modT.b, ppt.b], writes=[GS.b])
            OP("dve", lambda e: e.tensor_copy(out=GS[:, 8:16], in_=modT[:, 0:8]), reads=[modT.b], writes=[GS.b])

            def mod_part2():
                GS2 = Buf("GS2")
                pa2_, pa2b_ = PS(hold=True)
                for j in range(16, 48):
                    for kc in range(8):
                        MM(pa2_[:, j:j + 1], wada[:, kc, j * 128:(j + 1) * 128], cact[:, kc:kc + 1],
                           kc == 0, kc == 7, [wadaB_b[kc], cact.b], pa2b_)
                OP("dve", lambda e: e.tensor_tensor(out=modT[:, 16:48], in0=pa2_[:, 16:48], in1=ppt[:, PP_BADA + 16:PP_BADA + 48],
                                                    op=ALU.add), reads=[pa2b_, ppt.b], writes=[modT.b])
                PSREL(pa2_)
                OP("dve", lambda e: e.scalar_tensor_tensor(out=GS[:, 16:24], in0=modT[:, 32:40], scalar=1.0,
                                                           in1=ppt[:, PP_G2:PP_G2 + 8], op0=ALU.add, op1=ALU.mult),
                   reads=[modT.b, ppt.b], writes=[GS.b])
                OP("dve", lambda e: e.tensor_copy(out=GS[:, 24:32], in_=modT[:, 24:32]), reads=[modT.b], writes=[GS.b])
                for part, col0 in enumerate((2 * D, 5 * D)):
                    for half in range(2):
                        pg, pgb = PS(hold=True)
                        for kc in range(8):
                            MM(pg[:, :], crep[:, kc, :], wada[:, kc, col0 + half * 512: col0 + (half + 1) * 512],
                               kc == 0, False, [crep.b, wadaB_b[kc]], pgb)
                        o = part * 1024 + half * 512
                        MM(pg[:, :], onesf[0:1, :], brow[0:1, o:o + 512], False, True, [onesf.b, brow.b], pgb)
                        OP("act", lambda e, pg=pg, o=o: e.copy(out=gtbc[:, o:o + 512], in_=pg[:, :]),
                           reads=[pgb], writes=[gtbc.b])
                        PSREL(pg)
                DMA("sp", gt_d, gtbc[:], reads=[gtbc.b], writes=[B_gt])

            xb = [SB(s0, "xb%d" % i, [128, 4, D], F32) for i in range(2)]
            xn = [SB(s0, "xn%d" % i, [128, 4, D], BF16) for i in range(2)]
            hTs = [SB(s0, "hTs%d" % i, [128, 8, 512], BF16) for i in range(2)]
            junk = SB(s0, "junk0", [128, D], BF16)
            ssq = [SB(s0, "ssq%d" % i, [128, 4], F32) for i in range(2)]
            rstd = [SB(s0, "rstd%d" % i, [128, 4], F32) for i in range(2)]
            def load_xt(i):
                if i < NT:
                    DMA("sp", xb[i % 2][:], x_d[i * 512:(i + 1) * 512, :].rearrange("(s p) d -> p s d", p=128), writes=[xb[i % 2].b])

            xn_b = [[Buf("xn%d_%d" % (i_, s_)) for s_ in range(4)] for i_ in range(2)]
            ht_b = [[Buf("ht%d_%d" % (i_, k_)) for k_ in range(8)] for i_ in range(2)]

            def norm_a(i):
                xt, xnt, sq, rs = xb[i % 2], xn[i % 2], ssq[i % 2], rstd[i % 2]
                for s in range(4):
                    OP("act", lambda e, s=s: e.activation(out=junk[:], in_=xt[:, s, :], func=AF.Square, accum_out=sq[:, s:s + 1]),
                       reads=[xt.b], writes=[junk.b, sq.b])
                OP("dve", lambda e: e.tensor_scalar(out=rs[:], in0=sq[:], scalar1=1.0 / D, scalar2=NORM_EPS,
                                                    op0=ALU.mult, op1=ALU.add), reads=[sq.b], writes=[rs.b])
                OP("act", lambda e: e.activation(out=rs[:], in_=rs[:], func=AF.Sqrt), reads=[rs.b], writes=[rs.b])
                OP("dve", lambda e: e.reciprocal(out=rs[:], in_=rs[:]), reads=[rs.b], writes=[rs.b])
                for s in range(4):
                    if s % 2 == 0:
                        OP("dve", lambda e, s=s: e.tensor_scalar(out=xnt[:, s, :], in0=xt[:, s, :],
                                                                 scalar1=rs[:, s:s + 1], scalar2=None, op0=ALU.mult),
                           reads=[xt.b, rs.b], writes=[xn_b[i % 2][s]])
                    else:
                        OP("act", lambda e, s=s: e.activation(out=xnt[:, s, :], in_=xt[:, s, :], func=AF.Copy, scale=rs[:, s:s + 1]),
                           reads=[xt.b, rs.b], writes=[xn_b[i % 2][s]])

            def norm_b(i):
                xnt, ht = xn[i % 2], hTs[i % 2]
                for kc in range(8):
                    p, pb = PS()
                    pv = p.bitcast(BF16)
                    for s in range(4):
                        TR(pv[:, s * 128:(s + 1) * 128], xnt[:, s, kc * 128:(kc + 1) * 128], ident[:], [xn_b[i % 2][s], ident.b], pb)
                    if kc % 2 == 0:
                        OP("dve", lambda e, kc=kc, pv=pv: e.tensor_scalar(
                            out=ht[:, kc, :], in0=pv[:, 0:512], scalar1=GS[:, kc:kc + 1], scalar2=GS[:, 8 + kc:9 + kc],
                            op0=ALU.mult, op1=ALU.add), reads=[pb, GS.b], writes=[ht_b[i % 2][kc]])
                    else:
                        OP("act", lambda e, kc=kc, pv=pv: e.activation(
                            out=ht[:, kc, :], in_=pv[:, 0:512], func=AF.Identity, scale=GS[:, kc:kc + 1],
                            bias=GS[:, 8 + kc:9 + kc]), reads=[pb, GS.b], writes=[ht_b[i % 2][kc]])
                DMA("sp", hT_d[:, :, i * 512:(i + 1) * 512], ht[:], reads=ht_b[i % 2], writes=[B_hT])

            load_xt(0)
            load_xt(1)
            norm_a(0)
            for i in range(NT):
                if i + 1 < NT:
                    norm_a(i + 1)
                load_xt(i + 2)
                norm_b(i)
            mod_part2()


        sy.barrier()
        try:
          with ExitStack() as sr:
              WA = SB(sr, "WA", [128, 8, NRW], BF16)
              WB = SB(sr, "WB", [128, 8, NRW], BF16)
              wdu = SB(sr, "wdu", [64, 512], BF16)
              wiu = SB(sr, "wiu", [128, 512], BF16)
              wgu = SB(sr, "wgu", [128, 512], BF16)
              lnw = SB(sr, "lnw", [128, 1024], F32)
              DMA("pool", wdu[:], wdu_d, writes=[wdu.b])
              DMA("pool", wiu[64:128, :], wiu_d, writes=[wiu.b])
              DMA("pool", wgu[:], wgu_d, writes=[wgu.b])
              DMA("sp", lnw[:], bc_d[:, BC_LW:BC_LW + 1024], writes=[lnw.b])
              with ExitStack() as sw:
                  mub = SB(sw, "mub", [128, NRW], F32)
                  omub = SB(sw, "omub", [128, NRW], F32)
                  wst = [SB(sw, "wst%d" % i, [128, NRW], F32) for i in range(2)]
                  DMA("sp", mub[:], bc_d[:, BC_MU:BC_MU + NRW], writes=[mub.b])
                  OP("dve", lambda e: e.tensor_scalar(out=omub[:], in0=mub[:], scalar1=-1.0, scalar2=1.0,
                                                      op0=ALU.mult, op1=ALU.add), reads=[mub.b], writes=[omub.b])
                  segs = [(0, 512, 0), (576, 1088, 512), (1088, 1600, 1024), (512, 576, 1536), (1600, 1664, 1600),
                          (1664, 1792, 1664)]
                  seg_b = [[Buf("wseg%d_%d" % (i_, j_)) for j_ in range(len(segs))] for i_ in range(2)]
                  for kc in range(8):
                      w = wst[kc % 2]
                      for j_, (a, b_, o) in enumerate(segs):
                          DMA("sp", w[:, o:o + (b_ - a)], win_d[kc * 128:(kc + 1) * 128, a:b_], reads=[], writes=[seg_b[kc % 2][j_]])
                      OP("dve", lambda e, kc=kc, w=w: e.tensor_tensor(out=WB[:, kc, :], in0=w[:], in1=mub[:], op=ALU.mult),
                         reads=seg_b[kc % 2] + [mub.b], writes=[WB.b])
                      OP("dve", lambda e, kc=kc, w=w: e.tensor_tensor(out=WA[:, kc, :], in0=w[:], in1=omub[:], op=ALU.mult),
                         reads=seg_b[kc % 2] + [omub.b], writes=[WA.b])
              sy.barrier()
              CP('w')

              blk1 = SB(sr, "blk1", [128, 128], F32)
              hsel = SB(sr, "hsel", [128, 2], BF16)
              mskX = SB(sr, "mskX", [128, 4, 2, 64], F32)
              mskL = SB(sr, "mskL", [128, 8, 64], F32)
              i64 = SB(sr, "i64", [128, 64], F32)
              scm = SB(sr, "scm", [128, 512], F32)
              omix = SB(sr, "omix", [128, 4], F32)
              OP("pool", lambda e: e.memset(blk1[:], 0.0), writes=[blk1.b])
              OP("pool", lambda e: e.memset(blk1[0:64, 0:64], 1.0), writes=[blk1.b])
              OP("pool", lambda e: e.memset(blk1[64:128, 64:128], 1.0), writes=[blk1.b])
              OP("pool", lambda e: e.memset(hsel[:], 0.0), writes=[hsel.b])
              OP("pool", lambda e: e.memset(hsel[0:64, 0:1], 1.0), writes=[hsel.b])
              OP("pool", lambda e: e.memset(hsel[64:128, 1:2], 1.0), writes=[hsel.b])
              OP("pool", lambda e: e.memset(mskX[:], 1.0), writes=[mskX.b])
              OP("pool", lambda e: e.memset(mskL[:], 1.0), writes=[mskL.b])
              OP("pool", lambda e: e.memset(i64[:], 1.0), writes=[i64.b])
              for hf in range(2):
                  ps_ = slice(hf * 64, (hf + 1) * 64)
                  OP("pool", lambda e, ps_=ps_: e.affine_select(out=mskX[ps_], in_=mskX[ps_], pattern=[[0, 4], [1, 2], [1, 64]],
                                                                compare_op=ALU.is_gt, fill=0.0, base=0, channel_multiplier=-1),
                     reads=[mskX.b], writes=[mskX.b])
                  OP("pool", lambda e, ps_=ps_: e.affine_select(out=mskL[ps_], in_=mskL[ps_], pattern=[[0, 8], [-1, 64]],
                                                                compare_op=ALU.is_gt, fill=0.0, base=0, channel_multiplier=1),
                     reads=[mskL.b], writes=[mskL.b])
                  OP("pool", lambda e, ps_=ps_: e.affine_select(out=i64[ps_], in_=i64[ps_], pattern=[[-1, 64]],
                                                                compare_op=ALU.is_equal, fill=0.0, base=0, channel_multiplier=1),
                     reads=[i64.b], writes=[i64.b])
              OP("pool", lambda e: e.memset(scm[:], 1.0), writes=[scm.b])
              OP("pool", lambda e: e.memset(scm[:].rearrange("p (a b) -> p a b", b=64)[:, :, 0:1], 0.0), writes=[scm.b])
              OP("dve", lambda e: e.tensor_scalar(out=omix[:], in0=ppt[:, PP_MIX:PP_MIX + 4], scalar1=-1.0, scalar2=1.0,
                                                  op0=ALU.mult, op1=ALU.add), reads=[ppt.b], writes=[omix.b])

              CP('c')

              def bc4(col0):
                  return ppt[:, col0:col0 + 4].unsqueeze(2).broadcast_to([128, 4, 128])

              Hs = SB(sr, "Hs", [128, 4, 64], F32)
              Hb2 = [SB(sr, "Hb%d" % i_, [128, 4, 64], BF16) for i_ in range(2)]
              OP("dve", lambda e: e.memset(Hs[:], 0.0), writes=[Hs.b])
              OP("dve", lambda e: e.memset(Hb2[0][:], 0.0), writes=[Hb2[0].b])

              def DB(name, shape, dt):
                  return [SB(sr, "%s_%d" % (name, i_), shape, dt) for i_ in range(2)]
              hcs = DB("hc", [128, 8, 128], BF16)
              hps = DB("hp", [128, 8, 128], BF16)
              rk2 = DB("rk", [128, 8, 128], F32)
              tw2 = DB("tw", [64, 128], BF16)
              adb2 = DB("adb", [128, 128], BF16)
              sgd2 = DB("sgd", [128, 128], BF16)
              Vf2 = [SB(sr, "Vf_%d" % i_, [128, 512], F32) for i_ in range(4)]
              Vb2 = [SB(sr, "Vb_%d" % i_, [128, 512], BF16) for i_ in range(4)]

              def TB(name, shape, dt):
                  return [SB(sr, "%s_%d" % (name, i_), shape, dt) for i_ in range(3)]
              sw_ = SB(sr, "sw_", [128, 512], F32)
              aic = SB(sr, "aic", [128, 4, 128], F32)
              gsb2 = TB("gsb", [128, 512], F32)
              ld = SB(sr, "ld", [128, 512], F32)
              lP = SB(sr, "lP", [128, 512], F32)
              lPx = SB(sr, "lPx", [128, 512], F32)
              Pc2 = TB("Pc", [128, 4, 128], F32)
              Pinv = SB(sr, "Pinv", [128, 4, 128], F32)
              Pex = SB(sr, "Pex", [128, 4, 128], F32)
              ksc = SB(sr, "ksc", [128, 4, 128], F32)
              ksq = SB(sr, "ksq", [128, 4, 128], F32)
              rn = SB(sr, "rn", [128, 4, 128], F32)
              kk = SB(sr, "kk", [128, 4, 128], F32)
              t1 = SB(sr, "t1", [128, 4, 128], F32)
              kmod = SB(sr, "kmod", [128, 4, 128], F32)
              t2 = SB(sr, "t2", [128, 4, 128], F32)
              AR2 = TB("AR", [128, 4, 2, 2, 64], BF16)
              bT = SB(sr, "bT", [128, 4, 128], BF16)
              kT = SB(sr, "kT", [128, 4, 128], BF16)
              rkb2 = TB("rkb", [128, 4, 128], BF16)
              bt2 = TB("bt", [128, 512], BF16)
              kt2 = TB("kt", [128, 512], BF16)
              MB2 = TB("MB", [128, 8, 2, 64], BF16)
              MK2 = TB("MK", [128, 8, 2, 64], BF16)
              Acur2 = [[SB(sr, "Acur%d_%d" % (j_, i), [128, 8, 64], BF16) for i in range(2)] for j_ in range(2)]
              Mcur2 = [[SB(sr, "Mcur%d_%d" % (j_, i), [128, 8, 64], BF16) for i in range(2)] for j_ in range(2)]
              Xc2 = [[SB(sr, "Xc%d_%d" % (j_, i), [128, 8, 64], BF16) for i in range(2)] for j_ in range(2)]
              W0sb2 = TB("W0sb", [128, 512], F32)
              Wsb = SB(sr, "Wsb", [128, 512], BF16)
              Usb = SB(sr, "Usb", [128, 512], BF16)
              Htmp = SB(sr, "Htmp", [128, 4, 64], F32)
              yf = SB(sr, "yf", [128, 8, 64], F32)
              ysq = SB(sr, "ysq", [128, 8, 64], F32)
              st = SB(sr, "st", [128, 32], F32)
              bs = SB(sr, "bs", [128, 8], F32)
              yg = SB(sr, "yg", [128, 512], BF16)
              yaTs = [SB(sr, "yaTs%d" % i, [128, 4, 128], BF16) for i in range(2)]

              def hsl(h):
                  return h // 2, slice((h % 2) * 64, (h % 2) * 64 + 64)
              fl = lambda t_: t_[:].rearrange("p a t -> p (a t)")
              v4 = lambda t_: t_[:].rearrange("p a (c t) -> p a c t", c=2)
              par = lambda ap_, e_: ap_.rearrange("p (a e v) -> p e a v", e=2, v=64)[:, e_]
              b8 = lambda a_: a_.unsqueeze(2).broadcast_to([128, 8, 64])
              Xfin = {}

              def phaseP(n):
                  d = n % 2
                  hh, hold, hp = hcs[d], hcs[1 - d], hps[d]
                  rk, tw, adb, sgd = rk2[d], tw2[d], adb2[d], sgd2[d]
                  Vf, Vb = Vf2[n % 4], Vb2[n % 4]
                  DMA("sp", hh[:], hT_d[:, :, n * 128:(n + 1) * 128], writes=[hh.b])
                  if n == 0:
                      OP("pool", lambda e: e.memset(hp[:, :, 0:1], 0.0), writes=[hp.b])
                      DMA("sp", hp[:, :, 1:128], hT_d[:, :, 0:127], writes=[hp.b])
                  else:
                      DMA("sp", hp[:], hT_d[:, :, n * 128 - 1:(n + 1) * 128 - 1], writes=[hp.b])
                  yield

                  def fproj(pt, ptb, slot, col0):
                      for i_ in range(16):
                          kc, sh = i_ % 8, i_ // 8
                          W = WA if sh == 0 else WB
                          hx = hh if sh == 0 else hp
                          MM(pt[:, slot * 128:(slot + 1) * 128], W[:, kc, col0:col0 + 128], hx[:, kc, :], i_ == 0, i_ == 15,
                             [W.b, hx.b], ptb)
                  pr, prb = PS(hold=True)
                  for c4 in range(4):
                      fproj(pr, prb, c4, c4 * 128)
                      if c4 % 2 == 1:
                          yield
                  OP("act", lambda e: e.copy(out=rk[:, 0:4, :], in_=pr[:, :].rearrange("p (a t) -> p a t", a=4)), reads=[prb], writes=[rk.b])
                  PSREL(pr)
                  pk, pkb = PS(hold=True)
                  for c4 in range(4):
                      fproj(pk, pkb, c4, 512 + c4 * 128)
                      if c4 % 2 == 1:
                          yield
                  OP("dve", lambda e: e.tensor_scalar(out=rk[:, 4:8, :], in0=pk[:, :].rearrange("p (a t) -> p a t", a=4), scalar1=1.0,
                                                      scalar2=None, op0=ALU.mult), reads=[pkb], writes=[rk.b])
                  PSREL(pk)
                  pw, pwb = PS(hold=True)
                  fproj(pw, pwb, 0, 1536)
                  yield
                  fproj(pw, pwb, 1, 1664)
                  OP("act", lambda e: e.activation(out=tw[:], in_=pw[0:64, 0:128], func=AF.Tanh), reads=[pwb], writes=[tw.b])
                  OP("dve", lambda e: e.tensor_scalar(out=adb[64:128, :], in0=pw[64:128, 0:128], scalar1=1.0, scalar2=None, op0=ALU.mult),
                     reads=[pwb], writes=[adb.b])
                  OP("act", lambda e: e.activation(out=sgd[:], in_=pw[:, 128:256], func=AF.Sigmoid), reads=[pwb], writes=[sgd.b])
                  PSREL(pw)
                  yield
                  pv_, pvb = PS(hold=True)
                  for i_ in range(16):
                      kc, sh = i_ % 8, i_ // 8
                      W = WA if sh == 0 else WB
                      hx = hh if sh == 0 else hp
                      MM(pv_[:, :], hx[:, kc, :], W[:, kc, 1024:1536], i_ == 0, i_ == 15, [W.b, hx.b], pvb)
                      if i_ == 7:
                          yield
                  OP("act", lambda e: e.copy(out=Vf[:], in_=pv_[:, :]), reads=[pvb], writes=[Vf.b])
                  OP("dve", lambda e: e.tensor_scalar(out=Vb[:], in0=pv_[:, :], scalar1=1.0, scalar2=None, op0=ALU.mult), reads=[pvb], writes=[Vb.b])
                  PSREL(pv_)

              def phaseA(n):
                  d = n % 2
                  t3 = n % 3
                  rk, tw, adb, sgd = rk2[d], tw2[d], adb2[d], sgd2[d]
                  Vf, Vb = Vf2[n % 4], Vb2[n % 4]
                  gsb, Pc, AR, rkb, bt, kt, MB, MK, W0sb = (gsb2[t3], Pc2[t3], AR2[t3], rkb2[t3], bt2[t3],
                                                           kt2[t3], MB2[t3], MK2[t3], W0sb2[t3])
                  Xc = Xc2[d]
                  Acur, Mcur = Acur2[d], Mcur2[d]
                  r4 = rk[:, 0:4, :]
                  k4 = rk[:, 4:8, :]
                  pz, pzb = PS(hold=True)
                  pa_, pab_ = PS(hold=True)
                  pg_, pgb_ = PS(hold=True)
                  for c4 in range(4):
                      MM(pz[:, c4 * 128:(c4 + 1) * 128], wdu[0:64, c4 * 128:(c4 + 1) * 128], tw[0:64, :], True, True, [wdu.b, tw.b], pzb)
                  for c4 in range(4):
                      MM(pa_[:, c4 * 128:(c4 + 1) * 128], wiu[64:128, c4 * 128:(c4 + 1) * 128], adb[64:128, :], True, True, [wiu.b, adb.b], pab_)
                  MM(pg_[:, :], sgd[:, :], wgu[:, :], True, True, [sgd.b, wgu.b], pgb_)
                  for c4 in range(4):
                      OP("act", lambda e, c4=c4: e.activation(out=sw_[:, c4 * 128:(c4 + 1) * 128], in_=pz[:, c4 * 128:(c4 + 1) * 128],
                                                              func=AF.Sigmoid, bias=ppt[:, PP_DB + c4:PP_DB + c4 + 1]),
                         reads=[pzb, ppt.b], writes=[sw_.b])
                  PSREL(pz)
                  for c4 in range(4):
                      OP("act", lambda e, c4=c4: e.activation(out=aic[:, c4, :], in_=pa_[:, c4 * 128:(c4 + 1) * 128],
                                                              func=AF.Sigmoid, bias=ppt[:, PP_IB + c4:PP_IB + c4 + 1]),
                         reads=[pab_, ppt.b], writes=[aic.b])
                  PSREL(pa_)
                  OP("act", lambda e: e.copy(out=gsb[:], in_=pg_[:, :]), reads=[pgb_], writes=[gsb.b])
                  PSREL(pg_)
                  yield
                  OP("dve", lambda e: e.tensor_tensor(out=ksc[:], in0=k4, in1=bc4(PP_KS), op=ALU.mult), reads=[rk.b, ppt.b], writes=[ksc.b])
                  OP("act", lambda e: e.activation(out=fl(ksq), in_=fl(ksc), func=AF.Square), reads=[ksc.b], writes=[ksq.b])
                  for c4 in range(4):
                      OP("act", lambda e, c4=c4: e.activation(out=t1[:, c4, :], in_=aic[:, c4, :], func=AF.Identity,
                                                              scale=ppt[:, PP_MIX + c4:PP_MIX + c4 + 1], bias=omix[:, c4:c4 + 1]),
                         reads=[aic.b, ppt.b, omix.b], writes=[t1.b])
                  OP("dve", lambda e: e.tensor_tensor(out=kmod[:], in0=k4, in1=t1[:], op=ALU.mult), reads=[rk.b, t1.b], writes=[kmod.b])
                  yield
                  OP("dve", lambda e: e.tensor_scalar(out=ld[:], in0=sw_[:], scalar1=-EXPM05, scalar2=None, op0=ALU.mult), reads=[sw_.b], writes=[ld.b])
                  OP("dve", lambda e: e.tensor_tensor_scan(out=lP[:], data0=scm[:], data1=ld[:], initial=0.0, op0=ALU.mult, op1=ALU.add),
                     reads=[scm.b, ld.b], writes=[lP.b])
                  OP("dve", lambda e: e.tensor_tensor(out=lPx[:], in0=lP[:], in1=ld[:], op=ALU.subtract), reads=[lP.b, ld.b], writes=[lPx.b])
                  pq, pqb = PS(hold=True)
                  MM(pq[:, :], blk1[:, :], fl(ksq), True, True, [blk1.b, ksq.b], pqb)
                  OP("act", lambda e: e.activation(out=fl(Pc), in_=lP[:], func=AF.Exp), reads=[lP.b], writes=[Pc.b])
                  OP("act", lambda e: e.activation(out=fl(Pinv), in_=lP[:], func=AF.Exp, scale=-1.0), reads=[lP.b], writes=[Pinv.b])
                  OP("act", lambda e: e.activation(out=fl(Pex), in_=lPx[:], func=AF.Exp), reads=[lPx.b], writes=[Pex.b])
                  OP("act", lambda e: e.activation(out=fl(rn), in_=pq[:, :], func=AF.Ln), reads=[pqb], writes=[rn.b])
                  PSREL(pq)
                  OP("act", lambda e: e.activation(out=fl(rn), in_=fl(rn), func=AF.Exp, scale=-0.5), reads=[rn.b], writes=[rn.b])
                  OP("dve", lambda e: e.tensor_tensor(out=kk[:], in0=ksc[:], in1=rn[:], op=ALU.mult), reads=[ksc.b, rn.b], writes=[kk.b])
                  yield
                  OP("dve", lambda e: e.scalar_tensor_tensor(out=AR[:, :, :, 0, :], in0=v4(kk), scalar=-1.0, in1=v4(Pex), op0=ALU.mult, op1=ALU.mult),
                     reads=[kk.b, Pex.b], writes=[AR.b])
                  OP("pool", lambda e: e.tensor_tensor(out=AR[:, :, :, 1, :], in0=r4.rearrange("p a (c t) -> p a c t", c=2), in1=v4(Pc), op=ALU.mult),
                     reads=[rk.b, Pc.b, AR.b], writes=[AR.b])
                  OP("dve", lambda e: e.tensor_tensor(out=t2[:], in0=kk[:], in1=aic[:], op=ALU.mult), reads=[kk.b, aic.b], writes=[t2.b])
                  OP("dve", lambda e: e.tensor_tensor(out=bT[:], in0=t2[:], in1=Pinv[:], op=ALU.mult), reads=[t2.b, Pinv.b], writes=[bT.b])
                  OP("pool", lambda e: e.tensor_tensor(out=kT[:], in0=kmod[:], in1=Pinv[:], op=ALU.mult), reads=[kmod.b, Pinv.b], writes=[kT.b])
                  OP("pool", lambda e: e.tensor_tensor(out=t1[:], in0=r4, in1=kmod[:], op=ALU.mult), reads=[rk.b, kmod.b, t1.b], writes=[t1.b])
                  OP("pool", lambda e: e.tensor_tensor(out=rkb[:], in0=t1[:], in1=bc4(PP_RB), op=ALU.mult), reads=[t1.b, ppt.b], writes=[rkb.b])
                  yield
                  yield
                  yield
                  ptb_, ptbb = PS(hold=True)
                  ptk_, ptkb = PS(hold=True)
                  ptbv = ptb_.bitcast(BF16)
                  ptkv = ptk_.bitcast(BF16)
                  for c4 in range(4):
                      TR(ptbv[:, c4 * 128:(c4 + 1) * 128], bT[:, c4, :], ident[:], [bT.b, ident.b], ptbb)
                  for c4 in range(4):
                      TR(ptkv[:, c4 * 128:(c4 + 1) * 128], kT[:, c4, :], ident[:], [kT.b, ident.b], ptkb)
                  OP("dve", lambda e: e.tensor_scalar(out=bt[:], in0=ptbv[:, 0:512], scalar1=1.0, scalar2=None, op0=ALU.mult), reads=[ptbb], writes=[bt.b])
                  OP("act", lambda e: e.copy(out=kt[:], in_=ptkv[:, 0:512]), reads=[ptkb], writes=[kt.b])
                  PSREL(ptb_)
                  PSREL(ptk_)
                  yield
                  mX3 = mskX[:].rearrange("p a k t -> p a (k t)")
                  A0, M0, X0 = Acur[0], Mcur[0], Xc[0]
                  for e_ in range(2):
                      hs = slice(e_ * 64, e_ * 64 + 64)
                      px, pxb = PS(hold=True)
                      py, pyb = PS(hold=True)
                      pA, pAb = PS(hold=True)
                      for pair in range(4):
                          for c in range(2):
                              cs = slice(c * 64, (c + 1) * 64)
                              mov = AR[hs, pair, c, :, :].rearrange("p k t -> p (k t)")
                              MM(px[cs, pair * 128:(pair + 1) * 128], bT[hs, pair, cs], mov, True, True, [bT.b, AR.b], pxb)
                              MM(py[cs, pair * 128:(pair + 1) * 128], kT[hs, pair, cs], mov, True, True, [kT.b, AR.b], pyb)
                              MM(pA[cs, pair * 64:(pair + 1) * 64], AR[hs, pair, c, 0, :], bT[hs, pair, cs], True, True, [bT.b, AR.b], pAb)
                      OP("dve", lambda e: e.tensor_tensor(out=MB[:].rearrange("p (a e) k t -> p e a (k t)", e=2)[:, e_],
                                                          in0=px[:, :].rearrange("p (a x) -> p a x", a=4), in1=mX3, op=ALU.mult),
                         reads=[pxb, mskX.b], writes=[MB.b])
                      OP("dve", lambda e: e.tensor_tensor(out=MK[:].rearrange("p (a e) k t -> p e a (k t)", e=2)[:, e_],
                                                          in0=py[:, :].rearrange("p (a x) -> p a x", a=4), in1=mX3, op=ALU.mult),
                         reads=[pyb, mskX.b], writes=[MK.b])
                      OP("dve", lambda e: e.tensor_tensor(out=A0[:].rearrange("p (a e) t -> p e a t", e=2)[:, e_],
                                                          in0=pA[:, 0:256].rearrange("p (a t) -> p a t", a=4), in1=mskL[:, 0:4, :], op=ALU.mult),
                         reads=[pAb, mskL.b], writes=[A0.b])
                      PSREL(px)
                      PSREL(py)
                      PSREL(pA)
                      yield
                  OP("dve", lambda e: e.tensor_tensor(out=X0[:], in0=MB[:, :, 0, :], in1=i64[:].unsqueeze(1).broadcast_to([128, 8, 64]), op=ALU.add),
                     reads=[MB.b, i64.b], writes=[X0.b])
                  pW0, pW0b = PS(hold=True)
                  for c in range(2):
                      cs = slice(c * 64, (c + 1) * 64)
                      for h in range(8):
                          MM(pW0[cs, h * 64:(h + 1) * 64], MK[cs, h, 0, :], Vb[cs, h * 64:(h + 1) * 64], True, True, [MK.b, Vb.b], pW0b)
                  OP("act", lambda e: e.copy(out=W0sb[:], in_=pW0[:, :]), reads=[pW0b], writes=[W0sb.b])
                  PSREL(pW0)


              def phaseA2(n):
                  d = n % 2
                  MB = MB2[n % 3]
                  Xc = Xc2[d]
                  Acur, Mcur = Acur2[d], Mcur2[d]
                  ci = 0
                  xi = 0
                  for lvl in range(5):
                      Ac, Mc = Acur[ci], Mcur[ci]
                      An, Mn = Acur[1 - ci], Mcur[1 - ci]
                      if lvl == 0:
                          class _MV:
                              b = MB.b

                              def __getitem__(self, idx):
                                  return MB[idx[0], idx[1], 0, :]
                          Mc = _MV()
                      p2, p2b = PS(hold=True)
                      for h in range(8):
                          for c in range(2):
                              cs = slice(c * 64, (c + 1) * 64)
                              MM(p2[cs, h * 64:(h + 1) * 64], Mc[cs, h, :], Ac[cs, h, :], True, True, [Mc.b, Ac.b], p2b)
                      OP("act", lambda e: e.copy(out=An[:].rearrange("p a t -> p (a t)"), in_=p2[:, :]), reads=[p2b], writes=[An.b])
                      PSREL(p2)
                      if lvl < 4:
                          p3, p3b = PS(hold=True)
                          for h in range(8):
                              for c in range(2):
                                  cs = slice(c * 64, (c + 1) * 64)
                                  MM(p3[cs, h * 64:(h + 1) * 64], Ac[cs, h, :], Mc[cs, h, :], True, True, [Mc.b, Ac.b], p3b)
                          OP("dve", lambda e: e.tensor_scalar(out=Mn[:].rearrange("p a t -> p (a t)"), in0=p3[:, :], scalar1=1.0, scalar2=None,
                                                              op0=ALU.mult), reads=[p3b], writes=[Mn.b])
                          PSREL(p3)
                      yield
                      Xo, Xn = Xc[xi], Xc[1 - xi]
                      p4, p4b = PS(hold=True)
                      for h in range(8):
                          for c in range(2):
                              cs = slice(c * 64, (c + 1) * 64)
                              MM(p4[cs, h * 64:(h + 1) * 64], An[cs, h, :], Xo[cs, h, :], True, True, [An.b, Xo.b], p4b)
                      OP("dve", lambda e: e.tensor_tensor(out=Xn[:].rearrange("p a t -> p (a t)"), in0=p4[:, :],
                                                          in1=Xo[:].rearrange("p a t -> p (a t)"), op=ALU.add), reads=[p4b, Xo.b], writes=[Xn.b])
                      PSREL(p4)
                      ci = 1 - ci
                      xi = 1 - xi
                      yield
                  Xfin[n] = Xc[xi]

              def phaseB(n):
                  d = n % 2
                  t3 = n % 3
                  Vf, Vb, gsb, Pc, AR, rkb, bt, kt, MB, MK, W0sb = (Vf2[n % 4], Vb2[n % 4], gsb2[t3], Pc2[t3], AR2[t3], rkb2[t3], bt2[t3],
                                                                   kt2[t3], MB2[t3], MK2[t3], W0sb2[t3])
                  X = Xfin.pop(n)
                  yf2 = yf[:].rearrange("p a v -> p (a v)")
                  for c in range(2):
                      cs = slice(c * 64, (c + 1) * 64)
                      Hb = Hb2[c]
                      Hbn = Hb2[1 - c]
                      pWe = [PS(hold=True), PS(hold=True)]
                      for e_ in range(2):
                          hs = slice(e_ * 64, e_ * 64 + 64)
                          for pair in range(4):
                              MM(pWe[e_][0][cs, pair * 64:(pair + 1) * 64], AR[hs, pair, c, 0, :], Hb[hs, pair, :], True, True,
                                 [AR.b, Hb.b], pWe[e_][1])
                      for e_ in range(2):
                          OP("dve", lambda e, e_=e_: e.tensor_tensor(
                              out=par(Wsb[cs, :], e_), in0=pWe[e_][0][cs, 0:256].rearrange("p (a v) -> p a v", a=4),
                              in1=par(W0sb[cs, :], e_), op=ALU.add), reads=[pWe[e_][1], W0sb.b], writes=[Wsb.b])
                          PSREL(pWe[e_][0])
                      yield
                      pU, pUb = PS(hold=True)
                      for h in range(8):
                          MM(pU[cs, h * 64:(h + 1) * 64], X[cs, h, :], Wsb[cs, h * 64:(h + 1) * 64], True, True, [X.b, Wsb.b], pUb)
                      OP("act", lambda e: e.copy(out=Usb[cs, :], in_=pU[cs, :]), reads=[pUb], writes=[Usb.b])
                      PSREL(pU)
                      yield
                      pH, pHb = PS(hold=True)
                      for h in range(8):
                          pair, hs = hsl(h)
                          o_ = pH[hs, pair * 64:(pair + 1) * 64]
                          MM(o_, kt[cs, h * 64:(h + 1) * 64], Vb[cs, h * 64:(h + 1) * 64], True, False, [kt.b, Vb.b], pHb)
                          MM(o_, bt[cs, h * 64:(h + 1) * 64], Usb[cs, h * 64:(h + 1) * 64], False, True, [bt.b, Usb.b], pHb)
                      OP("dve", lambda e: e.tensor_tensor(out=Htmp[:].rearrange("p a v -> p (a v)"), in0=pH[:, 0:256],
                                                          in1=Hs[:].rearrange("p a v -> p (a v)"), op=ALU.add), reads=[pHb, Hs.b], writes=[Htmp.b])
                      PSREL(pH)
                      pcb = Pc[:].rearrange("p a (c t) -> p a c t", c=2)[:, :, c, 63:64].broadcast_to([128, 4, 64])
                      OP("dve", lambda e: e.tensor_tensor(out=Hs[:], in0=Htmp[:], in1=pcb, op=ALU.mult), reads=[Htmp.b, Pc.b], writes=[Hs.b])
                      OP("act", lambda e: e.copy(out=Hbn[:], in_=Hs[:]), reads=[Hs.b], writes=[Hbn.b])
                      pYe = [PS(hold=True), PS(hold=True)]
                      for e_ in range(2):
                          hs = slice(e_ * 64, e_ * 64 + 64)
                          for pair in range(4):
                              MM(pYe[e_][0][cs, pair * 64:(pair + 1) * 64], AR[hs, pair, c, 1, :], Hb[hs, pair, :], True, True,
                                 [AR.b, Hb.b], pYe[e_][1])
                      pYc, pYcb = PS(hold=True)
                      for h in range(8):
                          o_ = pYc[cs, h * 64:(h + 1) * 64]
                          MM(o_, MK[cs, h, 1, :], Vb[cs, h * 64:(h + 1) * 64], True, False, [MK.b, Vb.b], pYcb)
                          MM(o_, MB[cs, h, 1, :], Usb[cs, h * 64:(h + 1) * 64], False, True, [MB.b, Usb.b], pYcb)
                      OP("act", lambda e: e.copy(out=yf2[cs, :], in_=pYc[cs, :]), reads=[pYcb], writes=[yf.b])
                      PSREL(pYc)
                      for e_ in range(2):
                          OP("dve", lambda e, e_=e_: e.tensor_tensor(
                              out=par(yf2[cs, :], e_), in0=pYe[e_][0][cs, 0:256].rearrange("p (a v) -> p a v", a=4),
                              in1=par(yf2[cs, :], e_), op=ALU.add), reads=[pYe[e_][1], yf.b], writes=[yf.b])
                          PSREL(pYe[e_][0])
                      yield
                  OP("dve", lambda e: e.tensor_reduce(out=st[:, 0:8], in_=yf[:], axis=AX.X, op=ALU.add), reads=[yf.b], writes=[st.b])
                  OP("act", lambda e: e.activation(out=ysq[:].rearrange("p a v -> p (a v)"), in_=yf2, func=AF.Square), reads=[yf.b], writes=[ysq.b])
                  OP("dve", lambda e: e.tensor_reduce(out=st[:, 8:16], in_=ysq[:], axis=AX.X, op=ALU.add), reads=[ysq.b], writes=[st.b])
                  OP("dve", lambda e: e.tensor_scalar(out=st[:, 16:24], in0=st[:, 0:8], scalar1=1.0 / 64, scalar2=None, op0=ALU.mult),
                     reads=[st.b], writes=[st.b])
                  OP("dve", lambda e: e.tensor_tensor(out=st[:, 24:32], in0=st[:, 16:24], in1=st[:, 16:24], op=ALU.mult), reads=[st.b], writes=[st.b])
                  OP("dve", lambda e: e.scalar_tensor_tensor(out=st[:, 24:32], in0=st[:, 8:16], scalar=1.0 / 64, in1=st[:, 24:32],
                                                             op0=ALU.mult, op1=ALU.subtract), reads=[st.b], writes=[st.b])
                  OP("dve", lambda e: e.tensor_scalar(out=st[:, 24:32], in0=st[:, 24:32], scalar1=GN_EPS, scalar2=None, op0=ALU.add),
                     reads=[st.b], writes=[st.b])
                  OP("act", lambda e: e.activation(out=st[:, 24:32], in_=st[:, 24:32], func=AF.Ln), reads=[st.b], writes=[st.b])
                  OP("act", lambda e: e.activation(out=st[:, 24:32], in_=st[:, 24:32], func=AF.Exp, scale=-0.5), reads=[st.b], writes=[st.b])
                  yield
                  OP("dve", lambda e: e.tensor_tensor(out=yf[:], in0=yf[:], in1=b8(st[:, 16:24]), op=ALU.subtract), reads=[yf.b, st.b], writes=[yf.b])
                  OP("dve", lambda e: e.tensor_tensor(out=yf[:], in0=yf[:], in1=b8(st[:, 24:32]), op=ALU.mult), reads=[yf.b, st.b], writes=[yf.b])
                  OP("pool", lambda e: e.tensor_tensor(out=yf2, in0=yf2, in1=lnw[:, 0:512], op=ALU.mult), reads=[yf.b, lnw.b], writes=[yf.b])
                  OP("pool", lambda e: e.tensor_tensor(out=yf2, in0=yf2, in1=lnw[:, 512:1024], op=ALU.add), reads=[yf.b, lnw.b], writes=[yf.b])
                  pb_, pbb = PS(hold=True)
                  for c4 in range(4):
                      MM(pb_[:, c4 * 2:c4 * 2 + 2], rkb[:, c4, :], hsel[:, :], True, True, [rkb.b, hsel.b], pbb)
                  OP("act", lambda e: e.copy(out=bs[:], in_=pb_[:, 0:8]), reads=[pbb], writes=[bs.b])
                  PSREL(pb_)
                  yield
                  OP("dve", lambda e: e.tensor_tensor(out=ysq[:], in0=Vf[:].rearrange("p (a v) -> p a v", a=8), in1=b8(bs[:, :]), op=ALU.mult),
                     reads=[Vf.b, bs.b, ysq.b], writes=[ysq.b])
                  OP("pool", lambda e: e.tensor_tensor(out=yf[:], in0=yf[:], in1=ysq[:], op=ALU.add), reads=[yf.b, ysq.b], writes=[yf.b])
                  OP("dve", lambda e: e.tensor_tensor(out=yg[:], in0=yf2, in1=gsb[:], op=ALU.mult), reads=[yf.b, gsb.b], writes=[yg.b])
                  yield
                  pt_, ptb2 = PS(hold=True)
                  ptv = pt_.bitcast(BF16)
                  for c4 in range(4):
                      TR(ptv[:, c4 * 128:(c4 + 1) * 128], yg[:, c4 * 128:(c4 + 1) * 128], ident[:], [yg.b, ident.b], ptb2)
                  ya_ = yaTs[n % 2]
                  OP("act", lambda e: e.copy(out=ya_[:].rearrange("p a t -> p (a t)"), in_=ptv[:, 0:512]), reads=[ptb2], writes=[ya_.b])
                  PSREL(pt_)
                  DMA("sp", yaT_d[:, :, n * 128:(n + 1) * 128], ya_[:], reads=[ya_.b], writes=[B_yaT])

              dummy = PS(hold=True) if NDUMMY else None

              def run_rr(gens):
                  alive = [g is not None for g, _ in gens]
                  while any(alive):
                      for i_, (g, k_) in enumerate(gens):
                          for _ in range(k_):
                              if not alive[i_]:
                                  break
                              try:
                                  next(g)
                              except StopIteration:
                                  alive[i_] = False
                          for _ in range(NDUMMY):
                              MM(dummy[0][:, 0:128], ident[:, :], ident[:, :], True, True, [ident.b], dummy[1])

              gP = lambda n: phaseP(n) if n < NS else None
              gA1 = lambda n: phaseA(n) if n < NS else None
              gA2 = lambda n: phaseA2(n) if n < NS else None
              run_rr([(gP(0), 1)])
              run_rr([(gA1(0), RR[2]), (gP(1), RR[3])])
              run_rr([(gA2(0), RR[1]), (gA1(1), RR[2]), (gP(2), RR[3])])
              for n in range(NS):
                  streams = [(phaseB(n), RR[0]), (gA2(n + 1), RR[1]), (gA1(n + 2), RR[2]), (gP(n + 3), RR[3])]
                  run_rr([streams[k_] for k_ in ORDER])
              if dummy is not None:
                  PSREL(dummy[0])

        except _Stop:
            pass
        sy.barrier()

        CP('F')
        with ExitStack() as sf:
            Wf = SB(sf, "Wf", [128, 8, NFX], BF16)
            Wf_b = [Buf("Wf%d" % k) for k in range(8)]
            KT = SB(sf, "KT", [65, 8, S], BF16)
            Va = SB(sf, "Va", [128, NS, 8, 65], BF16)
            QT = [SB(sf, "QT%d" % i, [65, 8, 512], BF16) for i in range(2)]
            ncum = SB(sf, "ncum", [128, NS, 8], F32)
            PTs = [SB(sf, "PT%d" % i, [128, 512], BF16) for i in range(4)]
            hTf = [SB(sf, "hTf%d" % i, [128, 8, 512], BF16) for i in range(2)]
            tri = SB(sf, "tri", [128, 128], F32)
            onesq = SB(sf, "onesq", [128, 128], F32)
            trib = SB(sf, "trib", [128, 128], BF16)
            acc = SB(sf, "acc", [128, 8], F32)
            fbb = SB(sf, "fbb", [128, 8], F32)
            lf = SB(sf, "lf", [128, 8], F32)
            cq8 = SB(sf, "cq8", [8, 512], BF16)
            oT = [SB(sf, "oT%d" % i, [65, 512], F32) for i in range(4)]
            ybh = [SB(sf, "ybh%d" % i, [64, 512], BF16) for i in range(4)]
            for kc in range(8):
                DMA("pool", Wf[:, kc, :], win_d[kc * 128:(kc + 1) * 128, NRW:NRW + NFX], writes=[Wf_b[kc]])
            DMA("sp", fbb[:], bc_d[:, BC_FB:BC_FB + 8], writes=[fbb.b])
            OP("pool", lambda e: e.memset(tri[:], 1.0), writes=[tri.b])
            OP("pool", lambda e: e.affine_select(out=tri[:], in_=tri[:], pattern=[[1, 128]], compare_op=ALU.is_ge, fill=0.0,
                                                 base=0, channel_multiplier=-1), reads=[tri.b], writes=[tri.b])
            negm = SB(sf, "negm", [128, 128], F32)
            OP("pool", lambda e: e.memset(negm[:], 0.0), writes=[negm.b])
            OP("pool", lambda e: e.affine_select(out=negm[:], in_=negm[:], pattern=[[1, 128]], compare_op=ALU.is_ge, fill=-1.0e4,
                                                 base=0, channel_multiplier=-1), reads=[negm.b], writes=[negm.b])
            OP("pool", lambda e: e.memset(trib[:], 1.0), writes=[trib.b])
            OP("pool", lambda e: e.affine_select(out=trib[:], in_=trib[:], pattern=[[1, 128]], compare_op=ALU.is_ge, fill=0.0,
                                                 base=0, channel_multiplier=-1), reads=[trib.b], writes=[trib.b])
            OP("pool", lambda e: e.memset(onesq[:], 1.0), writes=[onesq.b])
            OP("dve", lambda e: e.memset(acc[:], 0.0), writes=[acc.b])
            lf4 = SB(sf, "lf4", [128, 4, 8], F32)
            acc5 = SB(sf, "acc5", [128, 5, 8], F32)
            cq8s = [SB(sf, "cq8_%d" % i_, [8, 512], BF16) for i_ in range(2)]
            qkst = [SB(sf, "qkst%d" % i_, [128, 512], BF16) for i_ in range(4)]
            PT6 = [SB(sf, "PTx%d" % i, [128, 512], BF16) for i in range(8)]
            OP("dve", lambda e: e.memset(acc5[:], 0.0), writes=[acc5.b])
            KT_b = [Buf("KTt%d" % i_) for i_ in range(NT)]
            Va_b = [Buf("Vat%d" % i_) for i_ in range(NT)]
            nc_b = [Buf("nct%d" % i_) for i_ in range(NT)]
            OP("dve", lambda e: e.memset(KT[64:65, :, :], 1.0), writes=KT_b)
            OP("dve", lambda e: e.memset(Va[:, :, :, 64:65], 1.0), writes=Va_b)
            LOOK = 6
            qrow_b = [[Buf("qrow%d_%d" % (i_, h_)) for h_ in range(8)] for i_ in range(2)]
            unit_ctr = [0]

            def front(i):
                ht = hTf[i % 2]
                qt = QT[i % 2]
                cq8_ = cq8s[i % 2]
                DMA("sp", ht[:], hT_d[:, :, i * 512:(i + 1) * 512], writes=[ht.b])
                yield
                pf, pfb = PS(hold=True)
                for sub in range(4):
                    ts_ = slice(sub * 128, (sub + 1) * 128)
                    for kc in range(8):
                        MM(pf[:, sub * 8:(sub + 1) * 8], ht[:, kc, ts_], Wf[:, kc, 1536:1544], kc == 0, kc == 7, [Wf_b[kc], ht.b], pfb)
                OP("dve", lambda e: e.tensor_tensor(out=lf4[:], in0=pf[:, 0:32].rearrange("p (a h) -> p a h", a=4),
                                                    in1=fbb[:].unsqueeze(1).broadcast_to([128, 4, 8]), op=ALU.add),
                   reads=[pfb, fbb.b], writes=[lf4.b])
                PSREL(pf)
                OP("act", lambda e: e.activation(out=lf4[:], in_=lf4[:], func=AF.Sigmoid), reads=[lf4.b], writes=[lf4.b])
                OP("act", lambda e: e.activation(out=lf4[:], in_=lf4[:], func=AF.Ln), reads=[lf4.b], writes=[lf4.b])
                if i > 0:
                    OP("dve", lambda e: e.tensor_scalar(out=acc5[:, 0, :], in0=acc5[:, 4, :], scalar1=1.0, scalar2=None, op0=ALU.mult),
                       reads=[acc5.b], writes=[acc5.b])
                for sub in range(4):
                    OP("dve", lambda e, sub=sub: e.tensor_tensor(out=acc5[:, sub + 1, :], in0=acc5[:, sub, :], in1=lf4[:, sub, :], op=ALU.add),
                       reads=[acc5.b, lf4.b], writes=[acc5.b])
                yield
                for pair in range(4):
                    for which in range(2):
                        col0 = which * 512 + pair * 128
                        pq_, pqb_ = PS(hold=True)
                        for kc in range(8):
                            MM(pq_[:, :], Wf[:, kc, col0:col0 + 128], ht[:, kc, :], kc == 0, kc == 7, [Wf_b[kc], ht.b], pqb_)
                        stg = qkst[(pair * 2 + which) % 4]
                        if which == 0:
                            dst_even = qt[0:64, 2 * pair, :]
                            dst_odd = qt[0:64, 2 * pair + 1, :]
                            dbuf = qt.b
                        else:
                            dst_even = KT[0:64, 2 * pair, i * 512:(i + 1) * 512]
                            dst_odd = KT[0:64, 2 * pair + 1, i * 512:(i + 1) * 512]
                            dbuf = KT_b[i]
                        OP("dve", lambda e: e.tensor_scalar(out=dst_even, in0=pq_[0:64, :], scalar1=1.0, scalar2=None, op0=ALU.mult),
                           reads=[pqb_], writes=[dbuf])
                        OP("act", lambda e: e.copy(out=stg[64:128, :], in_=pq_[64:128, :]), reads=[pqb_], writes=[stg.b])
                        PSREL(pq_)
                        DMA("sp", dst_odd, stg[64:128, :], reads=[stg.b], writes=[dbuf])
                        yield
                for sub in range(4):
                    g = i * 4 + sub
                    ts_ = slice(sub * 128, (sub + 1) * 128)
                    pv2, pv2b = PS(hold=True)
                    for kc in range(8):
                        MM(pv2[:, :], ht[:, kc, ts_], Wf[:, kc, 1024:1536], kc == 0, kc == 7, [Wf_b[kc], ht.b], pv2b)
                    OP("dve", lambda e: e.tensor_scalar(out=Va[:, g, :, 0:64], in0=pv2[:, :].rearrange("p (a v) -> p a v", a=8),
                                                        scalar1=1.0, scalar2=None, op0=ALU.mult), reads=[pv2b], writes=[Va_b[i]])
                    PSREL(pv2)
                    yield
                pc, pcb_ = PS(hold=True)
                pcT, pcTb = PS(hold=True)
                for sub in range(4):
                    ts_ = slice(sub * 128, (sub + 1) * 128)
                    MM(pc[:, sub * 8:(sub + 1) * 8], tri[:, :], lf4[:, sub, :], True, False, [tri.b, lf4.b], pcb_)
                    MM(pc[:, sub * 8:(sub + 1) * 8], onesq[:, :], acc5[:, sub, :], False, True, [onesq.b, acc5.b], pcb_)
                    MM(pcT[0:8, ts_], lf4[:, sub, :], tri[:, :], sub == 0, False, [tri.b, lf4.b], pcTb)
                    MM(pcT[0:8, ts_], acc5[:, sub, :], onesq[:, :], False, sub == 3, [onesq.b, acc5.b], pcTb)
                OP("dve", lambda e: e.tensor_scalar(out=ncum[:, i * 4:(i + 1) * 4, :], in0=pc[:, 0:32].rearrange("p (a h) -> p a h", a=4),
                                                    scalar1=-1.0, scalar2=None, op0=ALU.mult), reads=[pcb_], writes=[nc_b[i]])
                PSREL(pc)
                OP("dve", lambda e: e.tensor_scalar(out=cq8_[:], in0=pcT[0:8, :], scalar1=8.0, scalar2=None, op0=ALU.mult),
                   reads=[pcTb], writes=[cq8_.b])
                PSREL(pcT)
                for h in range(8):
                    DMA("sp", qt[64:65, h, :], cq8_[h:h + 1, :], reads=[cq8_.b], writes=[qrow_b[i % 2][h]])

            def attention(i):
                qt = QT[i % 2]
                nkb = 4 * i + 4
                tasks = [(h, j) for h in range(8) for j in range(nkb)]
                NTK = len(tasks)
                pts = {}
                pos = {}
                tails = []

                def emit_S(t):
                    h, j = tasks[t]
                    q0 = max(0, j - 4 * i) * 128
                    ps_, psb_ = PS()
                    MM(ps_[:, q0:512], KT[0:65, h, j * 128:(j + 1) * 128], qt[0:65, h, q0:512], True, True,
                       [KT_b[j // 4], qt.b, qrow_b[i % 2][h]], psb_)
                    PT = PT6[t % 8]
                    if j >= 4 * i:
                        OP("dve", lambda e: e.tensor_tensor(out=ps_[:, q0:q0 + 128], in0=ps_[:, q0:q0 + 128], in1=negm[:], op=ALU.add),
                           reads=[psb_, negm.b], writes=[psb_])
                    OP("act", lambda e: e.activation(out=PT[:, q0:512], in_=ps_[:, q0:512], func=AF.Exp, scale=0.125,
                                                     bias=ncum[:, j, h:h + 1]), reads=[psb_, nc_b[j // 4]], writes=[PT.b])
                    pts[t] = (PT, q0)

                def emit_PV(t):
                    h, j = tasks[t]
                    if j == 0:
                        pos[h] = PS(hold=True)
                    po, pob = pos[h]
                    PT, q0 = pts.pop(t)
                    MM(po[0:65, q0:512], Va[:, j, h, :], PT[:, q0:512], j == 0, j == nkb - 1, [Va_b[j // 4], PT.b], pob)
                    if j == nkb - 1:
                        u = unit_ctr[0]
                        unit_ctr[0] += 1
                        o_ = oT[u % 4]
                        yb_ = ybh[u % 4]
                        OP("dve", lambda e: e.tensor_scalar(out=o_[:], in0=po[0:65, :], scalar1=1.0, scalar2=None, op0=ALU.mult),
                           reads=[pob], writes=[o_.b])
                        PSREL(po)
                        OP("dve", lambda e: e.reciprocal(out=o_[64:65, :], in_=o_[64:65, :]), reads=[o_.b], writes=[o_.b])

                        def tail(h=h, o_=o_, yb_=yb_):
                            pr_, prb_ = PS()
                            MM(pr_[0:64, :], onesq[64:65, 0:64], o_[64:65, :], True, True, [onesq.b, o_.b], prb_)
                            OP("dve", lambda e: e.tensor_tensor(out=yb_[:], in0=pr_[0:64, :], in1=o_[0:64, :], op=ALU.mult),
                               reads=[prb_, o_.b], writes=[yb_.b])
                            DMA("sp", ybT_d[h * 64:(h + 1) * 64, i * 512:(i + 1) * 512], yb_[:], reads=[yb_.b], writes=[B_ybT])
                        tails.append((t + 11, tail))

                for t in range(min(LOOK, NTK)):
                    emit_S(t)
                for t in range(NTK):
                    if t + LOOK < NTK:
                        emit_S(t + LOOK)
                    emit_PV(t)
                    while tails and tails[0][0] <= t:
                        tails.pop(0)[1]()
                    yield
                while tails:
                    tails.pop(0)[1]()

            for _ in front(0):
                pass
            for i in range(NT):
                gA = attention(i)
                gF = front(i + 1) if i + 1 < NT else None
                ntk = 8 * (4 * i + 4)
                every = max(1, ntk // 28)
                cnt_ = 0
                for _ in gA:
                    cnt_ += 1
                    if gF is not None and cnt_ % every == 0:
                        try:
                            next(gF)
                        except StopIteration:
                            gF = None
                if gF is not None:
                    for _ in gF:
                        pass
        sy.barrier()

        CP('C1')
        with ExitStack() as sc:
            Wg = SB(sc, "Wg", [128, 8, 2048], BF16)
            Wg_b = [Buf("Wg%d" % k) for k in range(8)]
            Wor = SB(sc, "Wor", [128, 4, D], BF16)
            Wof = SB(sc, "Wof", [128, 4, D], BF16)
            Wout = SB(sc, "Wout", [128, 8, D], BF16)
            gtb = SB(sc, "gtb", [128, D], F32)
            hTc = [SB(sc, "hTc%d" % i, [128, 8, 512], BF16) for i in range(2)]
            yaTt = [SB(sc, "yaTt%d" % i, [128, 4, 512], BF16) for i in range(2)]
            ybTt = [SB(sc, "ybTt%d" % i, [128, 4, 512], BF16) for i in range(2)]
            Gsig2 = [SB(sc, "Gsig%d" % i_, [128, 16, 512], BF16) for i_ in range(2)]
            mrg = SB(sc, "mrg", [128, 8, 512], BF16)
            tm1 = [SB(sc, "tm1_%d" % i, [128, 512], F32) for i in range(2)]
            tm2 = [SB(sc, "tm2_%d" % i, [128, 512], F32) for i in range(2)]
            tm3c = [SB(sc, "tm3c_%d" % i, [128, 512], F32) for i in range(2)]
            xs = [SB(sc, "xs%d" % i, [128, D], F32) for i in range(3)]
            x1s = [SB(sc, "x1s%d" % i, [128, D], F32) for i in range(3)]
            xn2 = [SB(sc, "xn2_%d" % i, [128, D], BF16) for i in range(3)]
            h2t = [SB(sc, "h2t%d" % i, [128, 8, 512], BF16) for i in range(2)]
            junk1 = SB(sc, "junk1", [128, D], BF16)
            st2 = [SB(sc, "st2_%d" % i, [128, 2], F32) for i in range(3)]
            for kc in range(8):
                DMA("pool", Wg[:, kc, :], win_d[kc * 128:(kc + 1) * 128, NRW + NFX:NIN], writes=[Wg_b[kc]])
            DMA("pool", Wor[:], wor_d.rearrange("(c p) n -> p c n", p=128), writes=[Wor.b])
            DMA("pool", Wof[:], wof_d.rearrange("(c p) n -> p c n", p=128), writes=[Wof.b])
            DMA("pool", Wout[:], wout_d.rearrange("(c p) n -> p c n", p=128), writes=[Wout.b])
            DMA("sp", gtb[:], gt_d[:, 0:D], reads=[B_gt], writes=[gtb.b])
            ybT_v = ybT_d.rearrange("(c p) s -> p c s", p=128)

            def load_tile(i):
                tsl_ = slice(i * 512, (i + 1) * 512)
                DMA("sp", hTc[i % 2][:], hT_d[:, :, tsl_], reads=[B_hT], writes=[hTc[i % 2].b])
                DMA("sp", yaTt[i % 2][:], yaT_d[:, :, tsl_], reads=[B_yaT], writes=[yaTt[i % 2].b])
                DMA("sp", ybTt[i % 2][:], ybT_v[:, :, tsl_], reads=[B_ybT], writes=[ybTt[i % 2].b])

            def load_x(k_):
                if k_ < NS:
                    DMA("sp", xs[k_ % 3][:], x_d[k_ * 128:(k_ + 1) * 128, :], writes=[xs[k_ % 3].b])

            def gates(i, part):
                if i >= NT:
                    return
                ht_, G_ = hTc[i % 2], Gsig2[i % 2]
                for g in range(part * 8, part * 8 + 8):
                    pg2, pg2b = PS()
                    for kc in range(8):
                        MM(pg2[:, :], Wg[:, kc, g * 128:(g + 1) * 128], ht_[:, kc, :], kc == 0, kc == 7, [Wg_b[kc], ht_.b], pg2b)
                    OP("act", lambda e, g=g, pg2=pg2: e.activation(out=G_[:, g, :], in_=pg2[:, :], func=AF.Sigmoid),
                       reads=[pg2b], writes=[G_.b])

            h2_b = [[Buf("h2_%d_%d" % (i_, k_)) for k_ in range(8)] for i_ in range(2)]
            load_tile(0)
            load_x(0)
            gates(0, 0)
            gates(0, 1)
            for i in range(NT):
                ht, ya_t, yb_t, h2 = hTc[i % 2], yaTt[i % 2], ybTt[i % 2], h2t[i % 2]
                Gsig = Gsig2[i % 2]
                tsl = slice(i * 512, (i + 1) * 512)
                if i + 1 < NT:
                    load_tile(i + 1)
                for m in range(8):
                    pa2, pa2b = PS()
                    for c in range(4):
                        MM(pa2[:, :], Wor[:, c, m * 128:(m + 1) * 128], ya_t[:, c, :], c == 0, c == 3, [Wor.b, ya_t.b], pa2b)
                    pb2, pb2b = PS()
                    for c in range(4):
                        MM(pb2[:, :], Wof[:, c, m * 128:(m + 1) * 128], yb_t[:, c, :], c == 0, c == 3, [Wof.b, yb_t.b], pb2b)
                    t1_, t2_ = tm1[m % 2], tm2[m % 2]
                    OP("dve", lambda e, m=m, pa2=pa2, t1_=t1_: e.tensor_tensor(out=t1_[:], in0=pa2[:, :], in1=Gsig[:, m, :], op=ALU.mult),
                       reads=[pa2b, Gsig.b], writes=[t1_.b])
                    OP("dve", lambda e, m=m, pb2=pb2, t2_=t2_: e.tensor_tensor(out=t2_[:], in0=pb2[:, :], in1=Gsig[:, 8 + m, :], op=ALU.mult),
                       reads=[pb2b, Gsig.b], writes=[t2_.b])
                    OP("pool", lambda e, m=m, t1_=t1_, t2_=t2_: e.tensor_tensor(out=mrg[:, m, :], in0=t1_[:], in1=t2_[:], op=ALU.add),
                       reads=[t1_.b, t2_.b], writes=[mrg.b])
                gates(i + 1, 0)
                pT2 = [PS(hold=True) for _ in range(4)]

                def z_part(sub):
                    k_ = i * 4 + sub
                    ts_ = slice(sub * 128, (sub + 1) * 128)
                    xt, x1t, xnt, s2 = xs[k_ % 3], x1s[k_ % 3], xn2[k_ % 3], st2[k_ % 3]
                    load_x(k_ + 1)
                    for half in range(2):
                        hsl_ = slice(half * 512, (half + 1) * 512)
                        pz2, pz2b = PS()
                        for c in range(8):
                            MM(pz2[:, :], mrg[:, c, ts_], Wout[:, c, hsl_], c == 0, c == 7, [mrg.b, Wout.b], pz2b)
                        t1_ = tm3c[half]
                        OP("dve", lambda e, pz2=pz2, t1_=t1_, hsl_=hsl_: e.tensor_tensor(out=t1_[:], in0=pz2[:, :], in1=gtb[:, hsl_], op=ALU.mult),
                           reads=[pz2b, gtb.b], writes=[t1_.b])
                        OP("dve", lambda e, t1_=t1_, hsl_=hsl_, xt=xt, x1t=x1t: e.tensor_tensor(out=x1t[:, hsl_], in0=t1_[:], in1=xt[:, hsl_], op=ALU.add),
                           reads=[t1_.b, xt.b], writes=[x1t.b])
                    DMA("sp", x1_d[i * 512 + sub * 128: i * 512 + (sub + 1) * 128, :], x1t[:], reads=[x1t.b], writes=[B_x1])
                    OP("act", lambda e, x1t=x1t, s2=s2: e.activation(out=junk1[:], in_=x1t[:], func=AF.Square, accum_out=s2[:, 0:1]),
                       reads=[x1t.b], writes=[junk1.b, s2.b])
                    OP("dve", lambda e, s2=s2: e.tensor_scalar(out=s2[:, 1:2], in0=s2[:, 0:1], scalar1=1.0 / D, scalar2=NORM_EPS,
                                                               op0=ALU.mult, op1=ALU.add), reads=[s2.b], writes=[s2.b])
                    OP("act", lambda e, s2=s2: e.activation(out=s2[:, 1:2], in_=s2[:, 1:2], func=AF.Sqrt), reads=[s2.b], writes=[s2.b])
                    OP("dve", lambda e, s2=s2: e.reciprocal(out=s2[:, 1:2], in_=s2[:, 1:2]), reads=[s2.b], writes=[s2.b])
                    OP("act", lambda e, x1t=x1t, xnt=xnt, s2=s2: e.activation(out=xnt[:], in_=x1t[:], func=AF.Copy, scale=s2[:, 1:2]),
                       reads=[x1t.b, s2.b], writes=[xnt.b])

                def t_part(sub):
                    k_ = i * 4 + sub
                    xnt = xn2[k_ % 3]
                    for kc in range(8):
                        pv3 = pT2[kc // 2][0].bitcast(BF16)
                        TR(pv3[:, (kc % 2) * 512 + sub * 128:(kc % 2) * 512 + (sub + 1) * 128], xnt[:, kc * 128:(kc + 1) * 128], ident[:],
                           [xnt.b, ident.b], pT2[kc // 2][1])

                z_part(0)
                z_part(1)
                for sub in range(4):
                    if sub + 2 < 4:
                        z_part(sub + 2)
                    if sub == 1:
                        gates(i + 1, 1)
                    t_part(sub)
                for hb in range(4):
                    pv3 = pT2[hb][0].bitcast(BF16)
                    for kq in range(2):
                        kc = hb * 2 + kq
                        if kc % 2 == 0:
                            OP("dve", lambda e, kc=kc, kq=kq, pv3=pv3: e.tensor_scalar(
                                out=h2[:, kc, :], in0=pv3[:, kq * 512:(kq + 1) * 512], scalar1=GS[:, 16 + kc:17 + kc],
                                scalar2=GS[:, 24 + kc:25 + kc], op0=ALU.mult, op1=ALU.add), reads=[pT2[hb][1], GS.b], writes=[h2_b[i % 2][kc]])
                        else:
                            OP("act", lambda e, kc=kc, kq=kq, pv3=pv3: e.activation(
                                out=h2[:, kc, :], in_=pv3[:, kq * 512:(kq + 1) * 512], func=AF.Identity, scale=GS[:, 16 + kc:17 + kc],
                                bias=GS[:, 24 + kc:25 + kc]), reads=[pT2[hb][1], GS.b], writes=[h2_b[i % 2][kc]])
                for hb in range(4):
                    PSREL(pT2[hb][0])
                DMA("sp", h2T_d[:, :, tsl], h2[:], reads=h2_b[i % 2], writes=[B_h2T])
        sy.barrier()

        CP('C2')
        with ExitStack() as s2c:
            W1 = SB(s2c, "W1", [128, 8, 4 * D], BF16)
            W1_b = [Buf("W1_%d" % k) for k in range(8)]
            W2 = SB(s2c, "W2", [128, 32, D], BF16)
            W2_b = [Buf("W2_%d" % k) for k in range(4)]
            gt2 = SB(s2c, "gt2", [128, D], F32)
            fgb = SB(s2c, "fgb", [128, D], F32)
            h2c = [SB(s2c, "h2c%d" % i, [128, 8, 256], BF16) for i in range(2)]
            hid = SB(s2c, "hid", [128, 32, 256], BF16)
            rl = [SB(s2c, "rl%d" % i, [128, 512], F32) for i in range(2)]
            x1c = [SB(s2c, "x1c%d" % i, [128, D], F32) for i in range(2)]
            x2c = [SB(s2c, "x2c%d" % i, [128, D], F32) for i in range(2)]
            oc = [SB(s2c, "oc%d" % i, [128, D], F32) for i in range(2)]
            tm3 = [SB(s2c, "tm3_%d" % i, [128, 512], F32) for i in range(2)]
            junk2 = SB(s2c, "junk2", [128, D], BF16)
            st3 = [SB(s2c, "st3_%d" % i, [128, 2], F32) for i in range(2)]
            for kc in range(8):
                DMA("pool", W1[:, kc, :], wff1_d[kc * 128:(kc + 1) * 128, :], writes=[W1_b[kc]])
            for q4 in range(4):
                DMA("pool", W2[:, q4 * 8:(q4 + 1) * 8, :],
                    wff2_d[q4 * 1024:(q4 + 1) * 1024, :].rearrange("(k p) n -> p k n", p=128), writes=[W2_b[q4]])
            DMA("sp", gt2[:], gt_d[:, D:2 * D], reads=[B_gt], writes=[gt2.b])
            DMA("sp", fgb[:], bc_d[:, BC_FG:BC_FG + D], writes=[fgb.b])
            def load_h2(i2):
                if i2 < S // 256:
                    DMA("sp", h2c[i2 % 2][:], h2T_d[:, :, i2 * 256:(i2 + 1) * 256], reads=[B_h2T], writes=[h2c[i2 % 2].b])

            def load_x1(k_):
                if k_ < NS:
                    DMA("sp", x1c[k_ % 2][:], x1_d[k_ * 128:(k_ + 1) * 128, :], reads=[B_x1], writes=[x1c[k_ % 2].b])

            load_h2(0)
            load_x1(0)
            for i2 in range(S // 256):
                hc = h2c[i2 % 2]
                load_h2(i2 + 1)
                for f2 in range(16):
                    pf2, pf2b = PS()
                    for fh in range(2):
                        f = f2 * 2 + fh
                        for kc in range(8):
                            MM(pf2[:, fh * 256:(fh + 1) * 256], W1[:, kc, f * 128:(f + 1) * 128], hc[:, kc, :], kc == 0, kc == 7,
                               [W1_b[kc], hc.b], pf2b)
                    r_ = rl[f2 % 2]
                    OP("act", lambda e, pf2=pf2, r_=r_: e.activation(out=r_[:], in_=pf2[:, :], func=AF.Relu), reads=[pf2b], writes=[r_.b])
                    OP("pool", lambda e, f2=f2, r_=r_: e.tensor_tensor(out=hid[:, f2 * 2:f2 * 2 + 2, :].rearrange("p a t -> p (a t)"),
                                                                       in0=r_[:], in1=r_[:], op=ALU.mult), reads=[r_.b], writes=[hid.b])
                for sub in range(2):
                    k_ = i2 * 2 + sub
                    r0 = i2 * 256 + sub * 128
                    ts_ = slice(sub * 128, (sub + 1) * 128)
                    x1t, x2t, ot, s3 = x1c[k_ % 2], x2c[k_ % 2], oc[k_ % 2], st3[k_ % 2]
                    load_x1(k_ + 1)
                    for half in range(2):
                        hsl_ = slice(half * 512, (half + 1) * 512)
                        po2, po2b = PS()
                        for kk_ in range(32):
                            MM(po2[:, :], hid[:, kk_, ts_], W2[:, kk_, hsl_], kk_ == 0, kk_ == 31, [hid.b, W2_b[kk_ // 8]], po2b)
                        t3 = tm3[half]
                        OP("dve", lambda e, po2=po2, t3=t3, hsl_=hsl_: e.tensor_tensor(out=t3[:], in0=po2[:, :], in1=gt2[:, hsl_], op=ALU.mult),
                           reads=[po2b, gt2.b], writes=[t3.b])
                        OP("dve", lambda e, t3=t3, hsl_=hsl_, x1t=x1t, x2t=x2t: e.tensor_tensor(out=x2t[:, hsl_], in0=t3[:], in1=x1t[:, hsl_], op=ALU.add),
                           reads=[t3.b, x1t.b], writes=[x2t.b])
                    OP("act", lambda e, x2t=x2t, s3=s3: e.activation(out=junk2[:], in_=x2t[:], func=AF.Square, accum_out=s3[:, 0:1]),
                       reads=[x2t.b], writes=[junk2.b, s3.b])
                    OP("dve", lambda e, s3=s3: e.tensor_scalar(out=s3[:, 1:2], in0=s3[:, 0:1], scalar1=1.0 / D, scalar2=NORM_EPS,
                                                               op0=ALU.mult, op1=ALU.add), reads=[s3.b], writes=[s3.b])
                    OP("act", lambda e, s3=s3: e.activation(out=s3[:, 1:2], in_=s3[:, 1:2], func=AF.Sqrt), reads=[s3.b], writes=[s3.b])
                    OP("dve", lambda e, s3=s3: e.reciprocal(out=s3[:, 1:2], in_=s3[:, 1:2]), reads=[s3.b], writes=[s3.b])
                    OP("dve", lambda e, x2t=x2t, ot=ot, s3=s3: e.scalar_tensor_tensor(out=ot[:], in0=x2t[:], scalar=s3[:, 1:2], in1=fgb[:],
                                                                                     op0=ALU.mult, op1=ALU.mult),
                       reads=[x2t.b, s3.b, fgb.b], writes=[ot.b])
                    DMA("sp", out_d[r0:r0 + 128, :], ot[:], reads=[ot.b], writes=[B_out])
        sy.barrier()
        sy.drain("sp")
    return nc


def _pack_inputs(inp, b):
    f = lambda a: np.ascontiguousarray(np.asarray(a, dtype=np.float32))
    pp = np.zeros((128, NPP), np.float32)
    pp[:, PP_BADA:PP_BADA + 48] = f(inp["b_ada"])[0].reshape(48, 128).T
    pp[:, PP_G1:PP_G1 + 8] = f(inp["norm1_g"])[0].reshape(8, 128).T
    pp[:, PP_G2:PP_G2 + 8] = f(inp["norm2_g"])[0].reshape(8, 128).T
    pp[:, PP_C:PP_C + 8] = f(inp["c"])[b].reshape(8, 128).T
    pp[:, PP_DB:PP_DB + 4] = f(inp["decay_base"])[0].reshape(4, 128).T
    pp[:, PP_IB:PP_IB + 4] = f(inp["iclr_base"])[0].reshape(4, 128).T
    pp[:, PP_KS:PP_KS + 4] = f(inp["kk_scale"])[0].reshape(4, 128).T
    pp[:, PP_MIX:PP_MIX + 4] = f(inp["k_iclr_mix"])[0].reshape(4, 128).T
    pp[:, PP_RB:PP_RB + 4] = f(inp["r_bonus"])[0].reshape(512).reshape(4, 128).T
    mu = f(inp["mu_shift"])[0]
    mu_r = np.concatenate([mu[0:512], mu[576:1088], mu[1088:1600], mu[512:576], mu[1600:1664], mu[1664:1792]])
    row = np.concatenate([mu_r, f(inp["lnx_w"])[0], f(inp["lnx_b"])[0], f(inp["fox_f_bias"])[0], f(inp["final_g"])])
    bc = np.ascontiguousarray(np.broadcast_to(row[None, :], (128, NBC)))
    ba = f(inp["b_ada"])[0]
    brow = np.concatenate([ba[2 * D:3 * D], ba[5 * D:6 * D]])[None, :]
    return pp, bc, np.ascontiguousarray(brow)


_NC_CACHE = {}


def make_in_maps(inp, S, nb):
    shared = {
        "w_ada": np.ascontiguousarray(np.asarray(inp["w_ada"], np.float32)[0]),
        "w_in": np.ascontiguousarray(np.asarray(inp["w_in"], np.float32)[0]),
        "w_decay_up": np.ascontiguousarray(np.asarray(inp["w_decay_up"], np.float32)[0]),
        "w_iclr_up": np.ascontiguousarray(np.asarray(inp["w_iclr_up"], np.float32)[0]),
        "w_gate_up": np.ascontiguousarray(np.asarray(inp["w_gate_up"], np.float32)[0]),
        "w_o_rwkv": np.ascontiguousarray(np.asarray(inp["w_o_rwkv"], np.float32)[0]),
        "w_o_fox": np.ascontiguousarray(np.asarray(inp["w_o_fox"], np.float32)[0]),
        "w_out": np.ascontiguousarray(np.asarray(inp["w_out"], np.float32)[0]),
        "w_ff1": np.ascontiguousarray(np.asarray(inp["w_ff1"], np.float32)[0]),
        "w_ff2": np.ascontiguousarray(np.asarray(inp["w_ff2"], np.float32)[0]),
    }
    maps = []
    x = np.asarray(inp["x"], np.float32)
    for b in range(nb):
        pp, bc, brow = _pack_inputs(inp, b)
        m = dict(shared)
        m.update({"x": np.ascontiguousarray(x[b]), "pp": pp, "bc": bc, "brow": brow})
        maps.append(m)
    return maps


def kernel(**inputs):
    x = np.asarray(inputs["x"])
    nb, S = x.shape[0], x.shape[1]
    if S not in _NC_CACHE:
        _NC_CACHE[S] = build(S)
    nc = _NC_CACHE[S]
    maps = make_in_maps(inputs, S, nb)
    res = run_bass_kernel_spmd(nc, maps, core_ids=list(range(nb)))
    return np.stack([np.asarray(r["out"], np.float32) for r in res.results], axis=0)
```

```python
import numpy as np
from contextlib import ExitStack
import concourse.bass as bass
import concourse.mybir as mybir
from concourse.alu_op_type import AluOpType as ALU
from concourse.bass_utils import run_bass_kernel_spmd

F32 = mybir.dt.float32
BF16 = mybir.dt.bfloat16
AF = mybir.ActivationFunctionType
AX = mybir.AxisListType

D = 1024
NRW = 1792
NFX = 1544
NIN = 5384
EXPM05 = 0.6065306597126334
NORM_EPS = 1e-6
GN_EPS = 64e-5

PP_BADA, PP_G1, PP_G2, PP_C, PP_DB, PP_IB, PP_KS, PP_MIX, PP_RB, NPP = 0, 48, 56, 64, 72, 76, 80, 84, 88, 92
BC_MU, BC_LW, BC_LB, BC_FB, BC_FG, NBC = 0, 1792, 2304, 2816, 2824, 3848


class Buf:
    __slots__ = ("name", "lw", "rd")

    def __init__(self, name=""):
        self.name = name
        self.lw = None
        self.rd = {}


class Sync:
    def __init__(self, nc, es, n_dma_sems=32):
        self.nc = nc
        self.eng = {"pe": nc.tensor, "act": nc.scalar, "dve": nc.vector, "pool": nc.gpsimd, "sp": nc.sync}
        self.sem = {k: es.enter_context(nc.semaphore("sem_" + k)) for k in self.eng}
        self.cnt = {k: 0 for k in self.eng}
        self.dsem = [es.enter_context(nc.semaphore("dsem%d" % i)) for i in range(n_dma_sems)]
        self.dcnt = [0] * n_dma_sems
        self.dnext = 0
        self.dnext_sw = 0
        self.seen = {k: {} for k in self.eng}
        self.dead = False
        self.lazy = {"pe"}
        self.unflushed = {}
        self.last_inst = {}

    def _flush(self, key):
        if self.unflushed.get(key):
            self.last_inst[key].then_inc(self.sem[key], 1)
            self.cnt[key] += 1
            self.unflushed[key] = False

    def _wait(self, e, key, val):
        if self.seen[e].get(key, 0) >= val:
            return
        if isinstance(key, str) and val > self.cnt[key]:
            assert key in self.lazy and val == self.cnt[key] + 1, (key, val, self.cnt[key])
            self._flush(key)
        sem = self.sem[key] if isinstance(key, str) else self.dsem[key]
        self.eng[e].wait_ge(sem, val)
        self.seen[e][key] = val

    def _deps(self, e, reads, writes):
        deps = {}

        def add(k, v):
            if deps.get(k, 0) < v:
                deps[k] = v
        for b in reads:
            if b.lw is not None:
                add(*b.lw)
        for b in writes:
            if b.lw is not None:
                add(*b.lw)
            for k, v in b.rd.items():
                add(k, v)
        for k, v in deps.items():
            if k == e and e == "pe":
                continue
            self._wait(e, k, v)

    def _post(self, ev, reads, writes):
        for b in reads:
            if b.rd.get(ev[0], 0) < ev[1]:
                b.rd[ev[0]] = ev[1]
        for b in writes:
            b.lw = ev
            b.rd = {}

    def op(self, e, fn, reads=(), writes=()):
        if self.dead:
            return None
        if e != "pe":
            pr_ = [b for b in reads if b.name.startswith("bank")]
            if pr_:
                reads = [b for b in reads if not b.name.startswith("bank")]
                writes = list(writes) + pr_
        self._deps(e, reads, writes)
        inst = fn(self.eng[e])
        if e in self.lazy:
            self.last_inst[e] = inst
            self.unflushed[e] = True
            self._post((e, self.cnt[e] + 1), reads, writes)
            return inst
        self.cnt[e] += 1
        inst.then_inc(self.sem[e], 1)
        self._post((e, self.cnt[e]), reads, writes)
        return inst

    def dma(self, e, out, in_, reads=(), writes=()):
        if self.dead:
            return None
        nsw = 8
        if e == "pool":
            k = self.dnext_sw
            self.dnext_sw = (self.dnext_sw + 1) % nsw
        else:
            k = nsw + self.dnext
            self.dnext = (self.dnext + 1) % (len(self.dsem) - nsw)
        if self.dcnt[k] > 0:
            self._wait(e, k, self.dcnt[k])
        self._deps(e, reads, writes)
        inst = self.eng[e].dma_start(out=out, in_=in_)
        self.dcnt[k] += 16
        inst.then_inc(self.dsem[k], 16)
        self._post((k, self.dcnt[k]), reads, writes)
        return inst

    def barrier(self):
        for k in list(self.lazy):
            self._flush(k)
        for e in self.eng:
            for k in self.eng:
                if self.cnt[k]:
                    self._wait(e, k, self.cnt[k])
            for k in range(len(self.dsem)):
                if self.dcnt[k]:
                    self._wait(e, k, self.dcnt[k])

    def drain(self, e="sp"):
        for k in list(self.lazy):
            self._flush(k)
        for k in range(len(self.dsem)):
            if self.dcnt[k]:
                self._wait(e, k, self.dcnt[k])
        for k in self.eng:
            if k != e and self.cnt[k]:
                self._wait(e, k, self.cnt[k])


class T:
    def __init__(self, es, nc, name, shape, dtype):
        self.t = es.enter_context(nc.sbuf_tensor("sb_" + name, shape, dtype))
        self.b = Buf(name)

    def __getitem__(self, idx):
        return self.t[idx]


class _Stop(Exception):
    pass


RR = [1, 1, 1, 1]
ORDER = [0, 1, 2, 3]
NDUMMY = 0


def build(S, dbg=False, stop=None):
    assert S % 512 == 0

    def CP(name):
        if stop == name:
            sy_box[0].dead = True

    sy_box = [None]
    NT = S // 512
    NS = S // 128
    nc = bass.Bass("TRN2", target_bir_lowering=False)

    def din(n, shp, dt=F32):
        return nc.dram_tensor(n, shp, dt, kind="ExternalInput").ap()

    def dscr(n, shp, dt):
        return nc.dram_tensor(n, shp, dt, kind="ExternalOutput" if dbg else "Internal").ap()

    x_d = din("x", [S, D])
    pp_d = din("pp", [128, NPP])
    bc_d = din("bc", [128, NBC])
    brow_d = din("brow", [1, 2048])
    wada_d = din("w_ada", [D, 6 * D])
    win_d = din("w_in", [D, NIN])
    wdu_d = din("w_decay_up", [64, 512])
    wiu_d = din("w_iclr_up", [64, 512])
    wgu_d = din("w_gate_up", [128, 512])
    wor_d = din("w_o_rwkv", [512, D])
    wof_d = din("w_o_fox", [512, D])
    wout_d = din("w_out", [D, D])
    wff1_d = din("w_ff1", [D, 4 * D])
    wff2_d = din("w_ff2", [4 * D, D])
    out_d = nc.dram_tensor("out", [S, D], F32, kind="ExternalOutput").ap()

    hT_d = dscr("hT_s", [128, 8, S], BF16)
    yaT_d = dscr("yaT_s", [128, 4, S], BF16)
    ybT_d = dscr("ybT_s", [512, S], BF16)
    x1_d = dscr("x1_s", [S, D], F32)
    h2T_d = dscr("h2T_s", [128, 8, S], BF16)
    gt_d = dscr("gt_s", [128, 2048], F32)
    B_hT, B_yaT, B_ybT, B_x1, B_h2T, B_gt, B_out = (Buf(n) for n in "hT yaT ybT x1 h2T gt out".split())

    with ExitStack() as es:
        sy = Sync(nc, es)
        sy_box[0] = sy
        OP = sy.op
        DMA = sy.dma

        def SB(scope, name, shape, dt):
            return T(scope, nc, name, shape, dt)

        banks = [es.enter_context(nc.psum_tensor("bank%d" % i, [128, 512], F32)) for i in range(8)]
        bbufs = [Buf("bank%d" % i) for i in range(8)]
        pstate = {"i": 0}

        held = set()

        def PS(hold=False):
            assert len(held) < 8, "all PSUM banks held"
            while True:
                i = pstate["i"]
                pstate["i"] = (i + 1) % 8
                if i not in held:
                    break
            if hold:
                held.add(i)
            return banks[i], bbufs[i]

        def PSREL(bank):
            held.discard(banks.index(bank))

        def MM(out, lhsT, rhs, start, stop, reads, pb):
            OP("pe", lambda e: e.matmul(out, lhsT, rhs, start=start, stop=stop, skip_group_check=True),
               reads=reads, writes=[pb])

        def TR(out, in_, ident, reads, pb):
            OP("pe", lambda e: e.transpose(out, in_, ident), reads=reads, writes=[pb])

        ppt = SB(es, "ppt", [128, NPP], F32)
        ident = SB(es, "ident", [128, 128], BF16)
        modT = SB(es, "modT", [128, 48], F32)
        GS = SB(es, "GS", [128, 32], F32)
        DMA("sp", ppt[:], pp_d, writes=[ppt.b])
        OP("pool", lambda e: e.memset(ident[:], 1.0), writes=[ident.b])
        OP("pool", lambda e: e.affine_select(out=ident[:], in_=ident[:], pattern=[[-1, 128]],
                                             compare_op=ALU.is_equal, fill=0.0, base=0, channel_multiplier=1),
           reads=[ident.b], writes=[ident.b])

        with ExitStack() as s0:
            wada = SB(s0, "wada", [128, 8, 6 * D], BF16)
            wada_b = [Buf("wada%d" % k) for k in range(8)]
            cact = SB(s0, "cact", [128, 8], BF16)
            crep = SB(s0, "crep", [128, 8, 128], BF16)
            onesf = SB(s0, "onesf", [1, 128], F32)
            brow = SB(s0, "brow", [1, 2048], F32)
            gtbc = SB(s0, "gtbc", [128, 2048], F32)
            wadaB_b = [Buf("wadaB%d" % k) for k in range(8)]
            for kc in range(8):
                DMA("pool", wada[:, kc, 0:2 * D], wada_d[kc * 128:(kc + 1) * 128, 0:2 * D], writes=[wada_b[kc]])
            for kc in range(8):
                DMA("pool", wada[:, kc, 2 * D:6 * D], wada_d[kc * 128:(kc + 1) * 128, 2 * D:6 * D], writes=[wadaB_b[kc]])
            DMA("sp", brow[:], brow_d, writes=[brow.b])
            OP("dve", lambda e: e.memset(onesf[:], 1.0), writes=[onesf.b])
            OP("act", lambda e: e.activation(out=cact[:], in_=ppt[:, PP_C:PP_C + 8], func=AF.Silu),
               reads=[ppt.b], writes=[cact.b])
            for kc in range(8):
                OP("dve", lambda e, kc=kc: e.tensor_copy(out=crep[:, kc, :],
                                                         in_=cact[:, kc:kc + 1].broadcast_to([128, 128])),
                   reads=[cact.b], writes=[crep.b])
            pa, pab = PS(hold=True)
            for j in range(16):
                for kc in range(8):
                    MM(pa[:, j:j + 1], wada[:, kc, j * 128:(j + 1) * 128], cact[:, kc:kc + 1],
                       kc == 0, kc == 7, [wada_b[kc], cact.b], pab)
            OP("dve", lambda e: e.tensor_tensor(out=modT[:, 0:16], in0=pa[:, 0:16], in1=ppt[:, PP_BADA:PP_BADA + 16],
                                                op=ALU.add), reads=[pab, ppt.b], writes=[modT.b])
            PSREL(pa)
            OP("dve", lambda e: e.scalar_tensor_tensor(out=GS[:, 0:8], in0=modT[:, 8:16], scalar=1.0,
                                                       in1=ppt[:, PP_G1:PP_G1 + 8], op0=ALU.add, op1=ALU.mult),
               reads=[modT.b, ppt.b], writes=[GS.b])
            OP("dve", lambda e: e.tensor_copy(out=GS[:, 8:16], in_=modT[:, 0:8]), reads=[modT.b], writes=[GS.b])

            def mod_part2():
                GS2 = Buf("GS2")
                pa2_, pa2b_ = PS(hold=True)
                for j in range(16, 48):
                    for kc in range(8):
                        MM(pa2_[:, j:j + 1], wada[:, kc, j * 128:(j + 1) * 128], cact[:, kc:kc + 1],
                           kc == 0, kc == 7, [wadaB_b[kc], cact.b], pa2b_)
                OP("dve", lambda e: e.tensor_tensor(out=modT[:, 16:48], in0=pa2_[:, 16:48], in1=ppt[:, PP_BADA + 16:PP_BADA + 48],
                                                    op=ALU.add), reads=[pa2b_, ppt.b], writes=[modT.b])
                PSREL(pa2_)
                OP("dve", lambda e: e.scalar_tensor_tensor(out=GS[:, 16:24], in0=modT[:, 32:40], scalar=1.0,
                                                           in1=ppt[:, PP_G2:PP_G2 + 8], op0=ALU.add, op1=ALU.mult),
                   reads=[modT.b, ppt.b], writes=[GS.b])
                OP("dve", lambda e: e.tensor_copy(out=GS[:, 24:32], in_=modT[:, 24:32]), reads=[modT.b], writes=[GS.b])
                for part, col0 in enumerate((2 * D, 5 * D)):
                    for half in range(2):
                        pg, pgb = PS(hold=True)
                        for kc in range(8):
                            MM(pg[:, :], crep[:, kc, :], wada[:, kc, col0 + half * 512: col0 + (half + 1) * 512],
                               kc == 0, False, [crep.b, wadaB_b[kc]], pgb)
                        o = part * 1024 + half * 512
                        MM(pg[:, :], onesf[0:1, :], brow[0:1, o:o + 512], False, True, [onesf.b, brow.b], pgb)
                        OP("act", lambda e, pg=pg, o=o: e.copy(out=gtbc[:, o:o + 512], in_=pg[:, :]),
                           reads=[pgb], writes=[gtbc.b])
                        PSREL(pg)
                DMA("sp", gt_d, gtbc[:], reads=[gtbc.b], writes=[B_gt])

            xb = [SB(s0, "xb%d" % i, [128, 4, D], F32) for i in range(2)]
            xn = [SB(s0, "xn%d" % i, [128, 4, D], BF16) for i in range(2)]
            hTs = [SB(s0, "hTs%d" % i, [128, 8, 512], BF16) for i in range(2)]
            junk = SB(s0, "junk0", [128, D], BF16)
            ssq = [SB(s0, "ssq%d" % i, [128, 4], F32) for i in range(2)]
            rstd = [SB(s0, "rstd%d" % i, [128, 4], F32) for i in range(2)]
            def load_xt(i):
                if i < NT:
                    DMA("sp", xb[i % 2][:], x_d[i * 512:(i + 1) * 512, :].rearrange("(s p) d -> p s d", p=128), writes=[xb[i % 2].b])

            xn_b = [[Buf("xn%d_%d" % (i_, s_)) for s_ in range(4)] for i_ in range(2)]
            ht_b = [[Buf("ht%d_%d" % (i_, k_)) for k_ in range(8)] for i_ in range(2)]

            def norm_a(i):
                xt, xnt, sq, rs = xb[i % 2], xn[i % 2], ssq[i % 2], rstd[i % 2]
                for s in range(4):
                    OP("act", lambda e, s=s: e.activation(out=junk[:], in_=xt[:, s, :], func=AF.Square, accum_out=sq[:, s:s + 1]),
                       reads=[xt.b], writes=[junk.b, sq.b])
                OP("dve", lambda e: e.tensor_scalar(out=rs[:], in0=sq[:], scalar1=1.0 / D, scalar2=NORM_EPS,
                                                    op0=ALU.mult, op1=ALU.add), reads=[sq.b], writes=[rs.b])
                OP("act", lambda e: e.activation(out=rs[:], in_=rs[:], func=AF.Sqrt), reads=[rs.b], writes=[rs.b])
                OP("dve", lambda e: e.reciprocal(out=rs[:], in_=rs[:]), reads=[rs.b], writes=[rs.b])
                for s in range(4):
                    if s % 2 == 0:
                        OP("dve", lambda e, s=s: e.tensor_scalar(out=xnt[:, s, :], in0=xt[:, s, :],
                                                                 scalar1=rs[:, s:s + 1], scalar2=None, op0=ALU.mult),
                           reads=[xt.b, rs.b], writes=[xn_b[i % 2][s]])
                    else:
                        OP("act", lambda e, s=s: e.activation(out=xnt[:, s, :], in_=xt[:, s, :], func=AF.Copy, scale=rs[:, s:s + 1]),
                           reads=[xt.b, rs.b], writes=[xn_b[i % 2][s]])

            def norm_b(i):
                xnt, ht = xn[i % 2], hTs[i % 2]
                for kc in range(8):
                    p, pb = PS()
                    pv = p.bitcast(BF16)
                    for s in range(4):
                        TR(pv[:, s * 128:(s + 1) * 128], xnt[:, s, kc * 128:(kc + 1) * 128], ident[:], [xn_b[i % 2][s], ident.b], pb)
                    if kc % 2 == 0:
                        OP("dve", lambda e, kc=kc, pv=pv: e.tensor_scalar(
                            out=ht[:, kc, :], in0=pv[:, 0:512], scalar1=GS[:, kc:kc + 1], scalar2=GS[:, 8 + kc:9 + kc],
                            op0=ALU.mult, op1=ALU.add), reads=[pb, GS.b], writes=[ht_b[i % 2][kc]])
                    else:
                        OP("act", lambda e, kc=kc, pv=pv: e.activation(
                            out=ht[:, kc, :], in_=pv[:, 0:512], func=AF.Identity, scale=GS[:, kc:kc + 1],
                            bias=GS[:, 8 + kc:9 + kc]), reads=[pb, GS.b], writes=[ht_b[i % 2][kc]])
                DMA("sp", hT_d[:, :, i * 512:(i + 1) * 512], ht[:], reads=ht_b[i % 2], writes=[B_hT])

            load_xt(0)
            load_xt(1)
            norm_a(0)
            for i in range(NT):
                if i + 1 < NT:
                    norm_a(i + 1)
                load_xt(i + 2)
                norm_b(i)
            mod_part2()


        sy.barrier()
        try:
          with ExitStack() as sr:
              WA = SB(sr, "WA", [128, 8, NRW], BF16)
              WB = SB(sr, "WB", [128, 8, NRW], BF16)
              wdu = SB(sr, "wdu", [64, 512], BF16)
              wiu = SB(sr, "wiu", [128, 512], BF16)
              wgu = SB(sr, "wgu", [128, 512], BF16)
              lnw = SB(sr, "lnw", [128, 1024], F32)
              DMA("pool", wdu[:], wdu_d, writes=[wdu.b])
              DMA("pool", wiu[64:128, :], wiu_d, writes=[wiu.b])
              DMA("pool", wgu[:], wgu_d, writes=[wgu.b])
              DMA("sp", lnw[:], bc_d[:, BC_LW:BC_LW + 1024], writes=[lnw.b])
              with ExitStack() as sw:
                  mub = SB(sw, "mub", [128, NRW], F32)
                  omub = SB(sw, "omub", [128, NRW], F32)
                  wst = [SB(sw, "wst%d" % i, [128, NRW], F32) for i in range(2)]
                  DMA("sp", mub[:], bc_d[:, BC_MU:BC_MU + NRW], writes=[mub.b])
                  OP("dve", lambda e: e.tensor_scalar(out=omub[:], in0=mub[:], scalar1=-1.0, scalar2=1.0,
                                                      op0=ALU.mult, op1=ALU.add), reads=[mub.b], writes=[omub.b])
                  segs = [(0, 512, 0), (576, 1088, 512), (1088, 1600, 1024), (512, 576, 1536), (1600, 1664, 1600),
                          (1664, 1792, 1664)]
                  seg_b = [[Buf("wseg%d_%d" % (i_, j_)) for j_ in range(len(segs))] for i_ in range(2)]
                  for kc in range(8):
                      w = wst[kc % 2]
                      for j_, (a, b_, o) in enumerate(segs):
                          DMA("sp", w[:, o:o + (b_ - a)], win_d[kc * 128:(kc + 1) * 128, a:b_], reads=[], writes=[seg_b[kc % 2][j_]])
                      OP("dve", lambda e, kc=kc, w=w: e.tensor_tensor(out=WB[:, kc, :], in0=w[:], in1=mub[:], op=ALU.mult),
                         reads=seg_b[kc % 2] + [mub.b], writes=[WB.b])
                      OP("dve", lambda e, kc=kc, w=w: e.tensor_tensor(out=WA[:, kc, :], in0=w[:], in1=omub[:], op=ALU.mult),
                         reads=seg_b[kc % 2] + [omub.b], writes=[WA.b])
              sy.barrier()
              CP('w')

              blk1 = SB(sr, "blk1", [128, 128], F32)
              hsel = SB(sr, "hsel", [128, 2], BF16)
              mskX = SB(sr, "mskX", [128, 4, 2, 64], F32)
              mskL = SB(sr, "mskL", [128, 8, 64], F32)
              i64 = SB(sr, "i64", [128, 64], F32)
              scm = SB(sr, "scm", [128, 512], F32)
              omix = SB(sr, "omix", [128, 4], F32)
              OP("pool", lambda e: e.memset(blk1[:], 0.0), writes=[blk1.b])
              OP("pool", lambda e: e.memset(blk1[0:64, 0:64], 1.0), writes=[blk1.b])
              OP("pool", lambda e: e.memset(blk1[64:128, 64:128], 1.0), writes=[blk1.b])
              OP("pool", lambda e: e.memset(hsel[:], 0.0), writes=[hsel.b])
              OP("pool", lambda e: e.memset(hsel[0:64, 0:1], 1.0), writes=[hsel.b])
              OP("pool", lambda e: e.memset(hsel[64:128, 1:2], 1.0), writes=[hsel.b])
              OP("pool", lambda e: e.memset(mskX[:], 1.0), writes=[mskX.b])
              OP("pool", lambda e: e.memset(mskL[:], 1.0), writes=[mskL.b])
              OP("pool", lambda e: e.memset(i64[:], 1.0), writes=[i64.b])
              for hf in range(2):
                  ps_ = slice(hf * 64, (hf + 1) * 64)
                  OP("pool", lambda e, ps_=ps_: e.affine_select(out=mskX[ps_], in_=mskX[ps_], pattern=[[0, 4], [1, 2], [1, 64]],
                                                                compare_op=ALU.is_gt, fill=0.0, base=0, channel_multiplier=-1),
                     reads=[mskX.b], writes=[mskX.b])
                  OP("pool", lambda e, ps_=ps_: e.affine_select(out=mskL[ps_], in_=mskL[ps_], pattern=[[0, 8], [-1, 64]],
                                                                compare_op=ALU.is_gt, fill=0.0, base=0, channel_multiplier=1),
                     reads=[mskL.b], writes=[mskL.b])
                  OP("pool", lambda e, ps_=ps_: e.affine_select(out=i64[ps_], in_=i64[ps_], pattern=[[-1, 64]],
                                                                compare_op=ALU.is_equal, fill=0.0, base=0, channel_multiplier=1),
                     reads=[i64.b], writes=[i64.b])
              OP("pool", lambda e: e.memset(scm[:], 1.0), writes=[scm.b])
              OP("pool", lambda e: e.memset(scm[:].rearrange("p (a b) -> p a b", b=64)[:, :, 0:1], 0.0), writes=[scm.b])
              OP("dve", lambda e: e.tensor_scalar(out=omix[:], in0=ppt[:, PP_MIX:PP_MIX + 4], scalar1=-1.0, scalar2=1.0,
                                                  op0=ALU.mult, op1=ALU.add), reads=[ppt.b], writes=[omix.b])

              CP('c')

              def bc4(col0):
                  return ppt[:, col0:col0 + 4].unsqueeze(2).broadcast_to([128, 4, 128])

              Hs = SB(sr, "Hs", [128, 4, 64], F32)
              Hb2 = [SB(sr, "Hb%d" % i_, [128, 4, 64], BF16) for i_ in range(2)]
              OP("dve", lambda e: e.memset(Hs[:], 0.0), writes=[Hs.b])
              OP("dve", lambda e: e.memset(Hb2[0][:], 0.0), writes=[Hb2[0].b])

              def DB(name, shape, dt):
                  return [SB(sr, "%s_%d" % (name, i_), shape, dt) for i_ in range(2)]
              hcs = DB("hc", [128, 8, 128], BF16)
              hps = DB("hp", [128, 8, 128], BF16)
              rk2 = DB("rk", [128, 8, 128], F32)
              tw2 = DB("tw", [64, 128], BF16)
              adb2 = DB("adb", [128, 128], BF16)
              sgd2 = DB("sgd", [128, 128], BF16)
              Vf2 = [SB(sr, "Vf_%d" % i_, [128, 512], F32) for i_ in range(4)]
              Vb2 = [SB(sr, "Vb_%d" % i_, [128, 512], BF16) for i_ in range(4)]

              def TB(name, shape, dt):
                  return [SB(sr, "%s_%d" % (name, i_), shape, dt) for i_ in range(3)]
              sw_ = SB(sr, "sw_", [128, 512], F32)
              aic = SB(sr, "aic", [128, 4, 128], F32)
              gsb2 = TB("gsb", [128, 512], F32)
              ld = SB(sr, "ld", [128, 512], F32)
              lP = SB(sr, "lP", [128, 512], F32)
              lPx = SB(sr, "lPx", [128, 512], F32)
              Pc2 = TB("Pc", [128, 4, 128], F32)
              Pinv = SB(sr, "Pinv", [128, 4, 128], F32)
              Pex = SB(sr, "Pex", [128, 4, 128], F32)
              ksc = SB(sr, "ksc", [128, 4, 128], F32)
              ksq = SB(sr, "ksq", [128, 4, 128], F32)
              rn = SB(sr, "rn", [128, 4, 128], F32)
              kk = SB(sr, "kk", [128, 4, 128], F32)
              t1 = SB(sr, "t1", [128, 4, 128], F32)
              kmod = SB(sr, "kmod", [128, 4, 128], F32)
              t2 = SB(sr, "t2", [128, 4, 128], F32)
              AR2 = TB("AR", [128, 4, 2, 2, 64], BF16)
              bT = SB(sr, "bT", [128, 4, 128], BF16)
              kT = SB(sr, "kT", [128, 4, 128], BF16)
              rkb2 = TB("rkb", [128, 4, 128], BF16)
              bt2 = TB("bt", [128, 512], BF16)
              kt2 = TB("kt", [128, 512], BF16)
              MB2 = TB("MB", [128, 8, 2, 64], BF16)
              MK2 = TB("MK", [128, 8, 2, 64], BF16)
              Acur2 = [[SB(sr, "Acur%d_%d" % (j_, i), [128, 8, 64], BF16) for i in range(2)] for j_ in range(2)]
              Mcur2 = [[SB(sr, "Mcur%d_%d" % (j_, i), [128, 8, 64], BF16) for i in range(2)] for j_ in range(2)]
              Xc2 = [[SB(sr, "Xc%d_%d" % (j_, i), [128, 8, 64], BF16) for i in range(2)] for j_ in range(2)]
              W0sb2 = TB("W0sb", [128, 512], F32)
              Wsb = SB(sr, "Wsb", [128, 512], BF16)
              Usb = SB(sr, "Usb", [128, 512], BF16)
              Htmp = SB(sr, "Htmp", [128, 4, 64], F32)
              yf = SB(sr, "yf", [128, 8, 64], F32)
              ysq = SB(sr, "ysq", [128, 8, 64], F32)
              st = SB(sr, "st", [128, 32], F32)
              bs = SB(sr, "bs", [128, 8], F32)
              yg = SB(sr, "yg", [128, 512], BF16)
              yaTs = [SB(sr, "yaTs%d" % i, [128, 4, 128], BF16) for i in range(2)]

              def hsl(h):
                  return h // 2, slice((h % 2) * 64, (h % 2) * 64 + 64)
              fl = lambda t_: t_[:].rearrange("p a t -> p (a t)")
              v4 = lambda t_: t_[:].rearrange("p a (c t) -> p a c t", c=2)
              par = lambda ap_, e_: ap_.rearrange("p (a e v) -> p e a v", e=2, v=64)[:, e_]
              b8 = lambda a_: a_.unsqueeze(2).broadcast_to([128, 8, 64])
              Xfin = {}

              def phaseP(n):
                  d = n % 2
                  hh, hold, hp = hcs[d], hcs[1 - d], hps[d]
                  rk, tw, adb, sgd = rk2[d], tw2[d], adb2[d], sgd2[d]
                  Vf, Vb = Vf2[n % 4], Vb2[n % 4]
                  DMA("sp", hh[:], hT_d[:, :, n * 128:(n + 1) * 128], writes=[hh.b])
                  if n == 0:
                      OP("pool", lambda e: e.memset(hp[:, :, 0:1], 0.0), writes=[hp.b])
                      DMA("sp", hp[:, :, 1:128], hT_d[:, :, 0:127], writes=[hp.b])
                  else:
                      DMA("sp", hp[:], hT_d[:, :, n * 128 - 1:(n + 1) * 128 - 1], writes=[hp.b])
                  yield

                  def fproj(pt, ptb, slot, col0):
                      for i_ in range(16):
                          kc, sh = i_ % 8, i_ // 8
                          W = WA if sh == 0 else WB
                          hx = hh if sh == 0 else hp
                          MM(pt[:, slot * 128:(slot + 1) * 128], W[:, kc, col0:col0 + 128], hx[:, kc, :], i_ == 0, i_ == 15,
                             [W.b, hx.b], ptb)
                  pr, prb = PS(hold=True)
                  for c4 in range(4):
                      fproj(pr, prb, c4, c4 * 128)
                      if c4 % 2 == 1:
                          yield
                  OP("act", lambda e: e.copy(out=rk[:, 0:4, :], in_=pr[:, :].rearrange("p (a t) -> p a t", a=4)), reads=[prb], writes=[rk.b])
                  PSREL(pr)
                  pk, pkb = PS(hold=True)
                  for c4 in range(4):
                      fproj(pk, pkb, c4, 512 + c4 * 128)
                      if c4 % 2 == 1:
                          yield
                  OP("dve", lambda e: e.tensor_scalar(out=rk[:, 4:8, :], in0=pk[:, :].rearrange("p (a t) -> p a t", a=4), scalar1=1.0,
                                                      scalar2=None, op0=ALU.mult), reads=[pkb], writes=[rk.b])
                  PSREL(pk)
                  pw, pwb = PS(hold=True)
                  fproj(pw, pwb, 0, 1536)
                  yield
                  fproj(pw, pwb, 1, 1664)
                  OP("act", lambda e: e.activation(out=tw[:], in_=pw[0:64, 0:128], func=AF.Tanh), reads=[pwb], writes=[tw.b])
                  OP("dve", lambda e: e.tensor_scalar(out=adb[64:128, :], in0=pw[64:128, 0:128], scalar1=1.0, scalar2=None, op0=ALU.mult),
                     reads=[pwb], writes=[adb.b])
                  OP("act", lambda e: e.activation(out=sgd[:], in_=pw[:, 128:256], func=AF.Sigmoid), reads=[pwb], writes=[sgd.b])
                  PSREL(pw)
                  yield
                  pv_, pvb = PS(hold=True)
                  for i_ in range(16):
                      kc, sh = i_ % 8, i_ // 8
                      W = WA if sh == 0 else WB
                      hx = hh if sh == 0 else hp
                      MM(pv_[:, :], hx[:, kc, :], W[:, kc, 1024:1536], i_ == 0, i_ == 15, [W.b, hx.b], pvb)
                      if i_ == 7:
                          yield
                  OP("act", lambda e: e.copy(out=Vf[:], in_=pv_[:, :]), reads=[pvb], writes=[Vf.b])
                  OP("dve", lambda e: e.tensor_scalar(out=Vb[:], in0=pv_[:, :], scalar1=1.0, scalar2=None, op0=ALU.mult), reads=[pvb], writes=[Vb.b])
                  PSREL(pv_)

              def phaseA(n):
                  d = n % 2
                  t3 = n % 3
                  rk, tw, adb, sgd = rk2[d], tw2[d], adb2[d], sgd2[d]
                  Vf, Vb = Vf2[n % 4], Vb2[n % 4]
                  gsb, Pc, AR, rkb, bt, kt, MB, MK, W0sb = (gsb2[t3], Pc2[t3], AR2[t3], rkb2[t3], bt2[t3],
                                                           kt2[t3], MB2[t3], MK2[t3], W0sb2[t3])
                  Xc = Xc2[d]
                  Acur, Mcur = Acur2[d], Mcur2[d]
                  r4 = rk[:, 0:4, :]
                  k4 = rk[:, 4:8, :]
                  pz, pzb = PS(hold=True)
                  pa_, pab_ = PS(hold=True)
                  pg_, pgb_ = PS(hold=True)
                  for c4 in range(4):
                      MM(pz[:, c4 * 128:(c4 + 1) * 128], wdu[0:64, c4 * 128:(c4 + 1) * 128], tw[0:64, :], True, True, [wdu.b, tw.b], pzb)
                  for c4 in range(4):
                      MM(pa_[:, c4 * 128:(c4 + 1) * 128], wiu[64:128, c4 * 128:(c4 + 1) * 128], adb[64:128, :], True, True, [wiu.b, adb.b], pab_)
                  MM(pg_[:, :], sgd[:, :], wgu[:, :], True, True, [sgd.b, wgu.b], pgb_)
                  for c4 in range(4):
                      OP("act", lambda e, c4=c4: e.activation(out=sw_[:, c4 * 128:(c4 + 1) * 128], in_=pz[:, c4 * 128:(c4 + 1) * 128],
                                                              func=AF.Sigmoid, bias=ppt[:, PP_DB + c4:PP_DB + c4 + 1]),
                         reads=[pzb, ppt.b], writes=[sw_.b])
                  PSREL(pz)
                  for c4 in range(4):
                      OP("act", lambda e, c4=c4: e.activation(out=aic[:, c4, :], in_=pa_[:, c4 * 128:(c4 + 1) * 128],
                                                              func=AF.Sigmoid, bias=ppt[:, PP_IB + c4:PP_IB + c4 + 1]),
                         reads=[pab_, ppt.b], writes=[aic.b])
                  PSREL(pa_)
                  OP("act", lambda e: e.copy(out=gsb[:], in_=pg_[:, :]), reads=[pgb_], writes=[gsb.b])
                  PSREL(pg_)
                  yield
                  OP("dve", lambda e: e.tensor_tensor(out=ksc[:], in0=k4, in1=bc4(PP_KS), op=ALU.mult), reads=[rk.b, ppt.b], writes=[ksc.b])
                  OP("act", lambda e: e.activation(out=fl(ksq), in_=fl(ksc), func=AF.Square), reads=[ksc.b], writes=[ksq.b])
                  for c4 in range(4):
                      OP("act", lambda e, c4=c4: e.activation(out=t1[:, c4, :], in_=aic[:, c4, :], func=AF.Identity,
                                                              scale=ppt[:, PP_MIX + c4:PP_MIX + c4 + 1], bias=omix[:, c4:c4 + 1]),
                         reads=[aic.b, ppt.b, omix.b], writes=[t1.b])
                  OP("dve", lambda e: e.tensor_tensor(out=kmod[:], in0=k4, in1=t1[:], op=ALU.mult), reads=[rk.b, t1.b], writes=[kmod.b])
                  yield
                  OP("dve", lambda e: e.tensor_scalar(out=ld[:], in0=sw_[:], scalar1=-EXPM05, scalar2=None, op0=ALU.mult), reads=[sw_.b], writes=[ld.b])
                  OP("dve", lambda e: e.tensor_tensor_scan(out=lP[:], data0=scm[:], data1=ld[:], initial=0.0, op0=ALU.mult, op1=ALU.add),
                     reads=[scm.b, ld.b], writes=[lP.b])
                  OP("dve", lambda e: e.tensor_tensor(out=lPx[:], in0=lP[:], in1=ld[:], op=ALU.subtract), reads=[lP.b, ld.b], writes=[lPx.b])
                  pq, pqb = PS(hold=True)
                  MM(pq[:, :], blk1[:, :], fl(ksq), True, True, [blk1.b, ksq.b], pqb)
                  OP("act", lambda e: e.activation(out=fl(Pc), in_=lP[:], func=AF.Exp), reads=[lP.b], writes=[Pc.b])
                  OP("act", lambda e: e.activation(out=fl(Pinv), in_=lP[:], func=AF.Exp, scale=-1.0), reads=[lP.b], writes=[Pinv.b])
                  OP("act", lambda e: e.activation(out=fl(Pex), in_=lPx[:], func=AF.Exp), reads=[lPx.b], writes=[Pex.b])
                  OP("act", lambda e: e.activation(out=fl(rn), in_=pq[:, :], func=AF.Ln), reads=[pqb], writes=[rn.b])
                  PSREL(pq)
                  OP("act", lambda e: e.activation(out=fl(rn), in_=fl(rn), func=AF.Exp, scale=-0.5), reads=[rn.b], writes=[rn.b])
                  OP("dve", lambda e: e.tensor_tensor(out=kk[:], in0=ksc[:], in1=rn[:], op=ALU.mult), reads=[ksc.b, rn.b], writes=[kk.b])
                  yield
                  OP("dve", lambda e: e.scalar_tensor_tensor(out=AR[:, :, :, 0, :], in0=v4(kk), scalar=-1.0, in1=v4(Pex), op0=ALU.mult, op1=ALU.mult),
                     reads=[kk.b, Pex.b], writes=[AR.b])
                  OP("pool", lambda e: e.tensor_tensor(out=AR[:, :, :, 1, :], in0=r4.rearrange("p a (c t) -> p a c t", c=2), in1=v4(Pc), op=ALU.mult),
                     reads=[rk.b, Pc.b, AR.b], writes=[AR.b])
                  OP("dve", lambda e: e.tensor_tensor(out=t2[:], in0=kk[:], in1=aic[:], op=ALU.mult), reads=[kk.b, aic.b], writes=[t2.b])
                  OP("dve", lambda e: e.tensor_tensor(out=bT[:], in0=t2[:], in1=Pinv[:], op=ALU.mult), reads=[t2.b, Pinv.b], writes=[bT.b])
                  OP("pool", lambda e: e.tensor_tensor(out=kT[:], in0=kmod[:], in1=Pinv[:], op=ALU.mult), reads=[kmod.b, Pinv.b], writes=[kT.b])
                  OP("pool", lambda e: e.tensor_tensor(out=t1[:], in0=r4, in1=kmod[:], op=ALU.mult), reads=[rk.b, kmod.b, t1.b], writes=[t1.b])
                  OP("pool", lambda e: e.tensor_tensor(out=rkb[:], in0=t1[:], in1=bc4(PP_RB), op=ALU.mult), reads=[t1.b, ppt.b], writes=[rkb.b])
                  yield
                  yield
                  yield
                  ptb_, ptbb = PS(hold=True)
                  ptk_, ptkb = PS(hold=True)
                  ptbv = ptb_.bitcast(BF16)
                  ptkv = ptk_.bitcast(BF16)
                  for c4 in range(4):
                      TR(ptbv[:, c4 * 128:(c4 + 1) * 128], bT[:, c4, :], ident[:], [bT.b, ident.b], ptbb)
                  for c4 in range(4):
                      TR(ptkv[:, c4 * 128:(c4 + 1) * 128], kT[:, c4, :], ident[:], [kT.b, ident.b], ptkb)
                  OP("dve", lambda e: e.tensor_scalar(out=bt[:], in0=ptbv[:, 0:512], scalar1=1.0, scalar2=None, op0=ALU.mult), reads=[ptbb], writes=[bt.b])
                  OP("act", lambda e: e.copy(out=kt[:], in_=ptkv[:, 0:512]), reads=[ptkb], writes=[kt.b])
                  PSREL(ptb_)
                  PSREL(ptk_)
                  yield
                  mX3 = mskX[:].rearrange("p a k t -> p a (k t)")
                  A0, M0, X0 = Acur[0], Mcur[0], Xc[0]
                  for e_ in range(2):
                      hs = slice(e_ * 64, e_ * 64 + 64)
                      px, pxb = PS(hold=True)
                      py, pyb = PS(hold=True)
                      pA, pAb = PS(hold=True)
                      for pair in range(4):
                          for c in range(2):
                              cs = slice(c * 64, (c + 1) * 64)
                              mov = AR[hs, pair, c, :, :].rearrange("p k t -> p (k t)")
                              MM(px[cs, pair * 128:(pair + 1) * 128], bT[hs, pair, cs], mov, True, True, [bT.b, AR.b], pxb)
                              MM(py[cs, pair * 128:(pair + 1) * 128], kT[hs, pair, cs], mov, True, True, [kT.b, AR.b], pyb)
                              MM(pA[cs, pair * 64:(pair + 1) * 64], AR[hs, pair, c, 0, :], bT[hs, pair, cs], True, True, [bT.b, AR.b], pAb)
                      OP("dve", lambda e: e.tensor_tensor(out=MB[:].rearrange("p (a e) k t -> p e a (k t)", e=2)[:, e_],
                                                          in0=px[:, :].rearrange("p (a x) -> p a x", a=4), in1=mX3, op=ALU.mult),
                         reads=[pxb, mskX.b], writes=[MB.b])
                      OP("dve", lambda e: e.tensor_tensor(out=MK[:].rearrange("p (a e) k t -> p e a (k t)", e=2)[:, e_],
                                                          in0=py[:, :].rearrange("p (a x) -> p a x", a=4), in1=mX3, op=ALU.mult),
                         reads=[pyb, mskX.b], writes=[MK.b])
                      OP("dve", lambda e: e.tensor_tensor(out=A0[:].rearrange("p (a e) t -> p e a t", e=2)[:, e_],
                                                          in0=pA[:, 0:256].rearrange("p (a t) -> p a t", a=4), in1=mskL[:, 0:4, :], op=ALU.mult),
                         reads=[pAb, mskL.b], writes=[A0.b])
                      PSREL(px)
                      PSREL(py)
                      PSREL(pA)
                      yield
                  OP("dve", lambda e: e.tensor_tensor(out=X0[:], in0=MB[:, :, 0, :], in1=i64[:].unsqueeze(1).broadcast_to([128, 8, 64]), op=ALU.add),
                     reads=[MB.b, i64.b], writes=[X0.b])
                  pW0, pW0b = PS(hold=True)
                  for c in range(2):
                      cs = slice(c * 64, (c + 1) * 64)
                      for h in range(8):
                          MM(pW0[cs, h * 64:(h + 1) * 64], MK[cs, h, 0, :], Vb[cs, h * 64:(h + 1) * 64], True, True, [MK.b, Vb.b], pW0b)
                  OP("act", lambda e: e.copy(out=W0sb[:], in_=pW0[:, :]), reads=[pW0b], writes=[W0sb.b])
                  PSREL(pW0)


              def phaseA2(n):
                  d = n % 2
                  MB = MB2[n % 3]
                  Xc = Xc2[d]
                  Acur, Mcur = Acur2[d], Mcur2[d]
                  ci = 0
                  xi = 0
                  for lvl in range(5):
                      Ac, Mc = Acur[ci], Mcur[ci]
                      An, Mn = Acur[1 - ci], Mcur[1 - ci]
                      if lvl == 0:
                          class _MV:
                              b = MB.b

                              def __getitem__(self, idx):
                                  return MB[idx[0], idx[1], 0, :]
                          Mc = _MV()
                      p2, p2b = PS(hold=True)
                      for h in range(8):
                          for c in range(2):
                              cs = slice(c * 64, (c + 1) * 64)
                              MM(p2[cs, h * 64:(h + 1) * 64], Mc[cs, h, :], Ac[cs, h, :], True, True, [Mc.b, Ac.b], p2b)
                      OP("act", lambda e: e.copy(out=An[:].rearrange("p a t -> p (a t)"), in_=p2[:, :]), reads=[p2b], writes=[An.b])
                      PSREL(p2)
                      if lvl < 4:
                          p3, p3b = PS(hold=True)
                          for h in range(8):
                              for c in range(2):
                                  cs = slice(c * 64, (c + 1) * 64)
                                  MM(p3[cs, h * 64:(h + 1) * 64], Ac[cs, h, :], Mc[cs, h, :], True, True, [Mc.b, Ac.b], p3b)
                          OP("dve", lambda e: e.tensor_scalar(out=Mn[:].rearrange("p a t -> p (a t)"), in0=p3[:, :], scalar1=1.0, scalar2=None,
                                                              op0=ALU.mult), reads=[p3b], writes=[Mn.b])
                          PSREL(p3)
                      yield
                      Xo, Xn = Xc[xi], Xc[1 - xi]
                      p4, p4b = PS(hold=True)
                      for h in range(8):
                          for c in range(2):
                              cs = slice(c * 64, (c + 1) * 64)
                              MM(p4[cs, h * 64:(h + 1) * 64], An[cs, h, :], Xo[cs, h, :], True, True, [An.b, Xo.b], p4b)
                      OP("dve", lambda e: e.tensor_tensor(out=Xn[:].rearrange("p a t -> p (a t)"), in0=p4[:, :],
                                                          in1=Xo[:].rearrange("p a t -> p (a t)"), op=ALU.add), reads=[p4b, Xo.b], writes=[Xn.b])
                      PSREL(p4)
                      ci = 1 - ci
                      xi = 1 - xi
                      yield
                  Xfin[n] = Xc[xi]

              def phaseB(n):
                  d = n % 2
                  t3 = n % 3
                  Vf, Vb, gsb, Pc, AR, rkb, bt, kt, MB, MK, W0sb = (Vf2[n % 4], Vb2[n % 4], gsb2[t3], Pc2[t3], AR2[t3], rkb2[t3], bt2[t3],
                                                                   kt2[t3], MB2[t3], MK2[t3], W0sb2[t3])
                  X = Xfin.pop(n)
                  yf2 = yf[:].rearrange("p a v -> p (a v)")
                  for c in range(2):
                      cs = slice(c * 64, (c + 1) * 64)
                      Hb = Hb2[c]
                      Hbn = Hb2[1 - c]
                      pWe = [PS(hold=True), PS(hold=True)]
                      for e_ in range(2):
                          hs = slice(e_ * 64, e_ * 64 + 64)
                          for pair in range(4):
                              MM(pWe[e_][0][cs, pair * 64:(pair + 1) * 64], AR[hs, pair, c, 0, :], Hb[hs, pair, :], True, True,
                                 [AR.b, Hb.b], pWe[e_][1])
                      for e_ in range(2):
                          OP("dve", lambda e, e_=e_: e.tensor_tensor(
                              out=par(Wsb[cs, :], e_), in0=pWe[e_][0][cs, 0:256].rearrange("p (a v) -> p a v", a=4),
                              in1=par(W0sb[cs, :], e_), op=ALU.add), reads=[pWe[e_][1], W0sb.b], writes=[Wsb.b])
                          PSREL(pWe[e_][0])
                      yield
                      pU, pUb = PS(hold=True)
                      for h in range(8):
                          MM(pU[cs, h * 64:(h + 1) * 64], X[cs, h, :], Wsb[cs, h * 64:(h + 1) * 64], True, True, [X.b, Wsb.b], pUb)
                      OP("act", lambda e: e.copy(out=Usb[cs, :], in_=pU[cs, :]), reads=[pUb], writes=[Usb.b])
                      PSREL(pU)
                      yield
                      pH, pHb = PS(hold=True)
                      for h in range(8):
                          pair, hs = hsl(h)
                          o_ = pH[hs, pair * 64:(pair + 1) * 64]
                          MM(o_, kt[cs, h * 64:(h + 1) * 64], Vb[cs, h * 64:(h + 1) * 64], True, False, [kt.b, Vb.b], pHb)
                          MM(o_, bt[cs, h * 64:(h + 1) * 64], Usb[cs, h * 64:(h + 1) * 64], False, True, [bt.b, Usb.b], pHb)
                      OP("dve", lambda e: e.tensor_tensor(out=Htmp[:].rearrange("p a v -> p (a v)"), in0=pH[:, 0:256],
                                                          in1=Hs[:].rearrange("p a v -> p (a v)"), op=ALU.add), reads=[pHb, Hs.b], writes=[Htmp.b])
                      PSREL(pH)
                      pcb = Pc[:].rearrange("p a (c t) -> p a c t", c=2)[:, :, c, 63:64].broadcast_to([128, 4, 64])
                      OP("dve", lambda e: e.tensor_tensor(out=Hbn[:], in0=Htmp[:], in1=pcb, op=ALU.mult), reads=[Htmp.b, Pc.b], writes=[Hbn.b])
                      OP("pool", lambda e: e.tensor_tensor(out=Hs[:], in0=Htmp[:], in1=pcb, op=ALU.mult), reads=[Htmp.b, Pc.b], writes=[Hs.b])
                      pYe = [PS(hold=True), PS(hold=True)]
                      for e_ in range(2):
                          hs = slice(e_ * 64, e_ * 64 + 64)
                          for pair in range(4):
                              MM(pYe[e_][0][cs, pair * 64:(pair + 1) * 64], AR[hs, pair, c, 1, :], Hb[hs, pair, :], True, True,
                                 [AR.b, Hb.b], pYe[e_][1])
                      pYc, pYcb = PS(hold=True)
                      for h in range(8):
                          o_ = pYc[cs, h * 64:(h + 1) * 64]
                          MM(o_, MK[cs, h, 1, :], Vb[cs, h * 64:(h + 1) * 64], True, False, [MK.b, Vb.b], pYcb)
                          MM(o_, MB[cs, h, 1, :], Usb[cs, h * 64:(h + 1) * 64], False, True, [MB.b, Usb.b], pYcb)
                      OP("act", lambda e: e.copy(out=yf2[cs, :], in_=pYc[cs, :]), reads=[pYcb], writes=[yf.b])
                      PSREL(pYc)
                      for e_ in range(2):
                          OP("dve", lambda e, e_=e_: e.tensor_tensor(
                              out=par(yf2[cs, :], e_), in0=pYe[e_][0][cs, 0:256].rearrange("p (a v) -> p a v", a=4),
                              in1=par(yf2[cs, :], e_), op=ALU.add), reads=[pYe[e_][1], yf.b], writes=[yf.b])
                          PSREL(pYe[e_][0])
                      yield
                  OP("dve", lambda e: e.tensor_reduce(out=st[:, 0:8], in_=yf[:], axis=AX.X, op=ALU.add), reads=[yf.b], writes=[st.b])
                  OP("act", lambda e: e.activation(out=ysq[:].rearrange("p a v -> p (a v)"), in_=yf2, func=AF.Square), reads=[yf.b], writes=[ysq.b])
                  OP("dve", lambda e: e.tensor_reduce(out=st[:, 8:16], in_=ysq[:], axis=AX.X, op=ALU.add), reads=[ysq.b], writes=[st.b])
                  OP("dve", lambda e: e.tensor_scalar(out=st[:, 16:24], in0=st[:, 0:8], scalar1=1.0 / 64, scalar2=None, op0=ALU.mult),
                     reads=[st.b], writes=[st.b])
                  OP("dve", lambda e: e.tensor_tensor(out=st[:, 24:32], in0=st[:, 16:24], in1=st[:, 16:24], op=ALU.mult), reads=[st.b], writes=[st.b])
                  OP("dve", lambda e: e.scalar_tensor_tensor(out=st[:, 24:32], in0=st[:, 8:16], scalar=1.0 / 64, in1=st[:, 24:32],
                                                             op0=ALU.mult, op1=ALU.subtract), reads=[st.b], writes=[st.b])
                  OP("dve", lambda e: e.tensor_scalar(out=st[:, 24:32], in0=st[:, 24:32], scalar1=GN_EPS, scalar2=None, op0=ALU.add),
                     reads=[st.b], writes=[st.b])
                  OP("act", lambda e: e.activation(out=st[:, 24:32], in_=st[:, 24:32], func=AF.Ln), reads=[st.b], writes=[st.b])
                  OP("act", lambda e: e.activation(out=st[:, 24:32], in_=st[:, 24:32], func=AF.Exp, scale=-0.5), reads=[st.b], writes=[st.b])
                  yield
                  OP("dve", lambda e: e.tensor_tensor(out=yf[:], in0=yf[:], in1=b8(st[:, 16:24]), op=ALU.subtract), reads=[yf.b, st.b], writes=[yf.b])
                  OP("dve", lambda e: e.tensor_tensor(out=yf[:], in0=yf[:], in1=b8(st[:, 24:32]), op=ALU.mult), reads=[yf.b, st.b], writes=[yf.b])
                  OP("pool", lambda e: e.tensor_tensor(out=yf2, in0=yf2, in1=lnw[:, 0:512], op=ALU.mult), reads=[yf.b, lnw.b], writes=[yf.b])
                  OP("pool", lambda e: e.tensor_tensor(out=yf2, in0=yf2, in1=lnw[:, 512:1024], op=ALU.add), reads=[yf.b, lnw.b], writes=[yf.b])
                  pb_, pbb = PS(hold=True)
                  for c4 in range(4):
                      MM(pb_[:, c4 * 2:c4 * 2 + 2], rkb[:, c4, :], hsel[:, :], True, True, [rkb.b, hsel.b], pbb)
                  OP("act", lambda e: e.copy(out=bs[:], in_=pb_[:, 0:8]), reads=[pbb], writes=[bs.b])
                  PSREL(pb_)
                  yield
                  OP("dve", lambda e: e.tensor_tensor(out=ysq[:], in0=Vf[:].rearrange("p (a v) -> p a v", a=8), in1=b8(bs[:, :]), op=ALU.mult),
                     reads=[Vf.b, bs.b, ysq.b], writes=[ysq.b])
                  OP("pool", lambda e: e.tensor_tensor(out=yf[:], in0=yf[:], in1=ysq[:], op=ALU.add), reads=[yf.b, ysq.b], writes=[yf.b])
                  OP("dve", lambda e: e.tensor_tensor(out=yg[:], in0=yf2, in1=gsb[:], op=ALU.mult), reads=[yf.b, gsb.b], writes=[yg.b])
                  yield
                  pt_, ptb2 = PS(hold=True)
                  ptv = pt_.bitcast(BF16)
                  for c4 in range(4):
                      TR(ptv[:, c4 * 128:(c4 + 1) * 128], yg[:, c4 * 128:(c4 + 1) * 128], ident[:], [yg.b, ident.b], ptb2)
                  ya_ = yaTs[n % 2]
                  OP("act", lambda e: e.copy(out=ya_[:].rearrange("p a t -> p (a t)"), in_=ptv[:, 0:512]), reads=[ptb2], writes=[ya_.b])
                  PSREL(pt_)
                  DMA("sp", yaT_d[:, :, n * 128:(n + 1) * 128], ya_[:], reads=[ya_.b], writes=[B_yaT])

              dummy = PS(hold=True) if NDUMMY else None

              def run_rr(gens):
                  alive = [g is not None for g, _ in gens]
                  while any(alive):
                      for i_, (g, k_) in enumerate(gens):
                          for _ in range(k_):
                              if not alive[i_]:
                                  break
                              try:
                                  next(g)
                              except StopIteration:
                                  alive[i_] = False
                          for _ in range(NDUMMY):
                              MM(dummy[0][:, 0:128], ident[:, :], ident[:, :], True, True, [ident.b], dummy[1])

              gP = lambda n: phaseP(n) if n < NS else None
              gA1 = lambda n: phaseA(n) if n < NS else None
              gA2 = lambda n: phaseA2(n) if n < NS else None
              run_rr([(gP(0), 1)])
              run_rr([(gA1(0), RR[2]), (gP(1), RR[3])])
              run_rr([(gA2(0), RR[1]), (gA1(1), RR[2]), (gP(2), RR[3])])
              for n in range(NS):
                  streams = [(phaseB(n), RR[0]), (gA2(n + 1), RR[1]), (gA1(n + 2), RR[2]), (gP(n + 3), RR[3])]
                  run_rr([streams[k_] for k_ in ORDER])
              if dummy is not None:
                  PSREL(dummy[0])

        except _Stop:
            pass
        sy.barrier()

        CP('F')
        with ExitStack() as sf:
            Wf = SB(sf, "Wf", [128, 8, NFX], BF16)
            Wf_b = [Buf("Wf%d" % k) for k in range(8)]
            KT = SB(sf, "KT", [65, 8, S], BF16)
            Va = SB(sf, "Va", [128, NS, 8, 65], BF16)
            QT = [SB(sf, "QT%d" % i, [65, 8, 512], BF16) for i in range(2)]
            ncum = SB(sf, "ncum", [128, NS, 8], F32)
            PTs = [SB(sf, "PT%d" % i, [128, 512], BF16) for i in range(4)]
            hTf = [SB(sf, "hTf%d" % i, [128, 8, 512], BF16) for i in range(2)]
            tri = SB(sf, "tri", [128, 128], F32)
            onesq = SB(sf, "onesq", [128, 128], F32)
            trib = SB(sf, "trib", [128, 128], BF16)
            acc = SB(sf, "acc", [128, 8], F32)
            fbb = SB(sf, "fbb", [128, 8], F32)
            lf = SB(sf, "lf", [128, 8], F32)
            cq8 = SB(sf, "cq8", [8, 512], BF16)
            oT = [SB(sf, "oT%d" % i, [65, 512], F32) for i in range(4)]
            ybh = [SB(sf, "ybh%d" % i, [64, 512], BF16) for i in range(4)]
            for kc in range(8):
                DMA("pool", Wf[:, kc, :], win_d[kc * 128:(kc + 1) * 128, NRW:NRW + NFX], writes=[Wf_b[kc]])
            DMA("sp", fbb[:], bc_d[:, BC_FB:BC_FB + 8], writes=[fbb.b])
            OP("pool", lambda e: e.memset(tri[:], 1.0), writes=[tri.b])
            OP("pool", lambda e: e.affine_select(out=tri[:], in_=tri[:], pattern=[[1, 128]], compare_op=ALU.is_ge, fill=0.0,
                                                 base=0, channel_multiplier=-1), reads=[tri.b], writes=[tri.b])
            negm = SB(sf, "negm", [128, 128], F32)
            OP("pool", lambda e: e.memset(negm[:], 0.0), writes=[negm.b])
            OP("pool", lambda e: e.affine_select(out=negm[:], in_=negm[:], pattern=[[1, 128]], compare_op=ALU.is_ge, fill=-1.0e4,
                                                 base=0, channel_multiplier=-1), reads=[negm.b], writes=[negm.b])
            OP("pool", lambda e: e.memset(trib[:], 1.0), writes=[trib.b])
            OP("pool", lambda e: e.affine_select(out=trib[:], in_=trib[:], pattern=[[1, 128]], compare_op=ALU.is_ge, fill=0.0,
                                                 base=0, channel_multiplier=-1), reads=[trib.b], writes=[trib.b])
            OP("pool", lambda e: e.memset(onesq[:], 1.0), writes=[onesq.b])
            OP("dve", lambda e: e.memset(acc[:], 0.0), writes=[acc.b])
            lf4 = SB(sf, "lf4", [128, 4, 8], F32)
            acc5 = SB(sf, "acc5", [128, 5, 8], F32)
            cq8s = [SB(sf, "cq8_%d" % i_, [8, 512], BF16) for i_ in range(2)]
            qkst = [SB(sf, "qkst%d" % i_, [128, 512], BF16) for i_ in range(4)]
            PT6 = [SB(sf, "PTx%d" % i, [128, 512], BF16) for i in range(8)]
            OP("dve", lambda e: e.memset(acc5[:], 0.0), writes=[acc5.b])
            KT_b = [Buf("KTt%d" % i_) for i_ in range(NT)]
            Va_b = [Buf("Vat%d" % i_) for i_ in range(NT)]
            nc_b = [Buf("nct%d" % i_) for i_ in range(NT)]
            OP("dve", lambda e: e.memset(KT[64:65, :, :], 1.0), writes=KT_b)
            OP("dve", lambda e: e.memset(Va[:, :, :, 64:65], 1.0), writes=Va_b)
            LOOK = 6
            qrow_b = [[Buf("qrow%d_%d" % (i_, h_)) for h_ in range(8)] for i_ in range(2)]
            unit_ctr = [0]

            def front(i):
                ht = hTf[i % 2]
                qt = QT[i % 2]
                cq8_ = cq8s[i % 2]
                DMA("sp", ht[:], hT_d[:, :, i * 512:(i + 1) * 512], writes=[ht.b])
                yield
                pf, pfb = PS(hold=True)
                for sub in range(4):
                    ts_ = slice(sub * 128, (sub + 1) * 128)
                    for kc in range(8):
                        MM(pf[:, sub * 8:(sub + 1) * 8], ht[:, kc, ts_], Wf[:, kc, 1536:1544], kc == 0, kc == 7, [Wf_b[kc], ht.b], pfb)
                OP("dve", lambda e: e.tensor_tensor(out=lf4[:], in0=pf[:, 0:32].rearrange("p (a h) -> p a h", a=4),
                                                    in1=fbb[:].unsqueeze(1).broadcast_to([128, 4, 8]), op=ALU.add),
                   reads=[pfb, fbb.b], writes=[lf4.b])
                PSREL(pf)
                OP("act", lambda e: e.activation(out=lf4[:], in_=lf4[:], func=AF.Sigmoid), reads=[lf4.b], writes=[lf4.b])
                OP("act", lambda e: e.activation(out=lf4[:], in_=lf4[:], func=AF.Ln), reads=[lf4.b], writes=[lf4.b])
                if i > 0:
                    OP("dve", lambda e: e.tensor_scalar(out=acc5[:, 0, :], in0=acc5[:, 4, :], scalar1=1.0, scalar2=None, op0=ALU.mult),
                       reads=[acc5.b], writes=[acc5.b])
                for sub in range(4):
                    OP("dve", lambda e, sub=sub: e.tensor_tensor(out=acc5[:, sub + 1, :], in0=acc5[:, sub, :], in1=lf4[:, sub, :], op=ALU.add),
                       reads=[acc5.b, lf4.b], writes=[acc5.b])
                yield
                for pair in range(4):
                    for which in range(2):
                        col0 = which * 512 + pair * 128
                        pq_, pqb_ = PS(hold=True)
                        for kc in range(8):
                            MM(pq_[:, :], Wf[:, kc, col0:col0 + 128], ht[:, kc, :], kc == 0, kc == 7, [Wf_b[kc], ht.b], pqb_)
                        stg = qkst[(pair * 2 + which) % 4]
                        if which == 0:
                            dst_even = qt[0:64, 2 * pair, :]
                            dst_odd = qt[0:64, 2 * pair + 1, :]
                            dbuf = qt.b
                        else:
                            dst_even = KT[0:64, 2 * pair, i * 512:(i + 1) * 512]
                            dst_odd = KT[0:64, 2 * pair + 1, i * 512:(i + 1) * 512]
                            dbuf = KT_b[i]
                        OP("dve", lambda e: e.tensor_scalar(out=dst_even, in0=pq_[0:64, :], scalar1=1.0, scalar2=None, op0=ALU.mult),
                           reads=[pqb_], writes=[dbuf])
                        OP("act", lambda e: e.copy(out=stg[64:128, :], in_=pq_[64:128, :]), reads=[pqb_], writes=[stg.b])
                        PSREL(pq_)
                        DMA("sp", dst_odd, stg[64:128, :], reads=[stg.b], writes=[dbuf])
                        yield
                for sub in range(4):
                    g = i * 4 + sub
                    ts_ = slice(sub * 128, (sub + 1) * 128)
                    pv2, pv2b = PS(hold=True)
                    for kc in range(8):
                        MM(pv2[:, :], ht[:, kc, ts_], Wf[:, kc, 1024:1536], kc == 0, kc == 7, [Wf_b[kc], ht.b], pv2b)
                    OP("dve", lambda e: e.tensor_scalar(out=Va[:, g, :, 0:64], in0=pv2[:, :].rearrange("p (a v) -> p a v", a=8),
                                                        scalar1=1.0, scalar2=None, op0=ALU.mult), reads=[pv2b], writes=[Va_b[i]])
                    PSREL(pv2)
                    yield
                pc, pcb_ = PS(hold=True)
                pcT, pcTb = PS(hold=True)
                for sub in range(4):
                    ts_ = slice(sub * 128, (sub + 1) * 128)
                    MM(pc[:, sub * 8:(sub + 1) * 8], tri[:, :], lf4[:, sub, :], True, False, [tri.b, lf4.b], pcb_)
                    MM(pc[:, sub * 8:(sub + 1) * 8], onesq[:, :], acc5[:, sub, :], False, True, [onesq.b, acc5.b], pcb_)
                    MM(pcT[0:8, ts_], lf4[:, sub, :], tri[:, :], sub == 0, False, [tri.b, lf4.b], pcTb)
                    MM(pcT[0:8, ts_], acc5[:, sub, :], onesq[:, :], False, sub == 3, [onesq.b, acc5.b], pcTb)
                OP("dve", lambda e: e.tensor_scalar(out=ncum[:, i * 4:(i + 1) * 4, :], in0=pc[:, 0:32].rearrange("p (a h) -> p a h", a=4),
                                                    scalar1=-1.0, scalar2=None, op0=ALU.mult), reads=[pcb_], writes=[nc_b[i]])
                PSREL(pc)
                OP("dve", lambda e: e.tensor_scalar(out=cq8_[:], in0=pcT[0:8, :], scalar1=8.0, scalar2=None, op0=ALU.mult),
                   reads=[pcTb], writes=[cq8_.b])
                PSREL(pcT)
                for h in range(8):
                    DMA("sp", qt[64:65, h, :], cq8_[h:h + 1, :], reads=[cq8_.b], writes=[qrow_b[i % 2][h]])

            def attention(i):
                qt = QT[i % 2]
                nkb = 4 * i + 4
                tasks = [(h, j) for h in range(8) for j in range(nkb)]
                NTK = len(tasks)
                pts = {}
                pos = {}
                tails = []

                def emit_S(t):
                    h, j = tasks[t]
                    q0 = max(0, j - 4 * i) * 128
                    ps_, psb_ = PS()
                    MM(ps_[:, q0:512], KT[0:65, h, j * 128:(j + 1) * 128], qt[0:65, h, q0:512], True, True,
                       [KT_b[j // 4], qt.b, qrow_b[i % 2][h]], psb_)
                    PT = PT6[t % 8]
                    if j >= 4 * i:
                        OP("dve", lambda e: e.tensor_tensor(out=ps_[:, q0:q0 + 128], in0=ps_[:, q0:q0 + 128], in1=negm[:], op=ALU.add),
                           reads=[psb_, negm.b], writes=[psb_])
                    OP("act", lambda e: e.activation(out=PT[:, q0:512], in_=ps_[:, q0:512], func=AF.Exp, scale=0.125,
                                                     bias=ncum[:, j, h:h + 1]), reads=[psb_, nc_b[j // 4]], writes=[PT.b])
                    pts[t] = (PT, q0)

                def emit_PV(t):
                    h, j = tasks[t]
                    if j == 0:
                        pos[h] = PS(hold=True)
                    po, pob = pos[h]
                    PT, q0 = pts.pop(t)
                    MM(po[0:65, q0:512], Va[:, j, h, :], PT[:, q0:512], j == 0, j == nkb - 1, [Va_b[j // 4], PT.b], pob)
                    if j == nkb - 1:
                        u = unit_ctr[0]
                        unit_ctr[0] += 1
                        o_ = oT[u % 4]
                        yb_ = ybh[u % 4]
                        OP("dve", lambda e: e.tensor_scalar(out=o_[:], in0=po[0:65, :], scalar1=1.0, scalar2=None, op0=ALU.mult),
                           reads=[pob], writes=[o_.b])
                        PSREL(po)
                        OP("dve", lambda e: e.reciprocal(out=o_[64:65, :], in_=o_[64:65, :]), reads=[o_.b], writes=[o_.b])

                        def tail(h=h, o_=o_, yb_=yb_):
                            pr_, prb_ = PS()
                            MM(pr_[0:64, :], onesq[64:65, 0:64], o_[64:65, :], True, True, [onesq.b, o_.b], prb_)
                            OP("dve", lambda e: e.tensor_tensor(out=yb_[:], in0=pr_[0:64, :], in1=o_[0:64, :], op=ALU.mult),
                               reads=[prb_, o_.b], writes=[yb_.b])
                            DMA("sp", ybT_d[h * 64:(h + 1) * 64, i * 512:(i + 1) * 512], yb_[:], reads=[yb_.b], writes=[B_ybT])
                        tails.append((t + 11, tail))

                for t in range(min(LOOK, NTK)):
                    emit_S(t)
                for t in range(NTK):
                    if t + LOOK < NTK:
                        emit_S(t + LOOK)
                    emit_PV(t)
                    while tails and tails[0][0] <= t:
                        tails.pop(0)[1]()
                    yield
                while tails:
                    tails.pop(0)[1]()

            for _ in front(0):
                pass
            for i in range(NT):
                gA = attention(i)
                gF = front(i + 1) if i + 1 < NT else None
                ntk = 8 * (4 * i + 4)
                every = max(1, ntk // 28)
                cnt_ = 0
                for _ in gA:
                    cnt_ += 1
                    if gF is not None and cnt_ % every == 0:
                        try:
                            next(gF)
                        except StopIteration:
                            gF = None
                if gF is not None:
                    for _ in gF:
                        pass
        sy.barrier()

        CP('C1')
        with ExitStack() as sc:
            Wg = SB(sc, "Wg", [128, 8, 2048], BF16)
            Wg_b = [Buf("Wg%d" % k) for k in range(8)]
            Wor = SB(sc, "Wor", [128, 4, D], BF16)
            Wof = SB(sc, "Wof", [128, 4, D], BF16)
            Wout = SB(sc, "Wout", [128, 8, D], BF16)
            gtb = SB(sc, "gtb", [128, D], F32)
            hTc = [SB(sc, "hTc%d" % i, [128, 8, 512], BF16) for i in range(2)]
            yaTt = [SB(sc, "yaTt%d" % i, [128, 4, 512], BF16) for i in range(2)]
            ybTt = [SB(sc, "ybTt%d" % i, [128, 4, 512], BF16) for i in range(2)]
            Gsig2 = [SB(sc, "Gsig%d" % i_, [128, 16, 512], BF16) for i_ in range(2)]
            mrg = SB(sc, "mrg", [128, 8, 512], BF16)
            tm1 = [SB(sc, "tm1_%d" % i, [128, 512], F32) for i in range(2)]
            tm2 = [SB(sc, "tm2_%d" % i, [128, 512], F32) for i in range(2)]
            tm3c = [SB(sc, "tm3c_%d" % i, [128, 512], F32) for i in range(2)]
            xs = [SB(sc, "xs%d" % i, [128, D], F32) for i in range(3)]
            x1s = [SB(sc, "x1s%d" % i, [128, D], F32) for i in range(3)]
            xn2 = [SB(sc, "xn2_%d" % i, [128, D], BF16) for i in range(3)]
            h2t = [SB(sc, "h2t%d" % i, [128, 8, 512], BF16) for i in range(2)]
            junk1 = SB(sc, "junk1", [128, D], BF16)
            st2 = [SB(sc, "st2_%d" % i, [128, 2], F32) for i in range(3)]
            for kc in range(8):
                DMA("pool", Wg[:, kc, :], win_d[kc * 128:(kc + 1) * 128, NRW + NFX:NIN], writes=[Wg_b[kc]])
            DMA("pool", Wor[:], wor_d.rearrange("(c p) n -> p c n", p=128), writes=[Wor.b])
            DMA("pool", Wof[:], wof_d.rearrange("(c p) n -> p c n", p=128), writes=[Wof.b])
            DMA("pool", Wout[:], wout_d.rearrange("(c p) n -> p c n", p=128), writes=[Wout.b])
            DMA("sp", gtb[:], gt_d[:, 0:D], reads=[B_gt], writes=[gtb.b])
            ybT_v = ybT_d.rearrange("(c p) s -> p c s", p=128)

            def load_tile(i):
                tsl_ = slice(i * 512, (i + 1) * 512)
                DMA("sp", hTc[i % 2][:], hT_d[:, :, tsl_], reads=[B_hT], writes=[hTc[i % 2].b])
                DMA("sp", yaTt[i % 2][:], yaT_d[:, :, tsl_], reads=[B_yaT], writes=[yaTt[i % 2].b])
                DMA("sp", ybTt[i % 2][:], ybT_v[:, :, tsl_], reads=[B_ybT], writes=[ybTt[i % 2].b])

            def load_x(k_):
                if k_ < NS:
                    DMA("sp", xs[k_ % 3][:], x_d[k_ * 128:(k_ + 1) * 128, :], writes=[xs[k_ % 3].b])

            def gates(i, part):
                if i >= NT:
                    return
                ht_, G_ = hTc[i % 2], Gsig2[i % 2]
                for g in range(part * 8, part * 8 + 8):
                    pg2, pg2b = PS()
                    for kc in range(8):
                        MM(pg2[:, :], Wg[:, kc, g * 128:(g + 1) * 128], ht_[:, kc, :], kc == 0, kc == 7, [Wg_b[kc], ht_.b], pg2b)
                    OP("act", lambda e, g=g, pg2=pg2: e.activation(out=G_[:, g, :], in_=pg2[:, :], func=AF.Sigmoid),
                       reads=[pg2b], writes=[G_.b])

            h2_b = [[Buf("h2_%d_%d" % (i_, k_)) for k_ in range(8)] for i_ in range(2)]
            load_tile(0)
            load_x(0)
            gates(0, 0)
            gates(0, 1)
            for i in range(NT):
                ht, ya_t, yb_t, h2 = hTc[i % 2], yaTt[i % 2], ybTt[i % 2], h2t[i % 2]
                Gsig = Gsig2[i % 2]
                tsl = slice(i * 512, (i + 1) * 512)
                if i + 1 < NT:
                    load_tile(i + 1)
                for m in range(8):
                    pa2, pa2b = PS()
                    for c in range(4):
                        MM(pa2[:, :], Wor[:, c, m * 128:(m + 1) * 128], ya_t[:, c, :], c == 0, c == 3, [Wor.b, ya_t.b], pa2b)
                    pb2, pb2b = PS()
                    for c in range(4):
                        MM(pb2[:, :], Wof[:, c, m * 128:(m + 1) * 128], yb_t[:, c, :], c == 0, c == 3, [Wof.b, yb_t.b], pb2b)
                    t1_, t2_ = tm1[m % 2], tm2[m % 2]
                    OP("dve", lambda e, m=m, pa2=pa2, t1_=t1_: e.tensor_tensor(out=t1_[:], in0=pa2[:, :], in1=Gsig[:, m, :], op=ALU.mult),
                       reads=[pa2b, Gsig.b], writes=[t1_.b])
                    OP("dve", lambda e, m=m, pb2=pb2, t2_=t2_: e.tensor_tensor(out=t2_[:], in0=pb2[:, :], in1=Gsig[:, 8 + m, :], op=ALU.mult),
                       reads=[pb2b, Gsig.b], writes=[t2_.b])
                    OP("pool", lambda e, m=m, t1_=t1_, t2_=t2_: e.tensor_tensor(out=mrg[:, m, :], in0=t1_[:], in1=t2_[:], op=ALU.add),
                       reads=[t1_.b, t2_.b], writes=[mrg.b])
                gates(i + 1, 0)
                pT2 = [PS(hold=True) for _ in range(4)]

                def z_part(sub):
                    k_ = i * 4 + sub
                    ts_ = slice(sub * 128, (sub + 1) * 128)
                    xt, x1t, xnt, s2 = xs[k_ % 3], x1s[k_ % 3], xn2[k_ % 3], st2[k_ % 3]
                    load_x(k_ + 1)
                    for half in range(2):
                        hsl_ = slice(half * 512, (half + 1) * 512)
                        pz2, pz2b = PS()
                        for c in range(8):
                            MM(pz2[:, :], mrg[:, c, ts_], Wout[:, c, hsl_], c == 0, c == 7, [mrg.b, Wout.b], pz2b)
                        t1_ = tm3c[half]
                        OP("dve", lambda e, pz2=pz2, t1_=t1_, hsl_=hsl_: e.tensor_tensor(out=t1_[:], in0=pz2[:, :], in1=gtb[:, hsl_], op=ALU.mult),
                           reads=[pz2b, gtb.b], writes=[t1_.b])
                        OP("dve", lambda e, t1_=t1_, hsl_=hsl_, xt=xt, x1t=x1t: e.tensor_tensor(out=x1t[:, hsl_], in0=t1_[:], in1=xt[:, hsl_], op=ALU.add),
                           reads=[t1_.b, xt.b], writes=[x1t.b])
                    DMA("sp", x1_d[i * 512 + sub * 128: i * 512 + (sub + 1) * 128, :], x1t[:], reads=[x1t.b], writes=[B_x1])
                    OP("act", lambda e, x1t=x1t, s2=s2: e.activation(out=junk1[:], in_=x1t[:], func=AF.Square, accum_out=s2[:, 0:1]),
                       reads=[x1t.b], writes=[junk1.b, s2.b])
                    OP("dve", lambda e, s2=s2: e.tensor_scalar(out=s2[:, 1:2], in0=s2[:, 0:1], scalar1=1.0 / D, scalar2=NORM_EPS,
                                                               op0=ALU.mult, op1=ALU.add), reads=[s2.b], writes=[s2.b])
                    OP("act", lambda e, s2=s2: e.activation(out=s2[:, 1:2], in_=s2[:, 1:2], func=AF.Sqrt), reads=[s2.b], writes=[s2.b])
                    OP("dve", lambda e, s2=s2: e.reciprocal(out=s2[:, 1:2], in_=s2[:, 1:2]), reads=[s2.b], writes=[s2.b])
                    OP("act", lambda e, x1t=x1t, xnt=xnt, s2=s2: e.activation(out=xnt[:], in_=x1t[:], func=AF.Copy, scale=s2[:, 1:2]),
                       reads=[x1t.b, s2.b], writes=[xnt.b])

                def t_part(sub):
                    k_ = i * 4 + sub
                    xnt = xn2[k_ % 3]
                    for kc in range(8):
                        pv3 = pT2[kc // 2][0].bitcast(BF16)
                        TR(pv3[:, (kc % 2) * 512 + sub * 128:(kc % 2) * 512 + (sub + 1) * 128], xnt[:, kc * 128:(kc + 1) * 128], ident[:],
                           [xnt.b, ident.b], pT2[kc // 2][1])

                z_part(0)
                z_part(1)
                for sub in range(4):
                    if sub + 2 < 4:
                        z_part(sub + 2)
                    if sub == 1:
                        gates(i + 1, 1)
                    t_part(sub)
                for hb in range(4):
                    pv3 = pT2[hb][0].bitcast(BF16)
                    for kq in range(2):
                        kc = hb * 2 + kq
                        if kc % 2 == 0:
                            OP("dve", lambda e, kc=kc, kq=kq, pv3=pv3: e.tensor_scalar(
                                out=h2[:, kc, :], in0=pv3[:, kq * 512:(kq + 1) * 512], scalar1=GS[:, 16 + kc:17 + kc],
                                scalar2=GS[:, 24 + kc:25 + kc], op0=ALU.mult, op1=ALU.add), reads=[pT2[hb][1], GS.b], writes=[h2_b[i % 2][kc]])
                        else:
                            OP("act", lambda e, kc=kc, kq=kq, pv3=pv3: e.activation(
                                out=h2[:, kc, :], in_=pv3[:, kq * 512:(kq + 1) * 512], func=AF.Identity, scale=GS[:, 16 + kc:17 + kc],
                                bias=GS[:, 24 + kc:25 + kc]), reads=[pT2[hb][1], GS.b], writes=[h2_b[i % 2][kc]])
                for hb in range(4):
                    PSREL(pT2[hb][0])
                DMA("sp", h2T_d[:, :, tsl], h2[:], reads=h2_b[i % 2], writes=[B_h2T])
        sy.barrier()

        CP('C2')
        with ExitStack() as s2c:
            W1 = SB(s2c, "W1", [128, 8, 4 * D], BF16)
            W1_b = [Buf("W1_%d" % k) for k in range(8)]
            W2 = SB(s2c, "W2", [128, 32, D], BF16)
            W2_b = [Buf("W2_%d" % k) for k in range(4)]
            gt2 = SB(s2c, "gt2", [128, D], F32)
            fgb = SB(s2c, "fgb", [128, D], F32)
            h2c = [SB(s2c, "h2c%d" % i, [128, 8, 256], BF16) for i in range(2)]
            hid = SB(s2c, "hid", [128, 32, 256], BF16)
            rl = [SB(s2c, "rl%d" % i, [128, 512], F32) for i in range(2)]
            x1c = [SB(s2c, "x1c%d" % i, [128, D], F32) for i in range(2)]
            x2c = [SB(s2c, "x2c%d" % i, [128, D], F32) for i in range(2)]
            oc = [SB(s2c, "oc%d" % i, [128, D], F32) for i in range(2)]
            tm3 = [SB(s2c, "tm3_%d" % i, [128, 512], F32) for i in range(2)]
            junk2 = SB(s2c, "junk2", [128, D], BF16)
            st3 = [SB(s2c, "st3_%d" % i, [128, 2], F32) for i in range(2)]
            for kc in range(8):
                DMA("pool", W1[:, kc, :], wff1_d[kc * 128:(kc + 1) * 128, :], writes=[W1_b[kc]])
            for q4 in range(4):
                DMA("pool", W2[:, q4 * 8:(q4 + 1) * 8, :],
                    wff2_d[q4 * 1024:(q4 + 1) * 1024, :].rearrange("(k p) n -> p k n", p=128), writes=[W2_b[q4]])
            DMA("sp", gt2[:], gt_d[:, D:2 * D], reads=[B_gt], writes=[gt2.b])
            DMA("sp", fgb[:], bc_d[:, BC_FG:BC_FG + D], writes=[fgb.b])
            def load_h2(i2):
                if i2 < S // 256:
                    DMA("sp", h2c[i2 % 2][:], h2T_d[:, :, i2 * 256:(i2 + 1) * 256], reads=[B_h2T], writes=[h2c[i2 % 2].b])

            def load_x1(k_):
                if k_ < NS:
                    DMA("sp", x1c[k_ % 2][:], x1_d[k_ * 128:(k_ + 1) * 128, :], reads=[B_x1], writes=[x1c[k_ % 2].b])

            load_h2(0)
            load_x1(0)
            for i2 in range(S // 256):
                hc = h2c[i2 % 2]
                load_h2(i2 + 1)
                for f2 in range(16):
                    pf2, pf2b = PS()
                    for fh in range(2):
                        f = f2 * 2 + fh
                        for kc in range(8):
                            MM(pf2[:, fh * 256:(fh + 1) * 256], W1[:, kc, f * 128:(f + 1) * 128], hc[:, kc, :], kc == 0, kc == 7,
                               [W1_b[kc], hc.b], pf2b)
                    r_ = rl[f2 % 2]
                    OP("act", lambda e, pf2=pf2, r_=r_: e.activation(out=r_[:], in_=pf2[:, :], func=AF.Relu), reads=[pf2b], writes=[r_.b])
                    OP("pool", lambda e, f2=f2, r_=r_: e.tensor_tensor(out=hid[:, f2 * 2:f2 * 2 + 2, :].rearrange("p a t -> p (a t)"),
                                                                       in0=r_[:], in1=r_[:], op=ALU.mult), reads=[r_.b], writes=[hid.b])
                for sub in range(2):
                    k_ = i2 * 2 + sub
                    r0 = i2 * 256 + sub * 128
                    ts_ = slice(sub * 128, (sub + 1) * 128)
                    x1t, x2t, ot, s3 = x1c[k_ % 2], x2c[k_ % 2], oc[k_ % 2], st3[k_ % 2]
                    load_x1(k_ + 1)
                    for half in range(2):
                        hsl_ = slice(half * 512, (half + 1) * 512)
                        po2, po2b = PS()
                        for kk_ in range(32):
                            MM(po2[:, :], hid[:, kk_, ts_], W2[:, kk_, hsl_], kk_ == 0, kk_ == 31, [hid.b, W2_b[kk_ // 8]], po2b)
                        t3 = tm3[half]
                        OP("dve", lambda e, po2=po2, t3=t3, hsl_=hsl_: e.tensor_tensor(out=t3[:], in0=po2[:, :], in1=gt2[:, hsl_], op=ALU.mult),
                           reads=[po2b, gt2.b], writes=[t3.b])
                        OP("dve", lambda e, t3=t3, hsl_=hsl_, x1t=x1t, x2t=x2t: e.tensor_tensor(out=x2t[:, hsl_], in0=t3[:], in1=x1t[:, hsl_], op=ALU.add),
                           reads=[t3.b, x1t.b], writes=[x2t.b])
                    OP("act", lambda e, x2t=x2t, s3=s3: e.activation(out=junk2[:], in_=x2t[:], func=AF.Square, accum_out=s3[:, 0:1]),
                       reads=[x2t.b], writes=[junk2.b, s3.b])
                    OP("dve", lambda e, s3=s3: e.tensor_scalar(out=s3[:, 1:2], in0=s3[:, 0:1], scalar1=1.0 / D, scalar2=NORM_EPS,
                                                               op0=ALU.mult, op1=ALU.add), reads=[s3.b], writes=[s3.b])
                    OP("act", lambda e, s3=s3: e.activation(out=s3[:, 1:2], in_=s3[:, 1:2], func=AF.Sqrt), reads=[s3.b], writes=[s3.b])
                    OP("dve", lambda e, s3=s3: e.reciprocal(out=s3[:, 1:2], in_=s3[:, 1:2]), reads=[s3.b], writes=[s3.b])
                    OP("dve", lambda e, x2t=x2t, ot=ot, s3=s3: e.scalar_tensor_tensor(out=ot[:], in0=x2t[:], scalar=s3[:, 1:2], in1=fgb[:],
                                                                                     op0=ALU.mult, op1=ALU.mult),
                       reads=[x2t.b, s3.b, fgb.b], writes=[ot.b])
                    DMA("sp", out_d[r0:r0 + 128, :], ot[:], reads=[ot.b], writes=[B_out])
        sy.barrier()
        sy.drain("sp")
    return nc


def _pack_inputs(inp, b):
    f = lambda a: np.ascontiguousarray(np.asarray(a, dtype=np.float32))
    pp = np.zeros((128, NPP), np.float32)
    pp[:, PP_BADA:PP_BADA + 48] = f(inp["b_ada"])[0].reshape(48, 128).T
    pp[:, PP_G1:PP_G1 + 8] = f(inp["norm1_g"])[0].reshape(8, 128).T
    pp[:, PP_G2:PP_G2 + 8] = f(inp["norm2_g"])[0].reshape(8, 128).T
    pp[:, PP_C:PP_C + 8] = f(inp["c"])[b].reshape(8, 128).T
    pp[:, PP_DB:PP_DB + 4] = f(inp["decay_base"])[0].reshape(4, 128).T
    pp[:, PP_IB:PP_IB + 4] = f(inp["iclr_base"])[0].reshape(4, 128).T
    pp[:, PP_KS:PP_KS + 4] = f(inp["kk_scale"])[0].reshape(4, 128).T
    pp[:, PP_MIX:PP_MIX + 4] = f(inp["k_iclr_mix"])[0].reshape(4, 128).T
    pp[:, PP_RB:PP_RB + 4] = f(inp["r_bonus"])[0].reshape(512).reshape(4, 128).T
    mu = f(inp["mu_shift"])[0]
    mu_r = np.concatenate([mu[0:512], mu[576:1088], mu[1088:1600], mu[512:576], mu[1600:1664], mu[1664:1792]])
    row = np.concatenate([mu_r, f(inp["lnx_w"])[0], f(inp["lnx_b"])[0], f(inp["fox_f_bias"])[0], f(inp["final_g"])])
    bc = np.ascontiguousarray(np.broadcast_to(row[None, :], (128, NBC)))
    ba = f(inp["b_ada"])[0]
    brow = np.concatenate([ba[2 * D:3 * D], ba[5 * D:6 * D]])[None, :]
    return pp, bc, np.ascontiguousarray(brow)


_NC_CACHE = {}


def make_in_maps(inp, S, nb):
    shared = {
        "w_ada": np.ascontiguousarray(np.asarray(inp["w_ada"], np.float32)[0]),
        "w_in": np.ascontiguousarray(np.asarray(inp["w_in"], np.float32)[0]),
        "w_decay_up": np.ascontiguousarray(np.asarray(inp["w_decay_up"], np.float32)[0]),
        "w_iclr_up": np.ascontiguousarray(np.asarray(inp["w_iclr_up"], np.float32)[0]),
        "w_gate_up": np.ascontiguousarray(np.asarray(inp["w_gate_up"], np.float32)[0]),
        "w_o_rwkv": np.ascontiguousarray(np.asarray(inp["w_o_rwkv"], np.float32)[0]),
        "w_o_fox": np.ascontiguousarray(np.asarray(inp["w_o_fox"], np.float32)[0]),
        "w_out": np.ascontiguousarray(np.asarray(inp["w_out"], np.float32)[0]),
        "w_ff1": np.ascontiguousarray(np.asarray(inp["w_ff1"], np.float32)[0]),
        "w_ff2": np.ascontiguousarray(np.asarray(inp["w_ff2"], np.float32)[0]),
    }
    maps = []
    x = np.asarray(inp["x"], np.float32)
    for b in range(nb):
        pp, bc, brow = _pack_inputs(inp, b)
        m = dict(shared)
        m.update({"x": np.ascontiguousarray(x[b]), "pp": pp, "bc": bc, "brow": brow})
        maps.append(m)
    return maps


def kernel(**inputs):
    x = np.asarray(inputs["x"])
    nb, S = x.shape[0], x.shape[1]
    if S not in _NC_CACHE:
        _NC_CACHE[S] = build(S)
    nc = _NC_CACHE[S]
    maps = make_in_maps(inputs, S, nb)
    res = run_bass_kernel_spmd(nc, maps, core_ids=list(range(nb)))
    return np.stack([np.asarray(r["out"], np.float32) for r in res.results], axis=0)
```
